# Optimizing a Trainium2 kernel written in Bass

```python
import math
import jax
import jax.numpy as jnp
from jax import lax
import numpy as np

D_MODEL = 1024
BATCH = 8
SEQ = 4096
DEPTH = 2
DEC_BATCH = 128
DEC_SEQ = 4
PAST_LEN = 16384
PAGE_SIZE = 128

N_EVEN = (DEPTH + 1) // 2
N_ODD = DEPTH // 2

D_FF = 2816
A_CHUNK = 128
A_HEADS = 8
A_WIDTH = D_MODEL
A_HEAD_DIM = A_WIDTH // A_HEADS
B_HEADS = 16
B_HEAD_DIM = 64
B_INNER = B_HEADS * B_HEAD_DIM
B_GROUPS = 2
B_STATE = 128
B_CONV = 4
B_CHUNK = 128
B_CONV_DIM = B_INNER + 2 * B_GROUPS * B_STATE
C_WIDTH = D_MODEL
C_WINDOWS = (2, 4, 8, 16)
C_GROUPS = len(C_WINDOWS)
C_GROUP_DIM = C_WIDTH // C_GROUPS
C_STATE_LEN = max(C_WINDOWS) - 1
D_Q_HEADS = 16
D_KV_HEADS = 4
D_HEAD_DIM = 64
D_WINDOW = 128
REL_BUCKETS = 32
REL_MAX_DIST = 128

EVEN_IN = 2 * A_WIDTH + B_INNER + B_CONV_DIM + B_HEADS
EVEN_MIX = A_WIDTH + B_INNER
ODD_IN = C_WIDTH + (D_Q_HEADS + 2 * D_KV_HEADS) * D_HEAD_DIM
ODD_MIX = C_WIDTH + D_Q_HEADS * D_HEAD_DIM
EPS = 1e-6
NEG = -1e30

kernel_name = 'hybrid_gmlp_ssd_pool_swa_decoder_step'


def rms_norm(x, g):
    xf = x.astype(jnp.float32)
    y = xf * lax.rsqrt(jnp.mean(xf * xf, axis=-1, keepdims=True) + EPS)
    return (y * g.astype(jnp.float32)).astype(x.dtype)


def layer_norm(x, g, b):
    xf = x.astype(jnp.float32)
    mu = jnp.mean(xf, axis=-1, keepdims=True)
    var = jnp.mean(jnp.square(xf - mu), axis=-1, keepdims=True)
    y = (xf - mu) * lax.rsqrt(var + EPS) * g.astype(jnp.float32) + b.astype(jnp.float32)
    return y.astype(x.dtype)


def swiglu(x, w_gu, w_down):
    gate, up = jnp.split(x @ w_gu, 2, axis=-1)
    return (jax.nn.silu(gate) * up) @ w_down


def macaron_half(h, g, w_gu, w_down):
    return h + 0.5 * swiglu(rms_norm(h, g), w_gu, w_down)


def t5_bucket(dist):
    n = np.maximum(dist, 0)
    max_exact = REL_BUCKETS // 2
    n_safe = np.maximum(n, 1).astype(np.float32)
    scale = np.float32((REL_BUCKETS - max_exact) / math.log(REL_MAX_DIST / max_exact))
    large = max_exact + (np.log(n_safe / max_exact) * scale).astype(np.int32)
    large = np.minimum(large, REL_BUCKETS - 1)
    return np.where(n < max_exact, n, large).astype(np.int32)


def band_bias(rel_table, q_pos, k_pos):
    bucket = t5_bucket(q_pos[:, None] - k_pos[None, :])
    return jnp.transpose(rel_table[bucket].astype(jnp.float32), (2, 0, 1))


def mixer_a(proj, ln_g, ln_b, w_s, b_s, n_chunk):
    b, L, _ = proj.shape
    u, v = jnp.split(jax.nn.gelu(proj), 2, axis=-1)
    v = layer_norm(v, ln_g, ln_b)
    shp = (b, L // n_chunk, n_chunk, A_HEADS, A_HEAD_DIM)
    causal = np.tril(np.ones((n_chunk, n_chunk), dtype=bool))
    w = jnp.where(causal, w_s[:, :n_chunk, :n_chunk], 0)
    gate = jnp.einsum('hij,bcjhe->bcihe', w, v.reshape(shp)) + b_s[:, :n_chunk].T[None, None, :, :, None]
    y = u.reshape(shp) * gate
    return y.reshape(b, L, A_WIDTH), v


def short_conv(xbc, conv_state, w, bias):
    L = xbc.shape[1]
    ext = jnp.concatenate([conv_state.astype(xbc.dtype), xbc], axis=1)
    out = bias
    for tap in range(B_CONV):
        out = out + ext[:, tap:tap + L] * w[tap]
    return jax.nn.silu(out), ext[:, ext.shape[1] - (B_CONV - 1):]


def ssd_scan(x, dt, a, bm, cm, init_state, chunk):
    f32 = jnp.float32
    b, L, H, P = x.shape
    G, N = bm.shape[2], bm.shape[3]
    R = H // G
    nc = L // chunk
    xd = (x.astype(f32) * dt[..., None]).reshape(b, nc, chunk, G, R, P)
    da = (dt * a).reshape(b, nc, chunk, G, R)
    bmc = bm.astype(f32).reshape(b, nc, chunk, G, N)
    cmc = cm.astype(f32).reshape(b, nc, chunk, G, N)
    a_cum = jnp.cumsum(da, axis=2)
    seg = a_cum[:, :, :, None] - a_cum[:, :, None, :]
    causal = np.tril(np.ones((chunk, chunk), dtype=bool))[None, None, :, :, None, None]
    lmat = jnp.exp(jnp.where(causal, seg, -jnp.inf))
    cb = jnp.einsum('bcign,bcjgn->bcijg', cmc, bmc)
    y_diag = jnp.einsum('bcijg,bcijgr,bcjgrp->bcigrp', cb, lmat, xd)
    decay = jnp.exp(a_cum[:, :, -1:] - a_cum)
    states = jnp.einsum('bcjgn,bcjgr,bcjgrp->bcgrpn', bmc, decay, xd)
    chunk_decay = jnp.exp(a_cum[:, :, -1])
    s0 = init_state.astype(f32).reshape(b, G, R, P, N)

    def step(s, inp):
        st, dec = inp
        return s * dec[..., None, None] + st, s

    final, prev = lax.scan(step, s0, (jnp.moveaxis(states, 1, 0), jnp.moveaxis(chunk_decay, 1, 0)))
    prev = jnp.moveaxis(prev, 0, 1)
    y_off = jnp.einsum('bcign,bcgrpn,bcigr->bcigrp', cmc, prev, jnp.exp(a_cum))
    y = (y_diag + y_off).reshape(b, L, H, P)
    return y, final.reshape(b, H, P, N)


def mixer_b(proj, conv_state, ssm_state, conv_w, conv_b, dt_bias, a_log, d_skip, norm_g, chunk):
    f32 = jnp.float32
    b, L, _ = proj.shape
    z = proj[..., :B_INNER]
    xbc, new_conv = short_conv(proj[..., B_INNER:B_INNER + B_CONV_DIM], conv_state, conv_w, conv_b)
    dt_raw = proj[..., B_INNER + B_CONV_DIM:]
    gn = B_GROUPS * B_STATE
    xs = xbc[..., :B_INNER].reshape(b, L, B_HEADS, B_HEAD_DIM)
    bm = xbc[..., B_INNER:B_INNER + gn].reshape(b, L, B_GROUPS, B_STATE)
    cm = xbc[..., B_INNER + gn:].reshape(b, L, B_GROUPS, B_STATE)
    dt = jax.nn.softplus(dt_raw.astype(f32) + dt_bias.astype(f32))
    a = -jnp.exp(a_log.astype(f32))
    y, new_ssm = ssd_scan(xs, dt, a, bm, cm, ssm_state, chunk)
    y = y + xs.astype(f32) * d_skip.astype(f32)[:, None]
    y = y.reshape(b, L, B_INNER) * jax.nn.silu(z.astype(f32))
    yg = y.reshape(b, L, B_GROUPS, B_INNER // B_GROUPS)
    yg = yg * lax.rsqrt(jnp.mean(yg * yg, axis=-1, keepdims=True) + EPS)
    y = yg.reshape(b, L, B_INNER) * norm_g.astype(f32)
    return y.astype(proj.dtype), new_conv, new_ssm.astype(ssm_state.dtype)


def mixer_c(c_in, pool_state, start_pos, lin_w, scale):
    f32 = jnp.float32
    b, L, _ = c_in.shape
    p0 = pool_state.shape[1]
    ext = jnp.concatenate([pool_state.astype(f32), c_in.astype(f32)], axis=1)
    csum = jnp.concatenate([jnp.zeros((b, 1, C_WIDTH), f32), jnp.cumsum(ext, axis=1)], axis=1)
    hi = np.arange(p0, p0 + L) + 1
    pos = start_pos + np.arange(L)
    cur = ext[:, p0:]
    outs = []
    for gi, win in enumerate(C_WINDOWS):
        sl = slice(gi * C_GROUP_DIM, (gi + 1) * C_GROUP_DIM)
        lo = np.maximum(hi - win, 0)
        count = np.minimum(pos + 1, win).astype(np.float32)[None, :, None]
        pooled = (csum[:, hi, sl] - csum[:, lo, sl]) / count - cur[..., sl]
        outs.append(jnp.einsum('blc,cd->bld', pooled, lin_w[gi].astype(f32)))
    y = jnp.concatenate(outs, axis=-1) * scale.astype(f32)
    return y.astype(c_in.dtype), ext[:, ext.shape[1] - C_STATE_LEN:].astype(c_in.dtype)


def d_qkv(proj, q_norm, k_norm):
    b, L, _ = proj.shape
    qw = D_Q_HEADS * D_HEAD_DIM
    kw = D_KV_HEADS * D_HEAD_DIM
    q = proj[..., :qw].reshape(b, L, D_KV_HEADS, D_Q_HEADS // D_KV_HEADS, D_HEAD_DIM)
    k = proj[..., qw:qw + kw].reshape(b, L, D_KV_HEADS, D_HEAD_DIM)
    v = proj[..., qw + kw:].reshape(b, L, D_KV_HEADS, D_HEAD_DIM)
    return rms_norm(q, q_norm), rms_norm(k, k_norm), v


def sink_attention(q, k, v, bias, valid, sinks):
    f32 = jnp.float32
    grp, lq, lk = q.shape[3], q.shape[1], k.shape[1]
    s = jnp.einsum('bqhgd,bkhd->bhgqk', q.astype(f32), k.astype(f32)) * (D_HEAD_DIM ** -0.5)
    s = jnp.where(valid, s + bias.reshape(D_KV_HEADS, grp, lq, lk), NEG)
    sink = sinks.astype(f32).reshape(1, D_KV_HEADS, grp, 1, 1)
    m = jnp.maximum(jnp.max(s, axis=-1, keepdims=True), sink)
    p = jnp.exp(s - m)
    denom = jnp.sum(p, axis=-1, keepdims=True) + jnp.exp(sink - m)
    o = jnp.einsum('bhgqk,bkhd->bqhgd', p / denom, v.astype(f32))
    return o.astype(v.dtype)


def swa_prompt(q, k, v, rel_table, sinks):
    b, S = q.shape[0], q.shape[1]
    W = D_WINDOW
    kp = jnp.pad(k, ((0, 0), (W, 0), (0, 0), (0, 0)))
    vp = jnp.pad(v, ((0, 0), (W, 0), (0, 0), (0, 0)))
    r = np.arange(W) + W
    c = np.arange(2 * W)
    bias = band_bias(rel_table, r, c)
    dist = r[:, None] - c[None, :]
    band = (dist >= 0) & (dist < W)

    def block(i):
        start = i * W
        qb = lax.dynamic_slice_in_dim(q, start, W, axis=1)
        kb = lax.dynamic_slice_in_dim(kp, start, 2 * W, axis=1)
        vb = lax.dynamic_slice_in_dim(vp, start, 2 * W, axis=1)
        valid = band & ((start - W + c) >= 0)[None, :]
        return sink_attention(qb, kb, vb, bias, valid, sinks)

    o = lax.map(block, jnp.arange(S // W))
    return jnp.moveaxis(o, 0, 1).reshape(b, S, D_Q_HEADS * D_HEAD_DIM)


def swa_sample(q, k, v, k_buf, v_buf, rel_table, sinks, start_pos):
    b, L = q.shape[0], q.shape[1]
    nbuf = k_buf.shape[1]
    kk = jnp.concatenate([k_buf.astype(k.dtype), k], axis=1)
    vv = jnp.concatenate([v_buf.astype(v.dtype), v], axis=1)
    q_pos = start_pos + np.arange(L)
    k_pos = start_pos - nbuf + np.arange(nbuf + L)
    bias = band_bias(rel_table, q_pos, k_pos)
    dist = q_pos[:, None] - k_pos[None, :]
    valid = (dist >= 0) & (dist < D_WINDOW) & (k_pos >= 0)[None, :]
    o = sink_attention(q, kk, vv, bias, valid, sinks)
    return o.reshape(b, L, D_Q_HEADS * D_HEAD_DIM), kk[:, L:], vv[:, L:]


def even_mixer(xn, w_in, w_out, a_ln_g, a_ln_b, a_w_s, a_b_s, conv_w, conv_b, dt_bias, a_log, d_skip,
               b_norm_g, conv_state, ssm_state, a_chunk, b_chunk):
    proj = xn @ w_in
    ya, v_rows = mixer_a(proj[..., :2 * A_WIDTH], a_ln_g, a_ln_b, a_w_s, a_b_s, a_chunk)
    yb, new_conv, new_ssm = mixer_b(proj[..., 2 * A_WIDTH:], conv_state, ssm_state, conv_w, conv_b,
                                    dt_bias, a_log, d_skip, b_norm_g, b_chunk)
    y = jnp.concatenate([ya, yb.astype(ya.dtype)], axis=-1) @ w_out
    return y, v_rows, new_conv, new_ssm


def odd_mixer(xn, w_in, w_out, c_lin_w, c_scale, q_norm, k_norm, sinks, rel_table, pool_state, k_buf,
              v_buf, start_pos):
    proj = xn @ w_in
    yc, new_pool = mixer_c(proj[..., :C_WIDTH], pool_state, start_pos, c_lin_w, c_scale)
    q, k, v = d_qkv(proj[..., C_WIDTH:], q_norm, k_norm)
    if k_buf is None:
        yd = swa_prompt(q, k, v, rel_table, sinks)
        new_k = k[:, k.shape[1] - D_WINDOW:]
        new_v = v[:, v.shape[1] - D_WINDOW:]
    else:
        yd, new_k, new_v = swa_sample(q, k, v, k_buf, v_buf, rel_table, sinks, start_pos)
    y = jnp.concatenate([yc, yd.astype(yc.dtype)], axis=-1) @ w_out
    return y, new_pool, new_k, new_v


def setup_inputs(seed: int = 0) -> dict:
    key = jax.random.key(seed)
    it = iter(list(jax.random.split(key, 40)))
    f32 = jnp.float32

    def nrm(shape, scale):
        return jax.random.normal(next(it), shape, f32) * scale

    def gain(shape):
        return 1.0 + 0.1 * jax.random.normal(next(it), shape, f32)

    win_buf = min(D_WINDOW, PAST_LEN)
    inp = {}
    inp['x_prompt'] = nrm((BATCH, SEQ, D_MODEL), 1.0)
    inp['x_sample'] = nrm((DEC_BATCH, DEC_SEQ, D_MODEL), 1.0)
    inp['state_ssm'] = nrm((N_EVEN, DEC_BATCH, B_HEADS, B_HEAD_DIM, B_STATE), 0.5)
    inp['state_conv'] = nrm((N_EVEN, DEC_BATCH, B_CONV - 1, B_CONV_DIM), 1.0)
    inp['state_pool'] = nrm((N_ODD, DEC_BATCH, C_STATE_LEN, C_WIDTH), 1.0)
    inp['cache_k_win'] = nrm((N_ODD, DEC_BATCH, win_buf, D_KV_HEADS, D_HEAD_DIM), 1.0)
    inp['cache_v_win'] = nrm((N_ODD, DEC_BATCH, win_buf, D_KV_HEADS, D_HEAD_DIM), 1.0)
    inp['ffn1_norm'] = gain((DEPTH, D_MODEL))
    inp['ffn1_w_gu'] = nrm((DEPTH, D_MODEL, 2 * D_FF), D_MODEL ** -0.5)
    inp['ffn1_w_down'] = nrm((DEPTH, D_FF, D_MODEL), D_FF ** -0.5)
    inp['mix_norm'] = gain((DEPTH, D_MODEL))
    inp['ffn2_norm'] = gain((DEPTH, D_MODEL))
    inp['ffn2_w_gu'] = nrm((DEPTH, D_MODEL, 2 * D_FF), D_MODEL ** -0.5)
    inp['ffn2_w_down'] = nrm((DEPTH, D_FF, D_MODEL), D_FF ** -0.5)
    inp['ev_w_in'] = nrm((N_EVEN, D_MODEL, EVEN_IN), D_MODEL ** -0.5)
    inp['ev_w_out'] = nrm((N_EVEN, EVEN_MIX, D_MODEL), EVEN_MIX ** -0.5)
    inp['a_ln_g'] = gain((N_EVEN, A_WIDTH))
    inp['a_ln_b'] = nrm((N_EVEN, A_WIDTH), 0.02)
    inp['a_w_s'] = nrm((N_EVEN, A_HEADS, A_CHUNK, A_CHUNK), A_CHUNK ** -0.5)
    inp['a_b_s'] = gain((N_EVEN, A_HEADS, A_CHUNK))
    inp['b_conv_w'] = nrm((N_EVEN, B_CONV, B_CONV_DIM), B_CONV ** -0.5)
    inp['b_conv_b'] = nrm((N_EVEN, B_CONV_DIM), 0.02)
    dt0 = jnp.exp(jax.random.uniform(next(it), (N_EVEN, B_HEADS), f32, math.log(1e-3), math.log(1e-1)))
    inp['b_dt_bias'] = dt0 + jnp.log(-jnp.expm1(-dt0))
    inp['b_a_log'] = jnp.log(jax.random.uniform(next(it), (N_EVEN, B_HEADS), f32, 1.0, 16.0))
    inp['b_d_skip'] = gain((N_EVEN, B_HEADS))
    inp['b_norm_g'] = gain((N_EVEN, B_INNER))
    inp['od_w_in'] = nrm((N_ODD, D_MODEL, ODD_IN), D_MODEL ** -0.5)
    inp['od_w_out'] = nrm((N_ODD, ODD_MIX, D_MODEL), ODD_MIX ** -0.5)
    inp['c_lin_w'] = nrm((N_ODD, C_GROUPS, C_GROUP_DIM, C_GROUP_DIM), C_GROUP_DIM ** -0.5)
    inp['c_scale'] = gain((N_ODD, C_WIDTH))
    inp['d_q_norm'] = gain((N_ODD, D_HEAD_DIM))
    inp['d_k_norm'] = gain((N_ODD, D_HEAD_DIM))
    inp['d_sinks'] = nrm((N_ODD, D_Q_HEADS), 0.5)
    inp['rel_bias_table'] = nrm((REL_BUCKETS, D_Q_HEADS), 0.5)
    return inp


def reference(x_prompt, x_sample, state_ssm, state_conv, state_pool, cache_k_win, cache_v_win,
              ffn1_norm, ffn1_w_gu, ffn1_w_down, mix_norm, ffn2_norm, ffn2_w_gu, ffn2_w_down,
              ev_w_in, ev_w_out, a_ln_g, a_ln_b, a_w_s, a_b_s, b_conv_w, b_conv_b, b_dt_bias, b_a_log,
              b_d_skip, b_norm_g, od_w_in, od_w_out, c_lin_w, c_scale, d_q_norm, d_k_norm, d_sinks,
              rel_bias_table):
    hp, hs = x_prompt, x_sample
    bp = x_prompt.shape[0]
    new_a_v_s, new_ssm_p, new_ssm_s, new_conv_p, new_conv_s = [], [], [], [], []
    new_pool_p, new_pool_s, new_k_p, new_k_s, new_v_p, new_v_s = [], [], [], [], [], []
    for layer in range(DEPTH):
        i = layer // 2
        hp = macaron_half(hp, ffn1_norm[layer], ffn1_w_gu[layer], ffn1_w_down[layer])
        hs = macaron_half(hs, ffn1_norm[layer], ffn1_w_gu[layer], ffn1_w_down[layer])
        xp = rms_norm(hp, mix_norm[layer])
        xs = rms_norm(hs, mix_norm[layer])
        if layer % 2 == 0:
            ev = (ev_w_in[i], ev_w_out[i], a_ln_g[i], a_ln_b[i], a_w_s[i], a_b_s[i], b_conv_w[i],
                  b_conv_b[i], b_dt_bias[i], b_a_log[i], b_d_skip[i], b_norm_g[i])
            yp, _, conv_p, ssm_p = even_mixer(
                xp, *ev, jnp.zeros((bp, B_CONV - 1, B_CONV_DIM), xp.dtype),
                jnp.zeros((bp, B_HEADS, B_HEAD_DIM, B_STATE), jnp.float32), A_CHUNK, B_CHUNK)
            ys, v_rows_s, conv_s, ssm_s = even_mixer(
                xs, *ev, state_conv[i], state_ssm[i], xs.shape[1], xs.shape[1])
            new_a_v_s.append(v_rows_s)
            new_ssm_p.append(ssm_p)
            new_ssm_s.append(ssm_s)
            new_conv_p.append(conv_p)
            new_conv_s.append(conv_s)
        else:
            od = (od_w_in[i], od_w_out[i], c_lin_w[i], c_scale[i], d_q_norm[i], d_k_norm[i], d_sinks[i],
                  rel_bias_table)
            yp, pool_p, k_p, v_p = odd_mixer(
                xp, *od, jnp.zeros((bp, 0, C_WIDTH), xp.dtype), None, None, 0)
            ys, pool_s, k_s, v_s = odd_mixer(
                xs, *od, state_pool[i], cache_k_win[i], cache_v_win[i], PAST_LEN)
            new_pool_p.append(pool_p)
            new_pool_s.append(pool_s)
            new_k_p.append(k_p)
            new_k_s.append(k_s)
            new_v_p.append(v_p)
            new_v_s.append(v_s)
        hp = hp + yp
        hs = hs + ys
        hp = macaron_half(hp, ffn2_norm[layer], ffn2_w_gu[layer], ffn2_w_down[layer])
        hs = macaron_half(hs, ffn2_norm[layer], ffn2_w_gu[layer], ffn2_w_down[layer])
    return (hp, hs, jnp.stack(new_a_v_s), jnp.stack(new_ssm_p), jnp.stack(new_ssm_s),
            jnp.stack(new_conv_p), jnp.stack(new_conv_s), jnp.stack(new_pool_p), jnp.stack(new_pool_s),
            jnp.stack(new_k_p), jnp.stack(new_k_s), jnp.stack(new_v_p), jnp.stack(new_v_s))
```

```python
import contextlib
import math
import types
import numpy as np
import concourse.bass as bass
import concourse.mybir as mybir
from concourse.bass_utils import run_bass_kernel_spmd

F32 = mybir.dt.float32
BF16 = mybir.dt.bfloat16
I32 = mybir.dt.int32
AF = mybir.ActivationFunctionType
ALU = mybir.AluOpType
AX = mybir.AxisListType

NCORES = 8
D = 1024
DC = 8
DFF = 2816
FC = 22
SEQ = 4096
NTP = 512
NS = 64
EPS = 1e-6
NEG = -1e30


NATIVE_GELU = True
RELAX_SAME_ENGINE = False


def freeze(fn):
    if fn.__closure__ is None:
        return fn
    cells = []
    for c in fn.__closure__:
        try:
            cells.append(types.CellType(c.cell_contents))
        except ValueError:
            cells.append(c)
    return types.FunctionType(fn.__code__, fn.__globals__, fn.__name__, fn.__defaults__, tuple(cells))


class Buf:
    __slots__ = ("name", "w", "r", "excl")

    def __init__(self, name, excl=False):
        self.name = name
        self.w = None
        self.r = {}
        self.excl = excl


class DmaSlot:
    def __init__(self, S, name):
        self.S = S
        self.name = name
        self.sem = S.new_sem("d" + name)
        self.count = 0
        self.last = None

    def next_token(self):
        if self.count >= 16 * 1200:
            self.sem = self.S.new_sem("d" + self.name)
            self.count = 0
        self.count += 16
        self.last = (id(self.sem), self.sem, self.count, "dma")
        return self.last


class Sched:
    ENG = ("pe", "act", "dve", "pool", "sp")
    EPOCH = 6000

    def __init__(self, nc, stack):
        self.nc = nc
        self.stack = stack
        self.nsem = 0
        self.streams = {e: [] for e in self.ENG}
        self.sem = {e: self.new_sem(e) for e in self.ENG}
        self.cnt = {e: 0 for e in self.ENG}
        self.seen = {e: {} for e in self.ENG}
        self.final_tokens = []
        self.pending_dma = []
        self.ninst = 0
        self.oplog = []

    def new_sem(self, name):
        self.nsem += 1
        return self.stack.enter_context(self.nc.semaphore(f"s{self.nsem}_{name}"))

    def _wait(self, eng, tok):
        sid, sem, val, _ = tok
        if self.seen[eng].get(sid, 0) >= val:
            return
        self.seen[eng][sid] = val
        self.streams[eng].append(("wait", sem, val))

    def barrier(self, engines=("pe", "act", "dve", "sp")):
        toks = [(id(self.sem[e]), self.sem[e], self.cnt[e], e) for e in ("pe", "act", "dve", "pool") if self.cnt[e] > 0]
        toks += self.pending_dma
        for e in engines:
            for tok in toks:
                if tok[3] != e:
                    self._wait(e, tok)
        self.pending_dma = []

    def op(self, eng, fns, reads=(), writes=(), dma=None, final=False, arena=False):
        if not isinstance(fns, (list, tuple)):
            fns = [fns]
        fns = [freeze(f) for f in fns]
        writes = list(writes) + [b for b in reads if b.excl]
        reads = [b for b in reads if not b.excl]
        for b in reads:
            if b.w is not None and not (b.w[3] == eng == "pe"):
                self._wait(eng, b.w)
        for b in writes:
            if b.w is not None and not (b.w[3] == eng == "pe") and not (RELAX_SAME_ENGINE and b.w[3] == eng):
                self._wait(eng, b.w)
            for tok in b.r.values():
                if not (tok[3] == eng == "pe") and not (RELAX_SAME_ENGINE and tok[3] == eng):
                    self._wait(eng, tok)
        if dma is not None and dma.last is not None:
            self._wait(eng, dma.last)
        if dma is None:
            if self.cnt[eng] >= self.EPOCH:
                self.sem[eng] = self.new_sem(eng)
                self.cnt[eng] = 0
            self.cnt[eng] += 1
            tok = (id(self.sem[eng]), self.sem[eng], self.cnt[eng], eng)
            inc = 1
        else:
            tok = dma.next_token()
            inc = 16
        self.oplog.append((eng, tok, [x.name for x in reads], [x.name for x in writes], len(self.streams[eng])))
        st = self.streams[eng]
        for fn in fns[:-1]:
            st.append(("inst", fn, None, 0))
        st.append(("inst", fns[-1], tok[1], inc))
        self.ninst += len(fns)
        for b in reads:
            b.r[tok[0]] = tok
        for b in writes:
            b.w = tok
            b.r = {}
        if final:
            self.final_tokens.append(tok)
        if arena and dma is not None:
            self.pending_dma.append(tok)
        return tok

    def emit(self, block):
        nc = self.nc
        for tok in self.final_tokens:
            self._wait("sp", tok)
        streams = self.streams

        def run(engine, items):
            for it in items:
                if it[0] == "wait":
                    engine.wait_ge(it[1], it[2])
                else:
                    r = it[1](engine)
                    if it[2] is not None:
                        r.then_inc(it[2], it[3])

        @block.tensor
        def _(e):
            run(e, streams["pe"])

        @block.scalar
        def _(e):
            run(e, streams["act"])

        @block.vector
        def _(e):
            run(e, streams["dve"])

        @block.gpsimd
        def _(e):
            run(e, streams["pool"])

        @block.sync
        def _(e):
            run(e, streams["sp"])


class Prog:
    def __init__(self, cfg):
        self.cfg = cfg
        self.nc = bass.Bass("TRN2", target_bir_lowering=False)
        self.stack = contextlib.ExitStack()
        self.S = None
        self.dram = {}

    def din(self, name, shape, dtype=F32):
        t = self.nc.dram_tensor(name, list(shape), dtype, kind="ExternalInput")
        self.dram[name] = t
        return t.ap()

    def dout(self, name, shape, dtype=F32):
        t = self.nc.dram_tensor(name, list(shape), dtype, kind="ExternalOutput")
        self.dram[name] = t
        return t.ap()

    def dscratch(self, name, shape, dtype=F32):
        t = self.nc.dram_tensor(name, list(shape), dtype, kind="Internal")
        return t.ap()

    def sb(self, name, shape, dtype=F32):
        return self.stack.enter_context(self.nc.sbuf_tensor(name, list(shape), dtype))

    def ps(self, name, shape, dtype=F32):
        return self.stack.enter_context(self.nc.psum_tensor(name, list(shape), dtype))


WSLOT_ELEMS = 8 * 2 * 256
NWSLOT = 4
ARENA_WORDS = 23 * 1024


def _prod(xs):
    r = 1
    for x in xs:
        r *= x
    return r


def rs(ap, dims):
    if len(dims) == 1:
        return ap
    if len(dims) == 2:
        return ap.rearrange("p (a b) -> p a b", a=dims[0])
    if len(dims) == 3:
        return ap.rearrange("p (a b c) -> p a b c", a=dims[0], b=dims[1])
    raise ValueError(dims)


class Arena:
    def __init__(self, t, words):
        self.t = t
        self.words = words
        self.off = 0

    def reset(self):
        self.off = 0

    def take(self, nparts, dims, dtype=F32):
        n = _prod(dims)
        w = n if dtype == F32 else (n + 1) // 2
        w = (w + 1) // 2 * 2
        ap = self.t[0:nparts, self.off:self.off + w]
        if dtype != F32:
            ap = ap.bitcast(dtype)
        ap = ap[:, 0:n]
        self.off += w
        assert self.off <= self.words, ("arena overflow", self.off, self.words)
        return rs(ap, dims)


def build_program(cfg):
    P = Prog(cfg)
    nc = P.nc
    n_ptiles = cfg.get("n_ptiles", SEQ // NTP)
    do_sample = cfg.get("sample", True)
    stages = cfg.get("stages", "all")
    nlayers = cfg.get("layers", 2)
    npt_tokens = n_ptiles * NTP

    def on(name):
        return stages == "all" or name in stages

    xp = P.din("xp", [npt_tokens, D])
    xs = P.din("xs", [NS, D])
    pvec = P.din("pvec", [128, PV_COLS])
    cst = P.din("cst", [128, CST_COLS])
    wgu = P.din("wgu", [2, 2, 11, 128, 8 * 2 * 256])
    wdn = P.din("wdn", [2, 2, 8, 128, FC * 128])
    wev_fm = P.din("wev_fm", [7, 128, 8 * 512])
    wev_v = P.din("wev_v", [2, 128, 8 * 512])
    wev_dt = P.din("wev_dt", [128, 8 * 16])
    wev_out = P.din("wev_out", [4, 128, 16 * 256])
    wsT = P.din("wsT", [128, 8 * 128])
    wsT4 = P.din("wsT4", [4, 8 * 4])
    bs_row = P.din("bs_row", [1, 8 * 128])
    lnrow = P.din("lnrow", [1, 2048])
    wod_fm = P.din("wod_fm", [4, 128, 8 * 512])
    wod_k = P.din("wod_k", [128, 8 * 256])
    wod_v = P.din("wod_v", [128, 8 * 256])
    wod_lin = P.din("wod_lin", [128, 4 * 2 * 256])
    wod_out = P.din("wod_out", [4, 128, 16 * 256])
    rel_tab = P.din("rel_tab", [32, 16])
    sink_row = P.din("sink_row", [1, 16])
    st_pool = P.din("st_pool", [240, 1024])
    st_k = P.din("st_k", [16, 128, 256])
    st_v = P.din("st_v", [16, 128, 256])
    dscr = P.dscratch("dscr", [16, 128, 384])
    o_pool_p = P.dout("o_pool_p", [15, 1024])
    o_pool_s = P.dout("o_pool_s", [16, 15, 1024])
    o_k_p = P.dout("o_k_p", [128, 256])
    o_k_s = P.dout("o_k_s", [16, 128, 256])
    o_v_p = P.dout("o_v_p", [128, 256])
    o_v_s = P.dout("o_v_s", [16, 128, 256])
    st_ssm = P.din("st_ssm", [16, 1024, 128])
    st_conv = P.din("st_conv", [48, 1536])
    yp = P.dout("yp", [npt_tokens, D])
    ys = P.dout("ys", [NS, D])
    o_av = P.dout("o_av", [NS, D])
    o_ssm_p = P.dout("o_ssm_p", [1024, 128])
    o_ssm_s = P.dout("o_ssm_s", [16, 1024, 128])
    o_conv_p = P.dout("o_conv_p", [3, 1536])
    o_conv_s = P.dout("o_conv_s", [16, 3, 1536])

    with P.stack:
        S = Sched(nc, P.stack)
        P.S = S
        ident = P.sb("ident", [128, 128], F32)
        identb = P.sb("identb", [128, 128], BF16)
        onesb = P.sb("onesb", [128, 128], BF16)
        pv = P.sb("pv", [128, PV_COLS], F32)
        cs = P.sb("cs", [128, CST_COLS], F32)
        xT = P.sb("xT", [128, DC, NTP], F32)
        xn = P.sb("xn", [128, DC, NTP], BF16)
        hTt = P.sb("hT", [128, FC * NTP], BF16)
        hT = hTt[:, :].rearrange("p (j n) -> p j n", j=FC)
        rstd = P.sb("rstd", [128, NTP], F32)
        sil = [P.sb(f"sil{i}", [128, NTP], F32) for i in range(2)]
        wring = [P.sb(f"wr{i}", [128, WSLOT_ELEMS], BF16) for i in range(NWSLOT)]
        wTm = P.sb("wTm", [128, 8, 128], BF16)
        T2 = P.sb("T2", [128, 8, 128], F32)
        wTms = P.sb("wTms", [64, 8, 64], BF16)
        T2s = P.sb("T2s", [128, 8, 64], F32)
        aneg = P.sb("aneg", [16, 1], F32)
        STf = P.sb("STf", [128, 1024], F32)
        STb = P.sb("STb", [128, 1024], BF16)
        ctail = P.sb("ctail", [128, 12, 3], BF16)
        ctail32 = P.sb("ctail32", [128, 12, 3], F32)
        bd64 = P.sb("bd64", [128, 128], BF16)
        hsq = P.sb("hsq", [128, NTP], BF16)
        hrs = P.sb("hrs", [128, NTP], F32)
        esink2 = P.sb("esink2", [128, 8], F32)
        ptail = P.sb("ptail", [128, 8, 15], F32)
        kprev = P.sb("kprev", [128, 4, 128], BF16)
        vprev = P.sb("vprev", [128, 256], BF16)
        arena_t = P.sb("arena", [128, ARENA_WORDS], F32)
        psall = P.ps("psall", [128, 8 * 512], F32)
        A = Arena(arena_t, ARENA_WORDS)
        sq = hT[:, 0:8, :]
        yst = hTt[:, 0:2 * 4 * D].bitcast(F32).rearrange("p (b d) -> p b d", b=4)
        xin = arena_t[:, 0:4 * D].rearrange("p (b d) -> p b d", b=4)

        b_ident = Buf("ident")
        b_pv = Buf("pv")
        b_cs = Buf("cs")
        b_xin = Buf("xin")
        b_xT = [Buf(f"xT{c}") for c in range(DC)]
        b_xn = Buf("xn")
        b_h = [Buf(f"h{j}") for j in range(FC)]
        b_rstd = Buf("rstd")
        b_sil = [Buf("sil0"), Buf("sil1")]
        b_wr = [Buf(f"wr{i}") for i in range(NWSLOT)]
        b_ps = [Buf(f"ps{i}", excl=True) for i in range(8)]
        b_gm = Buf("gmlp_consts")
        b_ST = Buf("ST")
        b_STb = Buf("STb")
        b_ctail = Buf("ctail")
        b_hsq, b_hrs, b_gm2, b_ptail, b_kprev, b_dscr = Buf("hsq"), Buf("hrs"), Buf("gm2"), Buf("ptail"), Buf("kprev"), Buf("dscr")
        d_wr = [DmaSlot(S, f"wr{i}") for i in range(NWSLOT)]
        d_xin = DmaSlot(S, "xin")
        d_yst = DmaSlot(S, "yst")
        d_misc = DmaSlot(S, "misc")
        d_out = [DmaSlot(S, f"out{i}") for i in range(8)]
        d_in = [DmaSlot(S, f"in{i}") for i in range(8)]

        state = {"w": 0, "ps": 0, "sil": 0, "out": 0, "in": 0}

        def bank(i):
            return psall[:, i * 512:(i + 1) * 512]

        reserved = set()

        def next_ps():
            i = state["ps"]
            while i in reserved:
                i = (i + 1) % 8
            state["ps"] = (i + 1) % 8
            return bank(i), b_ps[i]

        def ps_group(n):
            i = (state["ps"] + n - 1) // n * n % 8
            while any((i + k) in reserved for k in range(n)):
                i = (i + n) % 8
            state["ps"] = (i + n) % 8
            state["last_group"] = list(range(i, i + n))
            return psall[:, i * 512:(i + n) * 512], [b_ps[i + k] for k in range(n)]

        def next_out():
            i = state["out"]
            state["out"] = (i + 1) % 8
            return d_out[i]

        def next_in():
            i = state["in"]
            state["in"] = (i + 1) % 8
            return d_in[i]

        wcache = {}
        d_ws = [DmaSlot(S, f"ws{i}") for i in range(NWSLOT)]
        use_wcache = cfg.get("wcache", True) and (len([1 for _ in range(n_ptiles)]) + (1 if do_sample else 0)) > 1

        def wload(src_ap, nelem):
            i = state["w"]
            state["w"] = (i + 1) % NWSLOT
            dst = wring[i][:, 0:nelem]
            key = (src_ap.tensor.name, src_ap.offset)
            ent = wcache.get(key) if use_wcache else None
            if ent is None:
                S.op("pool", lambda e, dst=dst, src=src_ap: e.dma_start(out=dst, in_=src),
                     reads=(), writes=(b_wr[i],), dma=d_wr[i])
                if use_wcache:
                    scr = P.dscratch(f"wb{len(wcache)}", [128, nelem], BF16)
                    bscr = Buf(f"wb{len(wcache)}")
                    wcache[key] = (scr, bscr)
                    S.op("sp", lambda e, dst=dst, scr=scr: e.dma_start(out=scr, in_=dst),
                         reads=(b_wr[i],), writes=(bscr,), dma=d_ws[i])
            else:
                scr, bscr = ent
                S.op("pool", lambda e, dst=dst, scr=scr: e.dma_start(out=dst, in_=scr),
                     reads=(bscr,), writes=(b_wr[i],), dma=d_wr[i])
            return wring[i], b_wr[i]

        def cst_ap(name, nparts=128):
            o, n = CST_OFF[name]
            return cs[0:nparts, o:o + n]

        def pvc(name, c=0, nparts=128):
            o = PV_OFF[name]
            return pv[0:nparts, o + c:o + c + 1]

        S.op("pool", lambda e: e.memset(ident[:], 0.0), writes=(b_ident,))
        S.op("pool", lambda e: e.affine_select(out=ident[:], in_=ident[:], pattern=[[-1, 128]],
                                               compare_op=ALU.not_equal, fill=1.0, base=0,
                                               channel_multiplier=1),
             reads=(b_ident,), writes=(b_ident,))
        S.op("pool", lambda e: e.tensor_copy(identb[:], ident[:]), reads=(b_ident,), writes=(b_ident,))
        S.op("pool", lambda e: e.memset(onesb[:], 1.0), writes=(b_ident,))
        S.op("pool", lambda e: e.memset(STf[:], 0.0), writes=(b_ST,))
        S.op("pool", lambda e: e.memset(STb[:], 0.0), writes=(b_STb,))
        S.op("pool", lambda e: e.memset(ctail[:], 0.0), writes=(b_ctail,))
        S.op("sp", lambda e: e.dma_start(out=pv[:], in_=pvec), writes=(b_pv,), dma=d_misc)
        S.op("sp", lambda e: e.dma_start(out=cs[:], in_=cst), writes=(b_cs,), dma=next_in())

        if on("even"):
            A.reset()
            b_tmp = Buf("setup_tmp")
            w32 = A.take(128, [8, 128])
            bsbc = A.take(128, [8, 128])
            w32s = A.take(64, [8, 64])
            S.op("sp", lambda e: e.dma_start(out=w32, in_=wsT.rearrange("p (h i) -> p h i", h=8)),
                 writes=(b_tmp,), dma=next_in(), arena=True)
            S.op("sp", lambda e: e.dma_start(out=bsbc.rearrange("p h i -> p (h i)"),
                                             in_=bs_row.partition_broadcast(128)),
                 writes=(b_tmp,), dma=next_in(), arena=True)
            S.op("pool", lambda e: e.affine_select(out=w32, in_=w32, pattern=[[0, 8], [1, 128]],
                                                   compare_op=ALU.is_ge, fill=0.0, base=0,
                                                   channel_multiplier=-1),
                 reads=(b_tmp,), writes=(b_tmp,))
            S.op("pool", lambda e: e.tensor_copy(wTm[:], w32), reads=(b_tmp,), writes=(b_gm,))
            pg, bpg = ps_group(2)
            S.op("pe", [lambda e, k=k: e.matmul(pg[:, k * 512:(k + 1) * 512], onesb[:, :],
                                                 wTm[:, k * 4:(k + 1) * 4, :].rearrange("p h i -> p (h i)"),
                                                 start=True, stop=True) for k in range(2)],
                 reads=(b_gm, b_ident), writes=bpg)
            for h in range(8):
                S.op("dve", lambda e, h=h: e.scalar_tensor_tensor(
                    out=T2[:, h, :], in0=pg[:, h * 128:(h + 1) * 128], scalar=pvc("ln_b", h),
                    in1=bsbc[:, h, :], op0=ALU.mult, op1=ALU.add),
                    reads=bpg + [b_tmp, b_pv], writes=(b_gm,))
            SST = cfg.get("setup_stop", 99)
            S.op("pool", lambda e: e.memset(w32s, 0.0), writes=(b_tmp,))
            for b in range(16 if SST > 1 else 0):
                S.op("sp", lambda e, b=b: e.dma_start(out=w32s[4 * b:4 * b + 4, :, 4 * b:4 * b + 4],
                                                      in_=wsT4.rearrange("p (h i) -> p h i", h=8)),
                     writes=(b_tmp,), dma=next_in(), arena=True)
            S.op("pool", lambda e: e.affine_select(out=w32s, in_=w32s, pattern=[[0, 8], [1, 64]],
                                                   compare_op=ALU.is_ge, fill=0.0, base=0,
                                                   channel_multiplier=-1),
                 reads=(b_tmp,), writes=(b_tmp,))
            S.op("pool", lambda e: e.tensor_copy(wTms[:], w32s), reads=(b_tmp,), writes=(b_gm,))
            pg2, bpg2 = next_ps()
            S.op("pe", lambda e: e.matmul(pg2[:, 0:512], onesb[0:64, :],
                                          wTms[:, :, :].rearrange("p h i -> p (h i)"), start=True, stop=True),
                 reads=(b_gm, b_ident), writes=(bpg2,))
            for h in range(8 if SST > 2 else 0):
                S.op("dve", lambda e, h=h: e.scalar_tensor_tensor(
                    out=T2s[:, h, :].rearrange("p (b i) -> p b i", i=4),
                    in0=pg2[:, h * 64:(h + 1) * 64].rearrange("p (b i) -> p b i", i=4),
                    scalar=pvc("ln_b", h),
                    in1=bsbc[:, h, 0:4].unsqueeze(1).broadcast_to([128, 16, 4]),
                    op0=ALU.mult, op1=ALU.add),
                    reads=(bpg2, b_tmp, b_pv), writes=(b_gm,))
            S.op("act", lambda e: e.activation(out=aneg[:, :], in_=pvc("a_log", 0, 16), func=AF.Exp),
                 reads=(b_pv,), writes=(b_gm,))
            S.op("dve", lambda e: e.tensor_scalar(out=aneg[:, :], in0=aneg[:, :], scalar1=-1.0, scalar2=None,
                                                  op0=ALU.mult), reads=(b_gm,), writes=(b_gm,))
            S.barrier()

        if on("odd"):
            A.reset()
            b_tmp2 = Buf("setup_tmp2")
            S.op("pool", lambda e: e.memset(bd64[:], 0.0), writes=(b_ident,))
            S.op("pool", lambda e: e.memset(bd64[0:64, 0:64], 1.0), writes=(b_ident,))
            S.op("pool", lambda e: e.memset(bd64[64:128, 64:128], 1.0), writes=(b_ident,))
            S.op("pool", lambda e: e.memset(ptail[:], 0.0), writes=(b_ptail,))
            S.op("pool", lambda e: e.memset(kprev[:], 0.0), writes=(b_kprev,))
            S.op("pool", lambda e: e.memset(vprev[:], 0.0), writes=(b_kprev,))
            es = A.take(128, [16])
            rt = A.take(32, [16])
            dv = A.take(16, [384])
            S.op("sp", lambda e: e.dma_start(out=es, in_=sink_row.partition_broadcast(128)), writes=(b_tmp2,), dma=next_in(), arena=True)
            S.op("sp", lambda e: e.dma_start(out=rt, in_=rel_tab), writes=(b_tmp2,), dma=next_in(), arena=True)
            S.op("act", lambda e: e.activation(out=es, in_=es, func=AF.Exp), reads=(b_tmp2,), writes=(b_tmp2,))
            esv = es.rearrange("p (c t) -> p c t", t=2)
            S.op("dve", lambda e: e.tensor_copy(esink2[0:64, :], esv[0:64, :, 0]), reads=(b_tmp2,), writes=(b_gm2,))
            S.op("dve", lambda e: e.tensor_copy(esink2[64:128, :], esv[64:128, :, 1]), reads=(b_tmp2,), writes=(b_gm2,))
            pgd, bpgd = next_ps()
            S.op("pe", lambda e: e.matmul(pgd[0:16, 0:384], rt[:, :], cst_ap("onehot", 32), start=True, stop=True),
                 reads=(b_tmp2, b_cs), writes=(bpgd,))
            S.op("dve", lambda e: e.tensor_tensor(out=dv, in0=pgd[0:16, 0:384], in1=cst_ap("negmask", 16), op=ALU.add),
                 reads=(bpgd, b_cs), writes=(b_tmp2,))
            S.op("sp", lambda e: e.dma_start(out=dscr, in_=dv.unsqueeze(1).broadcast_to([16, 128, 384])),
                 reads=(b_tmp2,), writes=(b_dscr,), dma=next_in(), arena=True)
            S.barrier()

        def load_x_tile(src_rows, nblk, rows):
            S.op("sp", lambda e: e.dma_start(out=xin[0:rows, 0:nblk, :],
                                             in_=src_rows.rearrange("(b p) d -> p b d", p=rows)),
                 writes=(b_xin,), dma=d_xin, arena=True)

        def transpose_in(nblk, rows):
            nt = nblk * rows
            for c in range(DC):
                pt, bp = next_ps()
                fns = []
                for blk in range(nblk):
                    fns.append(lambda e, pt=pt, blk=blk, c=c: e.transpose(
                        pt[:, blk * rows:(blk + 1) * rows], xin[0:rows, blk, c * 128:(c + 1) * 128],
                        ident[0:rows, 0:rows]))
                S.op("pe", fns, reads=(b_xin, b_ident), writes=(bp,))
                if c % 2:
                    S.op("act", lambda e, pt=pt, c=c: e.copy(xT[:, c, 0:nt], pt[:, 0:nt]),
                         reads=(bp,), writes=(b_xT[c],))
                else:
                    S.op("dve", lambda e, pt=pt, c=c: e.tensor_copy(xT[:, c, 0:nt], pt[:, 0:nt]),
                         reads=(bp,), writes=(b_xT[c],))

        def transpose_out(dst_rows, nblk, rows):
            for blk in range(nblk):
                for half in range(2):
                    pt, bp = next_ps()
                    fns = []
                    for cc in range(4):
                        c = half * 4 + cc
                        fns.append(lambda e, pt=pt, blk=blk, c=c, cc=cc: e.transpose(
                            pt[0:rows, cc * 128:(cc + 1) * 128], xT[:, c, blk * rows:(blk + 1) * rows],
                            ident[:, :]))
                    S.op("pe", fns, reads=[b_xT[half * 4 + cc] for cc in range(4)] + [b_ident],
                         writes=(bp,))
                    if half:
                        S.op("act", lambda e, pt=pt, blk=blk: e.copy(yst[0:rows, blk, 512:1024], pt[0:rows, :]),
                             reads=(bp,), writes=b_h[0:16])
                    else:
                        S.op("dve", lambda e, pt=pt, blk=blk: e.tensor_copy(yst[0:rows, blk, 0:512], pt[0:rows, :]),
                             reads=(bp,), writes=b_h[0:16])
            S.op("sp", lambda e: e.dma_start(out=dst_rows.rearrange("(b p) d -> p b d", p=rows),
                                             in_=yst[0:rows, 0:nblk, :]),
                 reads=b_h[0:16], dma=d_yst, final=True)

        def rms_norm(gname, nt):
            S.op("act", lambda e: e.activation(out=sq[:, :, 0:nt], in_=xT[:, :, 0:nt], func=AF.Square),
                 reads=b_xT, writes=b_h[0:8])
            pt, bp = next_ps()
            S.op("pe", [lambda e, pt=pt, c=c: e.matmul(pt[:, 0:nt], onesb[:, :], sq[:, c, 0:nt],
                                                        start=(c == 0), stop=(c == DC - 1))
                        for c in range(DC)], reads=b_h[0:8] + [b_ident], writes=(bp,))
            S.op("act", lambda e, pt=pt: e.activation(out=rstd[:, 0:nt], in_=pt[:, 0:nt], func=AF.Sqrt,
                                                       bias=EPS, scale=1.0 / D),
                 reads=(bp,), writes=(b_rstd,))
            S.op("dve", lambda e: e.reciprocal(rstd[:, 0:nt], rstd[:, 0:nt]), reads=(b_rstd,), writes=(b_rstd,))
            for c in range(DC):
                S.op("dve", lambda e, c=c: e.scalar_tensor_tensor(
                    out=xn[:, c, 0:nt], in0=xT[:, c, 0:nt], scalar=pvc(gname, c),
                    in1=rstd[:, 0:nt], op0=ALU.mult, op1=ALU.mult),
                    reads=(b_xT[c], b_rstd, b_pv), writes=(b_xn,))

        def ffn(layer, which, nt):
            rms_norm(f"ffn_norm{layer}{which}", nt)
            for grp in range(11):
                w, bw = wload(wgu[layer, which, grp], 8 * 2 * 256)
                wv = w[:, :].rearrange("p (k g n) -> p k g n", k=8, g=2)
                for jj in range(2):
                    j = grp * 2 + jj
                    pg, bpg = next_ps()
                    S.op("pe", [lambda e, pg=pg, k=k, jj=jj, wv=wv: e.matmul(
                        pg[:, 0:nt], wv[:, k, 0, jj * 128:(jj + 1) * 128], xn[:, k, 0:nt],
                        start=(k == 0), stop=(k == 7)) for k in range(8)],
                        reads=(bw, b_xn), writes=(bpg,))
                    pu, bpu = next_ps()
                    S.op("pe", [lambda e, pu=pu, k=k, jj=jj, wv=wv: e.matmul(
                        pu[:, 0:nt], wv[:, k, 1, jj * 128:(jj + 1) * 128], xn[:, k, 0:nt],
                        start=(k == 0), stop=(k == 7)) for k in range(8)],
                        reads=(bw, b_xn), writes=(bpu,))
                    si = state["sil"]
                    state["sil"] = 1 - si
                    S.op("act", lambda e, pg=pg, si=si: e.activation(out=sil[si][:, 0:nt], in_=pg[:, 0:nt],
                                                                      func=AF.Silu),
                         reads=(bpg,), writes=(b_sil[si],))
                    S.op("dve", lambda e, pu=pu, si=si, j=j: e.tensor_tensor(
                        out=hT[:, j, 0:nt], in0=pu[:, 0:nt], in1=sil[si][:, 0:nt], op=ALU.mult),
                        reads=(bpu, b_sil[si]), writes=(b_h[j],))
            for m in range(DC):
                w, bw = wload(wdn[layer, which, m], FC * 128)
                wv = w[:, 0:FC * 128].rearrange("p (j n) -> p j n", j=FC)
                py, bpy = next_ps()
                S.op("pe", [lambda e, py=py, j=j, wv=wv: e.matmul(
                    py[:, 0:nt], wv[:, j, :], hT[:, j, 0:nt], start=(j == 0), stop=(j == FC - 1))
                    for j in range(FC)], reads=[bw] + b_h, writes=(bpy,))
                S.op("dve", lambda e, py=py, m=m: e.scalar_tensor_tensor(
                    out=xT[:, m, 0:nt], in0=py[:, 0:nt], scalar=0.5, in1=xT[:, m, 0:nt],
                    op0=ALU.mult, op1=ALU.add), reads=(bpy, b_xT[m]), writes=(b_xT[m],))

        def gelu_evac(pt, np_, nf, out_ap, bp, wbufs):
            if NATIVE_GELU:
                S.op("act", lambda e: e.activation(out=out_ap, in_=pt, func=AF.Gelu_apprx_tanh), reads=(bp,), writes=wbufs)
                return
            si = state["sil"]
            state["sil"] = 1 - si
            t1 = sil[si][0:np_, 0:nf]
            S.op("act", lambda e: e.activation(out=t1, in_=pt, func=AF.Square), reads=(bp,), writes=(b_sil[si],))
            S.op("dve", lambda e: e.tensor_scalar(out=t1, in0=t1, scalar1=0.044715, scalar2=1.0, op0=ALU.mult, op1=ALU.add),
                 reads=(b_sil[si],), writes=(b_sil[si],))
            S.op("dve", lambda e: e.tensor_tensor(out=t1, in0=t1, in1=pt, op=ALU.mult), reads=(b_sil[si], bp), writes=(b_sil[si],))
            S.op("act", lambda e: e.activation(out=t1, in_=t1, func=AF.Sigmoid, scale=1.5957691216057308),
                 reads=(b_sil[si],), writes=(b_sil[si],))
            S.op("dve", lambda e: e.tensor_tensor(out=out_ap, in0=t1, in1=pt, op=ALU.mult), reads=(b_sil[si], bp), writes=wbufs)

        def proj_fm(wsrc, ncc, nt, evac):
            w, bw = wload(wsrc, 8 * 512)
            wv = w[:, :].rearrange("p (k n) -> p k n", k=8)
            for cc in range(ncc):
                pt, bp = next_ps()
                S.op("pe", [lambda e, pt=pt, k=k, cc=cc, wv=wv: e.matmul(
                    pt[:, 0:nt], wv[:, k, cc * 128:(cc + 1) * 128], xn[:, k, 0:nt],
                    start=(k == 0), stop=(k == 7)) for k in range(8)], reads=(bw, b_xn), writes=(bp,))
                evac(cc, pt, bp)

        def even_mixer(kind, nt, is_last_ptile):
            STOP = cfg.get("even_stop", 99)
            if STOP <= 0:
                return
            S.barrier()
            A.reset()
            smp = (kind == "s")
            CL = 64 if smp else 128
            nblk = nt // CL
            Lc = 4 if smp else 128
            nch = nt // Lc
            uT = hT[:, 0:8, :]
            yaT = hT[:, 8:16, :]
            bcT = hT[:, 16:20, :]
            b_u, b_ya, b_bc = b_h[0:8], b_h[8:16], b_h[16:20]
            zT = A.take(128, [8, nt], BF16)
            ybT = A.take(128, [8, nt], BF16)
            xcT = A.take(128, [8, nt], BF16)
            ext = A.take(128, [12, (16 * 7) if smp else (NTP + 3)], BF16)
            cvA = A.take(128, [nt])
            cvB = A.take(128, [nt])
            vg = [A.take(128, [1024]) for _ in range(1 if smp else 2)] * (2 if smp else 1)
            vhb = [A.take(128, [1024], BF16) for _ in range(1 if smp else 2)] * (2 if smp else 1)
            mvst = A.take(128, [2, 6])
            mv = A.take(128, [2])
            gt = [A.take(128, [128]) for _ in range(2)]
            dsc = A.take(16, [4, nt])
            tsc = A.take(128, [64])
            absx = A.take(128, [16, CL])
            Eb = A.take(128, [16, CL], BF16)
            Csb = A.take(128, [16, CL], BF16)
            cbm = A.take(128, [2, CL])
            xd = A.take(128, [16, 64], BF16)
            xdd = A.take(128, [16, 64], BF16)
            Btok = A.take(128, [2, 128], BF16)
            ygb = A.take(128, [8, 128])
            sqg = A.take(128, [8, 128], BF16)
            rsg = A.take(128, [2, 128])
            bz, byb, bxc, bext, bcv = Buf("zT"), Buf("ybT"), Buf("xcT"), Buf("ext"), [Buf("cvA"), Buf("cvB")]
            bvg, bvh, bmv, bgt = [Buf("vg0"), Buf("vg1")], [Buf("vh0"), Buf("vh1")], Buf("mv"), [Buf("gt0"), Buf("gt1")]
            bdsc, btsc, babs, bEb, bCs, bcbm = Buf("dsc"), Buf("tsc"), Buf("absx"), Buf("Eb"), Buf("Csb"), Buf("cbm")
            bxd, bxdd, bBt, bygb, bsqg, brsg = Buf("xd"), Buf("xdd"), Buf("Btok"), Buf("ygb"), Buf("sqg"), Buf("rsg")
            if smp:
                xpre = A.take(128, [12, 64])
                cso = A.take(64, [1536])
                stc = cso[0:48, :]
                lnbc = A.take(128, [2048])
                vln = A.take(64, [1024])
                bmsk = cst_ap("bmask", 64)
                Bblk = A.take(64, [2, 16, 128], BF16)
                cdT = A.take(128, [8, 16])
                eal = A.take(16, [16])
                Snat = A.take(128, [2, 8, 128])
                STs = A.take(128, [2, 1024], BF16)
                bxpre, bcso, blnbc, bvln = Buf("xpre"), Buf("cso"), Buf("lnbc"), Buf("vln")
                bstc = bcso
                bBblk, bcdT, beal, bSnat, bSTs = Buf("Bblk"), Buf("cdT"), Buf("eal"), Buf("Snat"), Buf("STs")
                S.op("sp", lambda e: e.dma_start(out=lnbc, in_=lnrow.partition_broadcast(128)),
                     writes=(blnbc,), dma=next_in(), arena=True)
                S.op("sp", lambda e: e.dma_start(out=stc, in_=st_conv), writes=(bstc,), dma=next_in(), arena=True)

            rms_norm(f"mix_norm{0}", nt)
            if STOP <= 0.5:
                return

            for gi in range(2):
                def ev_u(cc, pt, bp, gi=gi):
                    c = gi * 4 + cc
                    gelu_evac(pt[:, 0:nt], 128, nt, uT[:, c, 0:nt], bp, (b_u[c],))
                proj_fm(wev_fm[gi], 4, nt, ev_u)

            if STOP <= 1:
                return
            wv0, bwv0 = wload(wev_v[0], 8 * 512)
            wv1, bwv1 = wload(wev_v[1], 8 * 512)
            wvv = [wv0[:, :].rearrange("p (k n) -> p k n", k=8), wv1[:, :].rearrange("p (k n) -> p k n", k=8)]
            bwv = [bwv0, bwv1]
            wg = wTms if smp else wTm
            T2x = T2s if smp else T2
            for blk in range(nblk):
                cols = slice(blk * CL, (blk + 1) * CL)
                bi = blk % 2
                for half in range(2):
                    pt, bp = next_ps()
                    S.op("pe", [lambda e, pt=pt, k=k, half=half: e.matmul(
                        pt[0:CL, :], xn[:, k, cols], wvv[half][:, k, :], start=(k == 0), stop=(k == 7))
                        for k in range(8)], reads=(bwv[half], b_xn), writes=(bp,))
                    gelu_evac(pt[0:CL, :], CL, 512, vg[bi][0:CL, half * 512:(half + 1) * 512], bp, (bvg[bi],))
                S.op("dve", [lambda e, bi=bi: e.bn_stats(mvst[0:CL, 0, :], vg[bi][0:CL, 0:512]),
                             lambda e, bi=bi: e.bn_stats(mvst[0:CL, 1, :], vg[bi][0:CL, 512:1024])],
                     reads=(bvg[bi],), writes=(bmv,))
                S.op("dve", lambda e: e.bn_aggr(mv[0:CL, :], mvst[0:CL, :, :].rearrange("p a b -> p (a b)")),
                     reads=(bmv,), writes=(bmv,))
                S.op("act", lambda e: e.activation(out=mv[0:CL, 1:2], in_=mv[0:CL, 1:2], func=AF.Sqrt, bias=EPS, scale=1.0),
                     reads=(bmv,), writes=(bmv,))
                S.op("dve", lambda e: e.reciprocal(mv[0:CL, 1:2], mv[0:CL, 1:2]), reads=(bmv,), writes=(bmv,))
                S.op("dve", lambda e, bi=bi: e.tensor_scalar(
                    out=vhb[bi][0:CL, :], in0=vg[bi][0:CL, :], scalar1=mv[0:CL, 0:1], scalar2=mv[0:CL, 1:2],
                    op0=ALU.subtract, op1=ALU.mult), reads=(bmv, bvg[bi]), writes=(bvh[bi],))
                if smp:
                    S.op("dve", lambda e, bi=bi: e.tensor_scalar(
                        out=vln[:, :], in0=vg[bi][0:CL, :], scalar1=mv[0:CL, 0:1], scalar2=mv[0:CL, 1:2],
                        op0=ALU.subtract, op1=ALU.mult), reads=(bmv, bvg[bi]), writes=(bvln,))
                    S.op("dve", lambda e: e.tensor_tensor(out=vln[:, :], in0=vln[:, :], in1=lnbc[0:64, 0:1024], op=ALU.mult),
                         reads=(bvln, blnbc), writes=(bvln,))
                    S.op("dve", lambda e: e.tensor_tensor(out=vln[:, :], in0=vln[:, :], in1=lnbc[0:64, 1024:2048], op=ALU.add),
                         reads=(bvln, blnbc), writes=(bvln,))
                    S.op("sp", lambda e: e.dma_start(out=o_av, in_=vln[:, :]), reads=(bvln,), dma=next_out(),
                         final=True, arena=True)
                for hg in range(2):
                    pt, bp = next_ps()
                    S.op("pe", [lambda e, pt=pt, hh=hh, hg=hg, bi=bi: e.matmul(
                        pt[:, hh * CL:(hh + 1) * CL], vhb[bi][0:CL, (hg * 4 + hh) * 128:(hg * 4 + hh + 1) * 128],
                        wg[0:CL, hg * 4 + hh, 0:CL], start=True, stop=True) for hh in range(4)],
                        reads=(bvh[bi], b_gm), writes=(bp,))
                    for hh in range(4):
                        h = hg * 4 + hh
                        gi_ = h % 2
                        S.op("dve", lambda e, pt=pt, hh=hh, h=h, gi_=gi_: e.scalar_tensor_tensor(
                            out=gt[gi_][:, 0:CL], in0=pt[:, hh * CL:(hh + 1) * CL], scalar=pvc("ln_g", h),
                            in1=T2x[:, h, 0:CL], op0=ALU.mult, op1=ALU.add),
                            reads=(bp, b_gm, b_pv), writes=(bgt[gi_],))
                        S.op("dve", lambda e, h=h, gi_=gi_: e.tensor_tensor(
                            out=yaT[:, h, cols], in0=gt[gi_][:, 0:CL], in1=uT[:, h, cols], op=ALU.mult),
                            reads=(bgt[gi_], b_u[h]), writes=(b_ya[h],))

            if STOP <= 2:
                return
            for gi in range(2):
                def ev_z(cc, pt, bp, gi=gi):
                    c = gi * 4 + cc
                    if cfg.get("zmode", 0) == 0:
                        S.op("act", lambda e: e.activation(out=zT[:, c, 0:nt], in_=pt[:, 0:nt], func=AF.Silu),
                             reads=(bp,), writes=(bz,))
                    elif cfg.get("zmode", 0) == 1:
                        S.op("act", lambda e: e.activation(out=ybT[:, c, 0:nt], in_=pt[:, 0:nt], func=AF.Silu),
                             reads=(bp,), writes=(bz,))
                proj_fm(wev_fm[2 + gi], 4, nt, ev_z)
            if STOP <= 2.2:
                return
            if smp:
                extv = ext.rearrange("p c (b r) -> p c b r", r=7)
                pg, bpg = ps_group(2)
                S.op("pe", [lambda e, c=c: e.transpose(pg[:, c * 64:c * 64 + 48], stc[0:48, c * 128:(c + 1) * 128],
                                                        ident[0:48, 0:48]) for c in range(12)],
                     reads=(bstc, b_ident), writes=bpg)
                S.op("dve", lambda e: e.tensor_copy(
                    extv[:, :, :, 0:3],
                    pg[:, 0:768].rearrange("p (c x) -> p c x", c=12)[:, :, 0:48].rearrange("p c (b r) -> p c b r", r=3)),
                     reads=bpg, writes=(bext,))
            else:
                S.op("dve", lambda e: e.tensor_copy(ext[:, :, 0:3], ctail[:, :, :]), reads=(b_ctail,), writes=(bext,))
            XM = cfg.get("xmode", 9)
            for gi in range(3 if XM >= 1 else 0):
                def ev_x(cc, pt, bp, gi=gi):
                    c = gi * 4 + cc
                    if XM < 2:
                        return
                    if smp:
                        S.op("act", lambda e: e.copy(extv[:, c, :, 3:7], pt[:, 0:64].rearrange("p (b i) -> p b i", i=4)),
                             reads=(bp,), writes=(bext,))
                        S.op("dve", lambda e: e.tensor_copy(xpre[:, c, :], pt[:, 0:64]), reads=(bp,), writes=(bxpre,))
                    else:
                        S.op("act", lambda e: e.copy(ext[:, c, 3:3 + nt], pt[:, 0:nt]), reads=(bp,), writes=(bext,))
                        if is_last_ptile and XM >= 3:
                            S.op("dve", lambda e: e.tensor_copy(ctail32[:, c, :], pt[:, nt - 3:nt]),
                                 reads=(bp,), writes=(b_ctail,))
                proj_fm(wev_fm[4 + gi], 4, nt, ev_x)
            if STOP <= 2.4:
                return
            wd, bwd = wload(wev_dt, 8 * 16)
            wdv = wd[:, 0:128].rearrange("p (k n) -> p k n", k=8)
            pt, bp = next_ps()
            S.op("pe", [lambda e, k=k: e.matmul(pt[0:16, 0:nt], wdv[:, k, :], xn[:, k, 0:nt],
                                                 start=(k == 0), stop=(k == 7)) for k in range(8)],
                 reads=(bwd, b_xn), writes=(bp,))
            S.op("act", lambda e: e.activation(out=dsc[:, 0, 0:nt], in_=pt[0:16, 0:nt], func=AF.Exp,
                                               bias=pvc("dt_bias", 0, 16), scale=1.0),
                 reads=(bp, b_pv), writes=(bdsc,))
            if STOP <= 2.6:
                return
            S.op("act", lambda e: e.activation(out=dsc[:, 0, 0:nt], in_=dsc[:, 0, 0:nt], func=AF.Ln, bias=1.0, scale=1.0),
                 reads=(bdsc,), writes=(bdsc,))
            S.op("dve", lambda e: e.tensor_scalar(out=dsc[:, 1, 0:nt], in0=dsc[:, 0, 0:nt], scalar1=aneg[:, 0:1],
                                                  scalar2=None, op0=ALU.mult), reads=(bdsc, b_gm), writes=(bdsc,))
            if STOP <= 2.8:
                return
            rmask = cst_ap("rmask_s", 16)[:, 0:nt] if smp else cst_ap("rmask_p", 16)[:, 0:nt]
            S.op("dve", lambda e: e.tensor_tensor_scan(out=dsc[:, 2, 0:nt], data0=rmask, data1=dsc[:, 1, 0:nt],
                                                       initial=0.0, op0=ALU.mult, op1=ALU.add),
                 reads=(bdsc, b_cs), writes=(bdsc,))
            S.op("dve", lambda e: e.tensor_copy(
                dsc[:, 3, 0:nt].rearrange("p (c l) -> p c l", l=Lc),
                dsc[:, 2, 0:nt].rearrange("p (c l) -> p c l", l=Lc)[:, :, Lc - 1:Lc].broadcast_to([16, nch, Lc])),
                reads=(bdsc,), writes=(bdsc,))

            if STOP <= 3:
                return
            for c in range(12):
                if smp:
                    src = lambda tap, c=c: extv[:, c, :, tap:tap + 4]
                    v3 = lambda ap: ap[:, 0:64].rearrange("p (b i) -> p b i", i=4)
                else:
                    src = lambda tap, c=c: ext[:, c, tap:tap + nt]
                    v3 = lambda ap: ap[:, 0:nt]
                S.op("dve", lambda e, c=c, src=src, v3=v3: e.tensor_scalar(
                    out=v3(cvA), in0=src(0), scalar1=pvc("conv_w", c * 4 + 0), scalar2=pvc("conv_b", c),
                    op0=ALU.mult, op1=ALU.add), reads=(bext, b_pv), writes=(bcv[0],))
                S.op("dve", lambda e, c=c, src=src, v3=v3: e.scalar_tensor_tensor(
                    out=v3(cvB), in0=src(1), scalar=pvc("conv_w", c * 4 + 1), in1=v3(cvA),
                    op0=ALU.mult, op1=ALU.add), reads=(bext, b_pv, bcv[0]), writes=(bcv[1],))
                S.op("dve", lambda e, c=c, src=src, v3=v3: e.scalar_tensor_tensor(
                    out=v3(cvA), in0=src(2), scalar=pvc("conv_w", c * 4 + 2), in1=v3(cvB),
                    op0=ALU.mult, op1=ALU.add), reads=(bext, b_pv, bcv[1]), writes=(bcv[0],))
                S.op("dve", lambda e, c=c, src=src, v3=v3: e.scalar_tensor_tensor(
                    out=v3(cvB), in0=src(3), scalar=pvc("conv_w", c * 4 + 3), in1=v3(cvA),
                    op0=ALU.mult, op1=ALU.add), reads=(bext, b_pv, bcv[0]), writes=(bcv[1],))
                if c < 8:
                    S.op("act", lambda e, c=c: e.activation(out=xcT[:, c, 0:nt], in_=cvB[:, 0:nt], func=AF.Silu),
                         reads=(bcv[1],), writes=(bxc,))
                else:
                    S.op("act", lambda e, c=c: e.activation(out=bcT[:, c - 8, 0:nt], in_=cvB[:, 0:nt], func=AF.Silu),
                         reads=(bcv[1],), writes=b_bc)
            if not smp:
                S.op("dve", lambda e: e.tensor_copy(ctail[:, :, :], ext[:, :, nt:nt + 3]), reads=(bext,), writes=(b_ctail,))
                if is_last_ptile:
                    for r in range(3):
                        S.op("sp", lambda e, r=r: e.dma_start(
                            out=o_conv_p[r:r + 1, :].rearrange("r (c p) -> p (r c)", p=128),
                            in_=ctail32[:, :, r], allow_slow_non_contiguous=True),
                            reads=(b_ctail,), dma=next_out(), final=True)
            else:
                pg, bpg = ps_group(4)
                S.op("pe", [lambda e, c=c: e.transpose(pg[0:64, c * 128:(c + 1) * 128], xpre[:, c, :], ident[:, :])
                            for c in range(12)], reads=(bxpre, b_ident), writes=bpg)
                S.op("act", lambda e: e.copy(cso[:, :], pg[0:64, 0:1536]), reads=bpg, writes=(bcso,))
                for r in range(3):
                    S.op("sp", lambda e, r=r: e.dma_start(out=o_conv_s[:, r, :], in_=cso[1 + r:64:4, :]),
                         reads=(bcso,), dma=next_out(), final=True, arena=True)

            if STOP <= 4:
                return
            if smp:
                S.op("act", lambda e: e.activation(out=eal[:, :].unsqueeze(2), in_=dsc[:, 2, 0:64].rearrange("p (b i) -> p b i", i=4)[:, :, 3:4],
                                                   func=AF.Exp), reads=(bdsc,), writes=(beal,))
                pt, bp = next_ps()
                S.op("pe", [lambda e, h=h: e.matmul(
                    pt[:, h * 16:(h + 1) * 16], ident[0:16, h:h + 1].broadcast_to([16, 128]),
                    eal[:, :], start=True, stop=True) for h in range(16)], reads=(beal, b_ident), writes=(bp,))
                ptv = pt[:, 0:256].rearrange("p (c t b) -> p c t b", c=8, t=2)
                S.op("dve", lambda e: e.tensor_copy(cdT[0:64, :, :], ptv[0:64, :, 0, :]), reads=(bp,), writes=(bcdT,))
                S.op("dve", lambda e: e.tensor_copy(cdT[64:128, :, :], ptv[64:128, :, 1, :]), reads=(bp,), writes=(bcdT,))
            maskT = cst_ap("maskT_blk", 64) if smp else cst_ap("maskT_causal", 128)
            for blk in range(cfg.get("nssd", nblk) if not smp else nblk):
                cols = slice(blk * CL, (blk + 1) * CL)
                pt, bp = next_ps()
                S.op("pe", [lambda e, q=q: e.transpose(pt[0:CL, q * 16:(q + 1) * 16], dsc[:, (0, 2, 3)[q], cols],
                                                        ident[0:16, 0:16]) for q in range(3)],
                     reads=(bdsc, b_ident), writes=(bp,))
                S.op("dve", lambda e, pt=pt: e.tensor_copy(tsc[0:CL, 0:48], pt[0:CL, 0:48]), reads=(bp,), writes=(btsc,))
                S.op("dve", lambda e: e.tensor_tensor(out=tsc[0:CL, 48:64], in0=tsc[0:CL, 32:48], in1=tsc[0:CL, 16:32],
                                                      op=ALU.subtract), reads=(btsc,), writes=(btsc,))
                S.op("act", lambda e: e.activation(out=tsc[0:CL, 48:64], in_=tsc[0:CL, 48:64], func=AF.Exp),
                     reads=(btsc,), writes=(btsc,))
                if cfg.get('sstage', 99) <= 1:
                    continue
                pa, bpa = ps_group(4)
                S.op("pe", [lambda e, h=h: e.matmul(pa[:, h * CL:(h + 1) * CL],
                                                     ident[0:16, h:h + 1].broadcast_to([16, 128]),
                                                     dsc[:, 2, cols], start=True, stop=True) for h in range(16)],
                     reads=(bdsc, b_ident), writes=bpa)
                pav = pa[:, 0:16 * CL].rearrange("p (h i) -> p h i", h=16)
                for h in range(16):
                    S.op("dve", lambda e, h=h: e.tensor_scalar(
                        out=absx[0:CL, h, 0:CL], in0=pav[0:CL, h, :], scalar1=tsc[0:CL, 16 + h:17 + h], scalar2=0.0,
                        op0=ALU.subtract, op1=ALU.min), reads=bpa + [btsc], writes=(babs,))
                S.op("act", lambda e: e.activation(out=Eb[0:CL, :, 0:CL], in_=absx[0:CL, :, 0:CL], func=AF.Exp),
                     reads=(babs,), writes=(bEb,))
                S.op("act", lambda e: e.activation(out=absx[:, :, 0:CL], in_=pav, func=AF.Exp),
                     reads=bpa, writes=(babs,))
                if cfg.get('sstage', 99) <= 2:
                    continue
                pc, bpc = next_ps()
                S.op("pe", [lambda e, g=g: e.matmul(pc[0:CL, g * CL:(g + 1) * CL], bcT[:, g, cols], bcT[:, 2 + g, cols],
                                                     start=True, stop=True) for g in range(2)],
                     reads=b_bc, writes=(bpc,))
                if cfg.get("cbx", 9) >= 1:
                  S.op("dve", lambda e, pc=pc: e.tensor_tensor(
                    out=cbm[0:CL, :, 0:CL], in0=pc[0:CL, 0:2 * CL].rearrange("p (g i) -> p g i", g=2),
                    in1=maskT[:, 0:CL].unsqueeze(1).broadcast_to([CL, 2, CL]), op=ALU.mult),
                    reads=(bpc, b_cs), writes=(bcbm,))
                for g in range(2 if cfg.get("cbx", 9) >= 2 else 0):
                    S.op("dve", lambda e, g=g: e.tensor_tensor(
                        out=Eb[0:CL, g * 8:(g + 1) * 8, 0:CL], in0=Eb[0:CL, g * 8:(g + 1) * 8, 0:CL],
                        in1=cbm[0:CL, g:g + 1, 0:CL].broadcast_to([CL, 8, CL]), op=ALU.mult),
                        reads=(bEb, bcbm), writes=(bEb,))
                    S.op("dve", lambda e, g=g: e.tensor_tensor(
                        out=Csb[:, g * 8:(g + 1) * 8, 0:CL], in0=absx[:, g * 8:(g + 1) * 8, 0:CL],
                        in1=bcT[:, 2 + g, cols].unsqueeze(1).broadcast_to([128, 8, CL]), op=ALU.mult),
                        reads=[babs] + b_bc, writes=(bCs,))
                if cfg.get('sstage', 99) <= 3:
                    continue
                px, bpx = next_ps()
                pxb = px.bitcast(BF16)
                S.op("pe", [lambda e, c=c: e.transpose(pxb[0:CL, c * 128:(c + 1) * 128], xcT[:, c, cols], identb[:, :])
                            for c in range(8)], reads=(bxc, b_ident), writes=(bpx,))
                S.op("dve", lambda e, pxb=pxb: e.tensor_tensor(
                    out=xd[0:CL, :, :], in0=pxb[0:CL, 0:1024].rearrange("p (h q) -> p h q", h=16),
                    in1=tsc[0:CL, 0:16].unsqueeze(2).broadcast_to([CL, 16, 64]), op=ALU.mult),
                    reads=(bpx, btsc), writes=(bxd,))
                S.op("dve", lambda e: e.tensor_tensor(
                    out=xdd[0:CL, :, :], in0=xd[0:CL, :, :],
                    in1=tsc[0:CL, 48:64].unsqueeze(2).broadcast_to([CL, 16, 64]), op=ALU.mult),
                    reads=(bxd, btsc), writes=(bxdd,))
                pb_, bpb = next_ps()
                pbb = pb_.bitcast(BF16)
                S.op("pe", [lambda e, g=g: e.transpose(pbb[0:CL, g * 128:(g + 1) * 128], bcT[:, g, cols], identb[:, :])
                            for g in range(2)], reads=b_bc + [b_ident], writes=(bpb,))
                S.op("act", lambda e, pbb=pbb: e.copy(Btok[0:CL, :, :], pbb[0:CL, 0:256].rearrange("p (g n) -> p g n", g=2)),
                     reads=(bpb,), writes=(bBt,))
                if cfg.get('sstage', 99) <= 4:
                    continue
                py, bpy = ps_group(2)
                py_banks = list(state["last_group"])
                fns = []
                for h in range(16):
                    dst = py[64 * (h % 2):64 * (h % 2) + 64, (h // 2) * CL:(h // 2 + 1) * CL]
                    first = (h < 2) or (not smp and ((h // 2) % 4 == 0))
                    if smp:
                        fns.append(lambda e, h=h, dst=dst, first=first: e.matmul(
                            dst, xd[0:CL, h, :], Eb[0:CL, h, 0:CL], start=first, stop=True, skip_group_check=True))
                    else:
                        fns.append(lambda e, h=h, dst=dst: e.matmul(
                            dst, xd[0:CL, h, :], Eb[0:CL, h, 0:CL], start=True, stop=False, skip_group_check=True))
                        fns.append(lambda e, h=h, dst=dst: e.matmul(
                            dst, STb[:, h * 64:(h + 1) * 64], Csb[:, h, 0:CL], start=False, stop=True,
                            skip_group_check=True))
                S.op("pe", fns, reads=(bxd, bEb, b_STb, bCs), writes=bpy)
                if smp:
                    reserved.update(py_banks)
                    sample_states(py, bpy, Snat, bSnat, STs, bSTs, Csb, bCs, xdd, bxdd, Btok, bBt, Bblk, bBblk,
                                  bmsk, cdT, bcdT)
                elif cfg.get('sstage', 99) > 5:
                    pst, bpst = ps_group(2)
                    S.op("pe", [lambda e, g=g: e.matmul(pst[:, g * 512:(g + 1) * 512], Btok[0:CL, g, :],
                                                         xdd[0:CL, g * 8:(g + 1) * 8, :].rearrange("p h q -> p (h q)"),
                                                         start=True, stop=True) for g in range(2)],
                         reads=(bBt, bxdd), writes=bpst)
                    S.op("dve", lambda e: e.tensor_tensor(
                        out=STf[:, :].rearrange("p (h q) -> p h q", h=16),
                        in0=STf[:, :].rearrange("p (h q) -> p h q", h=16),
                        in1=absx[:, :, CL - 1:CL].broadcast_to([128, 16, 64]), op=ALU.mult),
                        reads=(b_ST, babs), writes=(b_ST,))
                    S.op("dve", lambda e, pst=pst: e.tensor_tensor(out=STf[:, :], in0=STf[:, :], in1=pst[:, 0:1024], op=ALU.add),
                         reads=[b_ST] + bpst, writes=(b_ST,))
                    S.op("act", lambda e: e.copy(STb[:, :], STf[:, :]), reads=(b_ST,), writes=(b_STb,))
                if cfg.get('sstage', 99) <= 6:
                    continue
                for c in range(8):
                    S.op("dve", lambda e, c=c, py=py: e.scalar_tensor_tensor(
                        out=ygb[:, c, 0:CL], in0=xcT[:, c, cols], scalar=pvc("dskip", c), in1=py[:, c * CL:(c + 1) * CL],
                        op0=ALU.mult, op1=ALU.add), reads=[bxc, b_pv] + bpy, writes=(bygb,))
                    S.op("dve", lambda e, c=c: e.tensor_tensor(out=ygb[:, c, 0:CL], in0=ygb[:, c, 0:CL], in1=zT[:, c, cols],
                                                               op=ALU.mult), reads=(bygb, bz), writes=(bygb,))
                S.op("act", lambda e: e.activation(out=sqg[:, :, 0:CL], in_=ygb[:, :, 0:CL], func=AF.Square),
                     reads=(bygb,), writes=(bsqg,))
                pss, bpss = next_ps()
                for g in range(2):
                    S.op("pe", [lambda e, g=g, cc=cc, pss=pss: e.matmul(
                        pss[:, g * CL:(g + 1) * CL], onesb[:, :], sqg[:, g * 4 + cc, 0:CL], start=(cc == 0), stop=(cc == 3))
                        for cc in range(4)], reads=(bsqg, b_ident), writes=(bpss,))
                S.op("act", lambda e, pss=pss: e.activation(out=rsg[:, :, 0:CL], in_=pss[:, 0:2 * CL].rearrange("p (g i) -> p g i", g=2),
                                                             func=AF.Sqrt, bias=EPS, scale=1.0 / 512),
                     reads=(bpss,), writes=(brsg,))
                S.op("dve", lambda e: e.reciprocal(rsg[:, :, 0:CL], rsg[:, :, 0:CL]), reads=(brsg,), writes=(brsg,))
                reserved.difference_update(py_banks)
                for c in range(8):
                    S.op("dve", lambda e, c=c: e.scalar_tensor_tensor(
                        out=ybT[:, c, cols], in0=ygb[:, c, 0:CL], scalar=pvc("norm_g", c), in1=rsg[:, c // 4, 0:CL],
                        op0=ALU.mult, op1=ALU.mult), reads=(bygb, brsg, b_pv), writes=(byb,))

            if STOP <= 5:
                return
            if cfg.get("obar", 0):
                S.barrier(engines=("pe", "act", "dve", "sp", "pool"))
            for mi in range(4):
                if cfg.get("omode", 9) == 6:
                    w, bw = wring[mi], b_wr[mi]
                else:
                    w, bw = wload(wev_out[mi], 16 * 256)
                wv = w[:, :].rearrange("p (k n) -> p k n", k=16)
                for mm in range(2):
                    m = mi * 2 + mm
                    pt, bp = next_ps()
                    fns = []
                    for k in range(16):
                        rhs = yaT[:, k, 0:nt] if k < 8 else ybT[:, k - 8, 0:nt]
                        fns.append(lambda e, pt=pt, k=k, mm=mm, wv=wv, rhs=rhs: e.matmul(
                            pt[:, 0:nt], wv[:, k, mm * 128:(mm + 1) * 128], rhs, start=(k == 0), stop=(k == 15)))
                    OM = cfg.get("omode", 9)
                    if OM >= 1:
                        S.op("pe", fns[0:OM] if OM < 9 else fns, reads=[bw, byb] + b_ya, writes=(bp,))
                    if OM >= 9:
                        S.op("dve", lambda e, pt=pt, m=m: e.tensor_tensor(out=xT[:, m, 0:nt], in0=pt[:, 0:nt], in1=xT[:, m, 0:nt],
                                                                          op=ALU.add), reads=(bp, b_xT[m]), writes=(b_xT[m],))
            if STOP <= 6:
                return
            if is_last_ptile:
                so = absx[:, 0:8, :]
                bso = babs
                pg, bpg = ps_group(2)
                S.op("pe", [lambda e, c=c: e.transpose(pg[:, c * 128:(c + 1) * 128], STf[:, c * 128:(c + 1) * 128], ident[:, :])
                            for c in range(8)], reads=(b_ST, b_ident), writes=bpg)
                S.op("act", lambda e: e.copy(so.rearrange("p c n -> p (c n)"), pg[:, 0:1024]), reads=bpg, writes=(bso,))
                S.op("sp", lambda e: e.dma_start(out=o_ssm_p.rearrange("(c p) n -> p c n", p=128), in_=so),
                     reads=(bso,), dma=next_out(), final=True, arena=True)

        def sample_states(py, bpy, Snat, bSnat, STs, bSTs, Csb, bCs, xdd, bxdd, Btok, bBt, Bblk, bBblk, bmsk, cdT, bcdT):
            for g in range(2):
                S.op("dve", lambda e, g=g: e.tensor_tensor(
                    out=Bblk[:, g, :, :], in0=Btok[0:64, g, :].unsqueeze(1).broadcast_to([64, 16, 128]),
                    in1=bmsk.unsqueeze(2).broadcast_to([64, 16, 128]), op=ALU.mult),
                    reads=(bBt, b_cs), writes=(bBblk,))
            NB = 2
            for bg in range(16 // NB):
                S.op("sp", lambda e, bg=bg: e.dma_start(
                    out=Snat[:, :, :, :], in_=st_ssm[bg * NB:(bg + 1) * NB].rearrange("b (c p) n -> p b c n", p=128)),
                    writes=(bSnat,), dma=next_in(), arena=True)
                for bb in range(NB):
                    pg, bpg = ps_group(2)
                    S.op("pe", [lambda e, c=c, bb=bb, pg=pg: e.transpose(pg[:, c * 128:(c + 1) * 128], Snat[:, bb, c, :], ident[:, :])
                                for c in range(8)], reads=(bSnat, b_ident), writes=bpg)
                    if bb % 2:
                        S.op("act", lambda e, bb=bb, pg=pg: e.copy(STs[:, bb, :], pg[:, 0:1024]), reads=bpg, writes=(bSTs,))
                    else:
                        S.op("dve", lambda e, bb=bb, pg=pg: e.tensor_copy(STs[:, bb, :], pg[:, 0:1024]), reads=bpg, writes=(bSTs,))
                fns = []
                for bb in range(NB):
                    b = bg * NB + bb
                    for h in range(16):
                        dst = py[64 * (h % 2):64 * (h % 2) + 64, (h // 2) * 64 + 4 * b:(h // 2) * 64 + 4 * b + 4]
                        fns.append(lambda e, h=h, bb=bb, b=b, dst=dst: e.matmul(
                            dst, STs[:, bb, h * 64:(h + 1) * 64], Csb[:, h, 4 * b:4 * b + 4], start=False, stop=True,
                            skip_group_check=True))
                S.op("pe", fns, reads=(bSTs, bCs), writes=bpy)
                for c in range(8):
                    pt, bp = next_ps()
                    S.op("pe", lambda e, c=c, pt=pt, bg=bg: e.matmul(
                        pt[:, 0:NB * 128], xdd[0:64, 2 * c:2 * c + 2, :].rearrange("p h q -> p (h q)"),
                        Bblk[:, c // 4, bg * NB:(bg + 1) * NB, :].rearrange("p b n -> p (b n)"), start=True, stop=True),
                        reads=(bxdd, bBblk), writes=(bp,))
                    for bb in range(NB):
                        b = bg * NB + bb
                        S.op("dve", lambda e, c=c, bb=bb, b=b, pt=pt: e.scalar_tensor_tensor(
                            out=Snat[:, bb, c, :], in0=Snat[:, bb, c, :], scalar=cdT[:, c, b:b + 1],
                            in1=pt[:, bb * 128:(bb + 1) * 128], op0=ALU.mult, op1=ALU.add),
                            reads=(bSnat, bcdT, bp, bSTs), writes=(bSnat,))
                S.op("sp", lambda e, bg=bg: e.dma_start(
                    out=o_ssm_s[bg * NB:(bg + 1) * NB].rearrange("b (c p) n -> p b c n", p=128), in_=Snat[:, :, :, :]),
                    reads=(bSnat,), dma=next_out(), final=True, arena=True)

        WINS = (2, 4, 8, 16)

        def head_norm(pt, bp, nparts_dummy, nt, gname, out_ap, wbufs, scale, f32_out=None, f32_bufs=()):
            si = state["sil"]
            state["sil"] = 1 - si
            raw = sil[si][:, 0:nt]
            S.op("act", lambda e: e.copy(raw, pt), reads=(bp,), writes=(b_sil[si],))
            S.op("act", lambda e: e.activation(out=hsq[:, 0:nt], in_=pt, func=AF.Square), reads=(bp,), writes=(b_hsq,))
            ps2, bps2 = next_ps()
            S.op("pe", lambda e: e.matmul(ps2[:, 0:nt], bd64[:, :], hsq[:, 0:nt], start=True, stop=True),
                 reads=(b_hsq, b_ident), writes=(bps2,))
            if scale is None:
                S.op("act", lambda e: e.activation(out=hrs[:, 0:nt], in_=ps2[:, 0:nt], func=AF.Sqrt, bias=EPS, scale=1.0 / 64),
                     reads=(bps2,), writes=(b_hrs,))
            else:
                S.op("act", lambda e: e.activation(out=hrs[:, 0:nt], in_=ps2[:, 0:nt], func=AF.Sqrt, bias=EPS * scale * scale,
                                                   scale=scale * scale / 64), reads=(bps2,), writes=(b_hrs,))
            S.op("dve", lambda e: e.reciprocal(hrs[:, 0:nt], hrs[:, 0:nt]), reads=(b_hrs,), writes=(b_hrs,))
            S.op("dve", lambda e: e.scalar_tensor_tensor(out=out_ap, in0=raw, scalar=pvc(gname, 0), in1=hrs[:, 0:nt],
                                                         op0=ALU.mult, op1=ALU.mult),
                 reads=(b_sil[si], b_hrs, b_pv), writes=wbufs)
            if f32_out is not None:
                S.op("dve", lambda e: e.scalar_tensor_tensor(out=f32_out, in0=raw, scalar=pvc(gname, 0), in1=hrs[:, 0:nt],
                                                             op0=ALU.mult, op1=ALU.mult),
                     reads=(b_sil[si], b_hrs, b_pv), writes=f32_bufs)

        def odd_mixer(kind, nt, is_last_ptile, tile_idx):
            S.barrier()
            A.reset()
            smp = (kind == "s")
            qT = hT[:, 0:8, :]
            ycT = hT[:, 8:16, :]
            b_q, b_yc = b_h[0:8], b_h[8:16]
            ydT = A.take(128, [8, nt], BF16)
            byd = Buf("ydT")
            L = 19 if smp else (nt + 15)
            nb = 16 if smp else 1
            cext = A.take(128, [8, nb * L])
            bce = Buf("cext")
            pA = A.take(128, [nb * L])
            pB = A.take(128, [nb * L])
            bpA, bpB = Buf("pA"), Buf("pB")
            pooledT = A.take(128, [8, nt], BF16)
            bpool = Buf("pooled")
            KW = 64 if smp else (128 + nt)
            kdT = A.take(128, [4, KW], BF16)
            bkd = Buf("kdT")
            NVB = 1 if smp else 5
            vtok = A.take(128, [NVB, 256], BF16)
            bvt = Buf("vtok")
            kf32 = A.take(128, [4, 128 if not smp else 64])
            vf32 = A.take(128, [256])
            bkf, bvf = Buf("kf32"), Buf("vf32")
            ost = A.take(128, [1024])
            bost = Buf("ost")
            tS_ = [A.take(128, [1024]) for _ in range(2)]
            pT_ = [A.take(128, [1024], BF16) for _ in range(2)]
            btS_, bpT_ = [Buf("tS0"), Buf("tS1")], [Buf("pT0"), Buf("pT1")]
            tS, pT, btS, bpT = tS_[0], pT_[0], btS_[0], bpT_[0]
            rec = [A.take(128, [128]) for _ in range(2)]
            brec = [Buf("rec0"), Buf("rec1")]

            def cview(c, lo, n):
                if smp:
                    return cext[:, c, :].rearrange("p (b l) -> p b l", l=L)[:, :, lo:lo + n]
                return cext[:, c, lo:lo + n]

            def tview(t, lo, n):
                if smp:
                    return t[:, :].rearrange("p (b l) -> p b l", l=L)[:, :, lo:lo + n]
                return t[:, lo:lo + n]

            def ntv(ap2d):
                return ap2d.rearrange("p (b i) -> p b i", i=4) if smp else ap2d

            if smp:
                cin32 = A.take(128, [8, 64])
                bcin = Buf("cin32")
                stp = A.take(120, [2, 1024])
                bstp = Buf("stp")
                S.op("sp", lambda e: e.dma_start(out=stp, in_=st_pool.rearrange("(g r) d -> r g d", g=2)),
                     writes=(bstp,), dma=next_in(), arena=True)
                S.op("sp", lambda e: e.dma_start(out=o_pool_s[:, 0:11, :], in_=st_pool.rearrange("(b r) d -> b r d", r=15)[:, 4:15, :]),
                     dma=next_out(), final=True)
                S.op("sp", lambda e: e.dma_start(out=o_k_s[:, 0:124, :], in_=st_k[:, 4:128, :]), dma=next_out(), final=True)
                S.op("sp", lambda e: e.dma_start(out=o_v_s[:, 0:124, :], in_=st_v[:, 4:128, :]), dma=next_out(), final=True)
                for g in range(2):
                    for cq in range(2):
                        pg, bpg = ps_group(2)
                        S.op("pe", [lambda e, c4=c4, g=g, cq=cq, pg=pg: e.transpose(
                            pg[:, c4 * 128:c4 * 128 + 120], stp[0:120, g, (cq * 4 + c4) * 128:(cq * 4 + c4 + 1) * 128],
                            ident[0:120, 0:120]) for c4 in range(4)], reads=(bstp, b_ident), writes=bpg)
                        for c4 in range(4):
                            c = cq * 4 + c4
                            S.op("dve" if c4 % 2 else "act",
                                 (lambda e, c=c, c4=c4, g=g, pg=pg: e.tensor_copy(
                                     cext[:, c, :].rearrange("p (b l) -> p b l", l=L)[:, g * 8:(g + 1) * 8, 0:15],
                                     pg[:, c4 * 128:c4 * 128 + 120].rearrange("p (b r) -> p b r", r=15))) if c4 % 2 else
                                 (lambda e, c=c, c4=c4, g=g, pg=pg: e.copy(
                                     cext[:, c, :].rearrange("p (b l) -> p b l", l=L)[:, g * 8:(g + 1) * 8, 0:15],
                                     pg[:, c4 * 128:c4 * 128 + 120].rearrange("p (b r) -> p b r", r=15))),
                                 reads=bpg, writes=(bce,))
            else:
                S.op("dve", lambda e: e.tensor_copy(cext[:, :, 0:15], ptail[:, :, :]), reads=(b_ptail,), writes=(bce,))
                bmP = A.take(128, [2, 16, 128])
                bbm = Buf("bm")
                for kb, off in ((0, 256), (1, 128)):
                    S.op("sp", lambda e, kb=kb, off=off: e.dma_start(
                        out=bmP[:, kb, :, :], in_=bass.AP(tensor=dscr.tensor, offset=off, ap=[[383, 128], [128 * 384, 16], [1, 128]])),
                        reads=(b_dscr,), writes=(bbm,), dma=next_in(), arena=True)

            rms_norm("mix_norm1", nt)

            for gi in range(2):
                def ev_c(cc, pt, bp, gi=gi):
                    c = gi * 4 + cc
                    S.op("act", lambda e: e.copy(cview(c, 15, nt if not smp else 4), ntv(pt[:, 0:nt])), reads=(bp,), writes=(bce,))
                    if smp:
                        S.op("dve", lambda e: e.tensor_copy(cin32[:, c, :], pt[:, 0:nt]), reads=(bp,), writes=(bcin,))
                proj_fm(wod_fm[gi], 4, nt, ev_c)
            for gi in range(2):
                def ev_q(cc, pt, bp, gi=gi):
                    c = gi * 4 + cc
                    head_norm(pt[:, 0:nt], bp, 128, nt, "q_norm", qT[:, c, 0:nt], (b_q[c],), 8.0)
                proj_fm(wod_fm[2 + gi], 4, nt, ev_q)
            wk, bwk = wload(wod_k, 8 * 256)
            wkv = wk[:, 0:2048].rearrange("p (k n) -> p k n", k=8)
            k0 = 0 if smp else 128
            if not smp:
                S.op("dve", lambda e: e.tensor_copy(kdT[:, :, 0:128], kprev[:, :, :]), reads=(b_kprev,), writes=(bkd,))
                S.op("dve", lambda e: e.tensor_copy(vtok[:, 0, :], vprev[:, :]), reads=(b_kprev,), writes=(bvt,))
            for hk in range(4):
                pt, bp = next_ps()
                fns = []
                for half in range(2):
                    for k in range(8):
                        fns.append(lambda e, pt=pt, half=half, k=k, hk=hk: e.matmul(
                            pt[64 * half:64 * half + 64, 0:nt], wkv[:, k, hk * 64:(hk + 1) * 64], xn[:, k, 0:nt],
                            start=(k == 0), stop=(k == 7), skip_group_check=True))
                S.op("pe", fns, reads=(bwk, b_xn), writes=(bp,))
                _kn(pt, bp, hk, nt, k0, kdT, bkd, kf32, bkf, smp, is_last_ptile)
            wv_, bwv_ = wload(wod_v, 8 * 256)
            wvv_ = wv_[:, 0:2048].rearrange("p (k n) -> p k n", k=8)
            CLv = 64 if smp else 128
            for blk in range(nt // CLv):
                cols = slice(blk * CLv, (blk + 1) * CLv)
                pt, bp = next_ps()
                S.op("pe", [lambda e, pt=pt, k=k, cols=cols: e.matmul(pt[0:CLv, 0:256], xn[:, k, cols], wvv_[:, k, :],
                                                                       start=(k == 0), stop=(k == 7)) for k in range(8)],
                     reads=(bwv_, b_xn), writes=(bp,))
                vb = 0 if smp else blk + 1
                S.op("act", lambda e, pt=pt, vb=vb: e.copy(vtok[0:CLv, vb, :], pt[0:CLv, 0:256]), reads=(bp,), writes=(bvt,))
                if smp or (is_last_ptile and blk == 3):
                    S.op("dve", lambda e, pt=pt: e.tensor_copy(vf32[0:CLv, :], pt[0:CLv, 0:256]), reads=(bp,), writes=(bvf,))
            if smp or is_last_ptile:
                if smp:
                    for i in range(4):
                        S.op("sp", lambda e, i=i: e.dma_start(out=o_v_s[:, 124 + i, :], in_=vf32[i:64:4, :]),
                             reads=(bvf,), dma=next_out(), final=True, arena=True)
                else:
                    S.op("sp", lambda e: e.dma_start(out=o_v_p, in_=vf32[:, :]), reads=(bvf,), dma=next_out(), final=True, arena=True)
                pk, bpk = next_ps()
                nk = 64 if smp else 128
                S.op("pe", [lambda e, hk=hk: e.transpose(pk[0:nk, hk * 64:(hk + 1) * 64], kf32[0:64, hk, 0:nk], ident[0:64, 0:64])
                            for hk in range(4)], reads=(bkf, b_ident), writes=(bpk,))
                S.op("act", lambda e: e.copy(ost[0:nk, 0:256], pk[0:nk, 0:256]), reads=(bpk,), writes=(bost,))
                if smp:
                    for i in range(4):
                        S.op("sp", lambda e, i=i: e.dma_start(out=o_k_s[:, 124 + i, :], in_=ost[i:64:4, 0:256]),
                             reads=(bost,), dma=next_out(), final=True, arena=True)
                else:
                    S.op("sp", lambda e: e.dma_start(out=o_k_p, in_=ost[:, 0:256]), reads=(bost,), dma=next_out(), final=True, arena=True)

            first_tile = (not smp) and tile_idx == 0
            for c in range(8):
                gi = c // 2
                w = WINS[gi]
                Lx = L
                cur_t, cur_b, cur_lo = None, None, 0
                src_ap = lambda lo, n, c=c: cview(c, lo, n)
                width = 1
                bufs = [(pA, bpA), (pB, bpB)]
                bi = 0
                srcf, srcb, lo0 = src_ap, bce, 0
                while width < w:
                    dst, bdst = bufs[bi]
                    bi = 1 - bi
                    lo1 = lo0 + width
                    n = Lx - lo1
                    S.op("dve", lambda e, srcf=srcf, dst=dst, lo1=lo1, n=n, width=width: e.tensor_tensor(
                        out=tview(dst, lo1, n), in0=srcf(lo1, n), in1=srcf(lo1 - width, n), op=ALU.add),
                        reads=(srcb,), writes=(bdst,))
                    srcf = (lambda lo, n, dst=dst: tview(dst, lo, n))
                    srcb, lo0 = bdst, lo1
                    width *= 2
                nn = 4 if smp else nt
                S.op("dve", lambda e, srcf=srcf, c=c, w=w, nn=nn: e.scalar_tensor_tensor(
                    out=ntv(pooledT[:, c, 0:nt]), in0=srcf(15, nn), scalar=1.0 / w, in1=cview(c, 15, nn),
                    op0=ALU.mult, op1=ALU.subtract), reads=(srcb, bce), writes=(bpool,))
                if first_tile:
                    S.op("dve", lambda e, srcf=srcf, gi=gi: e.tensor_tensor(
                        out=rec[0][:, 0:16], in0=srcf(15, 16), in1=cst_ap("invc")[:, gi * 16:(gi + 1) * 16], op=ALU.mult),
                        reads=(srcb, b_cs), writes=(brec[0],))
                    S.op("dve", lambda e, c=c: e.tensor_tensor(
                        out=pooledT[:, c, 0:16], in0=rec[0][:, 0:16], in1=cview(c, 15, 16), op=ALU.subtract),
                        reads=(brec[0], bce), writes=(bpool,))
            if not smp:
                S.op("dve", lambda e: e.tensor_copy(ptail[:, :, :], cext[:, :, nt:nt + 15]), reads=(bce,), writes=(b_ptail,))
            if smp or is_last_ptile:
                nr = 64 if smp else 15
                pg, bpg = ps_group(2)
                if smp:
                    S.op("pe", [lambda e, c=c: e.transpose(pg[0:64, c * 128:(c + 1) * 128], cin32[:, c, :], ident[:, :])
                                for c in range(8)], reads=(bcin, b_ident), writes=bpg)
                else:
                    S.op("pe", [lambda e, c=c: e.transpose(pg[0:15, c * 128:(c + 1) * 128], cext[:, c, nt:nt + 15], ident[:, :])
                                for c in range(8)], reads=(bce, b_ident), writes=bpg)
                S.op("dve", lambda e: e.tensor_copy(ost[0:nr, :], pg[0:nr, 0:1024]), reads=bpg + [bost], writes=(bost,))
                if smp:
                    for i in range(4):
                        S.op("sp", lambda e, i=i: e.dma_start(out=o_pool_s[:, 11 + i, :], in_=ost[i:64:4, :]),
                             reads=(bost,), dma=next_out(), final=True, arena=True)
                else:
                    S.op("sp", lambda e: e.dma_start(out=o_pool_p, in_=ost[0:15, :]), reads=(bost,), dma=next_out(), final=True, arena=True)
            wl, bwl = wload(wod_lin, 4 * 2 * 256)
            wlv = wl[:, 0:2048].rearrange("p (g c d) -> p g c d", g=4, c=2)
            for c in range(8):
                gi, dd = c // 2, c % 2
                pt, bp = next_ps()
                S.op("pe", [lambda e, pt=pt, cc=cc, gi=gi, dd=dd: e.matmul(
                    pt[:, 0:nt], wlv[:, gi, cc, dd * 128:(dd + 1) * 128], pooledT[:, gi * 2 + cc, 0:nt],
                    start=(cc == 0), stop=(cc == 1)) for cc in range(2)], reads=(bwl, bpool), writes=(bp,))
                S.op("act", lambda e, pt=pt, c=c: e.activation(out=ycT[:, c, 0:nt], in_=pt[:, 0:nt], func=AF.Copy,
                                                               scale=pvc("c_scale", c)), reads=(bp, b_pv), writes=(b_yc[c],))

            if smp:
                sample_attention(nt, qT, b_q, kdT, bkd, vtok, bvt, ydT, byd, tS, btS, pT, bpT, rec, brec)
            else:
                po, bpo = ps_group(2)
                po_banks = list(state["last_group"])
                pd, bpd = ps_group(2)
                pd_banks = list(state["last_group"])
                reserved.update(po_banks + pd_banks)
                for qb in range(4):
                    qcols = slice(qb * 128, (qb + 1) * 128)
                    gq = tile_idx * 4 + qb
                    kbs = [1] if gq == 0 else [0, 1]
                    nkb = len(kbs)
                    for hq in range(4):
                        tS, pT, btS, bpT = tS_[hq % 2], pT_[hq % 2], btS_[hq % 2], bpT_[hq % 2]
                        psA, bpsA = next_ps()
                        psB, bpsB = next_ps()
                        pss_ = (psA, psB)
                        fns = []
                        for hl in range(4):
                            h = hq * 4 + hl
                            hh, s, hp = hl % 2, hl // 2, h // 2
                            for ki, kb in enumerate(kbs):
                                kc0 = qb * 128 + kb * 128
                                fns.append(lambda e, hh=hh, s=s, ki=ki, kc0=kc0, hp=hp, hq=hq, pss_=pss_: e.matmul(
                                    pss_[hh][:, (s * 2 + ki) * 128:(s * 2 + ki + 1) * 128],
                                    kdT[64 * hh:64 * hh + 64, hq, kc0:kc0 + 128], qT[64 * hh:64 * hh + 64, hp, qcols],
                                    start=True, stop=True))
                        S.op("pe", fns, reads=[bkd] + b_q[hq * 2:hq * 2 + 2], writes=(bpsA, bpsB))
                        for hl in range(4):
                            h = hq * 4 + hl
                            hh, s = hl % 2, hl // 2
                            for ki, kb in enumerate(kbs):
                                j = s * 2 + ki
                                S.op("dve", lambda e, hh=hh, j=j, kb=kb, h=h, pss_=pss_: e.tensor_tensor(
                                    out=tS[:, hh * 512 + j * 128:hh * 512 + (j + 1) * 128], in0=pss_[hh][:, j * 128:(j + 1) * 128],
                                    in1=bmP[:, kb, h, :], op=ALU.add), reads=((bpsA, bpsB)[hh], bbm), writes=(btS,))
                        if nkb == 2:
                            S.op("act", lambda e: e.activation(out=pT[:, 0:1024], in_=tS[:, 0:1024], func=AF.Exp),
                                 reads=(btS,), writes=(bpT,))
                        else:
                            S.op("act", lambda e: e.activation(
                                out=pT[:, 0:1024].rearrange("p (a b) -> p a b", a=4)[:, :, 0:128],
                                in_=tS[:, 0:1024].rearrange("p (a b) -> p a b", a=4)[:, :, 0:128], func=AF.Exp),
                                reads=(btS,), writes=(bpT,))
                        fns = []
                        for hl in range(4):
                            h = hq * 4 + hl
                            hh, s, hp = hl % 2, hl // 2, h // 2
                            for ki, kb in enumerate(kbs):
                                j = hh * 4 + s * 2 + ki
                                fns.append(lambda e, hh=hh, ki=ki, kb=kb, hq=hq, hp=hp, j=j, qb=qb: e.matmul(
                                    po[64 * hh:64 * hh + 64, hp * 128:(hp + 1) * 128], vtok[:, qb + kb, hq * 64:(hq + 1) * 64],
                                    pT[:, j * 128:(j + 1) * 128], start=(ki == 0), stop=(ki == nkb - 1), skip_group_check=True))
                            for ki, kb in enumerate(kbs):
                                j = hh * 4 + s * 2 + ki
                                fns.append(lambda e, hh=hh, ki=ki, hp=hp, j=j: e.matmul(
                                    pd[64 * hh:64 * hh + 64, hp * 128:(hp + 1) * 128], onesb[:, 0:64],
                                    pT[:, j * 128:(j + 1) * 128], start=(ki == 0), stop=(ki == nkb - 1), skip_group_check=True))
                        S.op("pe", fns, reads=(bvt, bpT, b_ident), writes=bpo + bpd)
                    for c in range(8):
                        ri = c % 2
                        S.op("dve", lambda e, c=c, ri=ri: e.tensor_scalar(
                            out=rec[ri][:, :], in0=pd[:, c * 128:(c + 1) * 128], scalar1=esink2[:, c:c + 1], scalar2=None,
                            op0=ALU.add), reads=bpd + [b_gm2], writes=(brec[ri],))
                        S.op("dve", lambda e, ri=ri: e.reciprocal(rec[ri][:, :], rec[ri][:, :]), reads=(brec[ri],), writes=(brec[ri],))
                        S.op("dve", lambda e, c=c, ri=ri: e.tensor_tensor(
                            out=ydT[:, c, qcols], in0=po[:, c * 128:(c + 1) * 128], in1=rec[ri][:, :], op=ALU.mult),
                            reads=bpo + [brec[ri]], writes=(byd,))
                reserved.difference_update(po_banks + pd_banks)
                S.op("dve", lambda e: e.tensor_copy(kprev[:, :, :], kdT[:, :, nt:nt + 128]), reads=(bkd,), writes=(b_kprev,))
                S.op("dve", lambda e: e.tensor_copy(vprev[:, :], vtok[:, 4, :]), reads=(bvt,), writes=(b_kprev,))

            for mi in range(4):
                w, bw = wload(wod_out[mi], 16 * 256)
                wv = w[:, :].rearrange("p (k n) -> p k n", k=16)
                for mm in range(2):
                    m = mi * 2 + mm
                    pt, bp = next_ps()
                    fns = []
                    ks = {"c": range(8), "d": range(8, 16)}.get(cfg.get("odd_part"), range(16))
                    for k in ks:
                        rhs = ycT[:, k, 0:nt] if k < 8 else ydT[:, k - 8, 0:nt]
                        fns.append(lambda e, pt=pt, k=k, mm=mm, wv=wv, rhs=rhs, ks=ks: e.matmul(
                            pt[:, 0:nt], wv[:, k, mm * 128:(mm + 1) * 128], rhs, start=(k == ks[0]), stop=(k == ks[-1])))
                    S.op("pe", fns, reads=[bw, byd] + b_yc, writes=(bp,))
                    S.op("dve", lambda e, pt=pt, m=m: e.tensor_tensor(out=xT[:, m, 0:nt], in0=pt[:, 0:nt], in1=xT[:, m, 0:nt],
                                                                      op=ALU.add), reads=(bp, b_xT[m]), writes=(b_xT[m],))

        def _kn(pt, bp, hk, nt, k0, kdT, bkd, kf32, bkf, smp, is_last_ptile):
            if smp:
                head_norm(pt[:, 0:nt], bp, 128, nt, "k_norm", kdT[:, hk, k0:k0 + nt], (bkd,), None,
                          f32_out=kf32[:, hk, 0:64], f32_bufs=(bkf,))
            elif is_last_ptile:
                head_norm(pt[:, 0:nt], bp, 128, nt, "k_norm", kdT[:, hk, k0:k0 + nt], (bkd,), None)
                si = 1 - state["sil"]
                S.op("dve", lambda e: e.scalar_tensor_tensor(out=kf32[:, hk, :], in0=sil[si][:, nt - 128:nt], scalar=pvc("k_norm", 0),
                                                             in1=hrs[:, nt - 128:nt], op0=ALU.mult, op1=ALU.mult),
                     reads=(b_sil[si], b_hrs, b_pv), writes=(bkf,))
            else:
                head_norm(pt[:, 0:nt], bp, 128, nt, "k_norm", kdT[:, hk, k0:k0 + nt], (bkd,), None)

        def sample_attention(nt, qT, b_q, kdT, bkd, vtok, bvt, ydT, byd, tS, btS, pT, bpT, rec, brec):
            kc32 = A.take(128, [256])
            kcb = A.take(128, [4, 2, 64], BF16)
            bkc, bkcb = Buf("kc32"), Buf("kcb")
            KdT = A.take(128, [16, 4, 128], BF16)
            bKd = Buf("KdT")
            vc32 = A.take(128, [2, 256])
            vcb = A.take(128, [16, 256], BF16)
            bvc, bvcb = Buf("vc32"), Buf("vcb")
            bm1 = A.take(128, [16, 4])
            bm2 = A.take(64, [16, 64])
            bbm1, bbm2 = Buf("bm1"), Buf("bm2")
            t1 = A.take(128, [16, 64])
            p1 = A.take(128, [16, 64], BF16)
            t2 = A.take(64, [16, 64])
            p2 = A.take(64, [16, 64], BF16)
            bt1, bp1, bt2, bp2 = Buf("t1"), Buf("p1"), Buf("t2"), Buf("p2")
            S.op("sp", lambda e: e.dma_start(out=bm1, in_=bass.AP(tensor=dscr.tensor, offset=256, ap=[[383, 128], [128 * 384, 16], [1, 4]])),
                 reads=(b_dscr,), writes=(bbm1,), dma=next_in(), arena=True)
            S.op("dve", lambda e: e.memset(bm2, NEG), writes=(bbm2,))
            for b in range(16):
                S.op("sp", lambda e, b=b: e.dma_start(
                    out=bm2[4 * b:4 * b + 4, :, 4 * b:4 * b + 4],
                    in_=bass.AP(tensor=dscr.tensor, offset=128, ap=[[383, 4], [128 * 384, 16], [1, 4]])),
                    reads=(b_dscr,), writes=(bbm2,), dma=next_in(), arena=True)
            for b in range(16):
                S.op("sp", lambda e, b=b: e.dma_start(out=kc32, in_=st_k[b]), writes=(bkc,), dma=next_in(), arena=True)
                S.op("dve", lambda e: e.tensor_copy(kcb[:, :, :, :], kc32[:, :].rearrange("p (h d) -> p h d", h=4).unsqueeze(2)
                                                    .broadcast_to([128, 4, 2, 64])), reads=(bkc,), writes=(bkcb,))
                pk, bpk = next_ps()
                pkb = pk.bitcast(BF16)
                S.op("pe", [lambda e, hk=hk, pkb=pkb: e.transpose(pkb[:, hk * 128:(hk + 1) * 128],
                                                                  kcb[:, hk, :, :].rearrange("p a d -> p (a d)"), identb[:, :])
                            for hk in range(4)], reads=(bkcb, b_ident), writes=(bpk,))
                S.op("act", lambda e, b=b, pkb=pkb: e.copy(KdT[:, b, :, :].rearrange("p h k -> p (h k)"), pkb[:, 0:512]),
                     reads=(bpk,), writes=(bKd,))
                if b % 2 == 0:
                    S.op("sp", lambda e, b=b: e.dma_start(out=vc32, in_=st_v[b:b + 2].rearrange("b k d -> k b d")),
                         writes=(bvc,), dma=next_in(), arena=True)
                    S.op("dve", lambda e, b=b: e.tensor_copy(vcb[:, b:b + 2, :], vc32[:, :, :]), reads=(bvc,), writes=(bvcb,))
            ps1, bps1 = ps_group(2)
            g1 = list(state["last_group"])
            reserved.update(g1)
            fns = []
            for b in range(16):
                for h in range(16):
                    hh, hp, hk = h % 2, h // 2, h // 4
                    col = hh * 512 + b * 32 + hp * 4
                    fns.append(lambda e, b=b, hh=hh, hp=hp, hk=hk, col=col: e.matmul(
                        ps1[:, col:col + 4], KdT[64 * hh:64 * hh + 64, b, hk, :],
                        qT[64 * hh:64 * hh + 64, hp, 4 * b:4 * b + 4], start=True, stop=True))
            S.op("pe", fns, reads=[bKd] + b_q, writes=bps1)
            bm1v = bm1.rearrange("p (hp t) i -> p hp t i", t=2)
            for hh in range(2):
                S.op("dve", lambda e, hh=hh: e.tensor_tensor(
                    out=t1[:, hh * 8:(hh + 1) * 8, :].rearrange("p a (c d) -> p (a c) d", d=32) if False else
                    t1.rearrange("p a x -> p (a x)")[:, hh * 512:(hh + 1) * 512].rearrange("p (b hp i) -> p b hp i", b=16, hp=8),
                    in0=ps1[:, hh * 512:(hh + 1) * 512].rearrange("p (b hp i) -> p b hp i", b=16, hp=8),
                    in1=bm1v[:, :, hh, :].unsqueeze(1).broadcast_to([128, 16, 8, 4]), op=ALU.add),
                    reads=bps1 + [bbm1], writes=(bt1,))
            reserved.difference_update(g1)
            S.op("act", lambda e: e.activation(out=p1[:, :, :], in_=t1[:, :, :], func=AF.Exp), reads=(bt1,), writes=(bp1,))
            p1f = p1.rearrange("p a x -> p (a x)")
            ps2, bps2 = ps_group(2)
            S.op("pe", [lambda e, h=h: e.matmul(ps2[0:64, (h % 2) * 512 + (h // 2) * 64:(h % 2) * 512 + (h // 2 + 1) * 64],
                                                 kdT[64 * (h % 2):64 * (h % 2) + 64, h // 4, 0:64],
                                                 qT[64 * (h % 2):64 * (h % 2) + 64, h // 2, 0:64], start=True, stop=True)
                        for h in range(16)], reads=[bkd] + b_q, writes=bps2)
            bm2v = bm2.rearrange("p (hp t) x -> p hp t x", t=2)
            t2f = t2.rearrange("p a x -> p (a x)")
            for hh in range(2):
                S.op("dve", lambda e, hh=hh: e.tensor_tensor(
                    out=t2f[:, hh * 512:(hh + 1) * 512].rearrange("p (hp x) -> p hp x", hp=8),
                    in0=ps2[0:64, hh * 512:(hh + 1) * 512].rearrange("p (hp x) -> p hp x", hp=8),
                    in1=bm2v[:, :, hh, :], op=ALU.add), reads=bps2 + [bbm2], writes=(bt2,))
            S.op("act", lambda e: e.activation(out=p2[:, :, :], in_=t2[:, :, :], func=AF.Exp), reads=(bt2,), writes=(bp2,))
            p2f = p2.rearrange("p a x -> p (a x)")
            po, bpo = next_ps()
            pd, bpd = next_ps()
            fns = []
            for h in range(16):
                hh, hp, hk = h % 2, h // 2, h // 4
                first = h < 2
                fns.append(lambda e, h=h, hh=hh, hp=hp, hk=hk, first=first: e.matmul(
                    po[64 * hh:64 * hh + 64, hp * 64:(hp + 1) * 64], vtok[0:64, 0, hk * 64:(hk + 1) * 64],
                    p2f[:, hh * 512 + hp * 64:hh * 512 + (hp + 1) * 64],
                    start=first, stop=False, skip_group_check=True))
                fns.append(lambda e, h=h, hh=hh, hp=hp, first=first: e.matmul(
                    pd[64 * hh:64 * hh + 64, hp * 64:(hp + 1) * 64], onesb[0:64, 0:64],
                    p2f[:, hh * 512 + hp * 64:hh * 512 + (hp + 1) * 64],
                    start=first, stop=False, skip_group_check=True))
            for b in range(16):
                for h in range(16):
                    hh, hp, hk = h % 2, h // 2, h // 4
                    fns.append(lambda e, b=b, h=h, hh=hh, hp=hp, hk=hk: e.matmul(
                        po[64 * hh:64 * hh + 64, hp * 64 + 4 * b:hp * 64 + 4 * b + 4], vcb[:, b, hk * 64:(hk + 1) * 64],
                        p1f[:, hh * 512 + b * 32 + hp * 4:hh * 512 + b * 32 + hp * 4 + 4], start=False, stop=True,
                        skip_group_check=True))
                    fns.append(lambda e, b=b, h=h, hh=hh, hp=hp: e.matmul(
                        pd[64 * hh:64 * hh + 64, hp * 64 + 4 * b:hp * 64 + 4 * b + 4], onesb[:, 0:64],
                        p1f[:, hh * 512 + b * 32 + hp * 4:hh * 512 + b * 32 + hp * 4 + 4], start=False, stop=True,
                        skip_group_check=True))
            S.op("pe", fns, reads=(bvt, bp2, bp1, bvcb, b_ident), writes=(bpo, bpd))
            for c in range(8):
                ri = c % 2
                S.op("dve", lambda e, c=c, ri=ri: e.tensor_scalar(
                    out=rec[ri][:, 0:64], in0=pd[:, c * 64:(c + 1) * 64], scalar1=esink2[:, c:c + 1], scalar2=None,
                    op0=ALU.add), reads=(bpd, b_gm2), writes=(brec[ri],))
                S.op("dve", lambda e, ri=ri: e.reciprocal(rec[ri][:, 0:64], rec[ri][:, 0:64]), reads=(brec[ri],), writes=(brec[ri],))
                S.op("dve", lambda e, c=c, ri=ri: e.tensor_tensor(
                    out=ydT[:, c, 0:64], in0=po[:, c * 64:(c + 1) * 64], in1=rec[ri][:, 0:64], op=ALU.mult),
                    reads=(bpo, brec[ri]), writes=(byd,))

        tiles = [("p", t) for t in range(n_ptiles)] + ([("s", 0)] if do_sample else [])

        def issue_load(kind, t):
            if kind == "p":
                load_x_tile(xp[t * NTP:(t + 1) * NTP, :], 4, 128)
            else:
                load_x_tile(xs[:, :], 1, NS)

        issue_load(*tiles[0])
        for ti, (kind, t) in enumerate(tiles):
            nt = NTP if kind == "p" else NS
            last_p = (kind == "p" and t == n_ptiles - 1)
            if kind == "p":
                transpose_in(4, 128)
            else:
                transpose_in(1, NS)
            for layer in range(nlayers):
                if on("ffn1"):
                    ffn(layer, 0, nt)
                if layer == 0 and on("even"):
                    even_mixer(kind, nt, last_p)
                if layer == 1 and on("odd"):
                    odd_mixer(kind, nt, last_p, t)
                if layer == nlayers - 1:
                    S.barrier()
                    if ti + 1 < len(tiles):
                        issue_load(*tiles[ti + 1])
                if on("ffn2"):
                    ffn(layer, 1, nt)
            if kind == "p":
                transpose_out(yp[t * NTP:(t + 1) * NTP, :], 4, 128)
            else:
                transpose_out(ys[:, :], 1, NS)

        with nc.Block() as block:
            S.emit(block)
    P.ninst = S.ninst
    return P


PV_OFF = {}
_col = 0


def _pv(name, n):
    global _col
    PV_OFF[name] = _col
    _col += n


for _l in range(2):
    for _w in range(2):
        _pv(f"ffn_norm{_l}{_w}", 8)
for _l in range(2):
    _pv(f"mix_norm{_l}", 8)
_pv("ln_g", 8)
_pv("ln_b", 8)
_pv("conv_w", 48)
_pv("conv_b", 12)
_pv("norm_g", 8)
_pv("dskip", 8)
_pv("c_scale", 8)
_pv("q_norm", 1)
_pv("k_norm", 1)
_pv("dt_bias", 1)
_pv("a_log", 1)
PV_COLS = _col

CST_OFF = {}
_ccol = 0


def _cs(name, n):
    global _ccol
    CST_OFF[name] = (_ccol, n)
    _ccol += n


_cs("maskT_causal", 128)
_cs("maskT_blk", 64)
_cs("bmask", 16)
_cs("rmask_p", 512)
_cs("rmask_s", 64)
_cs("onehot", 384)
_cs("negmask", 384)
_cs("invc", 64)
CST_COLS = _ccol


def make_consts():
    c = np.zeros((128, CST_COLS), np.float32)
    j = np.arange(128)[:, None]
    i = np.arange(128)[None, :]
    o, n = CST_OFF["maskT_causal"]
    c[:, o:o + n] = (i >= j)
    o, n = CST_OFF["maskT_blk"]
    jj = np.arange(64)[:, None]
    ii = np.arange(64)[None, :]
    c[:64, o:o + n] = (ii >= jj) & (ii // 4 == jj // 4)
    o, n = CST_OFF["bmask"]
    c[:64, o:o + n] = (np.arange(64)[:, None] // 4 == np.arange(16)[None, :])
    o, n = CST_OFF["rmask_p"]
    c[:, o:o + n] = (np.arange(512)[None, :] % 128 != 0)
    o, n = CST_OFF["rmask_s"]
    c[:, o:o + n] = (np.arange(64)[None, :] % 4 != 0)
    dist = np.arange(384) - 128
    valid = (dist >= 0) & (dist < 128)
    nn = np.maximum(dist, 0)
    n_safe = np.maximum(nn, 1).astype(np.float32)
    scale = np.float32((32 - 16) / math.log(128 / 16))
    large = np.minimum(16 + (np.log(n_safe / 16) * scale).astype(np.int32), 31)
    bucket = np.where(nn < 16, nn, large).astype(np.int32)
    o, n = CST_OFF["onehot"]
    oh = np.zeros((32, 384), np.float32)
    oh[bucket[valid], np.arange(384)[valid]] = 1.0
    c[:32, o:o + n] = oh
    o, n = CST_OFF["negmask"]
    c[:, o:o + n] = np.where(valid, 0.0, NEG)[None, :]
    o, n = CST_OFF["invc"]
    for gi, w in enumerate((2, 4, 8, 16)):
        c[:, o + gi * 16:o + (gi + 1) * 16] = (1.0 / np.minimum(np.arange(16) + 1, w))[None, :]
    return c


def fm(v):
    v = np.asarray(v, np.float32)
    return np.ascontiguousarray(v.reshape(-1, 128).T)


def pack_host(inp):
    f = lambda k: np.asarray(inp[k], np.float32)
    pvec = np.zeros((128, PV_COLS), np.float32)

    def put(name, arr):
        arr = np.asarray(arr, np.float32)
        pvec[:arr.shape[0], PV_OFF[name]:PV_OFF[name] + arr.shape[1]] = arr

    for l in range(2):
        put(f"ffn_norm{l}0", fm(f("ffn1_norm")[l]))
        put(f"ffn_norm{l}1", fm(f("ffn2_norm")[l]))
        put(f"mix_norm{l}", fm(f("mix_norm")[l]))
    put("ln_g", fm(f("a_ln_g")[0]))
    put("ln_b", fm(f("a_ln_b")[0]))
    cw = f("b_conv_w")[0]
    put("conv_w", cw.reshape(4, 12, 128).transpose(2, 1, 0).reshape(128, 48))
    put("conv_b", fm(f("b_conv_b")[0]))
    put("norm_g", fm(f("b_norm_g")[0]))
    put("dskip", fm(np.repeat(f("b_d_skip")[0], 64)))
    put("c_scale", fm(f("c_scale")[0]))
    put("q_norm", np.tile(f("d_q_norm")[0], 2).reshape(128, 1))
    put("k_norm", np.tile(f("d_k_norm")[0], 2).reshape(128, 1))
    put("dt_bias", f("b_dt_bias")[0].reshape(16, 1))
    put("a_log", f("b_a_log")[0].reshape(16, 1))

    wgu = np.empty((2, 2, 11, 128, 8, 2, 256), np.float32)
    wdn = np.empty((2, 2, 8, 128, FC, 128), np.float32)
    for l in range(2):
        for w, (kgu, kdn) in enumerate((("ffn1_w_gu", "ffn1_w_down"), ("ffn2_w_gu", "ffn2_w_down"))):
            g = f(kgu)[l].reshape(8, 128, 2, 11, 256)
            wgu[l, w] = g.transpose(3, 1, 0, 2, 4)
            d = f(kdn)[l].reshape(FC, 128, 8, 128)
            wdn[l, w] = d.transpose(2, 1, 0, 3)

    def fm_w(Wc):
        n = Wc.shape[1]
        return np.ascontiguousarray(Wc.reshape(8, 128, n).transpose(1, 0, 2).reshape(128, 8 * n))

    Wi = f("ev_w_in")[0]
    col_groups = [(0, 512), (512, 1024), (2048, 2560), (2560, 3072), (3072, 3584), (3584, 4096), (4096, 4608)]
    wev_fm = np.stack([fm_w(Wi[:, a:b]) for a, b in col_groups], 0)
    wev_v = np.stack([fm_w(Wi[:, 1024:1536]), fm_w(Wi[:, 1536:2048])], 0)
    wev_dt = fm_w(Wi[:, 4608:4624])
    Wo = f("ev_w_out")[0]
    wev_out = np.ascontiguousarray(Wo.reshape(16, 128, 4, 256).transpose(2, 1, 0, 3).reshape(4, 128, 16 * 256))
    Wd = f("od_w_in")[0]
    ws = f("a_w_s")[0]
    wsT = np.ascontiguousarray(ws.transpose(2, 0, 1).reshape(128, 8 * 128))
    wsT4 = np.ascontiguousarray(ws[:, :4, :4].transpose(2, 0, 1).reshape(4, 32))
    shared = {
        "pvec": pvec,
        "cst": make_consts(),
        "wgu": wgu.reshape(2, 2, 11, 128, 8 * 2 * 256),
        "wdn": wdn.reshape(2, 2, 8, 128, FC * 128),
        "wev_fm": wev_fm, "wev_v": wev_v, "wev_dt": wev_dt, "wev_out": wev_out,
        "wsT": wsT, "wsT4": wsT4,
        "bs_row": f("a_b_s")[0].reshape(1, 1024),
        "lnrow": np.concatenate([f("a_ln_g")[0], f("a_ln_b")[0]]).reshape(1, 2048),
        "wod_fm": np.stack([fm_w(Wd[:, a:a + 512]) for a in (0, 512, 1024, 1536)], 0),
        "wod_k": fm_w(Wd[:, 2048:2304]), "wod_v": fm_w(Wd[:, 2304:2560]),
        "wod_lin": np.ascontiguousarray(f("c_lin_w")[0].reshape(4, 2, 128, 256).transpose(2, 0, 1, 3).reshape(128, 2048)),
        "wod_out": np.ascontiguousarray(f("od_w_out")[0].reshape(16, 128, 4, 256).transpose(2, 1, 0, 3).reshape(4, 128, 16 * 256)),
        "rel_tab": f("rel_bias_table"), "sink_row": f("d_sinks")[0].reshape(1, 16),
    }
    return shared


def per_core_inputs(inp, c):
    f = lambda k: np.asarray(inp[k], np.float32)
    sl = slice(16 * c, 16 * c + 16)
    return {
        "xs": np.ascontiguousarray(f("x_sample")[sl].reshape(NS, D)),
        "st_ssm": np.ascontiguousarray(f("state_ssm")[0, sl].reshape(16, 1024, 128)),
        "st_conv": np.ascontiguousarray(f("state_conv")[0, sl].reshape(48, 1536)),
        "st_pool": np.ascontiguousarray(f("state_pool")[0, sl].reshape(240, 1024)),
        "st_k": np.ascontiguousarray(f("cache_k_win")[0, sl].reshape(16, 128, 256)),
        "st_v": np.ascontiguousarray(f("cache_v_win")[0, sl].reshape(16, 128, 256)),
    }


_CACHE = {}


def kernel(**inputs):
    cfg = {}
    key = "full"
    if key not in _CACHE:
        _CACHE[key] = build_program(cfg)
    P = _CACHE[key]
    shared = pack_host(inputs)
    xp = np.asarray(inputs["x_prompt"], np.float32)
    in_maps = []
    for c in range(NCORES):
        m = dict(shared)
        m["xp"] = np.ascontiguousarray(xp[c])
        m.update(per_core_inputs(inputs, c))
        in_maps.append(m)
    res = run_bass_kernel_spmd(P.nc, in_maps, core_ids=list(range(NCORES)))
    r = res.results
    y_p = np.stack([r[c]["yp"] for c in range(NCORES)], 0)
    y_s = np.stack([r[c]["ys"] for c in range(NCORES)], 0).reshape(128, 4, D)
    cat = lambda k: np.stack([r[c][k] for c in range(NCORES)], 0)
    av = cat("o_av").reshape(1, 128, 4, D)
    ssm_p = cat("o_ssm_p").reshape(1, 8, 16, 64, 128)
    ssm_s = cat("o_ssm_s").reshape(1, 128, 16, 64, 128)
    conv_p = cat("o_conv_p").reshape(1, 8, 3, 1536)
    conv_s = cat("o_conv_s").reshape(1, 128, 3, 1536)
    pool_p = cat("o_pool_p").reshape(1, 8, 15, 1024)
    pool_s = cat("o_pool_s").reshape(1, 128, 15, 1024)
    k_p = cat("o_k_p").reshape(1, 8, 128, 4, 64)
    k_s = cat("o_k_s").reshape(1, 128, 128, 4, 64)
    v_p = cat("o_v_p").reshape(1, 8, 128, 4, 64)
    v_s = cat("o_v_s").reshape(1, 128, 128, 4, 64)
    return (y_p, y_s, av, ssm_p, ssm_s, conv_p, conv_s, pool_p, pool_s, k_p, k_s, v_p, v_s)
```

```python
import contextlib
import math
import types
import numpy as np
import concourse.bass as bass
import concourse.mybir as mybir
from concourse.bass_utils import run_bass_kernel_spmd

F32 = mybir.dt.float32
BF16 = mybir.dt.bfloat16
I32 = mybir.dt.int32
AF = mybir.ActivationFunctionType
ALU = mybir.AluOpType
AX = mybir.AxisListType

NCORES = 8
D = 1024
DC = 8
DFF = 2816
FC = 22
SEQ = 4096
NTP = 512
NS = 64
EPS = 1e-6
NEG = -1e30


NATIVE_GELU = True
RELAX_SAME_ENGINE = False


def freeze(fn):
    if fn.__closure__ is None:
        return fn
    cells = []
    for c in fn.__closure__:
        try:
            cells.append(types.CellType(c.cell_contents))
        except ValueError:
            cells.append(c)
    return types.FunctionType(fn.__code__, fn.__globals__, fn.__name__, fn.__defaults__, tuple(cells))


class Buf:
    __slots__ = ("name", "w", "r", "excl")

    def __init__(self, name, excl=False):
        self.name = name
        self.w = None
        self.r = {}
        self.excl = excl


class DmaSlot:
    def __init__(self, S, name):
        self.S = S
        self.name = name
        self.sem = S.new_sem("d" + name)
        self.count = 0
        self.last = None

    def next_token(self):
        if self.count >= 16 * 1200:
            self.sem = self.S.new_sem("d" + self.name)
            self.count = 0
        self.count += 16
        self.last = (id(self.sem), self.sem, self.count, "dma")
        return self.last


class Sched:
    ENG = ("pe", "act", "dve", "pool", "sp")
    EPOCH = 6000

    def __init__(self, nc, stack):
        self.nc = nc
        self.stack = stack
        self.nsem = 0
        self.streams = {e: [] for e in self.ENG}
        self.sem = {e: self.new_sem(e) for e in self.ENG}
        self.cnt = {e: 0 for e in self.ENG}
        self.seen = {e: {} for e in self.ENG}
        self.final_tokens = []
        self.pending_dma = []
        self.ninst = 0
        self.oplog = []

    def new_sem(self, name):
        self.nsem += 1
        return self.stack.enter_context(self.nc.semaphore(f"s{self.nsem}_{name}"))

    def _wait(self, eng, tok):
        sid, sem, val, _ = tok
        if self.seen[eng].get(sid, 0) >= val:
            return
        self.seen[eng][sid] = val
        self.streams[eng].append(("wait", sem, val))

    def barrier(self, engines=("pe", "act", "dve", "sp")):
        toks = [(id(self.sem[e]), self.sem[e], self.cnt[e], e) for e in ("pe", "act", "dve", "pool") if self.cnt[e] > 0]
        toks += self.pending_dma
        for e in engines:
            for tok in toks:
                if tok[3] != e:
                    self._wait(e, tok)
        self.pending_dma = []

    def op(self, eng, fns, reads=(), writes=(), dma=None, final=False, arena=False):
        if not isinstance(fns, (list, tuple)):
            fns = [fns]
        fns = [freeze(f) for f in fns]
        writes = list(writes) + [b for b in reads if b.excl]
        reads = [b for b in reads if not b.excl]
        for b in reads:
            if b.w is not None and not (b.w[3] == eng == "pe"):
                self._wait(eng, b.w)
        for b in writes:
            if b.w is not None and not (b.w[3] == eng == "pe") and not (RELAX_SAME_ENGINE and b.w[3] == eng):
                self._wait(eng, b.w)
            for tok in b.r.values():
                if not (tok[3] == eng == "pe") and not (RELAX_SAME_ENGINE and tok[3] == eng):
                    self._wait(eng, tok)
        if dma is not None and dma.last is not None:
            self._wait(eng, dma.last)
        if dma is None:
            if self.cnt[eng] >= self.EPOCH:
                self.sem[eng] = self.new_sem(eng)
                self.cnt[eng] = 0
            self.cnt[eng] += 1
            tok = (id(self.sem[eng]), self.sem[eng], self.cnt[eng], eng)
            inc = 1
        else:
            tok = dma.next_token()
            inc = 16
        self.oplog.append((eng, tok, [x.name for x in reads], [x.name for x in writes], len(self.streams[eng])))
        st = self.streams[eng]
        for fn in fns[:-1]:
            st.append(("inst", fn, None, 0))
        st.append(("inst", fns[-1], tok[1], inc))
        self.ninst += len(fns)
        for b in reads:
            b.r[tok[0]] = tok
        for b in writes:
            b.w = tok
            b.r = {}
        if final:
            self.final_tokens.append(tok)
        if arena and dma is not None:
            self.pending_dma.append(tok)
        return tok

    def emit(self, block):
        nc = self.nc
        for tok in self.final_tokens:
            self._wait("sp", tok)
        streams = self.streams

        def run(engine, items):
            for it in items:
                if it[0] == "wait":
                    engine.wait_ge(it[1], it[2])
                else:
                    r = it[1](engine)
                    if it[2] is not None:
                        r.then_inc(it[2], it[3])

        @block.tensor
        def _(e):
            run(e, streams["pe"])

        @block.scalar
        def _(e):
            run(e, streams["act"])

        @block.vector
        def _(e):
            run(e, streams["dve"])

        @block.gpsimd
        def _(e):
            run(e, streams["pool"])

        @block.sync
        def _(e):
            run(e, streams["sp"])


class Prog:
    def __init__(self, cfg):
        self.cfg = cfg
        self.nc = bass.Bass("TRN2", target_bir_lowering=False)
        self.stack = contextlib.ExitStack()
        self.S = None
        self.dram = {}

    def din(self, name, shape, dtype=F32):
        t = self.nc.dram_tensor(name, list(shape), dtype, kind="ExternalInput")
        self.dram[name] = t
        return t.ap()

    def dout(self, name, shape, dtype=F32):
        t = self.nc.dram_tensor(name, list(shape), dtype, kind="ExternalOutput")
        self.dram[name] = t
        return t.ap()

    def dscratch(self, name, shape, dtype=F32):
        t = self.nc.dram_tensor(name, list(shape), dtype, kind="Internal")
        return t.ap()

    def sb(self, name, shape, dtype=F32):
        return self.stack.enter_context(self.nc.sbuf_tensor(name, list(shape), dtype))

    def ps(self, name, shape, dtype=F32):
        return self.stack.enter_context(self.nc.psum_tensor(name, list(shape), dtype))


WSLOT_ELEMS = 8 * 2 * 256
NWSLOT = 4
ARENA_WORDS = 23 * 1024 + 128


def _prod(xs):
    r = 1
    for x in xs:
        r *= x
    return r


def rs(ap, dims):
    if len(dims) == 1:
        return ap
    if len(dims) == 2:
        return ap.rearrange("p (a b) -> p a b", a=dims[0])
    if len(dims) == 3:
        return ap.rearrange("p (a b c) -> p a b c", a=dims[0], b=dims[1])
    raise ValueError(dims)


class Arena:
    def __init__(self, t, words):
        self.t = t
        self.words = words
        self.off = 0

    def reset(self):
        self.off = 0

    def take(self, nparts, dims, dtype=F32):
        n = _prod(dims)
        w = n if dtype == F32 else (n + 1) // 2
        w = (w + 1) // 2 * 2
        ap = self.t[0:nparts, self.off:self.off + w]
        if dtype != F32:
            ap = ap.bitcast(dtype)
        ap = ap[:, 0:n]
        self.off += w
        assert self.off <= self.words, ("arena overflow", self.off, self.words)
        return rs(ap, dims)


def build_program(cfg):
    P = Prog(cfg)
    nc = P.nc
    n_ptiles = cfg.get("n_ptiles", SEQ // NTP)
    do_sample = cfg.get("sample", True)
    stages = cfg.get("stages", "all")
    nlayers = cfg.get("layers", 2)
    npt_tokens = n_ptiles * NTP

    def on(name):
        return stages == "all" or name in stages

    xp = P.din("xp", [npt_tokens, D])
    xs = P.din("xs", [NS, D])
    pvec = P.din("pvec", [128, PV_COLS])
    cst = P.din("cst", [128, CST_COLS])
    wgu = P.din("wgu", [2, 2, 11, 128, 8 * 2 * 256])
    wdn = P.din("wdn", [2, 2, 8, 128, FC * 128])
    wev_fm = P.din("wev_fm", [7, 128, 8 * 512])
    wev_v = P.din("wev_v", [2, 128, 8 * 512])
    wev_dt = P.din("wev_dt", [128, 8 * 16])
    wev_out = P.din("wev_out", [4, 128, 16 * 256])
    wsT = P.din("wsT", [128, 8 * 128])
    wsT4 = P.din("wsT4", [4, 8 * 4])
    bs_row = P.din("bs_row", [1, 8 * 128])
    lnrow = P.din("lnrow", [1, 2048])
    wod_fm = P.din("wod_fm", [4, 128, 8 * 512])
    wod_k = P.din("wod_k", [128, 8 * 256])
    wod_v = P.din("wod_v", [128, 8 * 256])
    wod_lin = P.din("wod_lin", [128, 4 * 2 * 256])
    wod_out = P.din("wod_out", [4, 128, 16 * 256])
    rel_tab = P.din("rel_tab", [32, 16])
    sink_row = P.din("sink_row", [1, 16])
    st_pool = P.din("st_pool", [240, 1024])
    st_k = P.din("st_k", [16, 128, 256])
    st_v = P.din("st_v", [16, 128, 256])
    dscr = P.dscratch("dscr", [16, 128, 384])
    o_pool_p = P.dout("o_pool_p", [15, 1024])
    o_pool_s = P.dout("o_pool_s", [16, 15, 1024])
    o_k_p = P.dout("o_k_p", [128, 256])
    o_k_s = P.dout("o_k_s", [16, 128, 256])
    o_v_p = P.dout("o_v_p", [128, 256])
    o_v_s = P.dout("o_v_s", [16, 128, 256])
    st_ssm = P.din("st_ssm", [16, 1024, 128])
    st_conv = P.din("st_conv", [48, 1536])
    yp = P.dout("yp", [npt_tokens, D])
    ys = P.dout("ys", [NS, D])
    o_av = P.dout("o_av", [NS, D])
    o_ssm_p = P.dout("o_ssm_p", [1024, 128])
    o_ssm_s = P.dout("o_ssm_s", [16, 1024, 128])
    o_conv_p = P.dout("o_conv_p", [3, 1536])
    o_conv_s = P.dout("o_conv_s", [16, 3, 1536])

    with P.stack:
        S = Sched(nc, P.stack)
        P.S = S
        ident = P.sb("ident", [128, 128], F32)
        identb = P.sb("identb", [128, 128], BF16)
        onesb = P.sb("onesb", [128, 128], BF16)
        pv = P.sb("pv", [128, PV_COLS], F32)
        cs = P.sb("cs", [128, CST_COLS], F32)
        xT = P.sb("xT", [128, DC, NTP], F32)
        xn = P.sb("xn", [128, DC, NTP], BF16)
        hTt = P.sb("hT", [128, FC * NTP], BF16)
        hT = hTt[:, :].rearrange("p (j n) -> p j n", j=FC)
        rstd = P.sb("rstd", [128, NTP], F32)
        sil = [P.sb(f"sil{i}", [128, NTP], F32) for i in range(2)]
        wring = [P.sb(f"wr{i}", [128, WSLOT_ELEMS], BF16) for i in range(NWSLOT)]
        wTm = P.sb("wTm", [128, 8, 128], BF16)
        T2 = P.sb("T2", [128, 8, 128], F32)
        wTms = P.sb("wTms", [64, 8, 64], BF16)
        T2s = P.sb("T2s", [128, 8, 64], F32)
        aneg = P.sb("aneg", [16, 1], F32)
        STf = P.sb("STf", [128, 1024], F32)
        STb = P.sb("STb", [128, 1024], BF16)
        ctail = P.sb("ctail", [128, 12, 3], BF16)
        ctail32 = P.sb("ctail32", [128, 12, 3], F32)
        bd64 = P.sb("bd64", [128, 128], BF16)
        hsq = P.sb("hsq", [128, NTP], BF16)
        hrs = P.sb("hrs", [128, NTP], F32)
        esink2 = P.sb("esink2", [128, 8], F32)
        ptail = P.sb("ptail", [128, 8, 15], F32)
        kprev = P.sb("kprev", [128, 4, 128], BF16)
        vprev = P.sb("vprev", [128, 256], BF16)
        arena_t = P.sb("arena", [128, ARENA_WORDS], F32)
        psall = P.ps("psall", [128, 8 * 512], F32)
        A = Arena(arena_t, ARENA_WORDS)
        sq = hT[:, 0:8, :]
        yst = hTt[:, 0:2 * 4 * D].bitcast(F32).rearrange("p (b d) -> p b d", b=4)
        xin = arena_t[:, 0:4 * D].rearrange("p (b d) -> p b d", b=4)

        b_ident = Buf("ident")
        b_pv = Buf("pv")
        b_cs = Buf("cs")
        b_xin = Buf("xin")
        b_xT = [Buf(f"xT{c}") for c in range(DC)]
        b_xn = Buf("xn")
        b_h = [Buf(f"h{j}") for j in range(FC)]
        b_rstd = Buf("rstd")
        b_sil = [Buf("sil0"), Buf("sil1")]
        b_wr = [Buf(f"wr{i}") for i in range(NWSLOT)]
        b_ps = [Buf(f"ps{i}", excl=True) for i in range(8)]
        b_gm = Buf("gmlp_consts")
        b_ST = Buf("ST")
        b_STb = Buf("STb")
        b_ctail = Buf("ctail")
        b_hsq, b_hrs, b_gm2, b_ptail, b_kprev, b_dscr = Buf("hsq"), Buf("hrs"), Buf("gm2"), Buf("ptail"), Buf("kprev"), Buf("dscr")
        d_wr = [DmaSlot(S, f"wr{i}") for i in range(NWSLOT)]
        d_xin = DmaSlot(S, "xin")
        d_yst = DmaSlot(S, "yst")
        d_misc = DmaSlot(S, "misc")
        d_out = [DmaSlot(S, f"out{i}") for i in range(8)]
        d_in = [DmaSlot(S, f"in{i}") for i in range(8)]

        state = {"w": 0, "ps": 0, "sil": 0, "out": 0, "in": 0}

        def bank(i):
            return psall[:, i * 512:(i + 1) * 512]

        reserved = set()

        def next_ps():
            i = state["ps"]
            while i in reserved:
                i = (i + 1) % 8
            state["ps"] = (i + 1) % 8
            return bank(i), b_ps[i]

        def ps_group(n):
            i = (state["ps"] + n - 1) // n * n % 8
            while any((i + k) in reserved for k in range(n)):
                i = (i + n) % 8
            state["ps"] = (i + n) % 8
            state["last_group"] = list(range(i, i + n))
            return psall[:, i * 512:(i + n) * 512], [b_ps[i + k] for k in range(n)]

        def next_out():
            i = state["out"]
            state["out"] = (i + 1) % 8
            return d_out[i]

        def next_in():
            i = state["in"]
            state["in"] = (i + 1) % 8
            return d_in[i]

        wcache = {}
        d_ws = [DmaSlot(S, f"ws{i}") for i in range(NWSLOT)]
        use_wcache = cfg.get("wcache", True) and (len([1 for _ in range(n_ptiles)]) + (1 if do_sample else 0)) > 1

        def wload(src_ap, nelem):
            i = state["w"]
            state["w"] = (i + 1) % NWSLOT
            dst = wring[i][:, 0:nelem]
            key = (src_ap.tensor.name, src_ap.offset)
            ent = wcache.get(key) if use_wcache else None
            if ent is None:
                S.op("pool", lambda e, dst=dst, src=src_ap: e.dma_start(out=dst, in_=src),
                     reads=(), writes=(b_wr[i],), dma=d_wr[i])
                if use_wcache:
                    scr = P.dscratch(f"wb{len(wcache)}", [128, nelem], BF16)
                    bscr = Buf(f"wb{len(wcache)}")
                    wcache[key] = (scr, bscr)
                    S.op("sp", lambda e, dst=dst, scr=scr: e.dma_start(out=scr, in_=dst),
                         reads=(b_wr[i],), writes=(bscr,), dma=d_ws[i])
            else:
                scr, bscr = ent
                S.op("pool", lambda e, dst=dst, scr=scr: e.dma_start(out=dst, in_=scr),
                     reads=(bscr,), writes=(b_wr[i],), dma=d_wr[i])
            return wring[i], b_wr[i]

        def cst_ap(name, nparts=128):
            o, n = CST_OFF[name]
            return cs[0:nparts, o:o + n]

        def pvc(name, c=0, nparts=128):
            o = PV_OFF[name]
            return pv[0:nparts, o + c:o + c + 1]

        S.op("pool", lambda e: e.memset(ident[:], 0.0), writes=(b_ident,))
        S.op("pool", lambda e: e.affine_select(out=ident[:], in_=ident[:], pattern=[[-1, 128]],
                                               compare_op=ALU.not_equal, fill=1.0, base=0,
                                               channel_multiplier=1),
             reads=(b_ident,), writes=(b_ident,))
        S.op("pool", lambda e: e.tensor_copy(identb[:], ident[:]), reads=(b_ident,), writes=(b_ident,))
        S.op("pool", lambda e: e.memset(onesb[:], 1.0), writes=(b_ident,))
        S.op("pool", lambda e: e.memset(STf[:], 0.0), writes=(b_ST,))
        S.op("pool", lambda e: e.memset(STb[:], 0.0), writes=(b_STb,))
        S.op("pool", lambda e: e.memset(ctail[:], 0.0), writes=(b_ctail,))
        S.op("sp", lambda e: e.dma_start(out=pv[:], in_=pvec), writes=(b_pv,), dma=d_misc)
        S.op("sp", lambda e: e.dma_start(out=cs[:], in_=cst), writes=(b_cs,), dma=next_in())

        if on("even"):
            A.reset()
            b_tmp = Buf("setup_tmp")
            w32 = A.take(128, [8, 128])
            bsbc = A.take(128, [8, 128])
            w32s = A.take(64, [8, 64])
            S.op("sp", lambda e: e.dma_start(out=w32, in_=wsT.rearrange("p (h i) -> p h i", h=8)),
                 writes=(b_tmp,), dma=next_in(), arena=True)
            S.op("sp", lambda e: e.dma_start(out=bsbc.rearrange("p h i -> p (h i)"),
                                             in_=bs_row.partition_broadcast(128)),
                 writes=(b_tmp,), dma=next_in(), arena=True)
            S.op("pool", lambda e: e.affine_select(out=w32, in_=w32, pattern=[[0, 8], [1, 128]],
                                                   compare_op=ALU.is_ge, fill=0.0, base=0,
                                                   channel_multiplier=-1),
                 reads=(b_tmp,), writes=(b_tmp,))
            S.op("pool", lambda e: e.tensor_copy(wTm[:], w32), reads=(b_tmp,), writes=(b_gm,))
            pg, bpg = ps_group(2)
            S.op("pe", [lambda e, k=k: e.matmul(pg[:, k * 512:(k + 1) * 512], onesb[:, :],
                                                 wTm[:, k * 4:(k + 1) * 4, :].rearrange("p h i -> p (h i)"),
                                                 start=True, stop=True) for k in range(2)],
                 reads=(b_gm, b_ident), writes=bpg)
            for h in range(8):
                S.op("dve", lambda e, h=h: e.scalar_tensor_tensor(
                    out=T2[:, h, :], in0=pg[:, h * 128:(h + 1) * 128], scalar=pvc("ln_b", h),
                    in1=bsbc[:, h, :], op0=ALU.mult, op1=ALU.add),
                    reads=bpg + [b_tmp, b_pv], writes=(b_gm,))
            SST = cfg.get("setup_stop", 99)
            S.op("pool", lambda e: e.memset(w32s, 0.0), writes=(b_tmp,))
            for b in range(16 if SST > 1 else 0):
                S.op("sp", lambda e, b=b: e.dma_start(out=w32s[4 * b:4 * b + 4, :, 4 * b:4 * b + 4],
                                                      in_=wsT4.rearrange("p (h i) -> p h i", h=8)),
                     writes=(b_tmp,), dma=next_in(), arena=True)
            S.op("pool", lambda e: e.affine_select(out=w32s, in_=w32s, pattern=[[0, 8], [1, 64]],
                                                   compare_op=ALU.is_ge, fill=0.0, base=0,
                                                   channel_multiplier=-1),
                 reads=(b_tmp,), writes=(b_tmp,))
            S.op("pool", lambda e: e.tensor_copy(wTms[:], w32s), reads=(b_tmp,), writes=(b_gm,))
            pg2, bpg2 = next_ps()
            S.op("pe", lambda e: e.matmul(pg2[:, 0:512], onesb[0:64, :],
                                          wTms[:, :, :].rearrange("p h i -> p (h i)"), start=True, stop=True),
                 reads=(b_gm, b_ident), writes=(bpg2,))
            for h in range(8 if SST > 2 else 0):
                S.op("dve", lambda e, h=h: e.scalar_tensor_tensor(
                    out=T2s[:, h, :].rearrange("p (b i) -> p b i", i=4),
                    in0=pg2[:, h * 64:(h + 1) * 64].rearrange("p (b i) -> p b i", i=4),
                    scalar=pvc("ln_b", h),
                    in1=bsbc[:, h, 0:4].unsqueeze(1).broadcast_to([128, 16, 4]),
                    op0=ALU.mult, op1=ALU.add),
                    reads=(bpg2, b_tmp, b_pv), writes=(b_gm,))
            S.op("act", lambda e: e.activation(out=aneg[:, :], in_=pvc("a_log", 0, 16), func=AF.Exp),
                 reads=(b_pv,), writes=(b_gm,))
            S.op("dve", lambda e: e.tensor_scalar(out=aneg[:, :], in0=aneg[:, :], scalar1=-1.0, scalar2=None,
                                                  op0=ALU.mult), reads=(b_gm,), writes=(b_gm,))
            S.barrier()

        if on("odd"):
            A.reset()
            b_tmp2 = Buf("setup_tmp2")
            S.op("pool", lambda e: e.memset(bd64[:], 0.0), writes=(b_ident,))
            S.op("pool", lambda e: e.memset(bd64[0:64, 0:64], 1.0), writes=(b_ident,))
            S.op("pool", lambda e: e.memset(bd64[64:128, 64:128], 1.0), writes=(b_ident,))
            S.op("pool", lambda e: e.memset(ptail[:], 0.0), writes=(b_ptail,))
            S.op("pool", lambda e: e.memset(kprev[:], 0.0), writes=(b_kprev,))
            S.op("pool", lambda e: e.memset(vprev[:], 0.0), writes=(b_kprev,))
            es = A.take(128, [16])
            rt = A.take(32, [16])
            dv = A.take(16, [384])
            S.op("sp", lambda e: e.dma_start(out=es, in_=sink_row.partition_broadcast(128)), writes=(b_tmp2,), dma=next_in(), arena=True)
            S.op("sp", lambda e: e.dma_start(out=rt, in_=rel_tab), writes=(b_tmp2,), dma=next_in(), arena=True)
            S.op("act", lambda e: e.activation(out=es, in_=es, func=AF.Exp), reads=(b_tmp2,), writes=(b_tmp2,))
            esv = es.rearrange("p (c t) -> p c t", t=2)
            S.op("dve", lambda e: e.tensor_copy(esink2[0:64, :], esv[0:64, :, 0]), reads=(b_tmp2,), writes=(b_gm2,))
            S.op("dve", lambda e: e.tensor_copy(esink2[64:128, :], esv[64:128, :, 1]), reads=(b_tmp2,), writes=(b_gm2,))
            pgd, bpgd = next_ps()
            S.op("pe", lambda e: e.matmul(pgd[0:16, 0:384], rt[:, :], cst_ap("onehot", 32), start=True, stop=True),
                 reads=(b_tmp2, b_cs), writes=(bpgd,))
            S.op("dve", lambda e: e.tensor_tensor(out=dv, in0=pgd[0:16, 0:384], in1=cst_ap("negmask", 16), op=ALU.add),
                 reads=(bpgd, b_cs), writes=(b_tmp2,))
            S.op("sp", lambda e: e.dma_start(out=dscr, in_=dv.unsqueeze(1).broadcast_to([16, 128, 384])),
                 reads=(b_tmp2,), writes=(b_dscr,), dma=next_in(), arena=True)
            S.barrier()

        def load_x_tile(src_rows, nblk, rows):
            S.op("sp", lambda e: e.dma_start(out=xin[0:rows, 0:nblk, :],
                                             in_=src_rows.rearrange("(b p) d -> p b d", p=rows)),
                 writes=(b_xin,), dma=d_xin, arena=True)

        def transpose_in(nblk, rows):
            nt = nblk * rows
            for c in range(DC):
                pt, bp = next_ps()
                fns = []
                for blk in range(nblk):
                    fns.append(lambda e, pt=pt, blk=blk, c=c: e.transpose(
                        pt[:, blk * rows:(blk + 1) * rows], xin[0:rows, blk, c * 128:(c + 1) * 128],
                        ident[0:rows, 0:rows]))
                S.op("pe", fns, reads=(b_xin, b_ident), writes=(bp,))
                if c % 2:
                    S.op("act", lambda e, pt=pt, c=c: e.copy(xT[:, c, 0:nt], pt[:, 0:nt]),
                         reads=(bp,), writes=(b_xT[c],))
                else:
                    S.op("dve", lambda e, pt=pt, c=c: e.tensor_copy(xT[:, c, 0:nt], pt[:, 0:nt]),
                         reads=(bp,), writes=(b_xT[c],))

        def transpose_out(dst_rows, nblk, rows):
            for blk in range(nblk):
                for half in range(2):
                    pt, bp = next_ps()
                    fns = []
                    for cc in range(4):
                        c = half * 4 + cc
                        fns.append(lambda e, pt=pt, blk=blk, c=c, cc=cc: e.transpose(
                            pt[0:rows, cc * 128:(cc + 1) * 128], xT[:, c, blk * rows:(blk + 1) * rows],
                            ident[:, :]))
                    S.op("pe", fns, reads=[b_xT[half * 4 + cc] for cc in range(4)] + [b_ident],
                         writes=(bp,))
                    if half:
                        S.op("act", lambda e, pt=pt, blk=blk: e.copy(yst[0:rows, blk, 512:1024], pt[0:rows, :]),
                             reads=(bp,), writes=b_h[0:16])
                    else:
                        S.op("dve", lambda e, pt=pt, blk=blk: e.tensor_copy(yst[0:rows, blk, 0:512], pt[0:rows, :]),
                             reads=(bp,), writes=b_h[0:16])
            S.op("sp", lambda e: e.dma_start(out=dst_rows.rearrange("(b p) d -> p b d", p=rows),
                                             in_=yst[0:rows, 0:nblk, :]),
                 reads=b_h[0:16], dma=d_yst, final=True)

        def rms_norm(gname, nt):
            S.op("act", lambda e: e.activation(out=sq[:, :, 0:nt], in_=xT[:, :, 0:nt], func=AF.Square),
                 reads=b_xT, writes=b_h[0:8])
            pt, bp = next_ps()
            S.op("pe", [lambda e, pt=pt, c=c: e.matmul(pt[:, 0:nt], onesb[:, :], sq[:, c, 0:nt],
                                                        start=(c == 0), stop=(c == DC - 1))
                        for c in range(DC)], reads=b_h[0:8] + [b_ident], writes=(bp,))
            S.op("act", lambda e, pt=pt: e.activation(out=rstd[:, 0:nt], in_=pt[:, 0:nt], func=AF.Sqrt,
                                                       bias=EPS, scale=1.0 / D),
                 reads=(bp,), writes=(b_rstd,))
            S.op("dve", lambda e: e.reciprocal(rstd[:, 0:nt], rstd[:, 0:nt]), reads=(b_rstd,), writes=(b_rstd,))
            for c in range(DC):
                S.op("dve", lambda e, c=c: e.scalar_tensor_tensor(
                    out=xn[:, c, 0:nt], in0=xT[:, c, 0:nt], scalar=pvc(gname, c),
                    in1=rstd[:, 0:nt], op0=ALU.mult, op1=ALU.mult),
                    reads=(b_xT[c], b_rstd, b_pv), writes=(b_xn,))

        def ffn(layer, which, nt):
            rms_norm(f"ffn_norm{layer}{which}", nt)
            for grp in range(11):
                w, bw = wload(wgu[layer, which, grp], 8 * 2 * 256)
                wv = w[:, :].rearrange("p (k g n) -> p k g n", k=8, g=2)
                for jj in range(2):
                    j = grp * 2 + jj
                    pg, bpg = next_ps()
                    S.op("pe", [lambda e, pg=pg, k=k, jj=jj, wv=wv: e.matmul(
                        pg[:, 0:nt], wv[:, k, 0, jj * 128:(jj + 1) * 128], xn[:, k, 0:nt],
                        start=(k == 0), stop=(k == 7)) for k in range(8)],
                        reads=(bw, b_xn), writes=(bpg,))
                    pu, bpu = next_ps()
                    S.op("pe", [lambda e, pu=pu, k=k, jj=jj, wv=wv: e.matmul(
                        pu[:, 0:nt], wv[:, k, 1, jj * 128:(jj + 1) * 128], xn[:, k, 0:nt],
                        start=(k == 0), stop=(k == 7)) for k in range(8)],
                        reads=(bw, b_xn), writes=(bpu,))
                    si = state["sil"]
                    state["sil"] = 1 - si
                    S.op("act", lambda e, pg=pg, si=si: e.activation(out=sil[si][:, 0:nt], in_=pg[:, 0:nt],
                                                                      func=AF.Silu),
                         reads=(bpg,), writes=(b_sil[si],))
                    S.op("dve", lambda e, pu=pu, si=si, j=j: e.tensor_tensor(
                        out=hT[:, j, 0:nt], in0=pu[:, 0:nt], in1=sil[si][:, 0:nt], op=ALU.mult),
                        reads=(bpu, b_sil[si]), writes=(b_h[j],))
            for m in range(DC):
                w, bw = wload(wdn[layer, which, m], FC * 128)
                wv = w[:, 0:FC * 128].rearrange("p (j n) -> p j n", j=FC)
                py, bpy = next_ps()
                S.op("pe", [lambda e, py=py, j=j, wv=wv: e.matmul(
                    py[:, 0:nt], wv[:, j, :], hT[:, j, 0:nt], start=(j == 0), stop=(j == FC - 1))
                    for j in range(FC)], reads=[bw] + b_h, writes=(bpy,))
                S.op("dve", lambda e, py=py, m=m: e.scalar_tensor_tensor(
                    out=xT[:, m, 0:nt], in0=py[:, 0:nt], scalar=0.5, in1=xT[:, m, 0:nt],
                    op0=ALU.mult, op1=ALU.add), reads=(bpy, b_xT[m]), writes=(b_xT[m],))

        def gelu_evac(pt, np_, nf, out_ap, bp, wbufs):
            if NATIVE_GELU:
                S.op("act", lambda e: e.activation(out=out_ap, in_=pt, func=AF.Gelu_apprx_tanh), reads=(bp,), writes=wbufs)
                return
            si = state["sil"]
            state["sil"] = 1 - si
            t1 = sil[si][0:np_, 0:nf]
            S.op("act", lambda e: e.activation(out=t1, in_=pt, func=AF.Square), reads=(bp,), writes=(b_sil[si],))
            S.op("dve", lambda e: e.tensor_scalar(out=t1, in0=t1, scalar1=0.044715, scalar2=1.0, op0=ALU.mult, op1=ALU.add),
                 reads=(b_sil[si],), writes=(b_sil[si],))
            S.op("dve", lambda e: e.tensor_tensor(out=t1, in0=t1, in1=pt, op=ALU.mult), reads=(b_sil[si], bp), writes=(b_sil[si],))
            S.op("act", lambda e: e.activation(out=t1, in_=t1, func=AF.Sigmoid, scale=1.5957691216057308),
                 reads=(b_sil[si],), writes=(b_sil[si],))
            S.op("dve", lambda e: e.tensor_tensor(out=out_ap, in0=t1, in1=pt, op=ALU.mult), reads=(b_sil[si], bp), writes=wbufs)

        def proj_fm(wsrc, ncc, nt, evac):
            w, bw = wload(wsrc, 8 * 512)
            wv = w[:, :].rearrange("p (k n) -> p k n", k=8)
            for cc in range(ncc):
                pt, bp = next_ps()
                S.op("pe", [lambda e, pt=pt, k=k, cc=cc, wv=wv: e.matmul(
                    pt[:, 0:nt], wv[:, k, cc * 128:(cc + 1) * 128], xn[:, k, 0:nt],
                    start=(k == 0), stop=(k == 7)) for k in range(8)], reads=(bw, b_xn), writes=(bp,))
                evac(cc, pt, bp)

        def even_mixer(kind, nt, is_last_ptile):
            STOP = cfg.get("even_stop", 99)
            if STOP <= 0:
                return
            S.barrier()
            A.reset()
            smp = (kind == "s")
            CL = 64 if smp else 128
            nblk = nt // CL
            Lc = 4 if smp else 128
            nch = nt // Lc
            uT = hT[:, 0:8, :]
            yaT = hT[:, 8:16, :]
            bcT = hT[:, 16:20, :]
            b_u, b_ya, b_bc = b_h[0:8], b_h[8:16], b_h[16:20]
            zT = A.take(128, [8, nt], BF16)
            ybT = A.take(128, [8, nt], BF16)
            xcT = A.take(128, [8, nt], BF16)
            ext = A.take(128, [12, (16 * 7) if smp else (NTP + 3)], BF16)
            cvA = A.take(128, [nt])
            cvB = A.take(128, [nt])
            vg = [A.take(128, [1024]) for _ in range(1 if smp else 2)] * (2 if smp else 1)
            vhb = [A.take(128, [1024], BF16) for _ in range(1 if smp else 2)] * (2 if smp else 1)
            mvst = A.take(128, [2, 6])
            mv = A.take(128, [2])
            gt = None
            dsc = A.take(16, [4, nt])
            tsc = A.take(128, [64])
            absx = A.take(128, [16, CL])
            Eb = A.take(128, [16, CL], BF16)
            Csb = A.take(128, [16, CL], BF16)
            cbm = A.take(128, [2, CL])
            xd = A.take(128, [16, 64], BF16)
            xdd = A.take(128, [16, 64], BF16)
            Btok = A.take(128, [2, 128], BF16)
            ygb = A.take(128, [8, 128])
            sqg = A.take(128, [8, 128], BF16)
            rsg = A.take(128, [2, 128])
            bz, byb, bxc, bext, bcv = Buf("zT"), Buf("ybT"), Buf("xcT"), Buf("ext"), [Buf("cvA"), Buf("cvB")]
            bvg, bvh, bmv, bgt = [Buf("vg0"), Buf("vg1")], [Buf("vh0"), Buf("vh1")], Buf("mv"), [Buf("gt0"), Buf("gt1")]
            bdsc, btsc, babs, bEb, bCs, bcbm = Buf("dsc"), Buf("tsc"), Buf("absx"), Buf("Eb"), Buf("Csb"), Buf("cbm")
            bxd, bxdd, bBt, bygb, bsqg, brsg = Buf("xd"), Buf("xdd"), Buf("Btok"), Buf("ygb"), Buf("sqg"), Buf("rsg")
            if smp:
                xpre = A.take(128, [12, 64])
                cso = A.take(64, [1536])
                stc = cso[0:48, :]
                lnbc = A.take(128, [2048])
                vln = A.take(64, [1024])
                bmsk = cst_ap("bmask", 64)
                Bblk = A.take(64, [2, 16, 128], BF16)
                cdT = A.take(128, [8, 16])
                eal = A.take(16, [16])
                Snat = A.take(128, [2, 8, 128])
                STs = A.take(128, [2, 1024], BF16)
                bxpre, bcso, blnbc, bvln = Buf("xpre"), Buf("cso"), Buf("lnbc"), Buf("vln")
                bstc = bcso
                bBblk, bcdT, beal, bSnat, bSTs = Buf("Bblk"), Buf("cdT"), Buf("eal"), Buf("Snat"), Buf("STs")
                S.op("sp", lambda e: e.dma_start(out=lnbc, in_=lnrow.partition_broadcast(128)),
                     writes=(blnbc,), dma=next_in(), arena=True)
                S.op("sp", lambda e: e.dma_start(out=stc, in_=st_conv), writes=(bstc,), dma=next_in(), arena=True)

            rms_norm(f"mix_norm{0}", nt)
            if STOP <= 0.5:
                return

            for gi in range(2):
                def ev_u(cc, pt, bp, gi=gi):
                    c = gi * 4 + cc
                    gelu_evac(pt[:, 0:nt], 128, nt, uT[:, c, 0:nt], bp, (b_u[c],))
                proj_fm(wev_fm[gi], 4, nt, ev_u)

            if STOP <= 1:
                return
            wv0, bwv0 = wload(wev_v[0], 8 * 512)
            wv1, bwv1 = wload(wev_v[1], 8 * 512)
            wvv = [wv0[:, :].rearrange("p (k n) -> p k n", k=8), wv1[:, :].rearrange("p (k n) -> p k n", k=8)]
            bwv = [bwv0, bwv1]
            wg = wTms if smp else wTm
            T2x = T2s if smp else T2
            for blk in range(nblk):
                cols = slice(blk * CL, (blk + 1) * CL)
                bi = blk % 2
                for half in range(2):
                    pt, bp = next_ps()
                    S.op("pe", [lambda e, pt=pt, k=k, half=half: e.matmul(
                        pt[0:CL, :], xn[:, k, cols], wvv[half][:, k, :], start=(k == 0), stop=(k == 7))
                        for k in range(8)], reads=(bwv[half], b_xn), writes=(bp,))
                    gelu_evac(pt[0:CL, :], CL, 512, vg[bi][0:CL, half * 512:(half + 1) * 512], bp, (bvg[bi],))
                S.op("dve", [lambda e, bi=bi: e.bn_stats(mvst[0:CL, 0, :], vg[bi][0:CL, 0:512]),
                             lambda e, bi=bi: e.bn_stats(mvst[0:CL, 1, :], vg[bi][0:CL, 512:1024])],
                     reads=(bvg[bi],), writes=(bmv,))
                S.op("dve", lambda e: e.bn_aggr(mv[0:CL, :], mvst[0:CL, :, :].rearrange("p a b -> p (a b)")),
                     reads=(bmv,), writes=(bmv,))
                S.op("act", lambda e: e.activation(out=mv[0:CL, 1:2], in_=mv[0:CL, 1:2], func=AF.Sqrt, bias=EPS, scale=1.0),
                     reads=(bmv,), writes=(bmv,))
                S.op("dve", lambda e: e.reciprocal(mv[0:CL, 1:2], mv[0:CL, 1:2]), reads=(bmv,), writes=(bmv,))
                S.op("dve", lambda e, bi=bi: e.tensor_scalar(
                    out=vhb[bi][0:CL, :], in0=vg[bi][0:CL, :], scalar1=mv[0:CL, 0:1], scalar2=mv[0:CL, 1:2],
                    op0=ALU.subtract, op1=ALU.mult), reads=(bmv, bvg[bi]), writes=(bvh[bi],))
                if smp:
                    S.op("dve", lambda e, bi=bi: e.tensor_scalar(
                        out=vln[:, :], in0=vg[bi][0:CL, :], scalar1=mv[0:CL, 0:1], scalar2=mv[0:CL, 1:2],
                        op0=ALU.subtract, op1=ALU.mult), reads=(bmv, bvg[bi]), writes=(bvln,))
                    S.op("dve", lambda e: e.tensor_tensor(out=vln[:, :], in0=vln[:, :], in1=lnbc[0:64, 0:1024], op=ALU.mult),
                         reads=(bvln, blnbc), writes=(bvln,))
                    S.op("dve", lambda e: e.tensor_tensor(out=vln[:, :], in0=vln[:, :], in1=lnbc[0:64, 1024:2048], op=ALU.add),
                         reads=(bvln, blnbc), writes=(bvln,))
                    S.op("sp", lambda e: e.dma_start(out=o_av, in_=vln[:, :]), reads=(bvln,), dma=next_out(),
                         final=True, arena=True)
                for hg in range(2):
                    pt, bp = next_ps()
                    S.op("pe", [lambda e, pt=pt, hh=hh, hg=hg, bi=bi: e.matmul(
                        pt[:, hh * CL:(hh + 1) * CL], vhb[bi][0:CL, (hg * 4 + hh) * 128:(hg * 4 + hh + 1) * 128],
                        wg[0:CL, hg * 4 + hh, 0:CL], start=True, stop=True) for hh in range(4)],
                        reads=(bvh[bi], b_gm), writes=(bp,))
                    hs = slice(hg * 4, hg * 4 + 4)
                    gtb = ygb[:, 0:4, 0:CL]
                    o_lng = PV_OFF["ln_g"]
                    S.op("dve", lambda e, pt=pt, hg=hg: e.tensor_tensor(
                        out=gtb, in0=pt[:, 0:4 * CL].rearrange("p (h i) -> p h i", h=4),
                        in1=pv[:, o_lng + hg * 4:o_lng + hg * 4 + 4].unsqueeze(2).broadcast_to([128, 4, CL]), op=ALU.mult),
                        reads=(bp, b_pv), writes=(bygb,))
                    S.op("dve", lambda e, hs=hs: e.tensor_tensor(out=gtb, in0=gtb, in1=T2x[:, hs, 0:CL], op=ALU.add),
                         reads=(bygb, b_gm), writes=(bygb,))
                    S.op("dve", lambda e, hs=hs: e.tensor_tensor(out=yaT[:, hs, cols], in0=gtb, in1=uT[:, hs, cols], op=ALU.mult),
                         reads=[bygb] + b_u[hg * 4:hg * 4 + 4], writes=b_ya[hg * 4:hg * 4 + 4])

            if STOP <= 2:
                return
            for gi in range(2):
                def ev_z(cc, pt, bp, gi=gi):
                    c = gi * 4 + cc
                    if cfg.get("zmode", 0) == 0:
                        S.op("act", lambda e: e.activation(out=zT[:, c, 0:nt], in_=pt[:, 0:nt], func=AF.Silu),
                             reads=(bp,), writes=(bz,))
                    elif cfg.get("zmode", 0) == 1:
                        S.op("act", lambda e: e.activation(out=ybT[:, c, 0:nt], in_=pt[:, 0:nt], func=AF.Silu),
                             reads=(bp,), writes=(bz,))
                proj_fm(wev_fm[2 + gi], 4, nt, ev_z)
            if STOP <= 2.2:
                return
            if smp:
                extv = ext.rearrange("p c (b r) -> p c b r", r=7)
                pg, bpg = ps_group(2)
                S.op("pe", [lambda e, c=c: e.transpose(pg[:, c * 64:c * 64 + 48], stc[0:48, c * 128:(c + 1) * 128],
                                                        ident[0:48, 0:48]) for c in range(12)],
                     reads=(bstc, b_ident), writes=bpg)
                S.op("dve", lambda e: e.tensor_copy(
                    extv[:, :, :, 0:3],
                    pg[:, 0:768].rearrange("p (c x) -> p c x", c=12)[:, :, 0:48].rearrange("p c (b r) -> p c b r", r=3)),
                     reads=bpg, writes=(bext,))
            else:
                S.op("dve", lambda e: e.tensor_copy(ext[:, :, 0:3], ctail[:, :, :]), reads=(b_ctail,), writes=(bext,))
            XM = cfg.get("xmode", 9)
            for gi in range(3 if XM >= 1 else 0):
                def ev_x(cc, pt, bp, gi=gi):
                    c = gi * 4 + cc
                    if XM < 2:
                        return
                    if smp:
                        S.op("act", lambda e: e.copy(extv[:, c, :, 3:7], pt[:, 0:64].rearrange("p (b i) -> p b i", i=4)),
                             reads=(bp,), writes=(bext,))
                        S.op("dve", lambda e: e.tensor_copy(xpre[:, c, :], pt[:, 0:64]), reads=(bp,), writes=(bxpre,))
                    else:
                        S.op("act", lambda e: e.copy(ext[:, c, 3:3 + nt], pt[:, 0:nt]), reads=(bp,), writes=(bext,))
                        if is_last_ptile and XM >= 3:
                            S.op("dve", lambda e: e.tensor_copy(ctail32[:, c, :], pt[:, nt - 3:nt]),
                                 reads=(bp,), writes=(b_ctail,))
                proj_fm(wev_fm[4 + gi], 4, nt, ev_x)
            if STOP <= 2.4:
                return
            wd, bwd = wload(wev_dt, 8 * 16)
            wdv = wd[:, 0:128].rearrange("p (k n) -> p k n", k=8)
            pt, bp = next_ps()
            S.op("pe", [lambda e, k=k: e.matmul(pt[0:16, 0:nt], wdv[:, k, :], xn[:, k, 0:nt],
                                                 start=(k == 0), stop=(k == 7)) for k in range(8)],
                 reads=(bwd, b_xn), writes=(bp,))
            S.op("act", lambda e: e.activation(out=dsc[:, 0, 0:nt], in_=pt[0:16, 0:nt], func=AF.Exp,
                                               bias=pvc("dt_bias", 0, 16), scale=1.0),
                 reads=(bp, b_pv), writes=(bdsc,))
            if STOP <= 2.6:
                return
            S.op("act", lambda e: e.activation(out=dsc[:, 0, 0:nt], in_=dsc[:, 0, 0:nt], func=AF.Ln, bias=1.0, scale=1.0),
                 reads=(bdsc,), writes=(bdsc,))
            S.op("dve", lambda e: e.tensor_scalar(out=dsc[:, 1, 0:nt], in0=dsc[:, 0, 0:nt], scalar1=aneg[:, 0:1],
                                                  scalar2=None, op0=ALU.mult), reads=(bdsc, b_gm), writes=(bdsc,))
            if STOP <= 2.8:
                return
            rmask = cst_ap("rmask_s", 16)[:, 0:nt] if smp else cst_ap("rmask_p", 16)[:, 0:nt]
            S.op("dve", lambda e: e.tensor_tensor_scan(out=dsc[:, 2, 0:nt], data0=rmask, data1=dsc[:, 1, 0:nt],
                                                       initial=0.0, op0=ALU.mult, op1=ALU.add),
                 reads=(bdsc, b_cs), writes=(bdsc,))
            S.op("dve", lambda e: e.tensor_copy(
                dsc[:, 3, 0:nt].rearrange("p (c l) -> p c l", l=Lc),
                dsc[:, 2, 0:nt].rearrange("p (c l) -> p c l", l=Lc)[:, :, Lc - 1:Lc].broadcast_to([16, nch, Lc])),
                reads=(bdsc,), writes=(bdsc,))

            if STOP <= 3:
                return
            for c in range(12):
                if smp:
                    src = lambda tap, c=c: extv[:, c, :, tap:tap + 4]
                    v3 = lambda ap: ap[:, 0:64].rearrange("p (b i) -> p b i", i=4)
                else:
                    src = lambda tap, c=c: ext[:, c, tap:tap + nt]
                    v3 = lambda ap: ap[:, 0:nt]
                S.op("dve", lambda e, c=c, src=src, v3=v3: e.tensor_scalar(
                    out=v3(cvA), in0=src(0), scalar1=pvc("conv_w", c * 4 + 0), scalar2=pvc("conv_b", c),
                    op0=ALU.mult, op1=ALU.add), reads=(bext, b_pv), writes=(bcv[0],))
                S.op("dve", lambda e, c=c, src=src, v3=v3: e.scalar_tensor_tensor(
                    out=v3(cvB), in0=src(1), scalar=pvc("conv_w", c * 4 + 1), in1=v3(cvA),
                    op0=ALU.mult, op1=ALU.add), reads=(bext, b_pv, bcv[0]), writes=(bcv[1],))
                S.op("dve", lambda e, c=c, src=src, v3=v3: e.scalar_tensor_tensor(
                    out=v3(cvA), in0=src(2), scalar=pvc("conv_w", c * 4 + 2), in1=v3(cvB),
                    op0=ALU.mult, op1=ALU.add), reads=(bext, b_pv, bcv[1]), writes=(bcv[0],))
                S.op("dve", lambda e, c=c, src=src, v3=v3: e.scalar_tensor_tensor(
                    out=v3(cvB), in0=src(3), scalar=pvc("conv_w", c * 4 + 3), in1=v3(cvA),
                    op0=ALU.mult, op1=ALU.add), reads=(bext, b_pv, bcv[0]), writes=(bcv[1],))
                if c < 8:
                    S.op("act", lambda e, c=c: e.activation(out=xcT[:, c, 0:nt], in_=cvB[:, 0:nt], func=AF.Silu),
                         reads=(bcv[1],), writes=(bxc,))
                else:
                    S.op("act", lambda e, c=c: e.activation(out=bcT[:, c - 8, 0:nt], in_=cvB[:, 0:nt], func=AF.Silu),
                         reads=(bcv[1],), writes=b_bc)
            if not smp:
                S.op("dve", lambda e: e.tensor_copy(ctail[:, :, :], ext[:, :, nt:nt + 3]), reads=(bext,), writes=(b_ctail,))
                if is_last_ptile:
                    for r in range(3):
                        S.op("sp", lambda e, r=r: e.dma_start(
                            out=o_conv_p[r:r + 1, :].rearrange("r (c p) -> p (r c)", p=128),
                            in_=ctail32[:, :, r], allow_slow_non_contiguous=True),
                            reads=(b_ctail,), dma=next_out(), final=True)
            else:
                pg, bpg = ps_group(4)
                S.op("pe", [lambda e, c=c: e.transpose(pg[0:64, c * 128:(c + 1) * 128], xpre[:, c, :], ident[:, :])
                            for c in range(12)], reads=(bxpre, b_ident), writes=bpg)
                S.op("act", lambda e: e.copy(cso[:, :], pg[0:64, 0:1536]), reads=bpg, writes=(bcso,))
                for r in range(3):
                    S.op("sp", lambda e, r=r: e.dma_start(out=o_conv_s[:, r, :], in_=cso[1 + r:64:4, :]),
                         reads=(bcso,), dma=next_out(), final=True, arena=True)

            if STOP <= 4:
                return
            if smp:
                S.op("act", lambda e: e.activation(out=eal[:, :].unsqueeze(2), in_=dsc[:, 2, 0:64].rearrange("p (b i) -> p b i", i=4)[:, :, 3:4],
                                                   func=AF.Exp), reads=(bdsc,), writes=(beal,))
                pt, bp = next_ps()
                S.op("pe", [lambda e, h=h: e.matmul(
                    pt[:, h * 16:(h + 1) * 16], ident[0:16, h:h + 1].broadcast_to([16, 128]),
                    eal[:, :], start=True, stop=True) for h in range(16)], reads=(beal, b_ident), writes=(bp,))
                ptv = pt[:, 0:256].rearrange("p (c t b) -> p c t b", c=8, t=2)
                S.op("dve", lambda e: e.tensor_copy(cdT[0:64, :, :], ptv[0:64, :, 0, :]), reads=(bp,), writes=(bcdT,))
                S.op("dve", lambda e: e.tensor_copy(cdT[64:128, :, :], ptv[64:128, :, 1, :]), reads=(bp,), writes=(bcdT,))
            maskT = cst_ap("maskT_blk", 64) if smp else cst_ap("maskT_causal", 128)
            for blk in range(cfg.get("nssd", nblk) if not smp else nblk):
                cols = slice(blk * CL, (blk + 1) * CL)
                pt, bp = next_ps()
                S.op("pe", [lambda e, q=q: e.transpose(pt[0:CL, q * 16:(q + 1) * 16], dsc[:, (0, 2, 3)[q], cols],
                                                        ident[0:16, 0:16]) for q in range(3)],
                     reads=(bdsc, b_ident), writes=(bp,))
                S.op("dve", lambda e, pt=pt: e.tensor_copy(tsc[0:CL, 0:48], pt[0:CL, 0:48]), reads=(bp,), writes=(btsc,))
                S.op("dve", lambda e: e.tensor_tensor(out=tsc[0:CL, 48:64], in0=tsc[0:CL, 32:48], in1=tsc[0:CL, 16:32],
                                                      op=ALU.subtract), reads=(btsc,), writes=(btsc,))
                S.op("act", lambda e: e.activation(out=tsc[0:CL, 48:64], in_=tsc[0:CL, 48:64], func=AF.Exp),
                     reads=(btsc,), writes=(btsc,))
                if cfg.get('sstage', 99) <= 1:
                    continue
                pa, bpa = ps_group(4)
                S.op("pe", [lambda e, h=h: e.matmul(pa[:, h * CL:(h + 1) * CL],
                                                     ident[0:16, h:h + 1].broadcast_to([16, 128]),
                                                     dsc[:, 2, cols], start=True, stop=True) for h in range(16)],
                     reads=(bdsc, b_ident), writes=bpa)
                pav = pa[:, 0:16 * CL].rearrange("p (h i) -> p h i", h=16)
                S.op("dve", lambda e: e.tensor_tensor(
                    out=absx[0:CL, :, 0:CL], in0=pav[0:CL, :, :],
                    in1=tsc[0:CL, 16:32].unsqueeze(2).broadcast_to([CL, 16, CL]), op=ALU.subtract),
                    reads=bpa + [btsc], writes=(babs,))
                S.op("dve", lambda e: e.tensor_scalar(out=absx[0:CL, :, 0:CL], in0=absx[0:CL, :, 0:CL], scalar1=0.0, scalar2=None,
                                                      op0=ALU.min), reads=(babs,), writes=(babs,))
                S.op("act", lambda e: e.activation(out=Eb[0:CL, :, 0:CL], in_=absx[0:CL, :, 0:CL], func=AF.Exp),
                     reads=(babs,), writes=(bEb,))
                S.op("act", lambda e: e.activation(out=absx[:, :, 0:CL], in_=pav, func=AF.Exp),
                     reads=bpa, writes=(babs,))
                if cfg.get('sstage', 99) <= 2:
                    continue
                pc, bpc = next_ps()
                S.op("pe", [lambda e, g=g: e.matmul(pc[0:CL, g * CL:(g + 1) * CL], bcT[:, g, cols], bcT[:, 2 + g, cols],
                                                     start=True, stop=True) for g in range(2)],
                     reads=b_bc, writes=(bpc,))
                if cfg.get("cbx", 9) >= 1:
                  S.op("dve", lambda e, pc=pc: e.tensor_tensor(
                    out=cbm[0:CL, :, 0:CL], in0=pc[0:CL, 0:2 * CL].rearrange("p (g i) -> p g i", g=2),
                    in1=maskT[:, 0:CL].unsqueeze(1).broadcast_to([CL, 2, CL]), op=ALU.mult),
                    reads=(bpc, b_cs), writes=(bcbm,))
                for g in range(2 if cfg.get("cbx", 9) >= 2 else 0):
                    S.op("dve", lambda e, g=g: e.tensor_tensor(
                        out=Eb[0:CL, g * 8:(g + 1) * 8, 0:CL], in0=Eb[0:CL, g * 8:(g + 1) * 8, 0:CL],
                        in1=cbm[0:CL, g:g + 1, 0:CL].broadcast_to([CL, 8, CL]), op=ALU.mult),
                        reads=(bEb, bcbm), writes=(bEb,))
                    S.op("dve", lambda e, g=g: e.tensor_tensor(
                        out=Csb[:, g * 8:(g + 1) * 8, 0:CL], in0=absx[:, g * 8:(g + 1) * 8, 0:CL],
                        in1=bcT[:, 2 + g, cols].unsqueeze(1).broadcast_to([128, 8, CL]), op=ALU.mult),
                        reads=[babs] + b_bc, writes=(bCs,))
                if cfg.get('sstage', 99) <= 3:
                    continue
                px, bpx = next_ps()
                pxb = px.bitcast(BF16)
                S.op("pe", [lambda e, c=c: e.transpose(pxb[0:CL, c * 128:(c + 1) * 128], xcT[:, c, cols], identb[:, :])
                            for c in range(8)], reads=(bxc, b_ident), writes=(bpx,))
                S.op("dve", lambda e, pxb=pxb: e.tensor_tensor(
                    out=xd[0:CL, :, :], in0=pxb[0:CL, 0:1024].rearrange("p (h q) -> p h q", h=16),
                    in1=tsc[0:CL, 0:16].unsqueeze(2).broadcast_to([CL, 16, 64]), op=ALU.mult),
                    reads=(bpx, btsc), writes=(bxd,))
                S.op("dve", lambda e: e.tensor_tensor(
                    out=xdd[0:CL, :, :], in0=xd[0:CL, :, :],
                    in1=tsc[0:CL, 48:64].unsqueeze(2).broadcast_to([CL, 16, 64]), op=ALU.mult),
                    reads=(bxd, btsc), writes=(bxdd,))
                pb_, bpb = next_ps()
                pbb = pb_.bitcast(BF16)
                S.op("pe", [lambda e, g=g: e.transpose(pbb[0:CL, g * 128:(g + 1) * 128], bcT[:, g, cols], identb[:, :])
                            for g in range(2)], reads=b_bc + [b_ident], writes=(bpb,))
                S.op("act", lambda e, pbb=pbb: e.copy(Btok[0:CL, :, :], pbb[0:CL, 0:256].rearrange("p (g n) -> p g n", g=2)),
                     reads=(bpb,), writes=(bBt,))
                if cfg.get('sstage', 99) <= 4:
                    continue
                py, bpy = ps_group(2)
                py_banks = list(state["last_group"])
                fns = []
                for h in range(16):
                    dst = py[64 * (h % 2):64 * (h % 2) + 64, (h // 2) * CL:(h // 2 + 1) * CL]
                    first = (h < 2) or (not smp and ((h // 2) % 4 == 0))
                    if smp:
                        fns.append(lambda e, h=h, dst=dst, first=first: e.matmul(
                            dst, xd[0:CL, h, :], Eb[0:CL, h, 0:CL], start=first, stop=True, skip_group_check=True))
                    else:
                        fns.append(lambda e, h=h, dst=dst: e.matmul(
                            dst, xd[0:CL, h, :], Eb[0:CL, h, 0:CL], start=True, stop=False, skip_group_check=True))
                        fns.append(lambda e, h=h, dst=dst: e.matmul(
                            dst, STb[:, h * 64:(h + 1) * 64], Csb[:, h, 0:CL], start=False, stop=True,
                            skip_group_check=True))
                S.op("pe", fns, reads=(bxd, bEb, b_STb, bCs), writes=bpy)
                if smp:
                    reserved.update(py_banks)
                    sample_states(py, bpy, Snat, bSnat, STs, bSTs, Csb, bCs, xdd, bxdd, Btok, bBt, Bblk, bBblk,
                                  bmsk, cdT, bcdT)
                elif cfg.get('sstage', 99) > 5:
                    pst, bpst = ps_group(2)
                    S.op("pe", [lambda e, g=g: e.matmul(pst[:, g * 512:(g + 1) * 512], Btok[0:CL, g, :],
                                                         xdd[0:CL, g * 8:(g + 1) * 8, :].rearrange("p h q -> p (h q)"),
                                                         start=True, stop=True) for g in range(2)],
                         reads=(bBt, bxdd), writes=bpst)
                    S.op("dve", lambda e: e.tensor_tensor(
                        out=STf[:, :].rearrange("p (h q) -> p h q", h=16),
                        in0=STf[:, :].rearrange("p (h q) -> p h q", h=16),
                        in1=absx[:, :, CL - 1:CL].broadcast_to([128, 16, 64]), op=ALU.mult),
                        reads=(b_ST, babs), writes=(b_ST,))
                    S.op("dve", lambda e, pst=pst: e.tensor_tensor(out=STf[:, :], in0=STf[:, :], in1=pst[:, 0:1024], op=ALU.add),
                         reads=[b_ST] + bpst, writes=(b_ST,))
                    S.op("act", lambda e: e.copy(STb[:, :], STf[:, :]), reads=(b_ST,), writes=(b_STb,))
                if cfg.get('sstage', 99) <= 6:
                    continue
                o_ds = PV_OFF["dskip"]
                S.op("dve", lambda e: e.tensor_tensor(
                    out=ygb[:, :, 0:CL], in0=xcT[:, :, cols], in1=pv[:, o_ds:o_ds + 8].unsqueeze(2).broadcast_to([128, 8, CL]),
                    op=ALU.mult), reads=(bxc, b_pv), writes=(bygb,))
                S.op("dve", lambda e, py=py: e.tensor_tensor(
                    out=ygb[:, :, 0:CL], in0=ygb[:, :, 0:CL], in1=py[:, 0:8 * CL].rearrange("p (c i) -> p c i", c=8), op=ALU.add),
                    reads=[bygb] + bpy, writes=(bygb,))
                S.op("dve", lambda e: e.tensor_tensor(out=ygb[:, :, 0:CL], in0=ygb[:, :, 0:CL], in1=zT[:, :, cols], op=ALU.mult),
                     reads=(bygb, bz), writes=(bygb,))
                S.op("act", lambda e: e.activation(out=sqg[:, :, 0:CL], in_=ygb[:, :, 0:CL], func=AF.Square),
                     reads=(bygb,), writes=(bsqg,))
                pss, bpss = next_ps()
                for g in range(2):
                    S.op("pe", [lambda e, g=g, cc=cc, pss=pss: e.matmul(
                        pss[:, g * CL:(g + 1) * CL], onesb[:, :], sqg[:, g * 4 + cc, 0:CL], start=(cc == 0), stop=(cc == 3))
                        for cc in range(4)], reads=(bsqg, b_ident), writes=(bpss,))
                S.op("act", lambda e, pss=pss: e.activation(out=rsg[:, :, 0:CL], in_=pss[:, 0:2 * CL].rearrange("p (g i) -> p g i", g=2),
                                                             func=AF.Sqrt, bias=EPS, scale=1.0 / 512),
                     reads=(bpss,), writes=(brsg,))
                S.op("dve", lambda e: e.reciprocal(rsg[:, :, 0:CL], rsg[:, :, 0:CL]), reads=(brsg,), writes=(brsg,))
                reserved.difference_update(py_banks)
                o_ng = PV_OFF["norm_g"]
                S.op("dve", lambda e: e.tensor_tensor(
                    out=ygb[:, :, 0:CL], in0=ygb[:, :, 0:CL], in1=pv[:, o_ng:o_ng + 8].unsqueeze(2).broadcast_to([128, 8, CL]),
                    op=ALU.mult), reads=(bygb, b_pv), writes=(bygb,))
                for g in range(2):
                    S.op("dve", lambda e, g=g: e.tensor_tensor(
                        out=ybT[:, g * 4:(g + 1) * 4, cols], in0=ygb[:, g * 4:(g + 1) * 4, 0:CL],
                        in1=rsg[:, g:g + 1, 0:CL].broadcast_to([128, 4, CL]), op=ALU.mult),
                        reads=(bygb, brsg), writes=(byb,))

            if STOP <= 5:
                return
            if cfg.get("obar", 0):
                S.barrier(engines=("pe", "act", "dve", "sp", "pool"))
            for mi in range(4):
                if cfg.get("omode", 9) == 6:
                    w, bw = wring[mi], b_wr[mi]
                else:
                    w, bw = wload(wev_out[mi], 16 * 256)
                wv = w[:, :].rearrange("p (k n) -> p k n", k=16)
                for mm in range(2):
                    m = mi * 2 + mm
                    pt, bp = next_ps()
                    fns = []
                    for k in range(16):
                        rhs = yaT[:, k, 0:nt] if k < 8 else ybT[:, k - 8, 0:nt]
                        fns.append(lambda e, pt=pt, k=k, mm=mm, wv=wv, rhs=rhs: e.matmul(
                            pt[:, 0:nt], wv[:, k, mm * 128:(mm + 1) * 128], rhs, start=(k == 0), stop=(k == 15)))
                    OM = cfg.get("omode", 9)
                    if OM >= 1:
                        S.op("pe", fns[0:OM] if OM < 9 else fns, reads=[bw, byb] + b_ya, writes=(bp,))
                    if OM >= 9:
                        S.op("dve", lambda e, pt=pt, m=m: e.tensor_tensor(out=xT[:, m, 0:nt], in0=pt[:, 0:nt], in1=xT[:, m, 0:nt],
                                                                          op=ALU.add), reads=(bp, b_xT[m]), writes=(b_xT[m],))
            if STOP <= 6:
                return
            if is_last_ptile:
                so = absx[:, 0:8, :]
                bso = babs
                pg, bpg = ps_group(2)
                S.op("pe", [lambda e, c=c: e.transpose(pg[:, c * 128:(c + 1) * 128], STf[:, c * 128:(c + 1) * 128], ident[:, :])
                            for c in range(8)], reads=(b_ST, b_ident), writes=bpg)
                S.op("act", lambda e: e.copy(so.rearrange("p c n -> p (c n)"), pg[:, 0:1024]), reads=bpg, writes=(bso,))
                S.op("sp", lambda e: e.dma_start(out=o_ssm_p.rearrange("(c p) n -> p c n", p=128), in_=so),
                     reads=(bso,), dma=next_out(), final=True, arena=True)

        def sample_states(py, bpy, Snat, bSnat, STs, bSTs, Csb, bCs, xdd, bxdd, Btok, bBt, Bblk, bBblk, bmsk, cdT, bcdT):
            for g in range(2):
                S.op("dve", lambda e, g=g: e.tensor_tensor(
                    out=Bblk[:, g, :, :], in0=Btok[0:64, g, :].unsqueeze(1).broadcast_to([64, 16, 128]),
                    in1=bmsk.unsqueeze(2).broadcast_to([64, 16, 128]), op=ALU.mult),
                    reads=(bBt, b_cs), writes=(bBblk,))
            NB = 2
            for bg in range(16 // NB):
                S.op("sp", lambda e, bg=bg: e.dma_start(
                    out=Snat[:, :, :, :], in_=st_ssm[bg * NB:(bg + 1) * NB].rearrange("b (c p) n -> p b c n", p=128)),
                    writes=(bSnat,), dma=next_in(), arena=True)
                for bb in range(NB):
                    pg, bpg = ps_group(2)
                    S.op("pe", [lambda e, c=c, bb=bb, pg=pg: e.transpose(pg[:, c * 128:(c + 1) * 128], Snat[:, bb, c, :], ident[:, :])
                                for c in range(8)], reads=(bSnat, b_ident), writes=bpg)
                    if bb % 2:
                        S.op("act", lambda e, bb=bb, pg=pg: e.copy(STs[:, bb, :], pg[:, 0:1024]), reads=bpg, writes=(bSTs,))
                    else:
                        S.op("dve", lambda e, bb=bb, pg=pg: e.tensor_copy(STs[:, bb, :], pg[:, 0:1024]), reads=bpg, writes=(bSTs,))
                fns = []
                for bb in range(NB):
                    b = bg * NB + bb
                    for h in range(16):
                        dst = py[64 * (h % 2):64 * (h % 2) + 64, (h // 2) * 64 + 4 * b:(h // 2) * 64 + 4 * b + 4]
                        fns.append(lambda e, h=h, bb=bb, b=b, dst=dst: e.matmul(
                            dst, STs[:, bb, h * 64:(h + 1) * 64], Csb[:, h, 4 * b:4 * b + 4], start=False, stop=True,
                            skip_group_check=True))
                S.op("pe", fns, reads=(bSTs, bCs), writes=bpy)
                for c in range(8):
                    pt, bp = next_ps()
                    S.op("pe", lambda e, c=c, pt=pt, bg=bg: e.matmul(
                        pt[:, 0:NB * 128], xdd[0:64, 2 * c:2 * c + 2, :].rearrange("p h q -> p (h q)"),
                        Bblk[:, c // 4, bg * NB:(bg + 1) * NB, :].rearrange("p b n -> p (b n)"), start=True, stop=True),
                        reads=(bxdd, bBblk), writes=(bp,))
                    for bb in range(NB):
                        b = bg * NB + bb
                        S.op("dve", lambda e, c=c, bb=bb, b=b, pt=pt: e.scalar_tensor_tensor(
                            out=Snat[:, bb, c, :], in0=Snat[:, bb, c, :], scalar=cdT[:, c, b:b + 1],
                            in1=pt[:, bb * 128:(bb + 1) * 128], op0=ALU.mult, op1=ALU.add),
                            reads=(bSnat, bcdT, bp, bSTs), writes=(bSnat,))
                S.op("sp", lambda e, bg=bg: e.dma_start(
                    out=o_ssm_s[bg * NB:(bg + 1) * NB].rearrange("b (c p) n -> p b c n", p=128), in_=Snat[:, :, :, :]),
                    reads=(bSnat,), dma=next_out(), final=True, arena=True)

        WINS = (2, 4, 8, 16)

        def head_norm(pt, bp, nparts_dummy, nt, gname, out_ap, wbufs, scale, f32_out=None, f32_bufs=()):
            si = state["sil"]
            state["sil"] = 1 - si
            raw = sil[si][:, 0:nt]
            S.op("act", lambda e: e.copy(raw, pt), reads=(bp,), writes=(b_sil[si],))
            S.op("act", lambda e: e.activation(out=hsq[:, 0:nt], in_=pt, func=AF.Square), reads=(bp,), writes=(b_hsq,))
            ps2, bps2 = next_ps()
            S.op("pe", lambda e: e.matmul(ps2[:, 0:nt], bd64[:, :], hsq[:, 0:nt], start=True, stop=True),
                 reads=(b_hsq, b_ident), writes=(bps2,))
            if scale is None:
                S.op("act", lambda e: e.activation(out=hrs[:, 0:nt], in_=ps2[:, 0:nt], func=AF.Sqrt, bias=EPS, scale=1.0 / 64),
                     reads=(bps2,), writes=(b_hrs,))
            else:
                S.op("act", lambda e: e.activation(out=hrs[:, 0:nt], in_=ps2[:, 0:nt], func=AF.Sqrt, bias=EPS * scale * scale,
                                                   scale=scale * scale / 64), reads=(bps2,), writes=(b_hrs,))
            S.op("dve", lambda e: e.reciprocal(hrs[:, 0:nt], hrs[:, 0:nt]), reads=(b_hrs,), writes=(b_hrs,))
            S.op("dve", lambda e: e.scalar_tensor_tensor(out=out_ap, in0=raw, scalar=pvc(gname, 0), in1=hrs[:, 0:nt],
                                                         op0=ALU.mult, op1=ALU.mult),
                 reads=(b_sil[si], b_hrs, b_pv), writes=wbufs)
            if f32_out is not None:
                S.op("dve", lambda e: e.scalar_tensor_tensor(out=f32_out, in0=raw, scalar=pvc(gname, 0), in1=hrs[:, 0:nt],
                                                             op0=ALU.mult, op1=ALU.mult),
                     reads=(b_sil[si], b_hrs, b_pv), writes=f32_bufs)

        def odd_mixer(kind, nt, is_last_ptile, tile_idx):
            S.barrier()
            A.reset()
            smp = (kind == "s")
            qT = hT[:, 0:8, :]
            ycT = hT[:, 8:16, :]
            b_q, b_yc = b_h[0:8], b_h[8:16]
            ydT = A.take(128, [8, nt], BF16)
            byd = Buf("ydT")
            L = 19 if smp else (nt + 15)
            nb = 16 if smp else 1
            cext = A.take(128, [8, nb * L])
            bce = Buf("cext")
            pA = A.take(128, [nb * L])
            pB = A.take(128, [nb * L])
            bpA, bpB = Buf("pA"), Buf("pB")
            pooledT = A.take(128, [8, nt], BF16)
            bpool = Buf("pooled")
            KW = 64 if smp else (128 + nt)
            kdT = A.take(128, [4, KW], BF16)
            bkd = Buf("kdT")
            NVB = 1 if smp else 5
            vtok = A.take(128, [NVB, 256], BF16)
            bvt = Buf("vtok")
            kf32 = A.take(128, [4, 128 if not smp else 64])
            vf32 = A.take(128, [256])
            bkf, bvf = Buf("kf32"), Buf("vf32")
            ost = A.take(128, [1024])
            bost = Buf("ost")
            tS_ = [A.take(128, [1024]) for _ in range(2)]
            pT_ = [A.take(128, [1024], BF16) for _ in range(2)]
            btS_, bpT_ = [Buf("tS0"), Buf("tS1")], [Buf("pT0"), Buf("pT1")]
            tS, pT, btS, bpT = tS_[0], pT_[0], btS_[0], bpT_[0]
            rec = [A.take(128, [128]) for _ in range(2)]
            brec = [Buf("rec0"), Buf("rec1")]
            recb = A.take(128, [8, 128])
            brecb = Buf("recb")

            def cview(c, lo, n):
                if smp:
                    return cext[:, c, :].rearrange("p (b l) -> p b l", l=L)[:, :, lo:lo + n]
                return cext[:, c, lo:lo + n]

            def tview(t, lo, n):
                if smp:
                    return t[:, :].rearrange("p (b l) -> p b l", l=L)[:, :, lo:lo + n]
                return t[:, lo:lo + n]

            def ntv(ap2d):
                return ap2d.rearrange("p (b i) -> p b i", i=4) if smp else ap2d

            if smp:
                cin32 = A.take(128, [8, 64])
                bcin = Buf("cin32")
                stp = A.take(120, [2, 1024])
                bstp = Buf("stp")
                S.op("sp", lambda e: e.dma_start(out=stp, in_=st_pool.rearrange("(g r) d -> r g d", g=2)),
                     writes=(bstp,), dma=next_in(), arena=True)
                S.op("sp", lambda e: e.dma_start(out=o_pool_s[:, 0:11, :], in_=st_pool.rearrange("(b r) d -> b r d", r=15)[:, 4:15, :]),
                     dma=next_out(), final=True)
                S.op("sp", lambda e: e.dma_start(out=o_k_s[:, 0:124, :], in_=st_k[:, 4:128, :]), dma=next_out(), final=True)
                S.op("sp", lambda e: e.dma_start(out=o_v_s[:, 0:124, :], in_=st_v[:, 4:128, :]), dma=next_out(), final=True)
                for g in range(2):
                    for cq in range(2):
                        pg, bpg = ps_group(2)
                        S.op("pe", [lambda e, c4=c4, g=g, cq=cq, pg=pg: e.transpose(
                            pg[:, c4 * 128:c4 * 128 + 120], stp[0:120, g, (cq * 4 + c4) * 128:(cq * 4 + c4 + 1) * 128],
                            ident[0:120, 0:120]) for c4 in range(4)], reads=(bstp, b_ident), writes=bpg)
                        for c4 in range(4):
                            c = cq * 4 + c4
                            S.op("dve" if c4 % 2 else "act",
                                 (lambda e, c=c, c4=c4, g=g, pg=pg: e.tensor_copy(
                                     cext[:, c, :].rearrange("p (b l) -> p b l", l=L)[:, g * 8:(g + 1) * 8, 0:15],
                                     pg[:, c4 * 128:c4 * 128 + 120].rearrange("p (b r) -> p b r", r=15))) if c4 % 2 else
                                 (lambda e, c=c, c4=c4, g=g, pg=pg: e.copy(
                                     cext[:, c, :].rearrange("p (b l) -> p b l", l=L)[:, g * 8:(g + 1) * 8, 0:15],
                                     pg[:, c4 * 128:c4 * 128 + 120].rearrange("p (b r) -> p b r", r=15))),
                                 reads=bpg, writes=(bce,))
            else:
                S.op("dve", lambda e: e.tensor_copy(cext[:, :, 0:15], ptail[:, :, :]), reads=(b_ptail,), writes=(bce,))
                bmP = A.take(128, [2, 16, 128])
                bbm = Buf("bm")
                for kb, off in ((0, 256), (1, 128)):
                    S.op("sp", lambda e, kb=kb, off=off: e.dma_start(
                        out=bmP[:, kb, :, :], in_=bass.AP(tensor=dscr.tensor, offset=off, ap=[[383, 128], [128 * 384, 16], [1, 128]])),
                        reads=(b_dscr,), writes=(bbm,), dma=next_in(), arena=True)

            rms_norm("mix_norm1", nt)

            for gi in range(2):
                def ev_c(cc, pt, bp, gi=gi):
                    c = gi * 4 + cc
                    S.op("act", lambda e: e.copy(cview(c, 15, nt if not smp else 4), ntv(pt[:, 0:nt])), reads=(bp,), writes=(bce,))
                    if smp:
                        S.op("dve", lambda e: e.tensor_copy(cin32[:, c, :], pt[:, 0:nt]), reads=(bp,), writes=(bcin,))
                proj_fm(wod_fm[gi], 4, nt, ev_c)
            for gi in range(2):
                def ev_q(cc, pt, bp, gi=gi):
                    c = gi * 4 + cc
                    head_norm(pt[:, 0:nt], bp, 128, nt, "q_norm", qT[:, c, 0:nt], (b_q[c],), 8.0)
                proj_fm(wod_fm[2 + gi], 4, nt, ev_q)
            wk, bwk = wload(wod_k, 8 * 256)
            wkv = wk[:, 0:2048].rearrange("p (k n) -> p k n", k=8)
            k0 = 0 if smp else 128
            if not smp:
                S.op("dve", lambda e: e.tensor_copy(kdT[:, :, 0:128], kprev[:, :, :]), reads=(b_kprev,), writes=(bkd,))
                S.op("dve", lambda e: e.tensor_copy(vtok[:, 0, :], vprev[:, :]), reads=(b_kprev,), writes=(bvt,))
            for hk in range(4):
                pt, bp = next_ps()
                fns = []
                for half in range(2):
                    for k in range(8):
                        fns.append(lambda e, pt=pt, half=half, k=k, hk=hk: e.matmul(
                            pt[64 * half:64 * half + 64, 0:nt], wkv[:, k, hk * 64:(hk + 1) * 64], xn[:, k, 0:nt],
                            start=(k == 0), stop=(k == 7), skip_group_check=True))
                S.op("pe", fns, reads=(bwk, b_xn), writes=(bp,))
                _kn(pt, bp, hk, nt, k0, kdT, bkd, kf32, bkf, smp, is_last_ptile)
            wv_, bwv_ = wload(wod_v, 8 * 256)
            wvv_ = wv_[:, 0:2048].rearrange("p (k n) -> p k n", k=8)
            CLv = 64 if smp else 128
            for blk in range(nt // CLv):
                cols = slice(blk * CLv, (blk + 1) * CLv)
                pt, bp = next_ps()
                S.op("pe", [lambda e, pt=pt, k=k, cols=cols: e.matmul(pt[0:CLv, 0:256], xn[:, k, cols], wvv_[:, k, :],
                                                                       start=(k == 0), stop=(k == 7)) for k in range(8)],
                     reads=(bwv_, b_xn), writes=(bp,))
                vb = 0 if smp else blk + 1
                S.op("act", lambda e, pt=pt, vb=vb: e.copy(vtok[0:CLv, vb, :], pt[0:CLv, 0:256]), reads=(bp,), writes=(bvt,))
                if smp or (is_last_ptile and blk == 3):
                    S.op("dve", lambda e, pt=pt: e.tensor_copy(vf32[0:CLv, :], pt[0:CLv, 0:256]), reads=(bp,), writes=(bvf,))
            if smp or is_last_ptile:
                if smp:
                    for i in range(4):
                        S.op("sp", lambda e, i=i: e.dma_start(out=o_v_s[:, 124 + i, :], in_=vf32[i:64:4, :]),
                             reads=(bvf,), dma=next_out(), final=True, arena=True)
                else:
                    S.op("sp", lambda e: e.dma_start(out=o_v_p, in_=vf32[:, :]), reads=(bvf,), dma=next_out(), final=True, arena=True)
                pk, bpk = next_ps()
                nk = 64 if smp else 128
                S.op("pe", [lambda e, hk=hk: e.transpose(pk[0:nk, hk * 64:(hk + 1) * 64], kf32[0:64, hk, 0:nk], ident[0:64, 0:64])
                            for hk in range(4)], reads=(bkf, b_ident), writes=(bpk,))
                S.op("act", lambda e: e.copy(ost[0:nk, 0:256], pk[0:nk, 0:256]), reads=(bpk,), writes=(bost,))
                if smp:
                    for i in range(4):
                        S.op("sp", lambda e, i=i: e.dma_start(out=o_k_s[:, 124 + i, :], in_=ost[i:64:4, 0:256]),
                             reads=(bost,), dma=next_out(), final=True, arena=True)
                else:
                    S.op("sp", lambda e: e.dma_start(out=o_k_p, in_=ost[:, 0:256]), reads=(bost,), dma=next_out(), final=True, arena=True)

            first_tile = (not smp) and tile_idx == 0
            for c in range(8):
                gi = c // 2
                w = WINS[gi]
                Lx = L
                cur_t, cur_b, cur_lo = None, None, 0
                src_ap = lambda lo, n, c=c: cview(c, lo, n)
                width = 1
                bufs = [(pA, bpA), (pB, bpB)]
                bi = 0
                srcf, srcb, lo0 = src_ap, bce, 0
                while width < w:
                    dst, bdst = bufs[bi]
                    bi = 1 - bi
                    lo1 = lo0 + width
                    n = Lx - lo1
                    S.op("dve", lambda e, srcf=srcf, dst=dst, lo1=lo1, n=n, width=width: e.tensor_tensor(
                        out=tview(dst, lo1, n), in0=srcf(lo1, n), in1=srcf(lo1 - width, n), op=ALU.add),
                        reads=(srcb,), writes=(bdst,))
                    srcf = (lambda lo, n, dst=dst: tview(dst, lo, n))
                    srcb, lo0 = bdst, lo1
                    width *= 2
                nn = 4 if smp else nt
                S.op("dve", lambda e, srcf=srcf, c=c, w=w, nn=nn: e.scalar_tensor_tensor(
                    out=ntv(pooledT[:, c, 0:nt]), in0=srcf(15, nn), scalar=1.0 / w, in1=cview(c, 15, nn),
                    op0=ALU.mult, op1=ALU.subtract), reads=(srcb, bce), writes=(bpool,))
                if first_tile:
                    S.op("dve", lambda e, srcf=srcf, gi=gi: e.tensor_tensor(
                        out=rec[0][:, 0:16], in0=srcf(15, 16), in1=cst_ap("invc")[:, gi * 16:(gi + 1) * 16], op=ALU.mult),
                        reads=(srcb, b_cs), writes=(brec[0],))
                    S.op("dve", lambda e, c=c: e.tensor_tensor(
                        out=pooledT[:, c, 0:16], in0=rec[0][:, 0:16], in1=cview(c, 15, 16), op=ALU.subtract),
                        reads=(brec[0], bce), writes=(bpool,))
            if not smp:
                S.op("dve", lambda e: e.tensor_copy(ptail[:, :, :], cext[:, :, nt:nt + 15]), reads=(bce,), writes=(b_ptail,))
            if smp or is_last_ptile:
                nr = 64 if smp else 15
                pg, bpg = ps_group(2)
                if smp:
                    S.op("pe", [lambda e, c=c: e.transpose(pg[0:64, c * 128:(c + 1) * 128], cin32[:, c, :], ident[:, :])
                                for c in range(8)], reads=(bcin, b_ident), writes=bpg)
                else:
                    S.op("pe", [lambda e, c=c: e.transpose(pg[0:15, c * 128:(c + 1) * 128], cext[:, c, nt:nt + 15], ident[:, :])
                                for c in range(8)], reads=(bce, b_ident), writes=bpg)
                S.op("dve", lambda e: e.tensor_copy(ost[0:nr, :], pg[0:nr, 0:1024]), reads=bpg + [bost], writes=(bost,))
                if smp:
                    for i in range(4):
                        S.op("sp", lambda e, i=i: e.dma_start(out=o_pool_s[:, 11 + i, :], in_=ost[i:64:4, :]),
                             reads=(bost,), dma=next_out(), final=True, arena=True)
                else:
                    S.op("sp", lambda e: e.dma_start(out=o_pool_p, in_=ost[0:15, :]), reads=(bost,), dma=next_out(), final=True, arena=True)
            wl, bwl = wload(wod_lin, 4 * 2 * 256)
            wlv = wl[:, 0:2048].rearrange("p (g c d) -> p g c d", g=4, c=2)
            for c in range(8):
                gi, dd = c // 2, c % 2
                pt, bp = next_ps()
                S.op("pe", [lambda e, pt=pt, cc=cc, gi=gi, dd=dd: e.matmul(
                    pt[:, 0:nt], wlv[:, gi, cc, dd * 128:(dd + 1) * 128], pooledT[:, gi * 2 + cc, 0:nt],
                    start=(cc == 0), stop=(cc == 1)) for cc in range(2)], reads=(bwl, bpool), writes=(bp,))
                S.op("act", lambda e, pt=pt, c=c: e.activation(out=ycT[:, c, 0:nt], in_=pt[:, 0:nt], func=AF.Copy,
                                                               scale=pvc("c_scale", c)), reads=(bp, b_pv), writes=(b_yc[c],))

            if smp:
                sample_attention(nt, qT, b_q, kdT, bkd, vtok, bvt, ydT, byd, tS, btS, pT, bpT, rec, brec)
            else:
                po, bpo = ps_group(2)
                po_banks = list(state["last_group"])
                pd, bpd = ps_group(2)
                pd_banks = list(state["last_group"])
                reserved.update(po_banks + pd_banks)
                for qb in range(4):
                    qcols = slice(qb * 128, (qb + 1) * 128)
                    gq = tile_idx * 4 + qb
                    kbs = [1] if gq == 0 else [0, 1]
                    nkb = len(kbs)
                    for hq in range(4):
                        tS, pT, btS, bpT = tS_[hq % 2], pT_[hq % 2], btS_[hq % 2], bpT_[hq % 2]
                        psA, bpsA = next_ps()
                        psB, bpsB = next_ps()
                        pss_ = (psA, psB)
                        fns = []
                        for hl in range(4):
                            h = hq * 4 + hl
                            hh, s, hp = hl % 2, hl // 2, h // 2
                            for ki, kb in enumerate(kbs):
                                kc0 = qb * 128 + kb * 128
                                fns.append(lambda e, hh=hh, s=s, ki=ki, kc0=kc0, hp=hp, hq=hq, pss_=pss_: e.matmul(
                                    pss_[hh][:, (s * 2 + ki) * 128:(s * 2 + ki + 1) * 128],
                                    kdT[64 * hh:64 * hh + 64, hq, kc0:kc0 + 128], qT[64 * hh:64 * hh + 64, hp, qcols],
                                    start=True, stop=True))
                        S.op("pe", fns, reads=[bkd] + b_q[hq * 2:hq * 2 + 2], writes=(bpsA, bpsB))
                        if nkb == 2:
                            for hh in range(2):
                                h0 = hq * 4 + hh
                                S.op("dve", lambda e, hh=hh, h0=h0, pss_=pss_, tS=tS: e.tensor_tensor(
                                    out=tS[:, hh * 512:(hh + 1) * 512].rearrange("p (s k q) -> p s k q", s=2, k=2),
                                    in0=pss_[hh][:, 0:512].rearrange("p (s k q) -> p s k q", s=2, k=2),
                                    in1=bmP[:, :, h0:h0 + 3:2, :].rearrange("p k s q -> p s k q"), op=ALU.add),
                                    reads=((bpsA, bpsB)[hh], bbm), writes=(btS,))
                        else:
                            for hl in range(4):
                                h = hq * 4 + hl
                                hh, s = hl % 2, hl // 2
                                for ki, kb in enumerate(kbs):
                                    j = s * 2 + ki
                                    S.op("dve", lambda e, hh=hh, j=j, kb=kb, h=h, pss_=pss_, tS=tS: e.tensor_tensor(
                                        out=tS[:, hh * 512 + j * 128:hh * 512 + (j + 1) * 128], in0=pss_[hh][:, j * 128:(j + 1) * 128],
                                        in1=bmP[:, kb, h, :], op=ALU.add), reads=((bpsA, bpsB)[hh], bbm), writes=(btS,))
                        if nkb == 2:
                            S.op("act", lambda e: e.activation(out=pT[:, 0:1024], in_=tS[:, 0:1024], func=AF.Exp),
                                 reads=(btS,), writes=(bpT,))
                        else:
                            S.op("act", lambda e: e.activation(
                                out=pT[:, 0:1024].rearrange("p (a b) -> p a b", a=4)[:, :, 0:128],
                                in_=tS[:, 0:1024].rearrange("p (a b) -> p a b", a=4)[:, :, 0:128], func=AF.Exp),
                                reads=(btS,), writes=(bpT,))
                        fns = []
                        for hl in range(4):
                            h = hq * 4 + hl
                            hh, s, hp = hl % 2, hl // 2, h // 2
                            for ki, kb in enumerate(kbs):
                                j = hh * 4 + s * 2 + ki
                                fns.append(lambda e, hh=hh, ki=ki, kb=kb, hq=hq, hp=hp, j=j, qb=qb: e.matmul(
                                    po[64 * hh:64 * hh + 64, hp * 128:(hp + 1) * 128], vtok[:, qb + kb, hq * 64:(hq + 1) * 64],
                                    pT[:, j * 128:(j + 1) * 128], start=(ki == 0), stop=(ki == nkb - 1), skip_group_check=True))
                            for ki, kb in enumerate(kbs):
                                j = hh * 4 + s * 2 + ki
                                fns.append(lambda e, hh=hh, ki=ki, hp=hp, j=j: e.matmul(
                                    pd[64 * hh:64 * hh + 64, hp * 128:(hp + 1) * 128], onesb[:, 0:64],
                                    pT[:, j * 128:(j + 1) * 128], start=(ki == 0), stop=(ki == nkb - 1), skip_group_check=True))
                        S.op("pe", fns, reads=(bvt, bpT, b_ident), writes=bpo + bpd)
                    S.op("dve", lambda e: e.tensor_tensor(
                        out=recb, in0=pd[:, 0:1024].rearrange("p (c q) -> p c q", c=8),
                        in1=esink2[:, 0:8].unsqueeze(2).broadcast_to([128, 8, 128]), op=ALU.add),
                        reads=bpd + [b_gm2], writes=(brecb,))
                    S.op("dve", lambda e: e.reciprocal(recb, recb), reads=(brecb,), writes=(brecb,))
                    S.op("dve", lambda e, qcols=qcols: e.tensor_tensor(
                        out=ydT[:, :, qcols], in0=po[:, 0:1024].rearrange("p (c q) -> p c q", c=8), in1=recb, op=ALU.mult),
                        reads=bpo + [brecb], writes=(byd,))
                reserved.difference_update(po_banks + pd_banks)
                S.op("dve", lambda e: e.tensor_copy(kprev[:, :, :], kdT[:, :, nt:nt + 128]), reads=(bkd,), writes=(b_kprev,))
                S.op("dve", lambda e: e.tensor_copy(vprev[:, :], vtok[:, 4, :]), reads=(bvt,), writes=(b_kprev,))

            for mi in range(4):
                w, bw = wload(wod_out[mi], 16 * 256)
                wv = w[:, :].rearrange("p (k n) -> p k n", k=16)
                for mm in range(2):
                    m = mi * 2 + mm
                    pt, bp = next_ps()
                    fns = []
                    ks = {"c": range(8), "d": range(8, 16)}.get(cfg.get("odd_part"), range(16))
                    for k in ks:
                        rhs = ycT[:, k, 0:nt] if k < 8 else ydT[:, k - 8, 0:nt]
                        fns.append(lambda e, pt=pt, k=k, mm=mm, wv=wv, rhs=rhs, ks=ks: e.matmul(
                            pt[:, 0:nt], wv[:, k, mm * 128:(mm + 1) * 128], rhs, start=(k == ks[0]), stop=(k == ks[-1])))
                    S.op("pe", fns, reads=[bw, byd] + b_yc, writes=(bp,))
                    S.op("dve", lambda e, pt=pt, m=m: e.tensor_tensor(out=xT[:, m, 0:nt], in0=pt[:, 0:nt], in1=xT[:, m, 0:nt],
                                                                      op=ALU.add), reads=(bp, b_xT[m]), writes=(b_xT[m],))

        def _kn(pt, bp, hk, nt, k0, kdT, bkd, kf32, bkf, smp, is_last_ptile):
            if smp:
                head_norm(pt[:, 0:nt], bp, 128, nt, "k_norm", kdT[:, hk, k0:k0 + nt], (bkd,), None,
                          f32_out=kf32[:, hk, 0:64], f32_bufs=(bkf,))
            elif is_last_ptile:
                head_norm(pt[:, 0:nt], bp, 128, nt, "k_norm", kdT[:, hk, k0:k0 + nt], (bkd,), None)
                si = 1 - state["sil"]
                S.op("dve", lambda e: e.scalar_tensor_tensor(out=kf32[:, hk, :], in0=sil[si][:, nt - 128:nt], scalar=pvc("k_norm", 0),
                                                             in1=hrs[:, nt - 128:nt], op0=ALU.mult, op1=ALU.mult),
                     reads=(b_sil[si], b_hrs, b_pv), writes=(bkf,))
            else:
                head_norm(pt[:, 0:nt], bp, 128, nt, "k_norm", kdT[:, hk, k0:k0 + nt], (bkd,), None)

        def sample_attention(nt, qT, b_q, kdT, bkd, vtok, bvt, ydT, byd, tS, btS, pT, bpT, rec, brec):
            kc32 = A.take(128, [256])
            kcb = A.take(128, [4, 2, 64], BF16)
            bkc, bkcb = Buf("kc32"), Buf("kcb")
            KdT = A.take(128, [16, 4, 128], BF16)
            bKd = Buf("KdT")
            vc32 = A.take(128, [2, 256])
            vcb = A.take(128, [16, 256], BF16)
            bvc, bvcb = Buf("vc32"), Buf("vcb")
            bm1 = A.take(128, [16, 4])
            bm2 = A.take(64, [16, 64])
            bbm1, bbm2 = Buf("bm1"), Buf("bm2")
            t1 = A.take(128, [16, 64])
            p1 = A.take(128, [16, 64], BF16)
            t2 = A.take(64, [16, 64])
            p2 = A.take(64, [16, 64], BF16)
            bt1, bp1, bt2, bp2 = Buf("t1"), Buf("p1"), Buf("t2"), Buf("p2")
            S.op("sp", lambda e: e.dma_start(out=bm1, in_=bass.AP(tensor=dscr.tensor, offset=256, ap=[[383, 128], [128 * 384, 16], [1, 4]])),
                 reads=(b_dscr,), writes=(bbm1,), dma=next_in(), arena=True)
            S.op("dve", lambda e: e.memset(bm2, NEG), writes=(bbm2,))
            for b in range(16):
                S.op("sp", lambda e, b=b: e.dma_start(
                    out=bm2[4 * b:4 * b + 4, :, 4 * b:4 * b + 4],
                    in_=bass.AP(tensor=dscr.tensor, offset=128, ap=[[383, 4], [128 * 384, 16], [1, 4]])),
                    reads=(b_dscr,), writes=(bbm2,), dma=next_in(), arena=True)
            for b in range(16):
                S.op("sp", lambda e, b=b: e.dma_start(out=kc32, in_=st_k[b]), writes=(bkc,), dma=next_in(), arena=True)
                S.op("dve", lambda e: e.tensor_copy(kcb[:, :, :, :], kc32[:, :].rearrange("p (h d) -> p h d", h=4).unsqueeze(2)
                                                    .broadcast_to([128, 4, 2, 64])), reads=(bkc,), writes=(bkcb,))
                pk, bpk = next_ps()
                pkb = pk.bitcast(BF16)
                S.op("pe", [lambda e, hk=hk, pkb=pkb: e.transpose(pkb[:, hk * 128:(hk + 1) * 128],
                                                                  kcb[:, hk, :, :].rearrange("p a d -> p (a d)"), identb[:, :])
                            for hk in range(4)], reads=(bkcb, b_ident), writes=(bpk,))
                S.op("act", lambda e, b=b, pkb=pkb: e.copy(KdT[:, b, :, :].rearrange("p h k -> p (h k)"), pkb[:, 0:512]),
                     reads=(bpk,), writes=(bKd,))
                if b % 2 == 0:
                    S.op("sp", lambda e, b=b: e.dma_start(out=vc32, in_=st_v[b:b + 2].rearrange("b k d -> k b d")),
                         writes=(bvc,), dma=next_in(), arena=True)
                    S.op("dve", lambda e, b=b: e.tensor_copy(vcb[:, b:b + 2, :], vc32[:, :, :]), reads=(bvc,), writes=(bvcb,))
            ps1, bps1 = ps_group(2)
            g1 = list(state["last_group"])
            reserved.update(g1)
            fns = []
            for b in range(16):
                for h in range(16):
                    hh, hp, hk = h % 2, h // 2, h // 4
                    col = hh * 512 + b * 32 + hp * 4
                    fns.append(lambda e, b=b, hh=hh, hp=hp, hk=hk, col=col: e.matmul(
                        ps1[:, col:col + 4], KdT[64 * hh:64 * hh + 64, b, hk, :],
                        qT[64 * hh:64 * hh + 64, hp, 4 * b:4 * b + 4], start=True, stop=True))
            S.op("pe", fns, reads=[bKd] + b_q, writes=bps1)
            bm1v = bm1.rearrange("p (hp t) i -> p hp t i", t=2)
            for hh in range(2):
                S.op("dve", lambda e, hh=hh: e.tensor_tensor(
                    out=t1[:, hh * 8:(hh + 1) * 8, :].rearrange("p a (c d) -> p (a c) d", d=32) if False else
                    t1.rearrange("p a x -> p (a x)")[:, hh * 512:(hh + 1) * 512].rearrange("p (b hp i) -> p b hp i", b=16, hp=8),
                    in0=ps1[:, hh * 512:(hh + 1) * 512].rearrange("p (b hp i) -> p b hp i", b=16, hp=8),
                    in1=bm1v[:, :, hh, :].unsqueeze(1).broadcast_to([128, 16, 8, 4]), op=ALU.add),
                    reads=bps1 + [bbm1], writes=(bt1,))
            reserved.difference_update(g1)
            S.op("act", lambda e: e.activation(out=p1[:, :, :], in_=t1[:, :, :], func=AF.Exp), reads=(bt1,), writes=(bp1,))
            p1f = p1.rearrange("p a x -> p (a x)")
            ps2, bps2 = ps_group(2)
            S.op("pe", [lambda e, h=h: e.matmul(ps2[0:64, (h % 2) * 512 + (h // 2) * 64:(h % 2) * 512 + (h // 2 + 1) * 64],
                                                 kdT[64 * (h % 2):64 * (h % 2) + 64, h // 4, 0:64],
                                                 qT[64 * (h % 2):64 * (h % 2) + 64, h // 2, 0:64], start=True, stop=True)
                        for h in range(16)], reads=[bkd] + b_q, writes=bps2)
            bm2v = bm2.rearrange("p (hp t) x -> p hp t x", t=2)
            t2f = t2.rearrange("p a x -> p (a x)")
            for hh in range(2):
                S.op("dve", lambda e, hh=hh: e.tensor_tensor(
                    out=t2f[:, hh * 512:(hh + 1) * 512].rearrange("p (hp x) -> p hp x", hp=8),
                    in0=ps2[0:64, hh * 512:(hh + 1) * 512].rearrange("p (hp x) -> p hp x", hp=8),
                    in1=bm2v[:, :, hh, :], op=ALU.add), reads=bps2 + [bbm2], writes=(bt2,))
            S.op("act", lambda e: e.activation(out=p2[:, :, :], in_=t2[:, :, :], func=AF.Exp), reads=(bt2,), writes=(bp2,))
            p2f = p2.rearrange("p a x -> p (a x)")
            po, bpo = next_ps()
            pd, bpd = next_ps()
            fns = []
            for h in range(16):
                hh, hp, hk = h % 2, h // 2, h // 4
                first = h < 2
                fns.append(lambda e, h=h, hh=hh, hp=hp, hk=hk, first=first: e.matmul(
                    po[64 * hh:64 * hh + 64, hp * 64:(hp + 1) * 64], vtok[0:64, 0, hk * 64:(hk + 1) * 64],
                    p2f[:, hh * 512 + hp * 64:hh * 512 + (hp + 1) * 64],
                    start=first, stop=False, skip_group_check=True))
                fns.append(lambda e, h=h, hh=hh, hp=hp, first=first: e.matmul(
                    pd[64 * hh:64 * hh + 64, hp * 64:(hp + 1) * 64], onesb[0:64, 0:64],
                    p2f[:, hh * 512 + hp * 64:hh * 512 + (hp + 1) * 64],
                    start=first, stop=False, skip_group_check=True))
            for b in range(16):
                for h in range(16):
                    hh, hp, hk = h % 2, h // 2, h // 4
                    fns.append(lambda e, b=b, h=h, hh=hh, hp=hp, hk=hk: e.matmul(
                        po[64 * hh:64 * hh + 64, hp * 64 + 4 * b:hp * 64 + 4 * b + 4], vcb[:, b, hk * 64:(hk + 1) * 64],
                        p1f[:, hh * 512 + b * 32 + hp * 4:hh * 512 + b * 32 + hp * 4 + 4], start=False, stop=True,
                        skip_group_check=True))
                    fns.append(lambda e, b=b, h=h, hh=hh, hp=hp: e.matmul(
                        pd[64 * hh:64 * hh + 64, hp * 64 + 4 * b:hp * 64 + 4 * b + 4], onesb[:, 0:64],
                        p1f[:, hh * 512 + b * 32 + hp * 4:hh * 512 + b * 32 + hp * 4 + 4], start=False, stop=True,
                        skip_group_check=True))
            S.op("pe", fns, reads=(bvt, bp2, bp1, bvcb, b_ident), writes=(bpo, bpd))
            for c in range(8):
                ri = c % 2
                S.op("dve", lambda e, c=c, ri=ri: e.tensor_scalar(
                    out=rec[ri][:, 0:64], in0=pd[:, c * 64:(c + 1) * 64], scalar1=esink2[:, c:c + 1], scalar2=None,
                    op0=ALU.add), reads=(bpd, b_gm2), writes=(brec[ri],))
                S.op("dve", lambda e, ri=ri: e.reciprocal(rec[ri][:, 0:64], rec[ri][:, 0:64]), reads=(brec[ri],), writes=(brec[ri],))
                S.op("dve", lambda e, c=c, ri=ri: e.tensor_tensor(
                    out=ydT[:, c, 0:64], in0=po[:, c * 64:(c + 1) * 64], in1=rec[ri][:, 0:64], op=ALU.mult),
                    reads=(bpo, brec[ri]), writes=(byd,))

        tiles = [("p", t) for t in range(n_ptiles)] + ([("s", 0)] if do_sample else [])

        def issue_load(kind, t):
            if kind == "p":
                load_x_tile(xp[t * NTP:(t + 1) * NTP, :], 4, 128)
            else:
                load_x_tile(xs[:, :], 1, NS)

        issue_load(*tiles[0])
        for ti, (kind, t) in enumerate(tiles):
            nt = NTP if kind == "p" else NS
            last_p = (kind == "p" and t == n_ptiles - 1)
            if kind == "p":
                transpose_in(4, 128)
            else:
                transpose_in(1, NS)
            for layer in range(nlayers):
                if on("ffn1"):
                    ffn(layer, 0, nt)
                if layer == 0 and on("even"):
                    even_mixer(kind, nt, last_p)
                if layer == 1 and on("odd"):
                    odd_mixer(kind, nt, last_p, t)
                if layer == nlayers - 1:
                    S.barrier()
                    if ti + 1 < len(tiles):
                        issue_load(*tiles[ti + 1])
                if on("ffn2"):
                    ffn(layer, 1, nt)
            if kind == "p":
                transpose_out(yp[t * NTP:(t + 1) * NTP, :], 4, 128)
            else:
                transpose_out(ys[:, :], 1, NS)

        with nc.Block() as block:
            S.emit(block)
    P.ninst = S.ninst
    return P


PV_OFF = {}
_col = 0


def _pv(name, n):
    global _col
    PV_OFF[name] = _col
    _col += n


for _l in range(2):
    for _w in range(2):
        _pv(f"ffn_norm{_l}{_w}", 8)
for _l in range(2):
    _pv(f"mix_norm{_l}", 8)
_pv("ln_g", 8)
_pv("ln_b", 8)
_pv("conv_w", 48)
_pv("conv_b", 12)
_pv("norm_g", 8)
_pv("dskip", 8)
_pv("c_scale", 8)
_pv("q_norm", 1)
_pv("k_norm", 1)
_pv("dt_bias", 1)
_pv("a_log", 1)
PV_COLS = _col

CST_OFF = {}
_ccol = 0


def _cs(name, n):
    global _ccol
    CST_OFF[name] = (_ccol, n)
    _ccol += n


_cs("maskT_causal", 128)
_cs("maskT_blk", 64)
_cs("bmask", 16)
_cs("rmask_p", 512)
_cs("rmask_s", 64)
_cs("onehot", 384)
_cs("negmask", 384)
_cs("invc", 64)
CST_COLS = _ccol


def make_consts():
    c = np.zeros((128, CST_COLS), np.float32)
    j = np.arange(128)[:, None]
    i = np.arange(128)[None, :]
    o, n = CST_OFF["maskT_causal"]
    c[:, o:o + n] = (i >= j)
    o, n = CST_OFF["maskT_blk"]
    jj = np.arange(64)[:, None]
    ii = np.arange(64)[None, :]
    c[:64, o:o + n] = (ii >= jj) & (ii // 4 == jj // 4)
    o, n = CST_OFF["bmask"]
    c[:64, o:o + n] = (np.arange(64)[:, None] // 4 == np.arange(16)[None, :])
    o, n = CST_OFF["rmask_p"]
    c[:, o:o + n] = (np.arange(512)[None, :] % 128 != 0)
    o, n = CST_OFF["rmask_s"]
    c[:, o:o + n] = (np.arange(64)[None, :] % 4 != 0)
    dist = np.arange(384) - 128
    valid = (dist >= 0) & (dist < 128)
    nn = np.maximum(dist, 0)
    n_safe = np.maximum(nn, 1).astype(np.float32)
    scale = np.float32((32 - 16) / math.log(128 / 16))
    large = np.minimum(16 + (np.log(n_safe / 16) * scale).astype(np.int32), 31)
    bucket = np.where(nn < 16, nn, large).astype(np.int32)
    o, n = CST_OFF["onehot"]
    oh = np.zeros((32, 384), np.float32)
    oh[bucket[valid], np.arange(384)[valid]] = 1.0
    c[:32, o:o + n] = oh
    o, n = CST_OFF["negmask"]
    c[:, o:o + n] = np.where(valid, 0.0, NEG)[None, :]
    o, n = CST_OFF["invc"]
    for gi, w in enumerate((2, 4, 8, 16)):
        c[:, o + gi * 16:o + (gi + 1) * 16] = (1.0 / np.minimum(np.arange(16) + 1, w))[None, :]
    return c


def fm(v):
    v = np.asarray(v, np.float32)
    return np.ascontiguousarray(v.reshape(-1, 128).T)


def pack_host(inp):
    f = lambda k: np.asarray(inp[k], np.float32)
    pvec = np.zeros((128, PV_COLS), np.float32)

    def put(name, arr):
        arr = np.asarray(arr, np.float32)
        pvec[:arr.shape[0], PV_OFF[name]:PV_OFF[name] + arr.shape[1]] = arr

    for l in range(2):
        put(f"ffn_norm{l}0", fm(f("ffn1_norm")[l]))
        put(f"ffn_norm{l}1", fm(f("ffn2_norm")[l]))
        put(f"mix_norm{l}", fm(f("mix_norm")[l]))
    put("ln_g", fm(f("a_ln_g")[0]))
    put("ln_b", fm(f("a_ln_b")[0]))
    cw = f("b_conv_w")[0]
    put("conv_w", cw.reshape(4, 12, 128).transpose(2, 1, 0).reshape(128, 48))
    put("conv_b", fm(f("b_conv_b")[0]))
    put("norm_g", fm(f("b_norm_g")[0]))
    put("dskip", fm(np.repeat(f("b_d_skip")[0], 64)))
    put("c_scale", fm(f("c_scale")[0]))
    put("q_norm", np.tile(f("d_q_norm")[0], 2).reshape(128, 1))
    put("k_norm", np.tile(f("d_k_norm")[0], 2).reshape(128, 1))
    put("dt_bias", f("b_dt_bias")[0].reshape(16, 1))
    put("a_log", f("b_a_log")[0].reshape(16, 1))

    wgu = np.empty((2, 2, 11, 128, 8, 2, 256), np.float32)
    wdn = np.empty((2, 2, 8, 128, FC, 128), np.float32)
    for l in range(2):
        for w, (kgu, kdn) in enumerate((("ffn1_w_gu", "ffn1_w_down"), ("ffn2_w_gu", "ffn2_w_down"))):
            g = f(kgu)[l].reshape(8, 128, 2, 11, 256)
            wgu[l, w] = g.transpose(3, 1, 0, 2, 4)
            d = f(kdn)[l].reshape(FC, 128, 8, 128)
            wdn[l, w] = d.transpose(2, 1, 0, 3)

    def fm_w(Wc):
        n = Wc.shape[1]
        return np.ascontiguousarray(Wc.reshape(8, 128, n).transpose(1, 0, 2).reshape(128, 8 * n))

    Wi = f("ev_w_in")[0]
    col_groups = [(0, 512), (512, 1024), (2048, 2560), (2560, 3072), (3072, 3584), (3584, 4096), (4096, 4608)]
    wev_fm = np.stack([fm_w(Wi[:, a:b]) for a, b in col_groups], 0)
    wev_v = np.stack([fm_w(Wi[:, 1024:1536]), fm_w(Wi[:, 1536:2048])], 0)
    wev_dt = fm_w(Wi[:, 4608:4624])
    Wo = f("ev_w_out")[0]
    wev_out = np.ascontiguousarray(Wo.reshape(16, 128, 4, 256).transpose(2, 1, 0, 3).reshape(4, 128, 16 * 256))
    Wd = f("od_w_in")[0]
    ws = f("a_w_s")[0]
    wsT = np.ascontiguousarray(ws.transpose(2, 0, 1).reshape(128, 8 * 128))
    wsT4 = np.ascontiguousarray(ws[:, :4, :4].transpose(2, 0, 1).reshape(4, 32))
    shared = {
        "pvec": pvec,
        "cst": make_consts(),
        "wgu": wgu.reshape(2, 2, 11, 128, 8 * 2 * 256),
        "wdn": wdn.reshape(2, 2, 8, 128, FC * 128),
        "wev_fm": wev_fm, "wev_v": wev_v, "wev_dt": wev_dt, "wev_out": wev_out,
        "wsT": wsT, "wsT4": wsT4,
        "bs_row": f("a_b_s")[0].reshape(1, 1024),
        "lnrow": np.concatenate([f("a_ln_g")[0], f("a_ln_b")[0]]).reshape(1, 2048),
        "wod_fm": np.stack([fm_w(Wd[:, a:a + 512]) for a in (0, 512, 1024, 1536)], 0),
        "wod_k": fm_w(Wd[:, 2048:2304]), "wod_v": fm_w(Wd[:, 2304:2560]),
        "wod_lin": np.ascontiguousarray(f("c_lin_w")[0].reshape(4, 2, 128, 256).transpose(2, 0, 1, 3).reshape(128, 2048)),
        "wod_out": np.ascontiguousarray(f("od_w_out")[0].reshape(16, 128, 4, 256).transpose(2, 1, 0, 3).reshape(4, 128, 16 * 256)),
        "rel_tab": f("rel_bias_table"), "sink_row": f("d_sinks")[0].reshape(1, 16),
    }
    return shared


def per_core_inputs(inp, c):
    f = lambda k: np.asarray(inp[k], np.float32)
    sl = slice(16 * c, 16 * c + 16)
    return {
        "xs": np.ascontiguousarray(f("x_sample")[sl].reshape(NS, D)),
        "st_ssm": np.ascontiguousarray(f("state_ssm")[0, sl].reshape(16, 1024, 128)),
        "st_conv": np.ascontiguousarray(f("state_conv")[0, sl].reshape(48, 1536)),
        "st_pool": np.ascontiguousarray(f("state_pool")[0, sl].reshape(240, 1024)),
        "st_k": np.ascontiguousarray(f("cache_k_win")[0, sl].reshape(16, 128, 256)),
        "st_v": np.ascontiguousarray(f("cache_v_win")[0, sl].reshape(16, 128, 256)),
    }


_CACHE = {}


def kernel(**inputs):
    cfg = {}
    key = "full"
    if key not in _CACHE:
        _CACHE[key] = build_program(cfg)
    P = _CACHE[key]
    shared = pack_host(inputs)
    xp = np.asarray(inputs["x_prompt"], np.float32)
    in_maps = []
    for c in range(NCORES):
        m = dict(shared)
        m["xp"] = np.ascontiguousarray(xp[c])
        m.update(per_core_inputs(inputs, c))
        in_maps.append(m)
    res = run_bass_kernel_spmd(P.nc, in_maps, core_ids=list(range(NCORES)))
    r = res.results
    y_p = np.stack([r[c]["yp"] for c in range(NCORES)], 0)
    y_s = np.stack([r[c]["ys"] for c in range(NCORES)], 0).reshape(128, 4, D)
    cat = lambda k: np.stack([r[c][k] for c in range(NCORES)], 0)
    av = cat("o_av").reshape(1, 128, 4, D)
    ssm_p = cat("o_ssm_p").reshape(1, 8, 16, 64, 128)
    ssm_s = cat("o_ssm_s").reshape(1, 128, 16, 64, 128)
    conv_p = cat("o_conv_p").reshape(1, 8, 3, 1536)
    conv_s = cat("o_conv_s").reshape(1, 128, 3, 1536)
    pool_p = cat("o_pool_p").reshape(1, 8, 15, 1024)
    pool_s = cat("o_pool_s").reshape(1, 128, 15, 1024)
    k_p = cat("o_k_p").reshape(1, 8, 128, 4, 64)
    k_s = cat("o_k_s").reshape(1, 128, 128, 4, 64)
    v_p = cat("o_v_p").reshape(1, 8, 128, 4, 64)
    v_s = cat("o_v_s").reshape(1, 128, 128, 4, 64)
    return (y_p, y_s, av, ssm_p, ssm_s, conv_p, conv_s, pool_p, pool_s, k_p, k_s, v_p, v_s)
```

```python
import contextlib
import math
import types
import numpy as np
import concourse.bass as bass
import concourse.mybir as mybir
from concourse.bass_utils import run_bass_kernel_spmd

F32 = mybir.dt.float32
BF16 = mybir.dt.bfloat16
I32 = mybir.dt.int32
AF = mybir.ActivationFunctionType
ALU = mybir.AluOpType
AX = mybir.AxisListType

NCORES = 8
D = 1024
DC = 8
DFF = 2816
FC = 22
SEQ = 4096
NTP = 512
NS = 64
EPS = 1e-6
NEG = -1e30


NATIVE_GELU = True
RELAX_SAME_ENGINE = False


def freeze(fn):
    if fn.__closure__ is None:
        return fn
    cells = []
    for c in fn.__closure__:
        try:
            cells.append(types.CellType(c.cell_contents))
        except ValueError:
            cells.append(c)
    return types.FunctionType(fn.__code__, fn.__globals__, fn.__name__, fn.__defaults__, tuple(cells))


class Buf:
    __slots__ = ("name", "w", "r", "excl")

    def __init__(self, name, excl=False):
        self.name = name
        self.w = None
        self.r = {}
        self.excl = excl


class DmaSlot:
    def __init__(self, S, name):
        self.S = S
        self.name = name
        self.sem = S.new_sem("d" + name)
        self.count = 0
        self.last = None

    def next_token(self):
        if self.count >= 16 * 1200:
            self.sem = self.S.new_sem("d" + self.name)
            self.count = 0
        self.count += 16
        self.last = (id(self.sem), self.sem, self.count, "dma")
        return self.last


class Sched:
    ENG = ("pe", "act", "dve", "pool", "sp")
    EPOCH = 6000

    def __init__(self, nc, stack):
        self.nc = nc
        self.stack = stack
        self.nsem = 0
        self.streams = {e: [] for e in self.ENG}
        self.sem = {e: self.new_sem(e) for e in self.ENG}
        self.cnt = {e: 0 for e in self.ENG}
        self.seen = {e: {} for e in self.ENG}
        self.final_tokens = []
        self.pending_dma = []
        self.ninst = 0
        self.oplog = []

    def new_sem(self, name):
        self.nsem += 1
        return self.stack.enter_context(self.nc.semaphore(f"s{self.nsem}_{name}"))

    def _wait(self, eng, tok):
        sid, sem, val, _ = tok
        if self.seen[eng].get(sid, 0) >= val:
            return
        self.seen[eng][sid] = val
        self.streams[eng].append(("wait", sem, val))

    def barrier(self, engines=("pe", "act", "dve", "sp")):
        toks = [(id(self.sem[e]), self.sem[e], self.cnt[e], e) for e in ("pe", "act", "dve", "pool") if self.cnt[e] > 0]
        toks += self.pending_dma
        for e in engines:
            for tok in toks:
                if tok[3] != e:
                    self._wait(e, tok)
        self.pending_dma = []

    def op(self, eng, fns, reads=(), writes=(), dma=None, final=False, arena=False):
        if not isinstance(fns, (list, tuple)):
            fns = [fns]
        fns = [freeze(f) for f in fns]
        writes = list(writes) + [b for b in reads if b.excl]
        reads = [b for b in reads if not b.excl]
        for b in reads:
            if b.w is not None and not (b.w[3] == eng == "pe"):
                self._wait(eng, b.w)
        for b in writes:
            if b.w is not None and not (b.w[3] == eng == "pe") and not (RELAX_SAME_ENGINE and b.w[3] == eng):
                self._wait(eng, b.w)
            for tok in b.r.values():
                if not (tok[3] == eng == "pe") and not (RELAX_SAME_ENGINE and tok[3] == eng):
                    self._wait(eng, tok)
        if dma is not None and dma.last is not None:
            self._wait(eng, dma.last)
        if dma is None:
            if self.cnt[eng] >= self.EPOCH:
                self.sem[eng] = self.new_sem(eng)
                self.cnt[eng] = 0
            self.cnt[eng] += 1
            tok = (id(self.sem[eng]), self.sem[eng], self.cnt[eng], eng)
            inc = 1
        else:
            tok = dma.next_token()
            inc = 16
        self.oplog.append((eng, tok, [x.name for x in reads], [x.name for x in writes], len(self.streams[eng])))
        st = self.streams[eng]
        for fn in fns[:-1]:
            st.append(("inst", fn, None, 0))
        st.append(("inst", fns[-1], tok[1], inc))
        self.ninst += len(fns)
        for b in reads:
            b.r[tok[0]] = tok
        for b in writes:
            b.w = tok
            b.r = {}
        if final:
            self.final_tokens.append(tok)
        if arena and dma is not None:
            self.pending_dma.append(tok)
        return tok

    def emit(self, block):
        nc = self.nc
        for tok in self.final_tokens:
            self._wait("sp", tok)
        streams = self.streams

        def run(engine, items):
            for it in items:
                if it[0] == "wait":
                    engine.wait_ge(it[1], it[2])
                else:
                    r = it[1](engine)
                    if it[2] is not None:
                        r.then_inc(it[2], it[3])

        @block.tensor
        def _(e):
            run(e, streams["pe"])

        @block.scalar
        def _(e):
            run(e, streams["act"])

        @block.vector
        def _(e):
            run(e, streams["dve"])

        @block.gpsimd
        def _(e):
            run(e, streams["pool"])

        @block.sync
        def _(e):
            run(e, streams["sp"])


class Prog:
    def __init__(self, cfg):
        self.cfg = cfg
        self.nc = bass.Bass("TRN2", target_bir_lowering=False)
        self.stack = contextlib.ExitStack()
        self.S = None
        self.dram = {}

    def din(self, name, shape, dtype=F32):
        t = self.nc.dram_tensor(name, list(shape), dtype, kind="ExternalInput")
        self.dram[name] = t
        return t.ap()

    def dout(self, name, shape, dtype=F32):
        t = self.nc.dram_tensor(name, list(shape), dtype, kind="ExternalOutput")
        self.dram[name] = t
        return t.ap()

    def dscratch(self, name, shape, dtype=F32):
        t = self.nc.dram_tensor(name, list(shape), dtype, kind="Internal")
        return t.ap()

    def sb(self, name, shape, dtype=F32):
        return self.stack.enter_context(self.nc.sbuf_tensor(name, list(shape), dtype))

    def ps(self, name, shape, dtype=F32):
        return self.stack.enter_context(self.nc.psum_tensor(name, list(shape), dtype))


WSLOT_ELEMS = 8 * 2 * 256
NWSLOT = 4
ARENA_WORDS = 23 * 1024 + 128


def _prod(xs):
    r = 1
    for x in xs:
        r *= x
    return r


def rs(ap, dims):
    if len(dims) == 1:
        return ap
    if len(dims) == 2:
        return ap.rearrange("p (a b) -> p a b", a=dims[0])
    if len(dims) == 3:
        return ap.rearrange("p (a b c) -> p a b c", a=dims[0], b=dims[1])
    raise ValueError(dims)


class Arena:
    def __init__(self, t, words):
        self.t = t
        self.words = words
        self.off = 0

    def reset(self):
        self.off = 0

    def take(self, nparts, dims, dtype=F32):
        n = _prod(dims)
        w = n if dtype == F32 else (n + 1) // 2
        w = (w + 1) // 2 * 2
        ap = self.t[0:nparts, self.off:self.off + w]
        if dtype != F32:
            ap = ap.bitcast(dtype)
        ap = ap[:, 0:n]
        self.off += w
        assert self.off <= self.words, ("arena overflow", self.off, self.words)
        return rs(ap, dims)


def build_program(cfg):
    P = Prog(cfg)
    nc = P.nc
    n_ptiles = cfg.get("n_ptiles", SEQ // NTP)
    do_sample = cfg.get("sample", True)
    stages = cfg.get("stages", "all")
    nlayers = cfg.get("layers", 2)
    npt_tokens = n_ptiles * NTP

    def on(name):
        return stages == "all" or name in stages

    xp = P.din("xp", [npt_tokens, D])
    xs = P.din("xs", [NS, D])
    pvec = P.din("pvec", [128, PV_COLS])
    cst = P.din("cst", [128, CST_COLS])
    wgu = P.din("wgu", [2, 2, 11, 128, 8 * 2 * 256])
    wdn = P.din("wdn", [2, 2, 8, 128, FC * 128])
    wev_fm = P.din("wev_fm", [7, 128, 8 * 512])
    wev_v = P.din("wev_v", [2, 128, 8 * 512])
    wev_dt = P.din("wev_dt", [128, 8 * 16])
    wev_out = P.din("wev_out", [4, 128, 16 * 256])
    wsT = P.din("wsT", [128, 8 * 128])
    wsT4 = P.din("wsT4", [4, 8 * 4])
    bs_row = P.din("bs_row", [1, 8 * 128])
    lnrow = P.din("lnrow", [1, 2048])
    wod_fm = P.din("wod_fm", [4, 128, 8 * 512])
    wod_k = P.din("wod_k", [128, 8 * 256])
    wod_v = P.din("wod_v", [128, 8 * 256])
    wod_lin = P.din("wod_lin", [128, 4 * 2 * 256])
    wod_out = P.din("wod_out", [4, 128, 16 * 256])
    rel_tab = P.din("rel_tab", [32, 16])
    sink_row = P.din("sink_row", [1, 16])
    st_pool = P.din("st_pool", [240, 1024])
    st_k = P.din("st_k", [16, 128, 256])
    st_v = P.din("st_v", [16, 128, 256])
    dscr = P.dscratch("dscr", [16, 128, 384])
    o_pool_p = P.dout("o_pool_p", [15, 1024])
    o_pool_s = P.dout("o_pool_s", [16, 15, 1024])
    o_k_p = P.dout("o_k_p", [128, 256])
    o_k_s = P.dout("o_k_s", [16, 128, 256])
    o_v_p = P.dout("o_v_p", [128, 256])
    o_v_s = P.dout("o_v_s", [16, 128, 256])
    st_ssm = P.din("st_ssm", [16, 1024, 128])
    st_conv = P.din("st_conv", [48, 1536])
    yp = P.dout("yp", [npt_tokens, D])
    ys = P.dout("ys", [NS, D])
    o_av = P.dout("o_av", [NS, D])
    o_ssm_p = P.dout("o_ssm_p", [1024, 128])
    o_ssm_s = P.dout("o_ssm_s", [16, 1024, 128])
    o_conv_p = P.dout("o_conv_p", [3, 1536])
    o_conv_s = P.dout("o_conv_s", [16, 3, 1536])

    with P.stack:
        S = Sched(nc, P.stack)
        P.S = S
        ident = P.sb("ident", [128, 128], F32)
        identb = P.sb("identb", [128, 128], BF16)
        onesb = P.sb("onesb", [128, 128], BF16)
        pv = P.sb("pv", [128, PV_COLS], F32)
        cs = P.sb("cs", [128, CST_COLS], F32)
        xT = P.sb("xT", [128, DC, NTP], F32)
        xn = P.sb("xn", [128, DC, NTP], BF16)
        hTt = P.sb("hT", [128, FC * NTP], BF16)
        hT = hTt[:, :].rearrange("p (j n) -> p j n", j=FC)
        rstd = P.sb("rstd", [128, NTP], F32)
        sil = [P.sb(f"sil{i}", [128, NTP], F32) for i in range(2)]
        wring = [P.sb(f"wr{i}", [128, WSLOT_ELEMS], BF16) for i in range(NWSLOT)]
        wTm = P.sb("wTm", [128, 8, 128], BF16)
        T2 = P.sb("T2", [128, 8, 128], F32)
        wTms = P.sb("wTms", [64, 8, 64], BF16)
        T2s = P.sb("T2s", [128, 8, 64], F32)
        aneg = P.sb("aneg", [16, 1], F32)
        STf = P.sb("STf", [128, 1024], F32)
        STb = P.sb("STb", [128, 1024], BF16)
        ctail = P.sb("ctail", [128, 12, 3], BF16)
        ctail32 = P.sb("ctail32", [128, 12, 3], F32)
        bd64 = P.sb("bd64", [128, 128], BF16)
        hsq = P.sb("hsq", [128, NTP], BF16)
        hrs = P.sb("hrs", [128, NTP], F32)
        esink2 = P.sb("esink2", [128, 8], F32)
        ptail = P.sb("ptail", [128, 8, 15], F32)
        kprev = P.sb("kprev", [128, 4, 128], BF16)
        vprev = P.sb("vprev", [128, 256], BF16)
        arena_t = P.sb("arena", [128, ARENA_WORDS], F32)
        psall = P.ps("psall", [128, 8 * 512], F32)
        A = Arena(arena_t, ARENA_WORDS)
        sq = hT[:, 0:8, :]
        yst = hTt[:, 0:2 * 4 * D].bitcast(F32).rearrange("p (b d) -> p b d", b=4)
        xin = arena_t[:, 0:4 * D].rearrange("p (b d) -> p b d", b=4)

        b_ident = Buf("ident")
        b_pv = Buf("pv")
        b_cs = Buf("cs")
        b_xin = Buf("xin")
        b_xT = [Buf(f"xT{c}") for c in range(DC)]
        b_xn = Buf("xn")
        b_h = [Buf(f"h{j}") for j in range(FC)]
        b_rstd = Buf("rstd")
        b_sil = [Buf("sil0"), Buf("sil1")]
        b_wr = [Buf(f"wr{i}") for i in range(NWSLOT)]
        b_ps = [Buf(f"ps{i}", excl=True) for i in range(8)]
        b_gm = Buf("gmlp_consts")
        b_ST = Buf("ST")
        b_STb = Buf("STb")
        b_ctail = Buf("ctail")
        b_hsq, b_hrs, b_gm2, b_ptail, b_kprev, b_dscr = Buf("hsq"), Buf("hrs"), Buf("gm2"), Buf("ptail"), Buf("kprev"), Buf("dscr")
        d_wr = [DmaSlot(S, f"wr{i}") for i in range(NWSLOT)]
        d_xin = DmaSlot(S, "xin")
        d_yst = DmaSlot(S, "yst")
        d_misc = DmaSlot(S, "misc")
        d_out = [DmaSlot(S, f"out{i}") for i in range(8)]
        d_in = [DmaSlot(S, f"in{i}") for i in range(8)]

        state = {"w": 0, "ps": 0, "sil": 0, "out": 0, "in": 0}

        def bank(i):
            return psall[:, i * 512:(i + 1) * 512]

        reserved = set()

        def next_ps():
            i = state["ps"]
            while i in reserved:
                i = (i + 1) % 8
            state["ps"] = (i + 1) % 8
            return bank(i), b_ps[i]

        def ps_group(n):
            i = (state["ps"] + n - 1) // n * n % 8
            while any((i + k) in reserved for k in range(n)):
                i = (i + n) % 8
            state["ps"] = (i + n) % 8
            state["last_group"] = list(range(i, i + n))
            return psall[:, i * 512:(i + n) * 512], [b_ps[i + k] for k in range(n)]

        def next_out():
            i = state["out"]
            state["out"] = (i + 1) % 8
            return d_out[i]

        def next_in():
            i = state["in"]
            state["in"] = (i + 1) % 8
            return d_in[i]

        wcache = {}
        d_ws = [DmaSlot(S, f"ws{i}") for i in range(NWSLOT)]
        use_wcache = cfg.get("wcache", True) and (len([1 for _ in range(n_ptiles)]) + (1 if do_sample else 0)) > 1

        def wload(src_ap, nelem):
            i = state["w"]
            state["w"] = (i + 1) % NWSLOT
            dst = wring[i][:, 0:nelem]
            key = (src_ap.tensor.name, src_ap.offset)
            ent = wcache.get(key) if use_wcache else None
            if ent is None:
                S.op("pool", lambda e, dst=dst, src=src_ap: e.dma_start(out=dst, in_=src),
                     reads=(), writes=(b_wr[i],), dma=d_wr[i])
                if use_wcache:
                    scr = P.dscratch(f"wb{len(wcache)}", [128, nelem], BF16)
                    bscr = Buf(f"wb{len(wcache)}")
                    wcache[key] = (scr, bscr)
                    S.op("sp", lambda e, dst=dst, scr=scr: e.dma_start(out=scr, in_=dst),
                         reads=(b_wr[i],), writes=(bscr,), dma=d_ws[i])
            else:
                scr, bscr = ent
                S.op("pool", lambda e, dst=dst, scr=scr: e.dma_start(out=dst, in_=scr),
                     reads=(bscr,), writes=(b_wr[i],), dma=d_wr[i])
            return wring[i], b_wr[i]

        def cst_ap(name, nparts=128):
            o, n = CST_OFF[name]
            return cs[0:nparts, o:o + n]

        def pvc(name, c=0, nparts=128):
            o = PV_OFF[name]
            return pv[0:nparts, o + c:o + c + 1]

        S.op("pool", lambda e: e.memset(ident[:], 0.0), writes=(b_ident,))
        S.op("pool", lambda e: e.affine_select(out=ident[:], in_=ident[:], pattern=[[-1, 128]],
                                               compare_op=ALU.not_equal, fill=1.0, base=0,
                                               channel_multiplier=1),
             reads=(b_ident,), writes=(b_ident,))
        S.op("pool", lambda e: e.tensor_copy(identb[:], ident[:]), reads=(b_ident,), writes=(b_ident,))
        S.op("pool", lambda e: e.memset(onesb[:], 1.0), writes=(b_ident,))
        S.op("pool", lambda e: e.memset(STf[:], 0.0), writes=(b_ST,))
        S.op("pool", lambda e: e.memset(STb[:], 0.0), writes=(b_STb,))
        S.op("pool", lambda e: e.memset(ctail[:], 0.0), writes=(b_ctail,))
        S.op("sp", lambda e: e.dma_start(out=pv[:], in_=pvec), writes=(b_pv,), dma=d_misc)
        S.op("sp", lambda e: e.dma_start(out=cs[:], in_=cst), writes=(b_cs,), dma=next_in())

        if on("even"):
            A.reset()
            b_tmp = Buf("setup_tmp")
            w32 = A.take(128, [8, 128])
            bsbc = A.take(128, [8, 128])
            w32s = A.take(64, [8, 64])
            S.op("sp", lambda e: e.dma_start(out=w32, in_=wsT.rearrange("p (h i) -> p h i", h=8)),
                 writes=(b_tmp,), dma=next_in(), arena=True)
            S.op("sp", lambda e: e.dma_start(out=bsbc.rearrange("p h i -> p (h i)"),
                                             in_=bs_row.partition_broadcast(128)),
                 writes=(b_tmp,), dma=next_in(), arena=True)
            S.op("pool", lambda e: e.affine_select(out=w32, in_=w32, pattern=[[0, 8], [1, 128]],
                                                   compare_op=ALU.is_ge, fill=0.0, base=0,
                                                   channel_multiplier=-1),
                 reads=(b_tmp,), writes=(b_tmp,))
            S.op("pool", lambda e: e.tensor_copy(wTm[:], w32), reads=(b_tmp,), writes=(b_gm,))
            pg, bpg = ps_group(2)
            S.op("pe", [lambda e, k=k: e.matmul(pg[:, k * 512:(k + 1) * 512], onesb[:, :],
                                                 wTm[:, k * 4:(k + 1) * 4, :].rearrange("p h i -> p (h i)"),
                                                 start=True, stop=True) for k in range(2)],
                 reads=(b_gm, b_ident), writes=bpg)
            for h in range(8):
                S.op("dve", lambda e, h=h: e.scalar_tensor_tensor(
                    out=T2[:, h, :], in0=pg[:, h * 128:(h + 1) * 128], scalar=pvc("ln_b", h),
                    in1=bsbc[:, h, :], op0=ALU.mult, op1=ALU.add),
                    reads=bpg + [b_tmp, b_pv], writes=(b_gm,))
            SST = cfg.get("setup_stop", 99)
            S.op("pool", lambda e: e.memset(w32s, 0.0), writes=(b_tmp,))
            for b in range(16 if SST > 1 else 0):
                S.op("sp", lambda e, b=b: e.dma_start(out=w32s[4 * b:4 * b + 4, :, 4 * b:4 * b + 4],
                                                      in_=wsT4.rearrange("p (h i) -> p h i", h=8)),
                     writes=(b_tmp,), dma=next_in(), arena=True)
            S.op("pool", lambda e: e.affine_select(out=w32s, in_=w32s, pattern=[[0, 8], [1, 64]],
                                                   compare_op=ALU.is_ge, fill=0.0, base=0,
                                                   channel_multiplier=-1),
                 reads=(b_tmp,), writes=(b_tmp,))
            S.op("pool", lambda e: e.tensor_copy(wTms[:], w32s), reads=(b_tmp,), writes=(b_gm,))
            pg2, bpg2 = next_ps()
            S.op("pe", lambda e: e.matmul(pg2[:, 0:512], onesb[0:64, :],
                                          wTms[:, :, :].rearrange("p h i -> p (h i)"), start=True, stop=True),
                 reads=(b_gm, b_ident), writes=(bpg2,))
            for h in range(8 if SST > 2 else 0):
                S.op("dve", lambda e, h=h: e.scalar_tensor_tensor(
                    out=T2s[:, h, :].rearrange("p (b i) -> p b i", i=4),
                    in0=pg2[:, h * 64:(h + 1) * 64].rearrange("p (b i) -> p b i", i=4),
                    scalar=pvc("ln_b", h),
                    in1=bsbc[:, h, 0:4].unsqueeze(1).broadcast_to([128, 16, 4]),
                    op0=ALU.mult, op1=ALU.add),
                    reads=(bpg2, b_tmp, b_pv), writes=(b_gm,))
            S.op("act", lambda e: e.activation(out=aneg[:, :], in_=pvc("a_log", 0, 16), func=AF.Exp),
                 reads=(b_pv,), writes=(b_gm,))
            S.op("dve", lambda e: e.tensor_scalar(out=aneg[:, :], in0=aneg[:, :], scalar1=-1.0, scalar2=None,
                                                  op0=ALU.mult), reads=(b_gm,), writes=(b_gm,))
            S.barrier()

        if on("odd"):
            A.reset()
            b_tmp2 = Buf("setup_tmp2")
            S.op("pool", lambda e: e.memset(bd64[:], 0.0), writes=(b_ident,))
            S.op("pool", lambda e: e.memset(bd64[0:64, 0:64], 1.0), writes=(b_ident,))
            S.op("pool", lambda e: e.memset(bd64[64:128, 64:128], 1.0), writes=(b_ident,))
            S.op("pool", lambda e: e.memset(ptail[:], 0.0), writes=(b_ptail,))
            S.op("pool", lambda e: e.memset(kprev[:], 0.0), writes=(b_kprev,))
            S.op("pool", lambda e: e.memset(vprev[:], 0.0), writes=(b_kprev,))
            es = A.take(128, [16])
            rt = A.take(32, [16])
            dv = A.take(16, [384])
            S.op("sp", lambda e: e.dma_start(out=es, in_=sink_row.partition_broadcast(128)), writes=(b_tmp2,), dma=next_in(), arena=True)
            S.op("sp", lambda e: e.dma_start(out=rt, in_=rel_tab), writes=(b_tmp2,), dma=next_in(), arena=True)
            S.op("act", lambda e: e.activation(out=es, in_=es, func=AF.Exp), reads=(b_tmp2,), writes=(b_tmp2,))
            esv = es.rearrange("p (c t) -> p c t", t=2)
            S.op("dve", lambda e: e.tensor_copy(esink2[0:64, :], esv[0:64, :, 0]), reads=(b_tmp2,), writes=(b_gm2,))
            S.op("dve", lambda e: e.tensor_copy(esink2[64:128, :], esv[64:128, :, 1]), reads=(b_tmp2,), writes=(b_gm2,))
            pgd, bpgd = next_ps()
            S.op("pe", lambda e: e.matmul(pgd[0:16, 0:384], rt[:, :], cst_ap("onehot", 32), start=True, stop=True),
                 reads=(b_tmp2, b_cs), writes=(bpgd,))
            S.op("dve", lambda e: e.tensor_tensor(out=dv, in0=pgd[0:16, 0:384], in1=cst_ap("negmask", 16), op=ALU.add),
                 reads=(bpgd, b_cs), writes=(b_tmp2,))
            S.op("sp", lambda e: e.dma_start(out=dscr, in_=dv.unsqueeze(1).broadcast_to([16, 128, 384])),
                 reads=(b_tmp2,), writes=(b_dscr,), dma=next_in(), arena=True)
            S.barrier()

        def load_x_tile(src_rows, nblk, rows):
            S.op("sp", lambda e: e.dma_start(out=xin[0:rows, 0:nblk, :],
                                             in_=src_rows.rearrange("(b p) d -> p b d", p=rows)),
                 writes=(b_xin,), dma=d_xin, arena=True)

        def transpose_in(nblk, rows):
            nt = nblk * rows
            for c in range(DC):
                pt, bp = next_ps()
                fns = []
                for blk in range(nblk):
                    fns.append(lambda e, pt=pt, blk=blk, c=c: e.transpose(
                        pt[:, blk * rows:(blk + 1) * rows], xin[0:rows, blk, c * 128:(c + 1) * 128],
                        ident[0:rows, 0:rows]))
                S.op("pe", fns, reads=(b_xin, b_ident), writes=(bp,))
                if c % 2:
                    S.op("act", lambda e, pt=pt, c=c: e.copy(xT[:, c, 0:nt], pt[:, 0:nt]),
                         reads=(bp,), writes=(b_xT[c],))
                else:
                    S.op("dve", lambda e, pt=pt, c=c: e.tensor_copy(xT[:, c, 0:nt], pt[:, 0:nt]),
                         reads=(bp,), writes=(b_xT[c],))

        def transpose_out(dst_rows, nblk, rows):
            for blk in range(nblk):
                for half in range(2):
                    pt, bp = next_ps()
                    fns = []
                    for cc in range(4):
                        c = half * 4 + cc
                        fns.append(lambda e, pt=pt, blk=blk, c=c, cc=cc: e.transpose(
                            pt[0:rows, cc * 128:(cc + 1) * 128], xT[:, c, blk * rows:(blk + 1) * rows],
                            ident[:, :]))
                    S.op("pe", fns, reads=[b_xT[half * 4 + cc] for cc in range(4)] + [b_ident],
                         writes=(bp,))
                    if half:
                        S.op("act", lambda e, pt=pt, blk=blk: e.copy(yst[0:rows, blk, 512:1024], pt[0:rows, :]),
                             reads=(bp,), writes=b_h[0:16])
                    else:
                        S.op("dve", lambda e, pt=pt, blk=blk: e.tensor_copy(yst[0:rows, blk, 0:512], pt[0:rows, :]),
                             reads=(bp,), writes=b_h[0:16])
            S.op("sp", lambda e: e.dma_start(out=dst_rows.rearrange("(b p) d -> p b d", p=rows),
                                             in_=yst[0:rows, 0:nblk, :]),
                 reads=b_h[0:16], dma=d_yst, final=True)

        def rms_norm(gname, nt):
            S.op("act", lambda e: e.activation(out=sq[:, :, 0:nt], in_=xT[:, :, 0:nt], func=AF.Square),
                 reads=b_xT, writes=b_h[0:8])
            pt, bp = next_ps()
            S.op("pe", [lambda e, pt=pt, c=c: e.matmul(pt[:, 0:nt], onesb[:, :], sq[:, c, 0:nt],
                                                        start=(c == 0), stop=(c == DC - 1))
                        for c in range(DC)], reads=b_h[0:8] + [b_ident], writes=(bp,))
            S.op("act", lambda e, pt=pt: e.activation(out=rstd[:, 0:nt], in_=pt[:, 0:nt], func=AF.Sqrt,
                                                       bias=EPS, scale=1.0 / D),
                 reads=(bp,), writes=(b_rstd,))
            S.op("dve", lambda e: e.reciprocal(rstd[:, 0:nt], rstd[:, 0:nt]), reads=(b_rstd,), writes=(b_rstd,))
            for c in range(DC):
                S.op("dve", lambda e, c=c: e.scalar_tensor_tensor(
                    out=xn[:, c, 0:nt], in0=xT[:, c, 0:nt], scalar=pvc(gname, c),
                    in1=rstd[:, 0:nt], op0=ALU.mult, op1=ALU.mult),
                    reads=(b_xT[c], b_rstd, b_pv), writes=(b_xn,))

        def ffn(layer, which, nt):
            rms_norm(f"ffn_norm{layer}{which}", nt)
            for grp in range(11):
                w, bw = wload(wgu[layer, which, grp], 8 * 2 * 256)
                wv = w[:, :].rearrange("p (k g n) -> p k g n", k=8, g=2)
                for jj in range(2):
                    j = grp * 2 + jj
                    pg, bpg = next_ps()
                    S.op("pe", [lambda e, pg=pg, k=k, jj=jj, wv=wv: e.matmul(
                        pg[:, 0:nt], wv[:, k, 0, jj * 128:(jj + 1) * 128], xn[:, k, 0:nt],
                        start=(k == 0), stop=(k == 7)) for k in range(8)],
                        reads=(bw, b_xn), writes=(bpg,))
                    pu, bpu = next_ps()
                    S.op("pe", [lambda e, pu=pu, k=k, jj=jj, wv=wv: e.matmul(
                        pu[:, 0:nt], wv[:, k, 1, jj * 128:(jj + 1) * 128], xn[:, k, 0:nt],
                        start=(k == 0), stop=(k == 7)) for k in range(8)],
                        reads=(bw, b_xn), writes=(bpu,))
                    si = state["sil"]
                    state["sil"] = 1 - si
                    S.op("act", lambda e, pg=pg, si=si: e.activation(out=sil[si][:, 0:nt], in_=pg[:, 0:nt],
                                                                      func=AF.Silu),
                         reads=(bpg,), writes=(b_sil[si],))
                    S.op("dve", lambda e, pu=pu, si=si, j=j: e.tensor_tensor(
                        out=hT[:, j, 0:nt], in0=pu[:, 0:nt], in1=sil[si][:, 0:nt], op=ALU.mult),
                        reads=(bpu, b_sil[si]), writes=(b_h[j],))
            for m in range(DC):
                w, bw = wload(wdn[layer, which, m], FC * 128)
                wv = w[:, 0:FC * 128].rearrange("p (j n) -> p j n", j=FC)
                py, bpy = next_ps()
                S.op("pe", [lambda e, py=py, j=j, wv=wv: e.matmul(
                    py[:, 0:nt], wv[:, j, :], hT[:, j, 0:nt], start=(j == 0), stop=(j == FC - 1))
                    for j in range(FC)], reads=[bw] + b_h, writes=(bpy,))
                S.op("dve", lambda e, py=py, m=m: e.scalar_tensor_tensor(
                    out=xT[:, m, 0:nt], in0=py[:, 0:nt], scalar=0.5, in1=xT[:, m, 0:nt],
                    op0=ALU.mult, op1=ALU.add), reads=(bpy, b_xT[m]), writes=(b_xT[m],))

        def gelu_evac(pt, np_, nf, out_ap, bp, wbufs):
            if NATIVE_GELU:
                S.op("act", lambda e: e.activation(out=out_ap, in_=pt, func=AF.Gelu_apprx_tanh), reads=(bp,), writes=wbufs)
                return
            si = state["sil"]
            state["sil"] = 1 - si
            t1 = sil[si][0:np_, 0:nf]
            S.op("act", lambda e: e.activation(out=t1, in_=pt, func=AF.Square), reads=(bp,), writes=(b_sil[si],))
            S.op("dve", lambda e: e.tensor_scalar(out=t1, in0=t1, scalar1=0.044715, scalar2=1.0, op0=ALU.mult, op1=ALU.add),
                 reads=(b_sil[si],), writes=(b_sil[si],))
            S.op("dve", lambda e: e.tensor_tensor(out=t1, in0=t1, in1=pt, op=ALU.mult), reads=(b_sil[si], bp), writes=(b_sil[si],))
            S.op("act", lambda e: e.activation(out=t1, in_=t1, func=AF.Sigmoid, scale=1.5957691216057308),
                 reads=(b_sil[si],), writes=(b_sil[si],))
            S.op("dve", lambda e: e.tensor_tensor(out=out_ap, in0=t1, in1=pt, op=ALU.mult), reads=(b_sil[si], bp), writes=wbufs)

        def proj_fm(wsrc, ncc, nt, evac):
            w, bw = wload(wsrc, 8 * 512)
            wv = w[:, :].rearrange("p (k n) -> p k n", k=8)
            for cc in range(ncc):
                pt, bp = next_ps()
                S.op("pe", [lambda e, pt=pt, k=k, cc=cc, wv=wv: e.matmul(
                    pt[:, 0:nt], wv[:, k, cc * 128:(cc + 1) * 128], xn[:, k, 0:nt],
                    start=(k == 0), stop=(k == 7)) for k in range(8)], reads=(bw, b_xn), writes=(bp,))
                evac(cc, pt, bp)

        def even_mixer(kind, nt, is_last_ptile):
            STOP = cfg.get("even_stop", 99)
            if STOP <= 0:
                return
            S.barrier()
            A.reset()
            smp = (kind == "s")
            CL = 64 if smp else 128
            nblk = nt // CL
            Lc = 4 if smp else 128
            nch = nt // Lc
            uT = hT[:, 0:8, :]
            yaT = hT[:, 8:16, :]
            bcT = hT[:, 16:20, :]
            b_u, b_ya, b_bc = b_h[0:8], b_h[8:16], b_h[16:20]
            zT = A.take(128, [8, nt], BF16)
            ybT = A.take(128, [8, nt], BF16)
            xcT = A.take(128, [8, nt], BF16)
            ext = A.take(128, [12, (16 * 7) if smp else (NTP + 3)], BF16)
            cvA = A.take(128, [nt])
            cvB = A.take(128, [nt])
            vg = [A.take(128, [1024]) for _ in range(1 if smp else 2)] * (2 if smp else 1)
            vhb = [A.take(128, [1024], BF16) for _ in range(1 if smp else 2)] * (2 if smp else 1)
            mvst = A.take(128, [2, 6])
            mv = A.take(128, [2])
            gt = None
            dsc = A.take(16, [4, nt])
            tsc = A.take(128, [64])
            absx = A.take(128, [16, CL])
            Eb = A.take(128, [16, CL], BF16)
            Csb = A.take(128, [16, CL], BF16)
            cbm = A.take(128, [2, CL])
            xd = A.take(128, [16, 64], BF16)
            xdd = A.take(128, [16, 64], BF16)
            Btok = A.take(128, [2, 128], BF16)
            ygb = A.take(128, [8, 128])
            sqg = A.take(128, [8, 128], BF16)
            rsg = A.take(128, [2, 128])
            bz, byb, bxc, bext, bcv = Buf("zT"), Buf("ybT"), Buf("xcT"), Buf("ext"), [Buf("cvA"), Buf("cvB")]
            bvg, bvh, bmv, bgt = [Buf("vg0"), Buf("vg1")], [Buf("vh0"), Buf("vh1")], Buf("mv"), [Buf("gt0"), Buf("gt1")]
            bdsc, btsc, babs, bEb, bCs, bcbm = Buf("dsc"), Buf("tsc"), Buf("absx"), Buf("Eb"), Buf("Csb"), Buf("cbm")
            bxd, bxdd, bBt, bygb, bsqg, brsg = Buf("xd"), Buf("xdd"), Buf("Btok"), Buf("ygb"), Buf("sqg"), Buf("rsg")
            if smp:
                xpre = A.take(128, [12, 64])
                cso = A.take(64, [1536])
                stc = cso[0:48, :]
                lnbc = A.take(128, [2048])
                vln = A.take(64, [1024])
                bmsk = cst_ap("bmask", 64)
                Bblk = A.take(64, [2, 16, 128], BF16)
                cdT = A.take(128, [8, 16])
                eal = A.take(16, [16])
                Snat = A.take(128, [2, 8, 128])
                STs = A.take(128, [2, 1024], BF16)
                bxpre, bcso, blnbc, bvln = Buf("xpre"), Buf("cso"), Buf("lnbc"), Buf("vln")
                bstc = bcso
                bBblk, bcdT, beal, bSnat, bSTs = Buf("Bblk"), Buf("cdT"), Buf("eal"), Buf("Snat"), Buf("STs")
                S.op("sp", lambda e: e.dma_start(out=lnbc, in_=lnrow.partition_broadcast(128)),
                     writes=(blnbc,), dma=next_in(), arena=True)
                S.op("sp", lambda e: e.dma_start(out=stc, in_=st_conv), writes=(bstc,), dma=next_in(), arena=True)

            rms_norm(f"mix_norm{0}", nt)
            if STOP <= 0.5:
                return

            for gi in range(2):
                def ev_u(cc, pt, bp, gi=gi):
                    c = gi * 4 + cc
                    gelu_evac(pt[:, 0:nt], 128, nt, uT[:, c, 0:nt], bp, (b_u[c],))
                proj_fm(wev_fm[gi], 4, nt, ev_u)

            if STOP <= 1:
                return
            wv0, bwv0 = wload(wev_v[0], 8 * 512)
            wv1, bwv1 = wload(wev_v[1], 8 * 512)
            wvv = [wv0[:, :].rearrange("p (k n) -> p k n", k=8), wv1[:, :].rearrange("p (k n) -> p k n", k=8)]
            bwv = [bwv0, bwv1]
            wg = wTms if smp else wTm
            T2x = T2s if smp else T2
            for blk in range(nblk):
                cols = slice(blk * CL, (blk + 1) * CL)
                bi = blk % 2
                for half in range(2):
                    pt, bp = next_ps()
                    S.op("pe", [lambda e, pt=pt, k=k, half=half: e.matmul(
                        pt[0:CL, :], xn[:, k, cols], wvv[half][:, k, :], start=(k == 0), stop=(k == 7))
                        for k in range(8)], reads=(bwv[half], b_xn), writes=(bp,))
                    gelu_evac(pt[0:CL, :], CL, 512, vg[bi][0:CL, half * 512:(half + 1) * 512], bp, (bvg[bi],))
                S.op("dve", [lambda e, bi=bi: e.bn_stats(mvst[0:CL, 0, :], vg[bi][0:CL, 0:512]),
                             lambda e, bi=bi: e.bn_stats(mvst[0:CL, 1, :], vg[bi][0:CL, 512:1024])],
                     reads=(bvg[bi],), writes=(bmv,))
                S.op("dve", lambda e: e.bn_aggr(mv[0:CL, :], mvst[0:CL, :, :].rearrange("p a b -> p (a b)")),
                     reads=(bmv,), writes=(bmv,))
                S.op("act", lambda e: e.activation(out=mv[0:CL, 1:2], in_=mv[0:CL, 1:2], func=AF.Sqrt, bias=EPS, scale=1.0),
                     reads=(bmv,), writes=(bmv,))
                S.op("dve", lambda e: e.reciprocal(mv[0:CL, 1:2], mv[0:CL, 1:2]), reads=(bmv,), writes=(bmv,))
                S.op("dve", lambda e, bi=bi: e.tensor_scalar(
                    out=vhb[bi][0:CL, :], in0=vg[bi][0:CL, :], scalar1=mv[0:CL, 0:1], scalar2=mv[0:CL, 1:2],
                    op0=ALU.subtract, op1=ALU.mult), reads=(bmv, bvg[bi]), writes=(bvh[bi],))
                if smp:
                    S.op("dve", lambda e, bi=bi: e.tensor_scalar(
                        out=vln[:, :], in0=vg[bi][0:CL, :], scalar1=mv[0:CL, 0:1], scalar2=mv[0:CL, 1:2],
                        op0=ALU.subtract, op1=ALU.mult), reads=(bmv, bvg[bi]), writes=(bvln,))
                    S.op("dve", lambda e: e.tensor_tensor(out=vln[:, :], in0=vln[:, :], in1=lnbc[0:64, 0:1024], op=ALU.mult),
                         reads=(bvln, blnbc), writes=(bvln,))
                    S.op("dve", lambda e: e.tensor_tensor(out=vln[:, :], in0=vln[:, :], in1=lnbc[0:64, 1024:2048], op=ALU.add),
                         reads=(bvln, blnbc), writes=(bvln,))
                    S.op("sp", lambda e: e.dma_start(out=o_av, in_=vln[:, :]), reads=(bvln,), dma=next_out(),
                         final=True, arena=True)
                for hg in range(2):
                    pt, bp = next_ps()
                    S.op("pe", [lambda e, pt=pt, hh=hh, hg=hg, bi=bi: e.matmul(
                        pt[:, hh * CL:(hh + 1) * CL], vhb[bi][0:CL, (hg * 4 + hh) * 128:(hg * 4 + hh + 1) * 128],
                        wg[0:CL, hg * 4 + hh, 0:CL], start=True, stop=True) for hh in range(4)],
                        reads=(bvh[bi], b_gm), writes=(bp,))
                    hs = slice(hg * 4, hg * 4 + 4)
                    gtb = ygb[:, 0:4, 0:CL]
                    o_lng = PV_OFF["ln_g"]
                    S.op("dve", lambda e, pt=pt, hg=hg: e.tensor_tensor(
                        out=gtb, in0=pt[:, 0:4 * CL].rearrange("p (h i) -> p h i", h=4),
                        in1=pv[:, o_lng + hg * 4:o_lng + hg * 4 + 4].unsqueeze(2).broadcast_to([128, 4, CL]), op=ALU.mult),
                        reads=(bp, b_pv), writes=(bygb,))
                    S.op("dve", lambda e, hs=hs: e.tensor_tensor(out=gtb, in0=gtb, in1=T2x[:, hs, 0:CL], op=ALU.add),
                         reads=(bygb, b_gm), writes=(bygb,))
                    S.op("dve", lambda e, hs=hs: e.tensor_tensor(out=yaT[:, hs, cols], in0=gtb, in1=uT[:, hs, cols], op=ALU.mult),
                         reads=[bygb] + b_u[hg * 4:hg * 4 + 4], writes=b_ya[hg * 4:hg * 4 + 4])

            if STOP <= 2:
                return
            for gi in range(2):
                def ev_z(cc, pt, bp, gi=gi):
                    c = gi * 4 + cc
                    if cfg.get("zmode", 0) == 0:
                        S.op("act", lambda e: e.activation(out=zT[:, c, 0:nt], in_=pt[:, 0:nt], func=AF.Silu),
                             reads=(bp,), writes=(bz,))
                    elif cfg.get("zmode", 0) == 1:
                        S.op("act", lambda e: e.activation(out=ybT[:, c, 0:nt], in_=pt[:, 0:nt], func=AF.Silu),
                             reads=(bp,), writes=(bz,))
                proj_fm(wev_fm[2 + gi], 4, nt, ev_z)
            if STOP <= 2.2:
                return
            if smp:
                extv = ext.rearrange("p c (b r) -> p c b r", r=7)
                pg, bpg = ps_group(2)
                S.op("pe", [lambda e, c=c: e.transpose(pg[:, c * 64:c * 64 + 48], stc[0:48, c * 128:(c + 1) * 128],
                                                        ident[0:48, 0:48]) for c in range(12)],
                     reads=(bstc, b_ident), writes=bpg)
                S.op("dve", lambda e: e.tensor_copy(
                    extv[:, :, :, 0:3],
                    pg[:, 0:768].rearrange("p (c x) -> p c x", c=12)[:, :, 0:48].rearrange("p c (b r) -> p c b r", r=3)),
                     reads=bpg, writes=(bext,))
            else:
                S.op("dve", lambda e: e.tensor_copy(ext[:, :, 0:3], ctail[:, :, :]), reads=(b_ctail,), writes=(bext,))
            XM = cfg.get("xmode", 9)
            for gi in range(3 if XM >= 1 else 0):
                def ev_x(cc, pt, bp, gi=gi):
                    c = gi * 4 + cc
                    if XM < 2:
                        return
                    if smp:
                        S.op("act", lambda e: e.copy(extv[:, c, :, 3:7], pt[:, 0:64].rearrange("p (b i) -> p b i", i=4)),
                             reads=(bp,), writes=(bext,))
                        S.op("dve", lambda e: e.tensor_copy(xpre[:, c, :], pt[:, 0:64]), reads=(bp,), writes=(bxpre,))
                    else:
                        S.op("act", lambda e: e.copy(ext[:, c, 3:3 + nt], pt[:, 0:nt]), reads=(bp,), writes=(bext,))
                        if is_last_ptile and XM >= 3:
                            S.op("dve", lambda e: e.tensor_copy(ctail32[:, c, :], pt[:, nt - 3:nt]),
                                 reads=(bp,), writes=(b_ctail,))
                proj_fm(wev_fm[4 + gi], 4, nt, ev_x)
            if STOP <= 2.4:
                return
            wd, bwd = wload(wev_dt, 8 * 16)
            wdv = wd[:, 0:128].rearrange("p (k n) -> p k n", k=8)
            pt, bp = next_ps()
            S.op("pe", [lambda e, k=k: e.matmul(pt[0:16, 0:nt], wdv[:, k, :], xn[:, k, 0:nt],
                                                 start=(k == 0), stop=(k == 7)) for k in range(8)],
                 reads=(bwd, b_xn), writes=(bp,))
            S.op("act", lambda e: e.activation(out=dsc[:, 0, 0:nt], in_=pt[0:16, 0:nt], func=AF.Exp,
                                               bias=pvc("dt_bias", 0, 16), scale=1.0),
                 reads=(bp, b_pv), writes=(bdsc,))
            if STOP <= 2.6:
                return
            S.op("act", lambda e: e.activation(out=dsc[:, 0, 0:nt], in_=dsc[:, 0, 0:nt], func=AF.Ln, bias=1.0, scale=1.0),
                 reads=(bdsc,), writes=(bdsc,))
            S.op("dve", lambda e: e.tensor_scalar(out=dsc[:, 1, 0:nt], in0=dsc[:, 0, 0:nt], scalar1=aneg[:, 0:1],
                                                  scalar2=None, op0=ALU.mult), reads=(bdsc, b_gm), writes=(bdsc,))
            if STOP <= 2.8:
                return
            rmask = cst_ap("rmask_s", 16)[:, 0:nt] if smp else cst_ap("rmask_p", 16)[:, 0:nt]
            S.op("dve", lambda e: e.tensor_tensor_scan(out=dsc[:, 2, 0:nt], data0=rmask, data1=dsc[:, 1, 0:nt],
                                                       initial=0.0, op0=ALU.mult, op1=ALU.add),
                 reads=(bdsc, b_cs), writes=(bdsc,))
            S.op("dve", lambda e: e.tensor_copy(
                dsc[:, 3, 0:nt].rearrange("p (c l) -> p c l", l=Lc),
                dsc[:, 2, 0:nt].rearrange("p (c l) -> p c l", l=Lc)[:, :, Lc - 1:Lc].broadcast_to([16, nch, Lc])),
                reads=(bdsc,), writes=(bdsc,))

            if STOP <= 3:
                return
            for c in range(12):
                if smp:
                    src = lambda tap, c=c: extv[:, c, :, tap:tap + 4]
                    v3 = lambda ap: ap[:, 0:64].rearrange("p (b i) -> p b i", i=4)
                else:
                    src = lambda tap, c=c: ext[:, c, tap:tap + nt]
                    v3 = lambda ap: ap[:, 0:nt]
                S.op("dve", lambda e, c=c, src=src, v3=v3: e.tensor_scalar(
                    out=v3(cvA), in0=src(0), scalar1=pvc("conv_w", c * 4 + 0), scalar2=pvc("conv_b", c),
                    op0=ALU.mult, op1=ALU.add), reads=(bext, b_pv), writes=(bcv[0],))
                S.op("dve", lambda e, c=c, src=src, v3=v3: e.scalar_tensor_tensor(
                    out=v3(cvB), in0=src(1), scalar=pvc("conv_w", c * 4 + 1), in1=v3(cvA),
                    op0=ALU.mult, op1=ALU.add), reads=(bext, b_pv, bcv[0]), writes=(bcv[1],))
                S.op("dve", lambda e, c=c, src=src, v3=v3: e.scalar_tensor_tensor(
                    out=v3(cvA), in0=src(2), scalar=pvc("conv_w", c * 4 + 2), in1=v3(cvB),
                    op0=ALU.mult, op1=ALU.add), reads=(bext, b_pv, bcv[1]), writes=(bcv[0],))
                S.op("dve", lambda e, c=c, src=src, v3=v3: e.scalar_tensor_tensor(
                    out=v3(cvB), in0=src(3), scalar=pvc("conv_w", c * 4 + 3), in1=v3(cvA),
                    op0=ALU.mult, op1=ALU.add), reads=(bext, b_pv, bcv[0]), writes=(bcv[1],))
                if c < 8:
                    S.op("act", lambda e, c=c: e.activation(out=xcT[:, c, 0:nt], in_=cvB[:, 0:nt], func=AF.Silu),
                         reads=(bcv[1],), writes=(bxc,))
                else:
                    S.op("act", lambda e, c=c: e.activation(out=bcT[:, c - 8, 0:nt], in_=cvB[:, 0:nt], func=AF.Silu),
                         reads=(bcv[1],), writes=b_bc)
            if not smp:
                S.op("dve", lambda e: e.tensor_copy(ctail[:, :, :], ext[:, :, nt:nt + 3]), reads=(bext,), writes=(b_ctail,))
                if is_last_ptile:
                    for r in range(3):
                        S.op("sp", lambda e, r=r: e.dma_start(
                            out=o_conv_p[r:r + 1, :].rearrange("r (c p) -> p (r c)", p=128),
                            in_=ctail32[:, :, r], allow_slow_non_contiguous=True),
                            reads=(b_ctail,), dma=next_out(), final=True)
            else:
                pg, bpg = ps_group(4)
                S.op("pe", [lambda e, c=c: e.transpose(pg[0:64, c * 128:(c + 1) * 128], xpre[:, c, :], ident[:, :])
                            for c in range(12)], reads=(bxpre, b_ident), writes=bpg)
                S.op("act", lambda e: e.copy(cso[:, :], pg[0:64, 0:1536]), reads=bpg, writes=(bcso,))
                for r in range(3):
                    S.op("sp", lambda e, r=r: e.dma_start(out=o_conv_s[:, r, :], in_=cso[1 + r:64:4, :]),
                         reads=(bcso,), dma=next_out(), final=True, arena=True)

            if STOP <= 4:
                return
            if smp:
                S.op("act", lambda e: e.activation(out=eal[:, :].unsqueeze(2), in_=dsc[:, 2, 0:64].rearrange("p (b i) -> p b i", i=4)[:, :, 3:4],
                                                   func=AF.Exp), reads=(bdsc,), writes=(beal,))
                pt, bp = next_ps()
                S.op("pe", [lambda e, h=h: e.matmul(
                    pt[:, h * 16:(h + 1) * 16], ident[0:16, h:h + 1].broadcast_to([16, 128]),
                    eal[:, :], start=True, stop=True) for h in range(16)], reads=(beal, b_ident), writes=(bp,))
                ptv = pt[:, 0:256].rearrange("p (c t b) -> p c t b", c=8, t=2)
                S.op("dve", lambda e: e.tensor_copy(cdT[0:64, :, :], ptv[0:64, :, 0, :]), reads=(bp,), writes=(bcdT,))
                S.op("dve", lambda e: e.tensor_copy(cdT[64:128, :, :], ptv[64:128, :, 1, :]), reads=(bp,), writes=(bcdT,))
            maskT = cst_ap("maskT_blk", 64) if smp else cst_ap("maskT_causal", 128)
            for blk in range(cfg.get("nssd", nblk) if not smp else nblk):
                cols = slice(blk * CL, (blk + 1) * CL)
                pt, bp = next_ps()
                S.op("pe", [lambda e, q=q: e.transpose(pt[0:CL, q * 16:(q + 1) * 16], dsc[:, (0, 2, 3)[q], cols],
                                                        ident[0:16, 0:16]) for q in range(3)],
                     reads=(bdsc, b_ident), writes=(bp,))
                S.op("dve", lambda e, pt=pt: e.tensor_copy(tsc[0:CL, 0:48], pt[0:CL, 0:48]), reads=(bp,), writes=(btsc,))
                S.op("dve", lambda e: e.tensor_tensor(out=tsc[0:CL, 48:64], in0=tsc[0:CL, 32:48], in1=tsc[0:CL, 16:32],
                                                      op=ALU.subtract), reads=(btsc,), writes=(btsc,))
                S.op("act", lambda e: e.activation(out=tsc[0:CL, 48:64], in_=tsc[0:CL, 48:64], func=AF.Exp),
                     reads=(btsc,), writes=(btsc,))
                if cfg.get('sstage', 99) <= 1:
                    continue
                pa, bpa = ps_group(4)
                S.op("pe", [lambda e, h=h: e.matmul(pa[:, h * CL:(h + 1) * CL],
                                                     ident[0:16, h:h + 1].broadcast_to([16, 128]),
                                                     dsc[:, 2, cols], start=True, stop=True) for h in range(16)],
                     reads=(bdsc, b_ident), writes=bpa)
                pav = pa[:, 0:16 * CL].rearrange("p (h i) -> p h i", h=16)
                S.op("dve", lambda e: e.tensor_tensor(
                    out=absx[0:CL, :, 0:CL], in0=pav[0:CL, :, :],
                    in1=tsc[0:CL, 16:32].unsqueeze(2).broadcast_to([CL, 16, CL]), op=ALU.subtract),
                    reads=bpa + [btsc], writes=(babs,))
                S.op("dve", lambda e: e.tensor_scalar(out=absx[0:CL, :, 0:CL], in0=absx[0:CL, :, 0:CL], scalar1=0.0, scalar2=None,
                                                      op0=ALU.min), reads=(babs,), writes=(babs,))
                S.op("act", lambda e: e.activation(out=Eb[0:CL, :, 0:CL], in_=absx[0:CL, :, 0:CL], func=AF.Exp),
                     reads=(babs,), writes=(bEb,))
                S.op("act", lambda e: e.activation(out=absx[:, :, 0:CL], in_=pav, func=AF.Exp),
                     reads=bpa, writes=(babs,))
                if cfg.get('sstage', 99) <= 2:
                    continue
                pc, bpc = next_ps()
                S.op("pe", [lambda e, g=g: e.matmul(pc[0:CL, g * CL:(g + 1) * CL], bcT[:, g, cols], bcT[:, 2 + g, cols],
                                                     start=True, stop=True) for g in range(2)],
                     reads=b_bc, writes=(bpc,))
                if cfg.get("cbx", 9) >= 1:
                  S.op("dve", lambda e, pc=pc: e.tensor_tensor(
                    out=cbm[0:CL, :, 0:CL], in0=pc[0:CL, 0:2 * CL].rearrange("p (g i) -> p g i", g=2),
                    in1=maskT[:, 0:CL].unsqueeze(1).broadcast_to([CL, 2, CL]), op=ALU.mult),
                    reads=(bpc, b_cs), writes=(bcbm,))
                for g in range(2 if cfg.get("cbx", 9) >= 2 else 0):
                    S.op("dve", lambda e, g=g: e.tensor_tensor(
                        out=Eb[0:CL, g * 8:(g + 1) * 8, 0:CL], in0=Eb[0:CL, g * 8:(g + 1) * 8, 0:CL],
                        in1=cbm[0:CL, g:g + 1, 0:CL].broadcast_to([CL, 8, CL]), op=ALU.mult),
                        reads=(bEb, bcbm), writes=(bEb,))
                    S.op("dve", lambda e, g=g: e.tensor_tensor(
                        out=Csb[:, g * 8:(g + 1) * 8, 0:CL], in0=absx[:, g * 8:(g + 1) * 8, 0:CL],
                        in1=bcT[:, 2 + g, cols].unsqueeze(1).broadcast_to([128, 8, CL]), op=ALU.mult),
                        reads=[babs] + b_bc, writes=(bCs,))
                if cfg.get('sstage', 99) <= 3:
                    continue
                px, bpx = next_ps()
                pxb = px.bitcast(BF16)
                S.op("pe", [lambda e, c=c: e.transpose(pxb[0:CL, c * 128:(c + 1) * 128], xcT[:, c, cols], identb[:, :])
                            for c in range(8)], reads=(bxc, b_ident), writes=(bpx,))
                S.op("dve", lambda e, pxb=pxb: e.tensor_tensor(
                    out=xd[0:CL, :, :], in0=pxb[0:CL, 0:1024].rearrange("p (h q) -> p h q", h=16),
                    in1=tsc[0:CL, 0:16].unsqueeze(2).broadcast_to([CL, 16, 64]), op=ALU.mult),
                    reads=(bpx, btsc), writes=(bxd,))
                S.op("dve", lambda e: e.tensor_tensor(
                    out=xdd[0:CL, :, :], in0=xd[0:CL, :, :],
                    in1=tsc[0:CL, 48:64].unsqueeze(2).broadcast_to([CL, 16, 64]), op=ALU.mult),
                    reads=(bxd, btsc), writes=(bxdd,))
                pb_, bpb = next_ps()
                pbb = pb_.bitcast(BF16)
                S.op("pe", [lambda e, g=g: e.transpose(pbb[0:CL, g * 128:(g + 1) * 128], bcT[:, g, cols], identb[:, :])
                            for g in range(2)], reads=b_bc + [b_ident], writes=(bpb,))
                S.op("act", lambda e, pbb=pbb: e.copy(Btok[0:CL, :, :], pbb[0:CL, 0:256].rearrange("p (g n) -> p g n", g=2)),
                     reads=(bpb,), writes=(bBt,))
                if cfg.get('sstage', 99) <= 4:
                    continue
                py, bpy = ps_group(2)
                py_banks = list(state["last_group"])
                fns = []
                for h in range(16):
                    dst = py[64 * (h % 2):64 * (h % 2) + 64, (h // 2) * CL:(h // 2 + 1) * CL]
                    first = (h < 2) or (not smp and ((h // 2) % 4 == 0))
                    if smp:
                        fns.append(lambda e, h=h, dst=dst, first=first: e.matmul(
                            dst, xd[0:CL, h, :], Eb[0:CL, h, 0:CL], start=first, stop=True, skip_group_check=True))
                    else:
                        fns.append(lambda e, h=h, dst=dst: e.matmul(
                            dst, xd[0:CL, h, :], Eb[0:CL, h, 0:CL], start=True, stop=False, skip_group_check=True))
                        fns.append(lambda e, h=h, dst=dst: e.matmul(
                            dst, STb[:, h * 64:(h + 1) * 64], Csb[:, h, 0:CL], start=False, stop=True,
                            skip_group_check=True))
                S.op("pe", fns, reads=(bxd, bEb, b_STb, bCs), writes=bpy)
                if smp:
                    reserved.update(py_banks)
                    sample_states(py, bpy, Snat, bSnat, STs, bSTs, Csb, bCs, xdd, bxdd, Btok, bBt, Bblk, bBblk,
                                  bmsk, cdT, bcdT)
                elif cfg.get('sstage', 99) > 5:
                    pst, bpst = ps_group(2)
                    S.op("pe", [lambda e, g=g: e.matmul(pst[:, g * 512:(g + 1) * 512], Btok[0:CL, g, :],
                                                         xdd[0:CL, g * 8:(g + 1) * 8, :].rearrange("p h q -> p (h q)"),
                                                         start=True, stop=True) for g in range(2)],
                         reads=(bBt, bxdd), writes=bpst)
                    S.op("dve", lambda e: e.tensor_tensor(
                        out=STf[:, :].rearrange("p (h q) -> p h q", h=16),
                        in0=STf[:, :].rearrange("p (h q) -> p h q", h=16),
                        in1=absx[:, :, CL - 1:CL].broadcast_to([128, 16, 64]), op=ALU.mult),
                        reads=(b_ST, babs), writes=(b_ST,))
                    S.op("dve", lambda e, pst=pst: e.tensor_tensor(out=STf[:, :], in0=STf[:, :], in1=pst[:, 0:1024], op=ALU.add),
                         reads=[b_ST] + bpst, writes=(b_ST,))
                    S.op("act", lambda e: e.copy(STb[:, :], STf[:, :]), reads=(b_ST,), writes=(b_STb,))
                if cfg.get('sstage', 99) <= 6:
                    continue
                o_ds = PV_OFF["dskip"]
                S.op("dve", lambda e: e.tensor_tensor(
                    out=ygb[:, :, 0:CL], in0=xcT[:, :, cols], in1=pv[:, o_ds:o_ds + 8].unsqueeze(2).broadcast_to([128, 8, CL]),
                    op=ALU.mult), reads=(bxc, b_pv), writes=(bygb,))
                S.op("dve", lambda e, py=py: e.tensor_tensor(
                    out=ygb[:, :, 0:CL], in0=ygb[:, :, 0:CL], in1=py[:, 0:8 * CL].rearrange("p (c i) -> p c i", c=8), op=ALU.add),
                    reads=[bygb] + bpy, writes=(bygb,))
                S.op("dve", lambda e: e.tensor_tensor(out=ygb[:, :, 0:CL], in0=ygb[:, :, 0:CL], in1=zT[:, :, cols], op=ALU.mult),
                     reads=(bygb, bz), writes=(bygb,))
                S.op("act", lambda e: e.activation(out=sqg[:, :, 0:CL], in_=ygb[:, :, 0:CL], func=AF.Square),
                     reads=(bygb,), writes=(bsqg,))
                pss, bpss = next_ps()
                for g in range(2):
                    S.op("pe", [lambda e, g=g, cc=cc, pss=pss: e.matmul(
                        pss[:, g * CL:(g + 1) * CL], onesb[:, :], sqg[:, g * 4 + cc, 0:CL], start=(cc == 0), stop=(cc == 3))
                        for cc in range(4)], reads=(bsqg, b_ident), writes=(bpss,))
                S.op("act", lambda e, pss=pss: e.activation(out=rsg[:, :, 0:CL], in_=pss[:, 0:2 * CL].rearrange("p (g i) -> p g i", g=2),
                                                             func=AF.Sqrt, bias=EPS, scale=1.0 / 512),
                     reads=(bpss,), writes=(brsg,))
                S.op("dve", lambda e: e.reciprocal(rsg[:, :, 0:CL], rsg[:, :, 0:CL]), reads=(brsg,), writes=(brsg,))
                reserved.difference_update(py_banks)
                o_ng = PV_OFF["norm_g"]
                S.op("dve", lambda e: e.tensor_tensor(
                    out=ygb[:, :, 0:CL], in0=ygb[:, :, 0:CL], in1=pv[:, o_ng:o_ng + 8].unsqueeze(2).broadcast_to([128, 8, CL]),
                    op=ALU.mult), reads=(bygb, b_pv), writes=(bygb,))
                for g in range(2):
                    S.op("dve", lambda e, g=g: e.tensor_tensor(
                        out=ybT[:, g * 4:(g + 1) * 4, cols], in0=ygb[:, g * 4:(g + 1) * 4, 0:CL],
                        in1=rsg[:, g:g + 1, 0:CL].broadcast_to([128, 4, CL]), op=ALU.mult),
                        reads=(bygb, brsg), writes=(byb,))

            if STOP <= 5:
                return
            if cfg.get("obar", 0):
                S.barrier(engines=("pe", "act", "dve", "sp", "pool"))
            for mi in range(4):
                if cfg.get("omode", 9) == 6:
                    w, bw = wring[mi], b_wr[mi]
                else:
                    w, bw = wload(wev_out[mi], 16 * 256)
                wv = w[:, :].rearrange("p (k n) -> p k n", k=16)
                for mm in range(2):
                    m = mi * 2 + mm
                    pt, bp = next_ps()
                    fns = []
                    for k in range(16):
                        rhs = yaT[:, k, 0:nt] if k < 8 else ybT[:, k - 8, 0:nt]
                        fns.append(lambda e, pt=pt, k=k, mm=mm, wv=wv, rhs=rhs: e.matmul(
                            pt[:, 0:nt], wv[:, k, mm * 128:(mm + 1) * 128], rhs, start=(k == 0), stop=(k == 15)))
                    OM = cfg.get("omode", 9)
                    if OM >= 1:
                        S.op("pe", fns[0:OM] if OM < 9 else fns, reads=[bw, byb] + b_ya, writes=(bp,))
                    if OM >= 9:
                        S.op("dve", lambda e, pt=pt, m=m: e.tensor_tensor(out=xT[:, m, 0:nt], in0=pt[:, 0:nt], in1=xT[:, m, 0:nt],
                                                                          op=ALU.add), reads=(bp, b_xT[m]), writes=(b_xT[m],))
            if STOP <= 6:
                return
            if is_last_ptile:
                so = absx[:, 0:8, :]
                bso = babs
                pg, bpg = ps_group(2)
                S.op("pe", [lambda e, c=c: e.transpose(pg[:, c * 128:(c + 1) * 128], STf[:, c * 128:(c + 1) * 128], ident[:, :])
                            for c in range(8)], reads=(b_ST, b_ident), writes=bpg)
                S.op("act", lambda e: e.copy(so.rearrange("p c n -> p (c n)"), pg[:, 0:1024]), reads=bpg, writes=(bso,))
                S.op("sp", lambda e: e.dma_start(out=o_ssm_p.rearrange("(c p) n -> p c n", p=128), in_=so),
                     reads=(bso,), dma=next_out(), final=True, arena=True)

        def sample_states(py, bpy, Snat, bSnat, STs, bSTs, Csb, bCs, xdd, bxdd, Btok, bBt, Bblk, bBblk, bmsk, cdT, bcdT):
            for g in range(2):
                S.op("dve", lambda e, g=g: e.tensor_tensor(
                    out=Bblk[:, g, :, :], in0=Btok[0:64, g, :].unsqueeze(1).broadcast_to([64, 16, 128]),
                    in1=bmsk.unsqueeze(2).broadcast_to([64, 16, 128]), op=ALU.mult),
                    reads=(bBt, b_cs), writes=(bBblk,))
            NB = 2
            for bg in range(16 // NB):
                S.op("sp", lambda e, bg=bg: e.dma_start(
                    out=Snat[:, :, :, :], in_=st_ssm[bg * NB:(bg + 1) * NB].rearrange("b (c p) n -> p b c n", p=128)),
                    writes=(bSnat,), dma=next_in(), arena=True)
                for bb in range(NB):
                    pg, bpg = ps_group(2)
                    S.op("pe", [lambda e, c=c, bb=bb, pg=pg: e.transpose(pg[:, c * 128:(c + 1) * 128], Snat[:, bb, c, :], ident[:, :])
                                for c in range(8)], reads=(bSnat, b_ident), writes=bpg)
                    if bb % 2:
                        S.op("act", lambda e, bb=bb, pg=pg: e.copy(STs[:, bb, :], pg[:, 0:1024]), reads=bpg, writes=(bSTs,))
                    else:
                        S.op("dve", lambda e, bb=bb, pg=pg: e.tensor_copy(STs[:, bb, :], pg[:, 0:1024]), reads=bpg, writes=(bSTs,))
                fns = []
                for bb in range(NB):
                    b = bg * NB + bb
                    for h in range(16):
                        dst = py[64 * (h % 2):64 * (h % 2) + 64, (h // 2) * 64 + 4 * b:(h // 2) * 64 + 4 * b + 4]
                        fns.append(lambda e, h=h, bb=bb, b=b, dst=dst: e.matmul(
                            dst, STs[:, bb, h * 64:(h + 1) * 64], Csb[:, h, 4 * b:4 * b + 4], start=False, stop=True,
                            skip_group_check=True))
                S.op("pe", fns, reads=(bSTs, bCs), writes=bpy)
                for c in range(8):
                    pt, bp = next_ps()
                    S.op("pe", lambda e, c=c, pt=pt, bg=bg: e.matmul(
                        pt[:, 0:NB * 128], xdd[0:64, 2 * c:2 * c + 2, :].rearrange("p h q -> p (h q)"),
                        Bblk[:, c // 4, bg * NB:(bg + 1) * NB, :].rearrange("p b n -> p (b n)"), start=True, stop=True),
                        reads=(bxdd, bBblk), writes=(bp,))
                    for bb in range(NB):
                        b = bg * NB + bb
                        S.op("dve", lambda e, c=c, bb=bb, b=b, pt=pt: e.scalar_tensor_tensor(
                            out=Snat[:, bb, c, :], in0=Snat[:, bb, c, :], scalar=cdT[:, c, b:b + 1],
                            in1=pt[:, bb * 128:(bb + 1) * 128], op0=ALU.mult, op1=ALU.add),
                            reads=(bSnat, bcdT, bp, bSTs), writes=(bSnat,))
                S.op("sp", lambda e, bg=bg: e.dma_start(
                    out=o_ssm_s[bg * NB:(bg + 1) * NB].rearrange("b (c p) n -> p b c n", p=128), in_=Snat[:, :, :, :]),
                    reads=(bSnat,), dma=next_out(), final=True, arena=True)

        WINS = (2, 4, 8, 16)

        def head_norm(pt, bp, nparts_dummy, nt, gname, out_ap, wbufs, scale, f32_out=None, f32_bufs=()):
            si = state["sil"]
            state["sil"] = 1 - si
            raw = sil[si][:, 0:nt]
            S.op("act", lambda e: e.copy(raw, pt), reads=(bp,), writes=(b_sil[si],))
            S.op("act", lambda e: e.activation(out=hsq[:, 0:nt], in_=pt, func=AF.Square), reads=(bp,), writes=(b_hsq,))
            ps2, bps2 = next_ps()
            S.op("pe", lambda e: e.matmul(ps2[:, 0:nt], bd64[:, :], hsq[:, 0:nt], start=True, stop=True),
                 reads=(b_hsq, b_ident), writes=(bps2,))
            if scale is None:
                S.op("act", lambda e: e.activation(out=hrs[:, 0:nt], in_=ps2[:, 0:nt], func=AF.Sqrt, bias=EPS, scale=1.0 / 64),
                     reads=(bps2,), writes=(b_hrs,))
            else:
                S.op("act", lambda e: e.activation(out=hrs[:, 0:nt], in_=ps2[:, 0:nt], func=AF.Sqrt, bias=EPS * scale * scale,
                                                   scale=scale * scale / 64), reads=(bps2,), writes=(b_hrs,))
            S.op("dve", lambda e: e.reciprocal(hrs[:, 0:nt], hrs[:, 0:nt]), reads=(b_hrs,), writes=(b_hrs,))
            S.op("dve", lambda e: e.scalar_tensor_tensor(out=out_ap, in0=raw, scalar=pvc(gname, 0), in1=hrs[:, 0:nt],
                                                         op0=ALU.mult, op1=ALU.mult),
                 reads=(b_sil[si], b_hrs, b_pv), writes=wbufs)
            if f32_out is not None:
                S.op("dve", lambda e: e.scalar_tensor_tensor(out=f32_out, in0=raw, scalar=pvc(gname, 0), in1=hrs[:, 0:nt],
                                                             op0=ALU.mult, op1=ALU.mult),
                     reads=(b_sil[si], b_hrs, b_pv), writes=f32_bufs)

        def odd_mixer(kind, nt, is_last_ptile, tile_idx):
            S.barrier()
            A.reset()
            smp = (kind == "s")
            qT = hT[:, 0:8, :]
            ycT = hT[:, 8:16, :]
            b_q, b_yc = b_h[0:8], b_h[8:16]
            ydT = A.take(128, [8, nt], BF16)
            byd = Buf("ydT")
            L = 19 if smp else (nt + 15)
            nb = 16 if smp else 1
            cext = A.take(128, [8, nb * L])
            bce = Buf("cext")
            pA = A.take(128, [nb * L])
            pB = A.take(128, [nb * L])
            bpA, bpB = Buf("pA"), Buf("pB")
            pooledT = A.take(128, [8, nt], BF16)
            bpool = Buf("pooled")
            KW = 64 if smp else (128 + nt)
            kdT = A.take(128, [4, KW], BF16)
            bkd = Buf("kdT")
            NVB = 1 if smp else 5
            vtok = A.take(128, [NVB, 256], BF16)
            bvt = Buf("vtok")
            kf32 = A.take(128, [4, 128 if not smp else 64])
            vf32 = A.take(128, [256])
            bkf, bvf = Buf("kf32"), Buf("vf32")
            ost = A.take(128, [1024])
            bost = Buf("ost")
            tS_ = [A.take(128, [1024]) for _ in range(2)]
            pT_ = [A.take(128, [1024], BF16) for _ in range(2)]
            btS_, bpT_ = [Buf("tS0"), Buf("tS1")], [Buf("pT0"), Buf("pT1")]
            tS, pT, btS, bpT = tS_[0], pT_[0], btS_[0], bpT_[0]
            rec = [A.take(128, [128]) for _ in range(2)]
            brec = [Buf("rec0"), Buf("rec1")]
            recb = A.take(128, [8, 128])
            brecb = Buf("recb")

            def cview(c, lo, n):
                if smp:
                    return cext[:, c, :].rearrange("p (b l) -> p b l", l=L)[:, :, lo:lo + n]
                return cext[:, c, lo:lo + n]

            def tview(t, lo, n):
                if smp:
                    return t[:, :].rearrange("p (b l) -> p b l", l=L)[:, :, lo:lo + n]
                return t[:, lo:lo + n]

            def ntv(ap2d):
                return ap2d.rearrange("p (b i) -> p b i", i=4) if smp else ap2d

            if smp:
                cin32 = A.take(128, [8, 64])
                bcin = Buf("cin32")
                stp = A.take(120, [2, 1024])
                bstp = Buf("stp")
                S.op("sp", lambda e: e.dma_start(out=stp, in_=st_pool.rearrange("(g r) d -> r g d", g=2)),
                     writes=(bstp,), dma=next_in(), arena=True)
                S.op("sp", lambda e: e.dma_start(out=o_pool_s[:, 0:11, :], in_=st_pool.rearrange("(b r) d -> b r d", r=15)[:, 4:15, :]),
                     dma=next_out(), final=True)
                S.op("sp", lambda e: e.dma_start(out=o_k_s[:, 0:124, :], in_=st_k[:, 4:128, :]), dma=next_out(), final=True)
                S.op("sp", lambda e: e.dma_start(out=o_v_s[:, 0:124, :], in_=st_v[:, 4:128, :]), dma=next_out(), final=True)
                for g in range(2):
                    for cq in range(2):
                        pg, bpg = ps_group(2)
                        S.op("pe", [lambda e, c4=c4, g=g, cq=cq, pg=pg: e.transpose(
                            pg[:, c4 * 128:c4 * 128 + 120], stp[0:120, g, (cq * 4 + c4) * 128:(cq * 4 + c4 + 1) * 128],
                            ident[0:120, 0:120]) for c4 in range(4)], reads=(bstp, b_ident), writes=bpg)
                        for c4 in range(4):
                            c = cq * 4 + c4
                            S.op("dve" if c4 % 2 else "act",
                                 (lambda e, c=c, c4=c4, g=g, pg=pg: e.tensor_copy(
                                     cext[:, c, :].rearrange("p (b l) -> p b l", l=L)[:, g * 8:(g + 1) * 8, 0:15],
                                     pg[:, c4 * 128:c4 * 128 + 120].rearrange("p (b r) -> p b r", r=15))) if c4 % 2 else
                                 (lambda e, c=c, c4=c4, g=g, pg=pg: e.copy(
                                     cext[:, c, :].rearrange("p (b l) -> p b l", l=L)[:, g * 8:(g + 1) * 8, 0:15],
                                     pg[:, c4 * 128:c4 * 128 + 120].rearrange("p (b r) -> p b r", r=15))),
                                 reads=bpg, writes=(bce,))
            else:
                S.op("dve", lambda e: e.tensor_copy(cext[:, :, 0:15], ptail[:, :, :]), reads=(b_ptail,), writes=(bce,))
                bmP = A.take(128, [2, 16, 128])
                bbm = Buf("bm")
                for kb, off in ((0, 256), (1, 128)):
                    S.op("sp", lambda e, kb=kb, off=off: e.dma_start(
                        out=bmP[:, kb, :, :], in_=bass.AP(tensor=dscr.tensor, offset=off, ap=[[383, 128], [128 * 384, 16], [1, 128]])),
                        reads=(b_dscr,), writes=(bbm,), dma=next_in(), arena=True)

            rms_norm("mix_norm1", nt)

            for gi in range(2):
                def ev_c(cc, pt, bp, gi=gi):
                    c = gi * 4 + cc
                    S.op("act", lambda e: e.copy(cview(c, 15, nt if not smp else 4), ntv(pt[:, 0:nt])), reads=(bp,), writes=(bce,))
                    if smp:
                        S.op("dve", lambda e: e.tensor_copy(cin32[:, c, :], pt[:, 0:nt]), reads=(bp,), writes=(bcin,))
                proj_fm(wod_fm[gi], 4, nt, ev_c)
            for gi in range(2):
                def ev_q(cc, pt, bp, gi=gi):
                    c = gi * 4 + cc
                    head_norm(pt[:, 0:nt], bp, 128, nt, "q_norm", qT[:, c, 0:nt], (b_q[c],), 8.0)
                proj_fm(wod_fm[2 + gi], 4, nt, ev_q)
            wk, bwk = wload(wod_k, 8 * 256)
            wkv = wk[:, 0:2048].rearrange("p (k n) -> p k n", k=8)
            k0 = 0 if smp else 128
            if not smp:
                S.op("dve", lambda e: e.tensor_copy(kdT[:, :, 0:128], kprev[:, :, :]), reads=(b_kprev,), writes=(bkd,))
                S.op("dve", lambda e: e.tensor_copy(vtok[:, 0, :], vprev[:, :]), reads=(b_kprev,), writes=(bvt,))
            for hk in range(4):
                pt, bp = next_ps()
                fns = []
                for half in range(2):
                    for k in range(8):
                        fns.append(lambda e, pt=pt, half=half, k=k, hk=hk: e.matmul(
                            pt[64 * half:64 * half + 64, 0:nt], wkv[:, k, hk * 64:(hk + 1) * 64], xn[:, k, 0:nt],
                            start=(k == 0), stop=(k == 7), skip_group_check=True))
                S.op("pe", fns, reads=(bwk, b_xn), writes=(bp,))
                _kn(pt, bp, hk, nt, k0, kdT, bkd, kf32, bkf, smp, is_last_ptile)
            wv_, bwv_ = wload(wod_v, 8 * 256)
            wvv_ = wv_[:, 0:2048].rearrange("p (k n) -> p k n", k=8)
            CLv = 64 if smp else 128
            for blk in range(nt // CLv):
                cols = slice(blk * CLv, (blk + 1) * CLv)
                pt, bp = next_ps()
                S.op("pe", [lambda e, pt=pt, k=k, cols=cols: e.matmul(pt[0:CLv, 0:256], xn[:, k, cols], wvv_[:, k, :],
                                                                       start=(k == 0), stop=(k == 7)) for k in range(8)],
                     reads=(bwv_, b_xn), writes=(bp,))
                vb = 0 if smp else blk + 1
                S.op("act", lambda e, pt=pt, vb=vb: e.copy(vtok[0:CLv, vb, :], pt[0:CLv, 0:256]), reads=(bp,), writes=(bvt,))
                if smp or (is_last_ptile and blk == 3):
                    S.op("dve", lambda e, pt=pt: e.tensor_copy(vf32[0:CLv, :], pt[0:CLv, 0:256]), reads=(bp,), writes=(bvf,))
            if smp or is_last_ptile:
                if smp:
                    for i in range(4):
                        S.op("sp", lambda e, i=i: e.dma_start(out=o_v_s[:, 124 + i, :], in_=vf32[i:64:4, :]),
                             reads=(bvf,), dma=next_out(), final=True, arena=True)
                else:
                    S.op("sp", lambda e: e.dma_start(out=o_v_p, in_=vf32[:, :]), reads=(bvf,), dma=next_out(), final=True, arena=True)
                pk, bpk = next_ps()
                nk = 64 if smp else 128
                S.op("pe", [lambda e, hk=hk: e.transpose(pk[0:nk, hk * 64:(hk + 1) * 64], kf32[0:64, hk, 0:nk], ident[0:64, 0:64])
                            for hk in range(4)], reads=(bkf, b_ident), writes=(bpk,))
                S.op("act", lambda e: e.copy(ost[0:nk, 0:256], pk[0:nk, 0:256]), reads=(bpk,), writes=(bost,))
                if smp:
                    for i in range(4):
                        S.op("sp", lambda e, i=i: e.dma_start(out=o_k_s[:, 124 + i, :], in_=ost[i:64:4, 0:256]),
                             reads=(bost,), dma=next_out(), final=True, arena=True)
                else:
                    S.op("sp", lambda e: e.dma_start(out=o_k_p, in_=ost[:, 0:256]), reads=(bost,), dma=next_out(), final=True, arena=True)

            first_tile = (not smp) and tile_idx == 0
            for c in range(8):
                gi = c // 2
                w = WINS[gi]
                Lx = L
                cur_t, cur_b, cur_lo = None, None, 0
                src_ap = lambda lo, n, c=c: cview(c, lo, n)
                width = 1
                bufs = [(pA, bpA), (pB, bpB)]
                bi = 0
                srcf, srcb, lo0 = src_ap, bce, 0
                while width < w:
                    dst, bdst = bufs[bi]
                    bi = 1 - bi
                    lo1 = lo0 + width
                    n = Lx - lo1
                    S.op("dve", lambda e, srcf=srcf, dst=dst, lo1=lo1, n=n, width=width: e.tensor_tensor(
                        out=tview(dst, lo1, n), in0=srcf(lo1, n), in1=srcf(lo1 - width, n), op=ALU.add),
                        reads=(srcb,), writes=(bdst,))
                    srcf = (lambda lo, n, dst=dst: tview(dst, lo, n))
                    srcb, lo0 = bdst, lo1
                    width *= 2
                nn = 4 if smp else nt
                S.op("dve", lambda e, srcf=srcf, c=c, w=w, nn=nn: e.scalar_tensor_tensor(
                    out=ntv(pooledT[:, c, 0:nt]), in0=srcf(15, nn), scalar=1.0 / w, in1=cview(c, 15, nn),
                    op0=ALU.mult, op1=ALU.subtract), reads=(srcb, bce), writes=(bpool,))
                if first_tile:
                    S.op("dve", lambda e, srcf=srcf, gi=gi: e.tensor_tensor(
                        out=rec[0][:, 0:16], in0=srcf(15, 16), in1=cst_ap("invc")[:, gi * 16:(gi + 1) * 16], op=ALU.mult),
                        reads=(srcb, b_cs), writes=(brec[0],))
                    S.op("dve", lambda e, c=c: e.tensor_tensor(
                        out=pooledT[:, c, 0:16], in0=rec[0][:, 0:16], in1=cview(c, 15, 16), op=ALU.subtract),
                        reads=(brec[0], bce), writes=(bpool,))
            if not smp:
                S.op("dve", lambda e: e.tensor_copy(ptail[:, :, :], cext[:, :, nt:nt + 15]), reads=(bce,), writes=(b_ptail,))
            if smp or is_last_ptile:
                nr = 64 if smp else 15
                pg, bpg = ps_group(2)
                if smp:
                    S.op("pe", [lambda e, c=c: e.transpose(pg[0:64, c * 128:(c + 1) * 128], cin32[:, c, :], ident[:, :])
                                for c in range(8)], reads=(bcin, b_ident), writes=bpg)
                else:
                    S.op("pe", [lambda e, c=c: e.transpose(pg[0:15, c * 128:(c + 1) * 128], cext[:, c, nt:nt + 15], ident[:, :])
                                for c in range(8)], reads=(bce, b_ident), writes=bpg)
                S.op("dve", lambda e: e.tensor_copy(ost[0:nr, :], pg[0:nr, 0:1024]), reads=bpg + [bost], writes=(bost,))
                if smp:
                    for i in range(4):
                        S.op("sp", lambda e, i=i: e.dma_start(out=o_pool_s[:, 11 + i, :], in_=ost[i:64:4, :]),
                             reads=(bost,), dma=next_out(), final=True, arena=True)
                else:
                    S.op("sp", lambda e: e.dma_start(out=o_pool_p, in_=ost[0:15, :]), reads=(bost,), dma=next_out(), final=True, arena=True)
            wl, bwl = wload(wod_lin, 4 * 2 * 256)
            wlv = wl[:, 0:2048].rearrange("p (g c d) -> p g c d", g=4, c=2)
            for c in range(8):
                gi, dd = c // 2, c % 2
                pt, bp = next_ps()
                S.op("pe", [lambda e, pt=pt, cc=cc, gi=gi, dd=dd: e.matmul(
                    pt[:, 0:nt], wlv[:, gi, cc, dd * 128:(dd + 1) * 128], pooledT[:, gi * 2 + cc, 0:nt],
                    start=(cc == 0), stop=(cc == 1)) for cc in range(2)], reads=(bwl, bpool), writes=(bp,))
                S.op("act", lambda e, pt=pt, c=c: e.activation(out=ycT[:, c, 0:nt], in_=pt[:, 0:nt], func=AF.Copy,
                                                               scale=pvc("c_scale", c)), reads=(bp, b_pv), writes=(b_yc[c],))

            if smp:
                sample_attention(nt, qT, b_q, kdT, bkd, vtok, bvt, ydT, byd, tS, btS, pT, bpT, rec, brec)
            else:
                po, bpo = ps_group(2)
                po_banks = list(state["last_group"])
                pd, bpd = ps_group(2)
                pd_banks = list(state["last_group"])
                reserved.update(po_banks + pd_banks)
                def kbs_of(qb):
                    return [1] if tile_idx * 4 + qb == 0 else [0, 1]

                def stage_a(qb, hq, par):
                    qcols = slice(qb * 128, (qb + 1) * 128)
                    kbs = kbs_of(qb)
                    nkb = len(kbs)
                    tS, pT, btS, bpT = tS_[par], pT_[par], btS_[par], bpT_[par]
                    psA, bpsA = next_ps()
                    psB, bpsB = next_ps()
                    pss_ = (psA, psB)
                    fns = []
                    for hl in range(4):
                        h = hq * 4 + hl
                        hh, s, hp = hl % 2, hl // 2, h // 2
                        for ki, kb in enumerate(kbs):
                            kc0 = qb * 128 + kb * 128
                            fns.append(lambda e, hh=hh, s=s, ki=ki, kc0=kc0, hp=hp, hq=hq, pss_=pss_, qcols=qcols: e.matmul(
                                pss_[hh][:, (s * 2 + ki) * 128:(s * 2 + ki + 1) * 128],
                                kdT[64 * hh:64 * hh + 64, hq, kc0:kc0 + 128], qT[64 * hh:64 * hh + 64, hp, qcols],
                                start=True, stop=True))
                    S.op("pe", fns, reads=[bkd] + b_q[hq * 2:hq * 2 + 2], writes=(bpsA, bpsB))
                    if nkb == 2:
                        for hh in range(2):
                            h0 = hq * 4 + hh
                            S.op("dve", lambda e, hh=hh, h0=h0, pss_=pss_, tS=tS: e.tensor_tensor(
                                out=tS[:, hh * 512:(hh + 1) * 512].rearrange("p (s k q) -> p s k q", s=2, k=2),
                                in0=pss_[hh][:, 0:512].rearrange("p (s k q) -> p s k q", s=2, k=2),
                                in1=bmP[:, :, h0:h0 + 3:2, :].rearrange("p k s q -> p s k q"), op=ALU.add),
                                reads=((bpsA, bpsB)[hh], bbm), writes=(btS,))
                        S.op("act", lambda e, tS=tS, pT=pT: e.activation(out=pT[:, 0:1024], in_=tS[:, 0:1024], func=AF.Exp),
                             reads=(btS,), writes=(bpT,))
                    else:
                        for hl in range(4):
                            h = hq * 4 + hl
                            hh, s = hl % 2, hl // 2
                            for ki, kb in enumerate(kbs):
                                j = s * 2 + ki
                                S.op("dve", lambda e, hh=hh, j=j, kb=kb, h=h, pss_=pss_, tS=tS: e.tensor_tensor(
                                    out=tS[:, hh * 512 + j * 128:hh * 512 + (j + 1) * 128], in0=pss_[hh][:, j * 128:(j + 1) * 128],
                                    in1=bmP[:, kb, h, :], op=ALU.add), reads=((bpsA, bpsB)[hh], bbm), writes=(btS,))
                        S.op("act", lambda e, tS=tS, pT=pT: e.activation(
                            out=pT[:, 0:1024].rearrange("p (a b) -> p a b", a=4)[:, :, 0:128],
                            in_=tS[:, 0:1024].rearrange("p (a b) -> p a b", a=4)[:, :, 0:128], func=AF.Exp),
                            reads=(btS,), writes=(bpT,))

                def stage_b(qb, hq, par):
                    kbs = kbs_of(qb)
                    nkb = len(kbs)
                    pT, bpT = pT_[par], bpT_[par]
                    fns = []
                    for hl in range(4):
                        h = hq * 4 + hl
                        hh, s, hp = hl % 2, hl // 2, h // 2
                        for ki, kb in enumerate(kbs):
                            j = hh * 4 + s * 2 + ki
                            fns.append(lambda e, hh=hh, ki=ki, kb=kb, hq=hq, hp=hp, j=j, qb=qb, pT=pT, nkb=nkb: e.matmul(
                                po[64 * hh:64 * hh + 64, hp * 128:(hp + 1) * 128], vtok[:, qb + kb, hq * 64:(hq + 1) * 64],
                                pT[:, j * 128:(j + 1) * 128], start=(ki == 0), stop=(ki == nkb - 1), skip_group_check=True))
                        for ki, kb in enumerate(kbs):
                            j = hh * 4 + s * 2 + ki
                            fns.append(lambda e, hh=hh, ki=ki, hp=hp, j=j, pT=pT, nkb=nkb: e.matmul(
                                pd[64 * hh:64 * hh + 64, hp * 128:(hp + 1) * 128], onesb[:, 0:64],
                                pT[:, j * 128:(j + 1) * 128], start=(ki == 0), stop=(ki == nkb - 1), skip_group_check=True))
                    S.op("pe", fns, reads=(bvt, bpT, b_ident), writes=bpo + bpd)

                def epilogue(qb):
                    qcols = slice(qb * 128, (qb + 1) * 128)
                    S.op("dve", lambda e: e.tensor_tensor(
                        out=recb, in0=pd[:, 0:1024].rearrange("p (c q) -> p c q", c=8),
                        in1=esink2[:, 0:8].unsqueeze(2).broadcast_to([128, 8, 128]), op=ALU.add),
                        reads=bpd + [b_gm2], writes=(brecb,))
                    S.op("dve", lambda e: e.reciprocal(recb, recb), reads=(brecb,), writes=(brecb,))
                    S.op("dve", lambda e, qcols=qcols: e.tensor_tensor(
                        out=ydT[:, :, qcols], in0=po[:, 0:1024].rearrange("p (c q) -> p c q", c=8), in1=recb, op=ALU.mult),
                        reads=bpo + [brecb], writes=(byd,))

                groups = [(qb, hq) for qb in range(4) for hq in range(4)]
                stage_a(groups[0][0], groups[0][1], 0)
                for gi_, (qb, hq) in enumerate(groups):
                    if gi_ + 1 < len(groups):
                        stage_a(groups[gi_ + 1][0], groups[gi_ + 1][1], (gi_ + 1) % 2)
                    stage_b(qb, hq, gi_ % 2)
                    if hq == 3:
                        epilogue(qb)
                reserved.difference_update(po_banks + pd_banks)
                S.op("dve", lambda e: e.tensor_copy(kprev[:, :, :], kdT[:, :, nt:nt + 128]), reads=(bkd,), writes=(b_kprev,))
                S.op("dve", lambda e: e.tensor_copy(vprev[:, :], vtok[:, 4, :]), reads=(bvt,), writes=(b_kprev,))

            for mi in range(4):
                w, bw = wload(wod_out[mi], 16 * 256)
                wv = w[:, :].rearrange("p (k n) -> p k n", k=16)
                for mm in range(2):
                    m = mi * 2 + mm
                    pt, bp = next_ps()
                    fns = []
                    ks = {"c": range(8), "d": range(8, 16)}.get(cfg.get("odd_part"), range(16))
                    for k in ks:
                        rhs = ycT[:, k, 0:nt] if k < 8 else ydT[:, k - 8, 0:nt]
                        fns.append(lambda e, pt=pt, k=k, mm=mm, wv=wv, rhs=rhs, ks=ks: e.matmul(
                            pt[:, 0:nt], wv[:, k, mm * 128:(mm + 1) * 128], rhs, start=(k == ks[0]), stop=(k == ks[-1])))
                    S.op("pe", fns, reads=[bw, byd] + b_yc, writes=(bp,))
                    S.op("dve", lambda e, pt=pt, m=m: e.tensor_tensor(out=xT[:, m, 0:nt], in0=pt[:, 0:nt], in1=xT[:, m, 0:nt],
                                                                      op=ALU.add), reads=(bp, b_xT[m]), writes=(b_xT[m],))

        def _kn(pt, bp, hk, nt, k0, kdT, bkd, kf32, bkf, smp, is_last_ptile):
            if smp:
                head_norm(pt[:, 0:nt], bp, 128, nt, "k_norm", kdT[:, hk, k0:k0 + nt], (bkd,), None,
                          f32_out=kf32[:, hk, 0:64], f32_bufs=(bkf,))
            elif is_last_ptile:
                head_norm(pt[:, 0:nt], bp, 128, nt, "k_norm", kdT[:, hk, k0:k0 + nt], (bkd,), None)
                si = 1 - state["sil"]
                S.op("dve", lambda e: e.scalar_tensor_tensor(out=kf32[:, hk, :], in0=sil[si][:, nt - 128:nt], scalar=pvc("k_norm", 0),
                                                             in1=hrs[:, nt - 128:nt], op0=ALU.mult, op1=ALU.mult),
                     reads=(b_sil[si], b_hrs, b_pv), writes=(bkf,))
            else:
                head_norm(pt[:, 0:nt], bp, 128, nt, "k_norm", kdT[:, hk, k0:k0 + nt], (bkd,), None)

        def sample_attention(nt, qT, b_q, kdT, bkd, vtok, bvt, ydT, byd, tS, btS, pT, bpT, rec, brec):
            kc32 = A.take(128, [256])
            kcb = A.take(128, [4, 2, 64], BF16)
            bkc, bkcb = Buf("kc32"), Buf("kcb")
            KdT = A.take(128, [16, 4, 128], BF16)
            bKd = Buf("KdT")
            vc32 = A.take(128, [2, 256])
            vcb = A.take(128, [16, 256], BF16)
            bvc, bvcb = Buf("vc32"), Buf("vcb")
            bm1 = A.take(128, [16, 4])
            bm2 = A.take(64, [16, 64])
            bbm1, bbm2 = Buf("bm1"), Buf("bm2")
            t1 = A.take(128, [16, 64])
            p1 = A.take(128, [16, 64], BF16)
            t2 = A.take(64, [16, 64])
            p2 = A.take(64, [16, 64], BF16)
            bt1, bp1, bt2, bp2 = Buf("t1"), Buf("p1"), Buf("t2"), Buf("p2")
            S.op("sp", lambda e: e.dma_start(out=bm1, in_=bass.AP(tensor=dscr.tensor, offset=256, ap=[[383, 128], [128 * 384, 16], [1, 4]])),
                 reads=(b_dscr,), writes=(bbm1,), dma=next_in(), arena=True)
            S.op("dve", lambda e: e.memset(bm2, NEG), writes=(bbm2,))
            for b in range(16):
                S.op("sp", lambda e, b=b: e.dma_start(
                    out=bm2[4 * b:4 * b + 4, :, 4 * b:4 * b + 4],
                    in_=bass.AP(tensor=dscr.tensor, offset=128, ap=[[383, 4], [128 * 384, 16], [1, 4]])),
                    reads=(b_dscr,), writes=(bbm2,), dma=next_in(), arena=True)
            for b in range(16):
                S.op("sp", lambda e, b=b: e.dma_start(out=kc32, in_=st_k[b]), writes=(bkc,), dma=next_in(), arena=True)
                S.op("dve", lambda e: e.tensor_copy(kcb[:, :, :, :], kc32[:, :].rearrange("p (h d) -> p h d", h=4).unsqueeze(2)
                                                    .broadcast_to([128, 4, 2, 64])), reads=(bkc,), writes=(bkcb,))
                pk, bpk = next_ps()
                pkb = pk.bitcast(BF16)
                S.op("pe", [lambda e, hk=hk, pkb=pkb: e.transpose(pkb[:, hk * 128:(hk + 1) * 128],
                                                                  kcb[:, hk, :, :].rearrange("p a d -> p (a d)"), identb[:, :])
                            for hk in range(4)], reads=(bkcb, b_ident), writes=(bpk,))
                S.op("act", lambda e, b=b, pkb=pkb: e.copy(KdT[:, b, :, :].rearrange("p h k -> p (h k)"), pkb[:, 0:512]),
                     reads=(bpk,), writes=(bKd,))
                if b % 2 == 0:
                    S.op("sp", lambda e, b=b: e.dma_start(out=vc32, in_=st_v[b:b + 2].rearrange("b k d -> k b d")),
                         writes=(bvc,), dma=next_in(), arena=True)
                    S.op("dve", lambda e, b=b: e.tensor_copy(vcb[:, b:b + 2, :], vc32[:, :, :]), reads=(bvc,), writes=(bvcb,))
            ps1, bps1 = ps_group(2)
            g1 = list(state["last_group"])
            reserved.update(g1)
            fns = []
            for b in range(16):
                for h in range(16):
                    hh, hp, hk = h % 2, h // 2, h // 4
                    col = hh * 512 + b * 32 + hp * 4
                    fns.append(lambda e, b=b, hh=hh, hp=hp, hk=hk, col=col: e.matmul(
                        ps1[:, col:col + 4], KdT[64 * hh:64 * hh + 64, b, hk, :],
                        qT[64 * hh:64 * hh + 64, hp, 4 * b:4 * b + 4], start=True, stop=True))
            S.op("pe", fns, reads=[bKd] + b_q, writes=bps1)
            bm1v = bm1.rearrange("p (hp t) i -> p hp t i", t=2)
            for hh in range(2):
                S.op("dve", lambda e, hh=hh: e.tensor_tensor(
                    out=t1[:, hh * 8:(hh + 1) * 8, :].rearrange("p a (c d) -> p (a c) d", d=32) if False else
                    t1.rearrange("p a x -> p (a x)")[:, hh * 512:(hh + 1) * 512].rearrange("p (b hp i) -> p b hp i", b=16, hp=8),
                    in0=ps1[:, hh * 512:(hh + 1) * 512].rearrange("p (b hp i) -> p b hp i", b=16, hp=8),
                    in1=bm1v[:, :, hh, :].unsqueeze(1).broadcast_to([128, 16, 8, 4]), op=ALU.add),
                    reads=bps1 + [bbm1], writes=(bt1,))
            reserved.difference_update(g1)
            S.op("act", lambda e: e.activation(out=p1[:, :, :], in_=t1[:, :, :], func=AF.Exp), reads=(bt1,), writes=(bp1,))
            p1f = p1.rearrange("p a x -> p (a x)")
            ps2, bps2 = ps_group(2)
            S.op("pe", [lambda e, h=h: e.matmul(ps2[0:64, (h % 2) * 512 + (h // 2) * 64:(h % 2) * 512 + (h // 2 + 1) * 64],
                                                 kdT[64 * (h % 2):64 * (h % 2) + 64, h // 4, 0:64],
                                                 qT[64 * (h % 2):64 * (h % 2) + 64, h // 2, 0:64], start=True, stop=True)
                        for h in range(16)], reads=[bkd] + b_q, writes=bps2)
            bm2v = bm2.rearrange("p (hp t) x -> p hp t x", t=2)
            t2f = t2.rearrange("p a x -> p (a x)")
            for hh in range(2):
                S.op("dve", lambda e, hh=hh: e.tensor_tensor(
                    out=t2f[:, hh * 512:(hh + 1) * 512].rearrange("p (hp x) -> p hp x", hp=8),
                    in0=ps2[0:64, hh * 512:(hh + 1) * 512].rearrange("p (hp x) -> p hp x", hp=8),
                    in1=bm2v[:, :, hh, :], op=ALU.add), reads=bps2 + [bbm2], writes=(bt2,))
            S.op("act", lambda e: e.activation(out=p2[:, :, :], in_=t2[:, :, :], func=AF.Exp), reads=(bt2,), writes=(bp2,))
            p2f = p2.rearrange("p a x -> p (a x)")
            po, bpo = next_ps()
            pd, bpd = next_ps()
            fns = []
            for h in range(16):
                hh, hp, hk = h % 2, h // 2, h // 4
                first = h < 2
                fns.append(lambda e, h=h, hh=hh, hp=hp, hk=hk, first=first: e.matmul(
                    po[64 * hh:64 * hh + 64, hp * 64:(hp + 1) * 64], vtok[0:64, 0, hk * 64:(hk + 1) * 64],
                    p2f[:, hh * 512 + hp * 64:hh * 512 + (hp + 1) * 64],
                    start=first, stop=False, skip_group_check=True))
                fns.append(lambda e, h=h, hh=hh, hp=hp, first=first: e.matmul(
                    pd[64 * hh:64 * hh + 64, hp * 64:(hp + 1) * 64], onesb[0:64, 0:64],
                    p2f[:, hh * 512 + hp * 64:hh * 512 + (hp + 1) * 64],
                    start=first, stop=False, skip_group_check=True))
            for b in range(16):
                for h in range(16):
                    hh, hp, hk = h % 2, h // 2, h // 4
                    fns.append(lambda e, b=b, h=h, hh=hh, hp=hp, hk=hk: e.matmul(
                        po[64 * hh:64 * hh + 64, hp * 64 + 4 * b:hp * 64 + 4 * b + 4], vcb[:, b, hk * 64:(hk + 1) * 64],
                        p1f[:, hh * 512 + b * 32 + hp * 4:hh * 512 + b * 32 + hp * 4 + 4], start=False, stop=True,
                        skip_group_check=True))
                    fns.append(lambda e, b=b, h=h, hh=hh, hp=hp: e.matmul(
                        pd[64 * hh:64 * hh + 64, hp * 64 + 4 * b:hp * 64 + 4 * b + 4], onesb[:, 0:64],
                        p1f[:, hh * 512 + b * 32 + hp * 4:hh * 512 + b * 32 + hp * 4 + 4], start=False, stop=True,
                        skip_group_check=True))
            S.op("pe", fns, reads=(bvt, bp2, bp1, bvcb, b_ident), writes=(bpo, bpd))
            for c in range(8):
                ri = c % 2
                S.op("dve", lambda e, c=c, ri=ri: e.tensor_scalar(
                    out=rec[ri][:, 0:64], in0=pd[:, c * 64:(c + 1) * 64], scalar1=esink2[:, c:c + 1], scalar2=None,
                    op0=ALU.add), reads=(bpd, b_gm2), writes=(brec[ri],))
                S.op("dve", lambda e, ri=ri: e.reciprocal(rec[ri][:, 0:64], rec[ri][:, 0:64]), reads=(brec[ri],), writes=(brec[ri],))
                S.op("dve", lambda e, c=c, ri=ri: e.tensor_tensor(
                    out=ydT[:, c, 0:64], in0=po[:, c * 64:(c + 1) * 64], in1=rec[ri][:, 0:64], op=ALU.mult),
                    reads=(bpo, brec[ri]), writes=(byd,))

        tiles = [("p", t) for t in range(n_ptiles)] + ([("s", 0)] if do_sample else [])

        def issue_load(kind, t):
            if kind == "p":
                load_x_tile(xp[t * NTP:(t + 1) * NTP, :], 4, 128)
            else:
                load_x_tile(xs[:, :], 1, NS)

        issue_load(*tiles[0])
        for ti, (kind, t) in enumerate(tiles):
            nt = NTP if kind == "p" else NS
            last_p = (kind == "p" and t == n_ptiles - 1)
            if kind == "p":
                transpose_in(4, 128)
            else:
                transpose_in(1, NS)
            for layer in range(nlayers):
                if on("ffn1"):
                    ffn(layer, 0, nt)
                if layer == 0 and on("even"):
                    even_mixer(kind, nt, last_p)
                if layer == 1 and on("odd"):
                    odd_mixer(kind, nt, last_p, t)
                if layer == nlayers - 1:
                    S.barrier()
                    if ti + 1 < len(tiles):
                        issue_load(*tiles[ti + 1])
                if on("ffn2"):
                    ffn(layer, 1, nt)
            if kind == "p":
                transpose_out(yp[t * NTP:(t + 1) * NTP, :], 4, 128)
            else:
                transpose_out(ys[:, :], 1, NS)

        with nc.Block() as block:
            S.emit(block)
    P.ninst = S.ninst
    return P


PV_OFF = {}
_col = 0


def _pv(name, n):
    global _col
    PV_OFF[name] = _col
    _col += n


for _l in range(2):
    for _w in range(2):
        _pv(f"ffn_norm{_l}{_w}", 8)
for _l in range(2):
    _pv(f"mix_norm{_l}", 8)
_pv("ln_g", 8)
_pv("ln_b", 8)
_pv("conv_w", 48)
_pv("conv_b", 12)
_pv("norm_g", 8)
_pv("dskip", 8)
_pv("c_scale", 8)
_pv("q_norm", 1)
_pv("k_norm", 1)
_pv("dt_bias", 1)
_pv("a_log", 1)
PV_COLS = _col

CST_OFF = {}
_ccol = 0


def _cs(name, n):
    global _ccol
    CST_OFF[name] = (_ccol, n)
    _ccol += n


_cs("maskT_causal", 128)
_cs("maskT_blk", 64)
_cs("bmask", 16)
_cs("rmask_p", 512)
_cs("rmask_s", 64)
_cs("onehot", 384)
_cs("negmask", 384)
_cs("invc", 64)
CST_COLS = _ccol


def make_consts():
    c = np.zeros((128, CST_COLS), np.float32)
    j = np.arange(128)[:, None]
    i = np.arange(128)[None, :]
    o, n = CST_OFF["maskT_causal"]
    c[:, o:o + n] = (i >= j)
    o, n = CST_OFF["maskT_blk"]
    jj = np.arange(64)[:, None]
    ii = np.arange(64)[None, :]
    c[:64, o:o + n] = (ii >= jj) & (ii // 4 == jj // 4)
    o, n = CST_OFF["bmask"]
    c[:64, o:o + n] = (np.arange(64)[:, None] // 4 == np.arange(16)[None, :])
    o, n = CST_OFF["rmask_p"]
    c[:, o:o + n] = (np.arange(512)[None, :] % 128 != 0)
    o, n = CST_OFF["rmask_s"]
    c[:, o:o + n] = (np.arange(64)[None, :] % 4 != 0)
    dist = np.arange(384) - 128
    valid = (dist >= 0) & (dist < 128)
    nn = np.maximum(dist, 0)
    n_safe = np.maximum(nn, 1).astype(np.float32)
    scale = np.float32((32 - 16) / math.log(128 / 16))
    large = np.minimum(16 + (np.log(n_safe / 16) * scale).astype(np.int32), 31)
    bucket = np.where(nn < 16, nn, large).astype(np.int32)
    o, n = CST_OFF["onehot"]
    oh = np.zeros((32, 384), np.float32)
    oh[bucket[valid], np.arange(384)[valid]] = 1.0
    c[:32, o:o + n] = oh
    o, n = CST_OFF["negmask"]
    c[:, o:o + n] = np.where(valid, 0.0, NEG)[None, :]
    o, n = CST_OFF["invc"]
    for gi, w in enumerate((2, 4, 8, 16)):
        c[:, o + gi * 16:o + (gi + 1) * 16] = (1.0 / np.minimum(np.arange(16) + 1, w))[None, :]
    return c


def fm(v):
    v = np.asarray(v, np.float32)
    return np.ascontiguousarray(v.reshape(-1, 128).T)


def pack_host(inp):
    f = lambda k: np.asarray(inp[k], np.float32)
    pvec = np.zeros((128, PV_COLS), np.float32)

    def put(name, arr):
        arr = np.asarray(arr, np.float32)
        pvec[:arr.shape[0], PV_OFF[name]:PV_OFF[name] + arr.shape[1]] = arr

    for l in range(2):
        put(f"ffn_norm{l}0", fm(f("ffn1_norm")[l]))
        put(f"ffn_norm{l}1", fm(f("ffn2_norm")[l]))
        put(f"mix_norm{l}", fm(f("mix_norm")[l]))
    put("ln_g", fm(f("a_ln_g")[0]))
    put("ln_b", fm(f("a_ln_b")[0]))
    cw = f("b_conv_w")[0]
    put("conv_w", cw.reshape(4, 12, 128).transpose(2, 1, 0).reshape(128, 48))
    put("conv_b", fm(f("b_conv_b")[0]))
    put("norm_g", fm(f("b_norm_g")[0]))
    put("dskip", fm(np.repeat(f("b_d_skip")[0], 64)))
    put("c_scale", fm(f("c_scale")[0]))
    put("q_norm", np.tile(f("d_q_norm")[0], 2).reshape(128, 1))
    put("k_norm", np.tile(f("d_k_norm")[0], 2).reshape(128, 1))
    put("dt_bias", f("b_dt_bias")[0].reshape(16, 1))
    put("a_log", f("b_a_log")[0].reshape(16, 1))

    wgu = np.empty((2, 2, 11, 128, 8, 2, 256), np.float32)
    wdn = np.empty((2, 2, 8, 128, FC, 128), np.float32)
    for l in range(2):
        for w, (kgu, kdn) in enumerate((("ffn1_w_gu", "ffn1_w_down"), ("ffn2_w_gu", "ffn2_w_down"))):
            g = f(kgu)[l].reshape(8, 128, 2, 11, 256)
            wgu[l, w] = g.transpose(3, 1, 0, 2, 4)
            d = f(kdn)[l].reshape(FC, 128, 8, 128)
            wdn[l, w] = d.transpose(2, 1, 0, 3)

    def fm_w(Wc):
        n = Wc.shape[1]
        return np.ascontiguousarray(Wc.reshape(8, 128, n).transpose(1, 0, 2).reshape(128, 8 * n))

    Wi = f("ev_w_in")[0]
    col_groups = [(0, 512), (512, 1024), (2048, 2560), (2560, 3072), (3072, 3584), (3584, 4096), (4096, 4608)]
    wev_fm = np.stack([fm_w(Wi[:, a:b]) for a, b in col_groups], 0)
    wev_v = np.stack([fm_w(Wi[:, 1024:1536]), fm_w(Wi[:, 1536:2048])], 0)
    wev_dt = fm_w(Wi[:, 4608:4624])
    Wo = f("ev_w_out")[0]
    wev_out = np.ascontiguousarray(Wo.reshape(16, 128, 4, 256).transpose(2, 1, 0, 3).reshape(4, 128, 16 * 256))
    Wd = f("od_w_in")[0]
    ws = f("a_w_s")[0]
    wsT = np.ascontiguousarray(ws.transpose(2, 0, 1).reshape(128, 8 * 128))
    wsT4 = np.ascontiguousarray(ws[:, :4, :4].transpose(2, 0, 1).reshape(4, 32))
    shared = {
        "pvec": pvec,
        "cst": make_consts(),
        "wgu": wgu.reshape(2, 2, 11, 128, 8 * 2 * 256),
        "wdn": wdn.reshape(2, 2, 8, 128, FC * 128),
        "wev_fm": wev_fm, "wev_v": wev_v, "wev_dt": wev_dt, "wev_out": wev_out,
        "wsT": wsT, "wsT4": wsT4,
        "bs_row": f("a_b_s")[0].reshape(1, 1024),
        "lnrow": np.concatenate([f("a_ln_g")[0], f("a_ln_b")[0]]).reshape(1, 2048),
        "wod_fm": np.stack([fm_w(Wd[:, a:a + 512]) for a in (0, 512, 1024, 1536)], 0),
        "wod_k": fm_w(Wd[:, 2048:2304]), "wod_v": fm_w(Wd[:, 2304:2560]),
        "wod_lin": np.ascontiguousarray(f("c_lin_w")[0].reshape(4, 2, 128, 256).transpose(2, 0, 1, 3).reshape(128, 2048)),
        "wod_out": np.ascontiguousarray(f("od_w_out")[0].reshape(16, 128, 4, 256).transpose(2, 1, 0, 3).reshape(4, 128, 16 * 256)),
        "rel_tab": f("rel_bias_table"), "sink_row": f("d_sinks")[0].reshape(1, 16),
    }
    return shared


def per_core_inputs(inp, c):
    f = lambda k: np.asarray(inp[k], np.float32)
    sl = slice(16 * c, 16 * c + 16)
    return {
        "xs": np.ascontiguousarray(f("x_sample")[sl].reshape(NS, D)),
        "st_ssm": np.ascontiguousarray(f("state_ssm")[0, sl].reshape(16, 1024, 128)),
        "st_conv": np.ascontiguousarray(f("state_conv")[0, sl].reshape(48, 1536)),
        "st_pool": np.ascontiguousarray(f("state_pool")[0, sl].reshape(240, 1024)),
        "st_k": np.ascontiguousarray(f("cache_k_win")[0, sl].reshape(16, 128, 256)),
        "st_v": np.ascontiguousarray(f("cache_v_win")[0, sl].reshape(16, 128, 256)),
    }


_CACHE = {}


def kernel(**inputs):
    cfg = {}
    key = "full"
    if key not in _CACHE:
        _CACHE[key] = build_program(cfg)
    P = _CACHE[key]
    shared = pack_host(inputs)
    xp = np.asarray(inputs["x_prompt"], np.float32)
    in_maps = []
    for c in range(NCORES):
        m = dict(shared)
        m["xp"] = np.ascontiguousarray(xp[c])
        m.update(per_core_inputs(inputs, c))
        in_maps.append(m)
    res = run_bass_kernel_spmd(P.nc, in_maps, core_ids=list(range(NCORES)))
    r = res.results
    y_p = np.stack([r[c]["yp"] for c in range(NCORES)], 0)
    y_s = np.stack([r[c]["ys"] for c in range(NCORES)], 0).reshape(128, 4, D)
    cat = lambda k: np.stack([r[c][k] for c in range(NCORES)], 0)
    av = cat("o_av").reshape(1, 128, 4, D)
    ssm_p = cat("o_ssm_p").reshape(1, 8, 16, 64, 128)
    ssm_s = cat("o_ssm_s").reshape(1, 128, 16, 64, 128)
    conv_p = cat("o_conv_p").reshape(1, 8, 3, 1536)
    conv_s = cat("o_conv_s").reshape(1, 128, 3, 1536)
    pool_p = cat("o_pool_p").reshape(1, 8, 15, 1024)
    pool_s = cat("o_pool_s").reshape(1, 128, 15, 1024)
    k_p = cat("o_k_p").reshape(1, 8, 128, 4, 64)
    k_s = cat("o_k_s").reshape(1, 128, 128, 4, 64)
    v_p = cat("o_v_p").reshape(1, 8, 128, 4, 64)
    v_s = cat("o_v_s").reshape(1, 128, 128, 4, 64)
    return (y_p, y_s, av, ssm_p, ssm_s, conv_p, conv_s, pool_p, pool_s, k_p, k_s, v_p, v_s)
```

```python
import contextlib
import math
import types
import numpy as np
import concourse.bass as bass
import concourse.mybir as mybir
from concourse.bass_utils import run_bass_kernel_spmd

F32 = mybir.dt.float32
BF16 = mybir.dt.bfloat16
I32 = mybir.dt.int32
AF = mybir.ActivationFunctionType
ALU = mybir.AluOpType
AX = mybir.AxisListType

NCORES = 8
D = 1024
DC = 8
DFF = 2816
FC = 22
SEQ = 4096
NTP = 512
NS = 64
EPS = 1e-6
NEG = -1e30


NATIVE_GELU = True
RELAX_SAME_ENGINE = False


def freeze(fn):
    if fn.__closure__ is None:
        return fn
    cells = []
    for c in fn.__closure__:
        try:
            cells.append(types.CellType(c.cell_contents))
        except ValueError:
            cells.append(c)
    return types.FunctionType(fn.__code__, fn.__globals__, fn.__name__, fn.__defaults__, tuple(cells))


class Buf:
    __slots__ = ("name", "w", "r", "excl")

    def __init__(self, name, excl=False):
        self.name = name
        self.w = None
        self.r = {}
        self.excl = excl


class DmaSlot:
    def __init__(self, S, name):
        self.S = S
        self.name = name
        self.sem = S.new_sem("d" + name)
        self.count = 0
        self.last = None

    def next_token(self):
        if self.count >= 16 * 1200:
            self.sem = self.S.new_sem("d" + self.name)
            self.count = 0
        self.count += 16
        self.last = (id(self.sem), self.sem, self.count, "dma")
        return self.last


class Sched:
    ENG = ("pe", "act", "dve", "pool", "sp")
    EPOCH = 6000

    def __init__(self, nc, stack):
        self.nc = nc
        self.stack = stack
        self.nsem = 0
        self.streams = {e: [] for e in self.ENG}
        self.sem = {e: self.new_sem(e) for e in self.ENG}
        self.cnt = {e: 0 for e in self.ENG}
        self.seen = {e: {} for e in self.ENG}
        self.final_tokens = []
        self.pending_dma = []
        self.ninst = 0
        self.oplog = []

    def new_sem(self, name):
        self.nsem += 1
        return self.stack.enter_context(self.nc.semaphore(f"s{self.nsem}_{name}"))

    def _wait(self, eng, tok):
        sid, sem, val, _ = tok
        if self.seen[eng].get(sid, 0) >= val:
            return
        self.seen[eng][sid] = val
        self.streams[eng].append(("wait", sem, val))

    def barrier(self, engines=("pe", "act", "dve", "sp")):
        toks = [(id(self.sem[e]), self.sem[e], self.cnt[e], e) for e in ("pe", "act", "dve", "pool") if self.cnt[e] > 0]
        toks += self.pending_dma
        for e in engines:
            for tok in toks:
                if tok[3] != e:
                    self._wait(e, tok)
        self.pending_dma = []

    def op(self, eng, fns, reads=(), writes=(), dma=None, final=False, arena=False):
        if not isinstance(fns, (list, tuple)):
            fns = [fns]
        fns = [freeze(f) for f in fns]
        writes = list(writes) + [b for b in reads if b.excl]
        reads = [b for b in reads if not b.excl]
        for b in reads:
            if b.w is not None and not (b.w[3] == eng == "pe"):
                self._wait(eng, b.w)
        for b in writes:
            if b.w is not None and not (b.w[3] == eng == "pe") and not (RELAX_SAME_ENGINE and b.w[3] == eng):
                self._wait(eng, b.w)
            for tok in b.r.values():
                if not (tok[3] == eng == "pe") and not (RELAX_SAME_ENGINE and tok[3] == eng):
                    self._wait(eng, tok)
        if dma is not None and dma.last is not None:
            self._wait(eng, dma.last)
        if dma is None:
            if self.cnt[eng] >= self.EPOCH:
                self.sem[eng] = self.new_sem(eng)
                self.cnt[eng] = 0
            self.cnt[eng] += 1
            tok = (id(self.sem[eng]), self.sem[eng], self.cnt[eng], eng)
            inc = 1
        else:
            tok = dma.next_token()
            inc = 16
        self.oplog.append((eng, tok, [x.name for x in reads], [x.name for x in writes], len(self.streams[eng])))
        st = self.streams[eng]
        for fn in fns[:-1]:
            st.append(("inst", fn, None, 0))
        st.append(("inst", fns[-1], tok[1], inc))
        self.ninst += len(fns)
        for b in reads:
            b.r[tok[0]] = tok
        for b in writes:
            b.w = tok
            b.r = {}
        if final:
            self.final_tokens.append(tok)
        if arena and dma is not None:
            self.pending_dma.append(tok)
        return tok

    def emit(self, block):
        nc = self.nc
        for tok in self.final_tokens:
            self._wait("sp", tok)
        streams = self.streams

        def run(engine, items):
            for it in items:
                if it[0] == "wait":
                    engine.wait_ge(it[1], it[2])
                else:
                    r = it[1](engine)
                    if it[2] is not None:
                        r.then_inc(it[2], it[3])

        @block.tensor
        def _(e):
            run(e, streams["pe"])

        @block.scalar
        def _(e):
            run(e, streams["act"])

        @block.vector
        def _(e):
            run(e, streams["dve"])

        @block.gpsimd
        def _(e):
            run(e, streams["pool"])

        @block.sync
        def _(e):
            run(e, streams["sp"])


class Prog:
    def __init__(self, cfg):
        self.cfg = cfg
        self.nc = bass.Bass("TRN2", target_bir_lowering=False)
        self.stack = contextlib.ExitStack()
        self.S = None
        self.dram = {}

    def din(self, name, shape, dtype=F32):
        t = self.nc.dram_tensor(name, list(shape), dtype, kind="ExternalInput")
        self.dram[name] = t
        return t.ap()

    def dout(self, name, shape, dtype=F32):
        t = self.nc.dram_tensor(name, list(shape), dtype, kind="ExternalOutput")
        self.dram[name] = t
        return t.ap()

    def dscratch(self, name, shape, dtype=F32):
        t = self.nc.dram_tensor(name, list(shape), dtype, kind="Internal")
        return t.ap()

    def sb(self, name, shape, dtype=F32):
        return self.stack.enter_context(self.nc.sbuf_tensor(name, list(shape), dtype))

    def ps(self, name, shape, dtype=F32):
        return self.stack.enter_context(self.nc.psum_tensor(name, list(shape), dtype))


WSLOT_ELEMS = 8 * 2 * 256
NWSLOT = 4
ARENA_WORDS = 23 * 1024 + 128


def _prod(xs):
    r = 1
    for x in xs:
        r *= x
    return r


def rs(ap, dims):
    if len(dims) == 1:
        return ap
    if len(dims) == 2:
        return ap.rearrange("p (a b) -> p a b", a=dims[0])
    if len(dims) == 3:
        return ap.rearrange("p (a b c) -> p a b c", a=dims[0], b=dims[1])
    raise ValueError(dims)


class Arena:
    def __init__(self, t, words):
        self.t = t
        self.words = words
        self.off = 0

    def reset(self):
        self.off = 0

    def take(self, nparts, dims, dtype=F32):
        n = _prod(dims)
        w = n if dtype == F32 else (n + 1) // 2
        w = (w + 1) // 2 * 2
        ap = self.t[0:nparts, self.off:self.off + w]
        if dtype != F32:
            ap = ap.bitcast(dtype)
        ap = ap[:, 0:n]
        self.off += w
        assert self.off <= self.words, ("arena overflow", self.off, self.words)
        return rs(ap, dims)


def build_program(cfg):
    P = Prog(cfg)
    nc = P.nc
    n_ptiles = cfg.get("n_ptiles", SEQ // NTP)
    do_sample = cfg.get("sample", True)
    stages = cfg.get("stages", "all")
    nlayers = cfg.get("layers", 2)
    npt_tokens = n_ptiles * NTP

    def on(name):
        return stages == "all" or name in stages

    xp = P.din("xp", [npt_tokens, D])
    xs = P.din("xs", [NS, D])
    pvec = P.din("pvec", [128, PV_COLS])
    cst = P.din("cst", [128, CST_COLS])
    wgu = P.din("wgu", [2, 2, 11, 128, 8 * 2 * 256])
    wdn = P.din("wdn", [2, 2, 8, 128, FC * 128])
    wev_fm = P.din("wev_fm", [7, 128, 8 * 512])
    wev_v = P.din("wev_v", [2, 128, 8 * 512])
    wev_dt = P.din("wev_dt", [128, 8 * 16])
    wev_out = P.din("wev_out", [4, 128, 16 * 256])
    wsT = P.din("wsT", [128, 8 * 128])
    wsT4 = P.din("wsT4", [4, 8 * 4])
    bs_row = P.din("bs_row", [1, 8 * 128])
    lnrow = P.din("lnrow", [1, 2048])
    wod_fm = P.din("wod_fm", [4, 128, 8 * 512])
    wod_k = P.din("wod_k", [128, 8 * 256])
    wod_v = P.din("wod_v", [128, 8 * 256])
    wod_lin = P.din("wod_lin", [128, 4 * 2 * 256])
    wod_out = P.din("wod_out", [4, 128, 16 * 256])
    rel_tab = P.din("rel_tab", [32, 16])
    sink_row = P.din("sink_row", [1, 16])
    st_pool = P.din("st_pool", [240, 1024])
    st_k = P.din("st_k", [16, 128, 256])
    st_v = P.din("st_v", [16, 128, 256])
    dscr = P.dscratch("dscr", [16, 128, 384])
    o_pool_p = P.dout("o_pool_p", [15, 1024])
    o_pool_s = P.dout("o_pool_s", [16, 15, 1024])
    o_k_p = P.dout("o_k_p", [128, 256])
    o_k_s = P.dout("o_k_s", [16, 128, 256])
    o_v_p = P.dout("o_v_p", [128, 256])
    o_v_s = P.dout("o_v_s", [16, 128, 256])
    st_ssm = P.din("st_ssm", [16, 1024, 128])
    st_conv = P.din("st_conv", [48, 1536])
    yp = P.dout("yp", [npt_tokens, D])
    ys = P.dout("ys", [NS, D])
    o_av = P.dout("o_av", [NS, D])
    o_ssm_p = P.dout("o_ssm_p", [1024, 128])
    o_ssm_s = P.dout("o_ssm_s", [16, 1024, 128])
    o_conv_p = P.dout("o_conv_p", [3, 1536])
    o_conv_s = P.dout("o_conv_s", [16, 3, 1536])

    with P.stack:
        S = Sched(nc, P.stack)
        P.S = S
        ident = P.sb("ident", [128, 128], F32)
        identb = P.sb("identb", [128, 128], BF16)
        onesb = P.sb("onesb", [128, 128], BF16)
        pv = P.sb("pv", [128, PV_COLS], F32)
        cs = P.sb("cs", [128, CST_COLS], F32)
        xT = P.sb("xT", [128, DC, NTP], F32)
        xn = P.sb("xn", [128, DC, NTP], BF16)
        hTt = P.sb("hT", [128, FC * NTP], BF16)
        hT = hTt[:, :].rearrange("p (j n) -> p j n", j=FC)
        rstd = P.sb("rstd", [128, NTP], F32)
        sil = [P.sb(f"sil{i}", [128, NTP], F32) for i in range(2)]
        wring = [P.sb(f"wr{i}", [128, WSLOT_ELEMS], BF16) for i in range(NWSLOT)]
        wTm = P.sb("wTm", [128, 8, 128], BF16)
        T2 = P.sb("T2", [128, 8, 128], F32)
        wTms = P.sb("wTms", [64, 8, 64], BF16)
        T2s = P.sb("T2s", [128, 8, 64], F32)
        aneg = P.sb("aneg", [16, 1], F32)
        STf = P.sb("STf", [128, 1024], F32)
        STb = P.sb("STb", [128, 1024], BF16)
        ctail = P.sb("ctail", [128, 12, 3], BF16)
        ctail32 = P.sb("ctail32", [128, 12, 3], F32)
        bd64 = P.sb("bd64", [128, 128], BF16)
        hsq = P.sb("hsq", [128, NTP], BF16)
        hsq2 = P.sb("hsq2", [128, NTP], BF16)
        hrs = P.sb("hrs", [128, NTP], F32)
        esink2 = P.sb("esink2", [128, 8], F32)
        ptail = P.sb("ptail", [128, 8, 15], F32)
        kprev = P.sb("kprev", [128, 4, 128], BF16)
        vprev = P.sb("vprev", [128, 256], BF16)
        arena_t = P.sb("arena", [128, ARENA_WORDS], F32)
        psall = P.ps("psall", [128, 8 * 512], F32)
        A = Arena(arena_t, ARENA_WORDS)
        sq = hT[:, 0:8, :]
        yst = hTt[:, 0:2 * 4 * D].bitcast(F32).rearrange("p (b d) -> p b d", b=4)
        xin = arena_t[:, 0:4 * D].rearrange("p (b d) -> p b d", b=4)

        b_ident = Buf("ident")
        b_pv = Buf("pv")
        b_cs = Buf("cs")
        b_xin = Buf("xin")
        b_xT = [Buf(f"xT{c}") for c in range(DC)]
        b_xn = Buf("xn")
        b_h = [Buf(f"h{j}") for j in range(FC)]
        b_rstd = Buf("rstd")
        b_sil = [Buf("sil0"), Buf("sil1")]
        b_wr = [Buf(f"wr{i}") for i in range(NWSLOT)]
        b_ps = [Buf(f"ps{i}", excl=True) for i in range(8)]
        b_gm = Buf("gmlp_consts")
        b_ST = Buf("ST")
        b_STb = Buf("STb")
        b_ctail = Buf("ctail")
        b_hsq2 = Buf("hsq2")
        b_hsq, b_hrs, b_gm2, b_ptail, b_kprev, b_dscr = Buf("hsq"), Buf("hrs"), Buf("gm2"), Buf("ptail"), Buf("kprev"), Buf("dscr")
        d_wr = [DmaSlot(S, f"wr{i}") for i in range(NWSLOT)]
        d_xin = DmaSlot(S, "xin")
        d_yst = DmaSlot(S, "yst")
        d_misc = DmaSlot(S, "misc")
        d_out = [DmaSlot(S, f"out{i}") for i in range(8)]
        d_in = [DmaSlot(S, f"in{i}") for i in range(8)]

        state = {"w": 0, "ps": 0, "sil": 0, "out": 0, "in": 0}

        def bank(i):
            return psall[:, i * 512:(i + 1) * 512]

        reserved = set()

        def next_ps():
            i = state["ps"]
            while i in reserved:
                i = (i + 1) % 8
            state["ps"] = (i + 1) % 8
            return bank(i), b_ps[i]

        def ps_group(n):
            i = (state["ps"] + n - 1) // n * n % 8
            while any((i + k) in reserved for k in range(n)):
                i = (i + n) % 8
            state["ps"] = (i + n) % 8
            state["last_group"] = list(range(i, i + n))
            return psall[:, i * 512:(i + n) * 512], [b_ps[i + k] for k in range(n)]

        def next_out():
            i = state["out"]
            state["out"] = (i + 1) % 8
            return d_out[i]

        def next_in():
            i = state["in"]
            state["in"] = (i + 1) % 8
            return d_in[i]

        wcache = {}
        d_ws = [DmaSlot(S, f"ws{i}") for i in range(NWSLOT)]
        use_wcache = cfg.get("wcache", True) and (len([1 for _ in range(n_ptiles)]) + (1 if do_sample else 0)) > 1

        def wload(src_ap, nelem):
            i = state["w"]
            state["w"] = (i + 1) % NWSLOT
            dst = wring[i][:, 0:nelem]
            key = (src_ap.tensor.name, src_ap.offset)
            ent = wcache.get(key) if use_wcache else None
            if ent is None:
                S.op("pool", lambda e, dst=dst, src=src_ap: e.dma_start(out=dst, in_=src),
                     reads=(), writes=(b_wr[i],), dma=d_wr[i])
                if use_wcache:
                    scr = P.dscratch(f"wb{len(wcache)}", [128, nelem], BF16)
                    bscr = Buf(f"wb{len(wcache)}")
                    wcache[key] = (scr, bscr)
                    S.op("sp", lambda e, dst=dst, scr=scr: e.dma_start(out=scr, in_=dst),
                         reads=(b_wr[i],), writes=(bscr,), dma=d_ws[i])
            else:
                scr, bscr = ent
                S.op("pool", lambda e, dst=dst, scr=scr: e.dma_start(out=dst, in_=scr),
                     reads=(bscr,), writes=(b_wr[i],), dma=d_wr[i])
            return wring[i], b_wr[i]

        def cst_ap(name, nparts=128):
            o, n = CST_OFF[name]
            return cs[0:nparts, o:o + n]

        def pvc(name, c=0, nparts=128):
            o = PV_OFF[name]
            return pv[0:nparts, o + c:o + c + 1]

        S.op("pool", lambda e: e.memset(ident[:], 0.0), writes=(b_ident,))
        S.op("pool", lambda e: e.affine_select(out=ident[:], in_=ident[:], pattern=[[-1, 128]],
                                               compare_op=ALU.not_equal, fill=1.0, base=0,
                                               channel_multiplier=1),
             reads=(b_ident,), writes=(b_ident,))
        S.op("pool", lambda e: e.tensor_copy(identb[:], ident[:]), reads=(b_ident,), writes=(b_ident,))
        S.op("pool", lambda e: e.memset(onesb[:], 1.0), writes=(b_ident,))
        S.op("pool", lambda e: e.memset(STf[:], 0.0), writes=(b_ST,))
        S.op("pool", lambda e: e.memset(STb[:], 0.0), writes=(b_STb,))
        S.op("pool", lambda e: e.memset(ctail[:], 0.0), writes=(b_ctail,))
        S.op("sp", lambda e: e.dma_start(out=pv[:], in_=pvec), writes=(b_pv,), dma=d_misc)
        S.op("sp", lambda e: e.dma_start(out=cs[:], in_=cst), writes=(b_cs,), dma=next_in())

        if on("even"):
            A.reset()
            b_tmp = Buf("setup_tmp")
            w32 = A.take(128, [8, 128])
            bsbc = A.take(128, [8, 128])
            w32s = A.take(64, [8, 64])
            S.op("sp", lambda e: e.dma_start(out=w32, in_=wsT.rearrange("p (h i) -> p h i", h=8)),
                 writes=(b_tmp,), dma=next_in(), arena=True)
            S.op("sp", lambda e: e.dma_start(out=bsbc.rearrange("p h i -> p (h i)"),
                                             in_=bs_row.partition_broadcast(128)),
                 writes=(b_tmp,), dma=next_in(), arena=True)
            S.op("pool", lambda e: e.affine_select(out=w32, in_=w32, pattern=[[0, 8], [1, 128]],
                                                   compare_op=ALU.is_ge, fill=0.0, base=0,
                                                   channel_multiplier=-1),
                 reads=(b_tmp,), writes=(b_tmp,))
            S.op("pool", lambda e: e.tensor_copy(wTm[:], w32), reads=(b_tmp,), writes=(b_gm,))
            pg, bpg = ps_group(2)
            S.op("pe", [lambda e, k=k: e.matmul(pg[:, k * 512:(k + 1) * 512], onesb[:, :],
                                                 wTm[:, k * 4:(k + 1) * 4, :].rearrange("p h i -> p (h i)"),
                                                 start=True, stop=True) for k in range(2)],
                 reads=(b_gm, b_ident), writes=bpg)
            for h in range(8):
                S.op("dve", lambda e, h=h: e.scalar_tensor_tensor(
                    out=T2[:, h, :], in0=pg[:, h * 128:(h + 1) * 128], scalar=pvc("ln_b", h),
                    in1=bsbc[:, h, :], op0=ALU.mult, op1=ALU.add),
                    reads=bpg + [b_tmp, b_pv], writes=(b_gm,))
            SST = cfg.get("setup_stop", 99)
            S.op("pool", lambda e: e.memset(w32s, 0.0), writes=(b_tmp,))
            for b in range(16 if SST > 1 else 0):
                S.op("sp", lambda e, b=b: e.dma_start(out=w32s[4 * b:4 * b + 4, :, 4 * b:4 * b + 4],
                                                      in_=wsT4.rearrange("p (h i) -> p h i", h=8)),
                     writes=(b_tmp,), dma=next_in(), arena=True)
            S.op("pool", lambda e: e.affine_select(out=w32s, in_=w32s, pattern=[[0, 8], [1, 64]],
                                                   compare_op=ALU.is_ge, fill=0.0, base=0,
                                                   channel_multiplier=-1),
                 reads=(b_tmp,), writes=(b_tmp,))
            S.op("pool", lambda e: e.tensor_copy(wTms[:], w32s), reads=(b_tmp,), writes=(b_gm,))
            pg2, bpg2 = next_ps()
            S.op("pe", lambda e: e.matmul(pg2[:, 0:512], onesb[0:64, :],
                                          wTms[:, :, :].rearrange("p h i -> p (h i)"), start=True, stop=True),
                 reads=(b_gm, b_ident), writes=(bpg2,))
            for h in range(8 if SST > 2 else 0):
                S.op("dve", lambda e, h=h: e.scalar_tensor_tensor(
                    out=T2s[:, h, :].rearrange("p (b i) -> p b i", i=4),
                    in0=pg2[:, h * 64:(h + 1) * 64].rearrange("p (b i) -> p b i", i=4),
                    scalar=pvc("ln_b", h),
                    in1=bsbc[:, h, 0:4].unsqueeze(1).broadcast_to([128, 16, 4]),
                    op0=ALU.mult, op1=ALU.add),
                    reads=(bpg2, b_tmp, b_pv), writes=(b_gm,))
            S.op("act", lambda e: e.activation(out=aneg[:, :], in_=pvc("a_log", 0, 16), func=AF.Exp),
                 reads=(b_pv,), writes=(b_gm,))
            S.op("dve", lambda e: e.tensor_scalar(out=aneg[:, :], in0=aneg[:, :], scalar1=-1.0, scalar2=None,
                                                  op0=ALU.mult), reads=(b_gm,), writes=(b_gm,))
            S.barrier()

        if on("odd"):
            A.reset()
            b_tmp2 = Buf("setup_tmp2")
            S.op("pool", lambda e: e.memset(bd64[:], 0.0), writes=(b_ident,))
            S.op("pool", lambda e: e.memset(bd64[0:64, 0:64], 1.0), writes=(b_ident,))
            S.op("pool", lambda e: e.memset(bd64[64:128, 64:128], 1.0), writes=(b_ident,))
            S.op("pool", lambda e: e.memset(ptail[:], 0.0), writes=(b_ptail,))
            S.op("pool", lambda e: e.memset(kprev[:], 0.0), writes=(b_kprev,))
            S.op("pool", lambda e: e.memset(vprev[:], 0.0), writes=(b_kprev,))
            es = A.take(128, [16])
            rt = A.take(32, [16])
            dv = A.take(16, [384])
            S.op("sp", lambda e: e.dma_start(out=es, in_=sink_row.partition_broadcast(128)), writes=(b_tmp2,), dma=next_in(), arena=True)
            S.op("sp", lambda e: e.dma_start(out=rt, in_=rel_tab), writes=(b_tmp2,), dma=next_in(), arena=True)
            S.op("act", lambda e: e.activation(out=es, in_=es, func=AF.Exp), reads=(b_tmp2,), writes=(b_tmp2,))
            esv = es.rearrange("p (c t) -> p c t", t=2)
            S.op("dve", lambda e: e.tensor_copy(esink2[0:64, :], esv[0:64, :, 0]), reads=(b_tmp2,), writes=(b_gm2,))
            S.op("dve", lambda e: e.tensor_copy(esink2[64:128, :], esv[64:128, :, 1]), reads=(b_tmp2,), writes=(b_gm2,))
            pgd, bpgd = next_ps()
            S.op("pe", lambda e: e.matmul(pgd[0:16, 0:384], rt[:, :], cst_ap("onehot", 32), start=True, stop=True),
                 reads=(b_tmp2, b_cs), writes=(bpgd,))
            S.op("dve", lambda e: e.tensor_tensor(out=dv, in0=pgd[0:16, 0:384], in1=cst_ap("negmask", 16), op=ALU.add),
                 reads=(bpgd, b_cs), writes=(b_tmp2,))
            S.op("sp", lambda e: e.dma_start(out=dscr, in_=dv.unsqueeze(1).broadcast_to([16, 128, 384])),
                 reads=(b_tmp2,), writes=(b_dscr,), dma=next_in(), arena=True)
            S.barrier()

        def load_x_tile(src_rows, nblk, rows):
            S.op("sp", lambda e: e.dma_start(out=xin[0:rows, 0:nblk, :],
                                             in_=src_rows.rearrange("(b p) d -> p b d", p=rows)),
                 writes=(b_xin,), dma=d_xin, arena=True)

        def transpose_in(nblk, rows):
            nt = nblk * rows
            for c in range(DC):
                pt, bp = next_ps()
                fns = []
                for blk in range(nblk):
                    fns.append(lambda e, pt=pt, blk=blk, c=c: e.transpose(
                        pt[:, blk * rows:(blk + 1) * rows], xin[0:rows, blk, c * 128:(c + 1) * 128],
                        ident[0:rows, 0:rows]))
                S.op("pe", fns, reads=(b_xin, b_ident), writes=(bp,))
                if c % 2:
                    S.op("act", lambda e, pt=pt, c=c: e.copy(xT[:, c, 0:nt], pt[:, 0:nt]),
                         reads=(bp,), writes=(b_xT[c],))
                else:
                    S.op("dve", lambda e, pt=pt, c=c: e.tensor_copy(xT[:, c, 0:nt], pt[:, 0:nt]),
                         reads=(bp,), writes=(b_xT[c],))

        def transpose_out(dst_rows, nblk, rows):
            for blk in range(nblk):
                for half in range(2):
                    pt, bp = next_ps()
                    fns = []
                    for cc in range(4):
                        c = half * 4 + cc
                        fns.append(lambda e, pt=pt, blk=blk, c=c, cc=cc: e.transpose(
                            pt[0:rows, cc * 128:(cc + 1) * 128], xT[:, c, blk * rows:(blk + 1) * rows],
                            ident[:, :]))
                    S.op("pe", fns, reads=[b_xT[half * 4 + cc] for cc in range(4)] + [b_ident],
                         writes=(bp,))
                    if half:
                        S.op("act", lambda e, pt=pt, blk=blk: e.copy(yst[0:rows, blk, 512:1024], pt[0:rows, :]),
                             reads=(bp,), writes=b_h[0:16])
                    else:
                        S.op("dve", lambda e, pt=pt, blk=blk: e.tensor_copy(yst[0:rows, blk, 0:512], pt[0:rows, :]),
                             reads=(bp,), writes=b_h[0:16])
            S.op("sp", lambda e: e.dma_start(out=dst_rows.rearrange("(b p) d -> p b d", p=rows),
                                             in_=yst[0:rows, 0:nblk, :]),
                 reads=b_h[0:16], dma=d_yst, final=True)

        def rms_norm(gname, nt):
            S.op("act", lambda e: e.activation(out=sq[:, :, 0:nt], in_=xT[:, :, 0:nt], func=AF.Square),
                 reads=b_xT, writes=b_h[0:8])
            pt, bp = next_ps()
            S.op("pe", [lambda e, pt=pt, c=c: e.matmul(pt[:, 0:nt], onesb[:, :], sq[:, c, 0:nt],
                                                        start=(c == 0), stop=(c == DC - 1))
                        for c in range(DC)], reads=b_h[0:8] + [b_ident], writes=(bp,))
            S.op("act", lambda e, pt=pt: e.activation(out=rstd[:, 0:nt], in_=pt[:, 0:nt], func=AF.Sqrt,
                                                       bias=EPS, scale=1.0 / D),
                 reads=(bp,), writes=(b_rstd,))
            S.op("dve", lambda e: e.reciprocal(rstd[:, 0:nt], rstd[:, 0:nt]), reads=(b_rstd,), writes=(b_rstd,))
            for c in range(DC):
                S.op("dve", lambda e, c=c: e.scalar_tensor_tensor(
                    out=xn[:, c, 0:nt], in0=xT[:, c, 0:nt], scalar=pvc(gname, c),
                    in1=rstd[:, 0:nt], op0=ALU.mult, op1=ALU.mult),
                    reads=(b_xT[c], b_rstd, b_pv), writes=(b_xn,))

        def ffn(layer, which, nt):
            rms_norm(f"ffn_norm{layer}{which}", nt)
            for grp in range(11):
                w, bw = wload(wgu[layer, which, grp], 8 * 2 * 256)
                wv = w[:, :].rearrange("p (k g n) -> p k g n", k=8, g=2)
                for jj in range(2):
                    j = grp * 2 + jj
                    pg, bpg = next_ps()
                    S.op("pe", [lambda e, pg=pg, k=k, jj=jj, wv=wv: e.matmul(
                        pg[:, 0:nt], wv[:, k, 0, jj * 128:(jj + 1) * 128], xn[:, k, 0:nt],
                        start=(k == 0), stop=(k == 7)) for k in range(8)],
                        reads=(bw, b_xn), writes=(bpg,))
                    pu, bpu = next_ps()
                    S.op("pe", [lambda e, pu=pu, k=k, jj=jj, wv=wv: e.matmul(
                        pu[:, 0:nt], wv[:, k, 1, jj * 128:(jj + 1) * 128], xn[:, k, 0:nt],
                        start=(k == 0), stop=(k == 7)) for k in range(8)],
                        reads=(bw, b_xn), writes=(bpu,))
                    si = state["sil"]
                    state["sil"] = 1 - si
                    S.op("act", lambda e, pg=pg, si=si: e.activation(out=sil[si][:, 0:nt], in_=pg[:, 0:nt],
                                                                      func=AF.Silu),
                         reads=(bpg,), writes=(b_sil[si],))
                    S.op("dve", lambda e, pu=pu, si=si, j=j: e.tensor_tensor(
                        out=hT[:, j, 0:nt], in0=pu[:, 0:nt], in1=sil[si][:, 0:nt], op=ALU.mult),
                        reads=(bpu, b_sil[si]), writes=(b_h[j],))
            for m in range(DC):
                w, bw = wload(wdn[layer, which, m], FC * 128)
                wv = w[:, 0:FC * 128].rearrange("p (j n) -> p j n", j=FC)
                py, bpy = next_ps()
                S.op("pe", [lambda e, py=py, j=j, wv=wv: e.matmul(
                    py[:, 0:nt], wv[:, j, :], hT[:, j, 0:nt], start=(j == 0), stop=(j == FC - 1))
                    for j in range(FC)], reads=[bw] + b_h, writes=(bpy,))
                S.op("dve", lambda e, py=py, m=m: e.scalar_tensor_tensor(
                    out=xT[:, m, 0:nt], in0=py[:, 0:nt], scalar=0.5, in1=xT[:, m, 0:nt],
                    op0=ALU.mult, op1=ALU.add), reads=(bpy, b_xT[m]), writes=(b_xT[m],))

        def gelu_evac(pt, np_, nf, out_ap, bp, wbufs):
            if NATIVE_GELU:
                S.op("act", lambda e: e.activation(out=out_ap, in_=pt, func=AF.Gelu_apprx_tanh), reads=(bp,), writes=wbufs)
                return
            si = state["sil"]
            state["sil"] = 1 - si
            t1 = sil[si][0:np_, 0:nf]
            S.op("act", lambda e: e.activation(out=t1, in_=pt, func=AF.Square), reads=(bp,), writes=(b_sil[si],))
            S.op("dve", lambda e: e.tensor_scalar(out=t1, in0=t1, scalar1=0.044715, scalar2=1.0, op0=ALU.mult, op1=ALU.add),
                 reads=(b_sil[si],), writes=(b_sil[si],))
            S.op("dve", lambda e: e.tensor_tensor(out=t1, in0=t1, in1=pt, op=ALU.mult), reads=(b_sil[si], bp), writes=(b_sil[si],))
            S.op("act", lambda e: e.activation(out=t1, in_=t1, func=AF.Sigmoid, scale=1.5957691216057308),
                 reads=(b_sil[si],), writes=(b_sil[si],))
            S.op("dve", lambda e: e.tensor_tensor(out=out_ap, in0=t1, in1=pt, op=ALU.mult), reads=(b_sil[si], bp), writes=wbufs)

        def proj_fm(wsrc, ncc, nt, evac):
            w, bw = wload(wsrc, 8 * 512)
            wv = w[:, :].rearrange("p (k n) -> p k n", k=8)
            pending = None
            for cc in range(ncc):
                pt, bp = next_ps()
                S.op("pe", [lambda e, pt=pt, k=k, cc=cc, wv=wv: e.matmul(
                    pt[:, 0:nt], wv[:, k, cc * 128:(cc + 1) * 128], xn[:, k, 0:nt],
                    start=(k == 0), stop=(k == 7)) for k in range(8)], reads=(bw, b_xn), writes=(bp,))
                if pending is not None:
                    pending()
                pending = evac(cc, pt, bp)
            if pending is not None:
                pending()

        def even_mixer(kind, nt, is_last_ptile):
            STOP = cfg.get("even_stop", 99)
            if STOP <= 0:
                return
            S.barrier()
            A.reset()
            smp = (kind == "s")
            CL = 64 if smp else 128
            nblk = nt // CL
            Lc = 4 if smp else 128
            nch = nt // Lc
            uT = hT[:, 0:8, :]
            yaT = hT[:, 8:16, :]
            bcT = hT[:, 16:20, :]
            b_u, b_ya, b_bc = b_h[0:8], b_h[8:16], b_h[16:20]
            zT = A.take(128, [8, nt], BF16)
            ybT = A.take(128, [8, nt], BF16)
            xcT = A.take(128, [8, nt], BF16)
            ext = A.take(128, [12, (16 * 7) if smp else (NTP + 3)], BF16)
            cvA = A.take(128, [nt])
            cvB = A.take(128, [nt])
            vg = [A.take(128, [1024]) for _ in range(1 if smp else 2)] * (2 if smp else 1)
            vhb = [A.take(128, [1024], BF16) for _ in range(1 if smp else 2)] * (2 if smp else 1)
            mvst = A.take(128, [2, 6])
            mv = A.take(128, [2])
            gt = None
            dsc = A.take(16, [4, nt])
            tsc = A.take(128, [64])
            absx = A.take(128, [16, CL])
            Eb = A.take(128, [16, CL], BF16)
            Csb = A.take(128, [16, CL], BF16)
            cbm = A.take(128, [2, CL])
            xd = A.take(128, [16, 64], BF16)
            xdd = A.take(128, [16, 64], BF16)
            Btok = A.take(128, [2, 128], BF16)
            ygb = A.take(128, [8, 128])
            sqg = A.take(128, [8, 128], BF16)
            rsg = A.take(128, [2, 128])
            bz, byb, bxc, bext, bcv = Buf("zT"), Buf("ybT"), Buf("xcT"), Buf("ext"), [Buf("cvA"), Buf("cvB")]
            bvg, bvh, bmv, bgt = [Buf("vg0"), Buf("vg1")], [Buf("vh0"), Buf("vh1")], Buf("mv"), [Buf("gt0"), Buf("gt1")]
            bdsc, btsc, babs, bEb, bCs, bcbm = Buf("dsc"), Buf("tsc"), Buf("absx"), Buf("Eb"), Buf("Csb"), Buf("cbm")
            bxd, bxdd, bBt, bygb, bsqg, brsg = Buf("xd"), Buf("xdd"), Buf("Btok"), Buf("ygb"), Buf("sqg"), Buf("rsg")
            if smp:
                xpre = A.take(128, [12, 64])
                cso = A.take(64, [1536])
                stc = cso[0:48, :]
                lnbc = A.take(128, [2048])
                vln = A.take(64, [1024])
                bmsk = cst_ap("bmask", 64)
                Bblk = A.take(64, [2, 16, 128], BF16)
                cdT = A.take(128, [8, 16])
                eal = A.take(16, [16])
                Snat = A.take(128, [2, 8, 128])
                STs = A.take(128, [2, 1024], BF16)
                bxpre, bcso, blnbc, bvln = Buf("xpre"), Buf("cso"), Buf("lnbc"), Buf("vln")
                bstc = bcso
                bBblk, bcdT, beal, bSnat, bSTs = Buf("Bblk"), Buf("cdT"), Buf("eal"), Buf("Snat"), Buf("STs")
                S.op("sp", lambda e: e.dma_start(out=lnbc, in_=lnrow.partition_broadcast(128)),
                     writes=(blnbc,), dma=next_in(), arena=True)
                S.op("sp", lambda e: e.dma_start(out=stc, in_=st_conv), writes=(bstc,), dma=next_in(), arena=True)

            rms_norm(f"mix_norm{0}", nt)
            if STOP <= 0.5:
                return

            for gi in range(2):
                def ev_u(cc, pt, bp, gi=gi):
                    c = gi * 4 + cc
                    gelu_evac(pt[:, 0:nt], 128, nt, uT[:, c, 0:nt], bp, (b_u[c],))
                proj_fm(wev_fm[gi], 4, nt, ev_u)

            if STOP <= 1:
                return
            wv0, bwv0 = wload(wev_v[0], 8 * 512)
            wv1, bwv1 = wload(wev_v[1], 8 * 512)
            wvv = [wv0[:, :].rearrange("p (k n) -> p k n", k=8), wv1[:, :].rearrange("p (k n) -> p k n", k=8)]
            bwv = [bwv0, bwv1]
            wg = wTms if smp else wTm
            T2x = T2s if smp else T2
            def stage_v(blk):
                cols = slice(blk * CL, (blk + 1) * CL)
                bi = blk % 2
                for half in range(2):
                    pt, bp = next_ps()
                    S.op("pe", [lambda e, pt=pt, k=k, half=half: e.matmul(
                        pt[0:CL, :], xn[:, k, cols], wvv[half][:, k, :], start=(k == 0), stop=(k == 7))
                        for k in range(8)], reads=(bwv[half], b_xn), writes=(bp,))
                    gelu_evac(pt[0:CL, :], CL, 512, vg[bi][0:CL, half * 512:(half + 1) * 512], bp, (bvg[bi],))
                S.op("dve", [lambda e, bi=bi: e.bn_stats(mvst[0:CL, 0, :], vg[bi][0:CL, 0:512]),
                             lambda e, bi=bi: e.bn_stats(mvst[0:CL, 1, :], vg[bi][0:CL, 512:1024])],
                     reads=(bvg[bi],), writes=(bmv,))
                S.op("dve", lambda e: e.bn_aggr(mv[0:CL, :], mvst[0:CL, :, :].rearrange("p a b -> p (a b)")),
                     reads=(bmv,), writes=(bmv,))
                S.op("act", lambda e: e.activation(out=mv[0:CL, 1:2], in_=mv[0:CL, 1:2], func=AF.Sqrt, bias=EPS, scale=1.0),
                     reads=(bmv,), writes=(bmv,))
                S.op("dve", lambda e: e.reciprocal(mv[0:CL, 1:2], mv[0:CL, 1:2]), reads=(bmv,), writes=(bmv,))
                S.op("dve", lambda e, bi=bi: e.tensor_scalar(
                    out=vhb[bi][0:CL, :], in0=vg[bi][0:CL, :], scalar1=mv[0:CL, 0:1], scalar2=mv[0:CL, 1:2],
                    op0=ALU.subtract, op1=ALU.mult), reads=(bmv, bvg[bi]), writes=(bvh[bi],))
                if smp:
                    S.op("dve", lambda e, bi=bi: e.tensor_scalar(
                        out=vln[:, :], in0=vg[bi][0:CL, :], scalar1=mv[0:CL, 0:1], scalar2=mv[0:CL, 1:2],
                        op0=ALU.subtract, op1=ALU.mult), reads=(bmv, bvg[bi]), writes=(bvln,))
                    S.op("dve", lambda e: e.tensor_tensor(out=vln[:, :], in0=vln[:, :], in1=lnbc[0:64, 0:1024], op=ALU.mult),
                         reads=(bvln, blnbc), writes=(bvln,))
                    S.op("dve", lambda e: e.tensor_tensor(out=vln[:, :], in0=vln[:, :], in1=lnbc[0:64, 1024:2048], op=ALU.add),
                         reads=(bvln, blnbc), writes=(bvln,))
                    S.op("sp", lambda e: e.dma_start(out=o_av, in_=vln[:, :]), reads=(bvln,), dma=next_out(),
                         final=True, arena=True)

            def stage_g(blk):
                cols = slice(blk * CL, (blk + 1) * CL)
                bi = blk % 2
                for hg in range(2):
                    pt, bp = next_ps()
                    S.op("pe", [lambda e, pt=pt, hh=hh, hg=hg, bi=bi: e.matmul(
                        pt[:, hh * CL:(hh + 1) * CL], vhb[bi][0:CL, (hg * 4 + hh) * 128:(hg * 4 + hh + 1) * 128],
                        wg[0:CL, hg * 4 + hh, 0:CL], start=True, stop=True) for hh in range(4)],
                        reads=(bvh[bi], b_gm), writes=(bp,))
                    hs = slice(hg * 4, hg * 4 + 4)
                    gtb = ygb[:, 0:4, 0:CL]
                    o_lng = PV_OFF["ln_g"]
                    S.op("dve", lambda e, pt=pt, hg=hg: e.tensor_tensor(
                        out=gtb, in0=pt[:, 0:4 * CL].rearrange("p (h i) -> p h i", h=4),
                        in1=pv[:, o_lng + hg * 4:o_lng + hg * 4 + 4].unsqueeze(2).broadcast_to([128, 4, CL]), op=ALU.mult),
                        reads=(bp, b_pv), writes=(bygb,))
                    S.op("dve", lambda e, hs=hs: e.tensor_tensor(out=gtb, in0=gtb, in1=T2x[:, hs, 0:CL], op=ALU.add),
                         reads=(bygb, b_gm), writes=(bygb,))
                    S.op("dve", lambda e, hs=hs: e.tensor_tensor(out=yaT[:, hs, cols], in0=gtb, in1=uT[:, hs, cols], op=ALU.mult),
                         reads=[bygb] + b_u[hg * 4:hg * 4 + 4], writes=b_ya[hg * 4:hg * 4 + 4])


            stage_v(0)
            for blk in range(nblk):
                if blk + 1 < nblk:
                    stage_v(blk + 1)
                stage_g(blk)

            if STOP <= 2:
                return
            for gi in range(2):
                def ev_z(cc, pt, bp, gi=gi):
                    c = gi * 4 + cc
                    if cfg.get("zmode", 0) == 0:
                        S.op("act", lambda e: e.activation(out=zT[:, c, 0:nt], in_=pt[:, 0:nt], func=AF.Silu),
                             reads=(bp,), writes=(bz,))
                    elif cfg.get("zmode", 0) == 1:
                        S.op("act", lambda e: e.activation(out=ybT[:, c, 0:nt], in_=pt[:, 0:nt], func=AF.Silu),
                             reads=(bp,), writes=(bz,))
                proj_fm(wev_fm[2 + gi], 4, nt, ev_z)
            if STOP <= 2.2:
                return
            if smp:
                extv = ext.rearrange("p c (b r) -> p c b r", r=7)
                pg, bpg = ps_group(2)
                S.op("pe", [lambda e, c=c: e.transpose(pg[:, c * 64:c * 64 + 48], stc[0:48, c * 128:(c + 1) * 128],
                                                        ident[0:48, 0:48]) for c in range(12)],
                     reads=(bstc, b_ident), writes=bpg)
                S.op("dve", lambda e: e.tensor_copy(
                    extv[:, :, :, 0:3],
                    pg[:, 0:768].rearrange("p (c x) -> p c x", c=12)[:, :, 0:48].rearrange("p c (b r) -> p c b r", r=3)),
                     reads=bpg, writes=(bext,))
            else:
                S.op("dve", lambda e: e.tensor_copy(ext[:, :, 0:3], ctail[:, :, :]), reads=(b_ctail,), writes=(bext,))
            XM = cfg.get("xmode", 9)
            for gi in range(3 if XM >= 1 else 0):
                def ev_x(cc, pt, bp, gi=gi):
                    c = gi * 4 + cc
                    if XM < 2:
                        return
                    if smp:
                        S.op("act", lambda e: e.copy(extv[:, c, :, 3:7], pt[:, 0:64].rearrange("p (b i) -> p b i", i=4)),
                             reads=(bp,), writes=(bext,))
                        S.op("dve", lambda e: e.tensor_copy(xpre[:, c, :], pt[:, 0:64]), reads=(bp,), writes=(bxpre,))
                    else:
                        S.op("act", lambda e: e.copy(ext[:, c, 3:3 + nt], pt[:, 0:nt]), reads=(bp,), writes=(bext,))
                        if is_last_ptile and XM >= 3:
                            S.op("dve", lambda e: e.tensor_copy(ctail32[:, c, :], pt[:, nt - 3:nt]),
                                 reads=(bp,), writes=(b_ctail,))
                proj_fm(wev_fm[4 + gi], 4, nt, ev_x)
            if STOP <= 2.4:
                return
            wd, bwd = wload(wev_dt, 8 * 16)
            wdv = wd[:, 0:128].rearrange("p (k n) -> p k n", k=8)
            pt, bp = next_ps()
            S.op("pe", [lambda e, k=k: e.matmul(pt[0:16, 0:nt], wdv[:, k, :], xn[:, k, 0:nt],
                                                 start=(k == 0), stop=(k == 7)) for k in range(8)],
                 reads=(bwd, b_xn), writes=(bp,))
            S.op("act", lambda e: e.activation(out=dsc[:, 0, 0:nt], in_=pt[0:16, 0:nt], func=AF.Exp,
                                               bias=pvc("dt_bias", 0, 16), scale=1.0),
                 reads=(bp, b_pv), writes=(bdsc,))
            if STOP <= 2.6:
                return
            S.op("act", lambda e: e.activation(out=dsc[:, 0, 0:nt], in_=dsc[:, 0, 0:nt], func=AF.Ln, bias=1.0, scale=1.0),
                 reads=(bdsc,), writes=(bdsc,))
            S.op("dve", lambda e: e.tensor_scalar(out=dsc[:, 1, 0:nt], in0=dsc[:, 0, 0:nt], scalar1=aneg[:, 0:1],
                                                  scalar2=None, op0=ALU.mult), reads=(bdsc, b_gm), writes=(bdsc,))
            if STOP <= 2.8:
                return
            rmask = cst_ap("rmask_s", 16)[:, 0:nt] if smp else cst_ap("rmask_p", 16)[:, 0:nt]
            S.op("dve", lambda e: e.tensor_tensor_scan(out=dsc[:, 2, 0:nt], data0=rmask, data1=dsc[:, 1, 0:nt],
                                                       initial=0.0, op0=ALU.mult, op1=ALU.add),
                 reads=(bdsc, b_cs), writes=(bdsc,))
            S.op("dve", lambda e: e.tensor_copy(
                dsc[:, 3, 0:nt].rearrange("p (c l) -> p c l", l=Lc),
                dsc[:, 2, 0:nt].rearrange("p (c l) -> p c l", l=Lc)[:, :, Lc - 1:Lc].broadcast_to([16, nch, Lc])),
                reads=(bdsc,), writes=(bdsc,))

            if STOP <= 3:
                return
            for c in range(12):
                if smp:
                    src = lambda tap, c=c: extv[:, c, :, tap:tap + 4]
                    v3 = lambda ap: ap[:, 0:64].rearrange("p (b i) -> p b i", i=4)
                else:
                    src = lambda tap, c=c: ext[:, c, tap:tap + nt]
                    v3 = lambda ap: ap[:, 0:nt]
                S.op("dve", lambda e, c=c, src=src, v3=v3: e.tensor_scalar(
                    out=v3(cvA), in0=src(0), scalar1=pvc("conv_w", c * 4 + 0), scalar2=pvc("conv_b", c),
                    op0=ALU.mult, op1=ALU.add), reads=(bext, b_pv), writes=(bcv[0],))
                S.op("dve", lambda e, c=c, src=src, v3=v3: e.scalar_tensor_tensor(
                    out=v3(cvB), in0=src(1), scalar=pvc("conv_w", c * 4 + 1), in1=v3(cvA),
                    op0=ALU.mult, op1=ALU.add), reads=(bext, b_pv, bcv[0]), writes=(bcv[1],))
                S.op("dve", lambda e, c=c, src=src, v3=v3: e.scalar_tensor_tensor(
                    out=v3(cvA), in0=src(2), scalar=pvc("conv_w", c * 4 + 2), in1=v3(cvB),
                    op0=ALU.mult, op1=ALU.add), reads=(bext, b_pv, bcv[1]), writes=(bcv[0],))
                S.op("dve", lambda e, c=c, src=src, v3=v3: e.scalar_tensor_tensor(
                    out=v3(cvB), in0=src(3), scalar=pvc("conv_w", c * 4 + 3), in1=v3(cvA),
                    op0=ALU.mult, op1=ALU.add), reads=(bext, b_pv, bcv[0]), writes=(bcv[1],))
                if c < 8:
                    S.op("act", lambda e, c=c: e.activation(out=xcT[:, c, 0:nt], in_=cvB[:, 0:nt], func=AF.Silu),
                         reads=(bcv[1],), writes=(bxc,))
                else:
                    S.op("act", lambda e, c=c: e.activation(out=bcT[:, c - 8, 0:nt], in_=cvB[:, 0:nt], func=AF.Silu),
                         reads=(bcv[1],), writes=b_bc)
            if not smp:
                S.op("dve", lambda e: e.tensor_copy(ctail[:, :, :], ext[:, :, nt:nt + 3]), reads=(bext,), writes=(b_ctail,))
                if is_last_ptile:
                    for r in range(3):
                        S.op("sp", lambda e, r=r: e.dma_start(
                            out=o_conv_p[r:r + 1, :].rearrange("r (c p) -> p (r c)", p=128),
                            in_=ctail32[:, :, r], allow_slow_non_contiguous=True),
                            reads=(b_ctail,), dma=next_out(), final=True)
            else:
                pg, bpg = ps_group(4)
                S.op("pe", [lambda e, c=c: e.transpose(pg[0:64, c * 128:(c + 1) * 128], xpre[:, c, :], ident[:, :])
                            for c in range(12)], reads=(bxpre, b_ident), writes=bpg)
                S.op("act", lambda e: e.copy(cso[:, :], pg[0:64, 0:1536]), reads=bpg, writes=(bcso,))
                for r in range(3):
                    S.op("sp", lambda e, r=r: e.dma_start(out=o_conv_s[:, r, :], in_=cso[1 + r:64:4, :]),
                         reads=(bcso,), dma=next_out(), final=True, arena=True)

            if STOP <= 4:
                return
            if smp:
                S.op("act", lambda e: e.activation(out=eal[:, :].unsqueeze(2), in_=dsc[:, 2, 0:64].rearrange("p (b i) -> p b i", i=4)[:, :, 3:4],
                                                   func=AF.Exp), reads=(bdsc,), writes=(beal,))
                pt, bp = next_ps()
                S.op("pe", [lambda e, h=h: e.matmul(
                    pt[:, h * 16:(h + 1) * 16], ident[0:16, h:h + 1].broadcast_to([16, 128]),
                    eal[:, :], start=True, stop=True) for h in range(16)], reads=(beal, b_ident), writes=(bp,))
                ptv = pt[:, 0:256].rearrange("p (c t b) -> p c t b", c=8, t=2)
                S.op("dve", lambda e: e.tensor_copy(cdT[0:64, :, :], ptv[0:64, :, 0, :]), reads=(bp,), writes=(bcdT,))
                S.op("dve", lambda e: e.tensor_copy(cdT[64:128, :, :], ptv[64:128, :, 1, :]), reads=(bp,), writes=(bcdT,))
            maskT = cst_ap("maskT_blk", 64) if smp else cst_ap("maskT_causal", 128)
            for blk in range(cfg.get("nssd", nblk) if not smp else nblk):
                cols = slice(blk * CL, (blk + 1) * CL)
                pt, bp = next_ps()
                S.op("pe", [lambda e, q=q: e.transpose(pt[0:CL, q * 16:(q + 1) * 16], dsc[:, (0, 2, 3)[q], cols],
                                                        ident[0:16, 0:16]) for q in range(3)],
                     reads=(bdsc, b_ident), writes=(bp,))
                S.op("dve", lambda e, pt=pt: e.tensor_copy(tsc[0:CL, 0:48], pt[0:CL, 0:48]), reads=(bp,), writes=(btsc,))
                S.op("dve", lambda e: e.tensor_tensor(out=tsc[0:CL, 48:64], in0=tsc[0:CL, 32:48], in1=tsc[0:CL, 16:32],
                                                      op=ALU.subtract), reads=(btsc,), writes=(btsc,))
                S.op("act", lambda e: e.activation(out=tsc[0:CL, 48:64], in_=tsc[0:CL, 48:64], func=AF.Exp),
                     reads=(btsc,), writes=(btsc,))
                if cfg.get('sstage', 99) <= 1:
                    continue
                pa, bpa = ps_group(4)
                S.op("pe", [lambda e, h=h: e.matmul(pa[:, h * CL:(h + 1) * CL],
                                                     ident[0:16, h:h + 1].broadcast_to([16, 128]),
                                                     dsc[:, 2, cols], start=True, stop=True) for h in range(16)],
                     reads=(bdsc, b_ident), writes=bpa)
                pav = pa[:, 0:16 * CL].rearrange("p (h i) -> p h i", h=16)
                S.op("dve", lambda e: e.tensor_tensor(
                    out=absx[0:CL, :, 0:CL], in0=pav[0:CL, :, :],
                    in1=tsc[0:CL, 16:32].unsqueeze(2).broadcast_to([CL, 16, CL]), op=ALU.subtract),
                    reads=bpa + [btsc], writes=(babs,))
                S.op("dve", lambda e: e.tensor_scalar(out=absx[0:CL, :, 0:CL], in0=absx[0:CL, :, 0:CL], scalar1=0.0, scalar2=None,
                                                      op0=ALU.min), reads=(babs,), writes=(babs,))
                S.op("act", lambda e: e.activation(out=Eb[0:CL, :, 0:CL], in_=absx[0:CL, :, 0:CL], func=AF.Exp),
                     reads=(babs,), writes=(bEb,))
                S.op("act", lambda e: e.activation(out=absx[:, :, 0:CL], in_=pav, func=AF.Exp),
                     reads=bpa, writes=(babs,))
                if cfg.get('sstage', 99) <= 2:
                    continue
                pc, bpc = next_ps()
                S.op("pe", [lambda e, g=g: e.matmul(pc[0:CL, g * CL:(g + 1) * CL], bcT[:, g, cols], bcT[:, 2 + g, cols],
                                                     start=True, stop=True) for g in range(2)],
                     reads=b_bc, writes=(bpc,))
                if cfg.get("cbx", 9) >= 1:
                  S.op("dve", lambda e, pc=pc: e.tensor_tensor(
                    out=cbm[0:CL, :, 0:CL], in0=pc[0:CL, 0:2 * CL].rearrange("p (g i) -> p g i", g=2),
                    in1=maskT[:, 0:CL].unsqueeze(1).broadcast_to([CL, 2, CL]), op=ALU.mult),
                    reads=(bpc, b_cs), writes=(bcbm,))
                for g in range(2 if cfg.get("cbx", 9) >= 2 else 0):
                    S.op("dve", lambda e, g=g: e.tensor_tensor(
                        out=Eb[0:CL, g * 8:(g + 1) * 8, 0:CL], in0=Eb[0:CL, g * 8:(g + 1) * 8, 0:CL],
                        in1=cbm[0:CL, g:g + 1, 0:CL].broadcast_to([CL, 8, CL]), op=ALU.mult),
                        reads=(bEb, bcbm), writes=(bEb,))
                    S.op("dve", lambda e, g=g: e.tensor_tensor(
                        out=Csb[:, g * 8:(g + 1) * 8, 0:CL], in0=absx[:, g * 8:(g + 1) * 8, 0:CL],
                        in1=bcT[:, 2 + g, cols].unsqueeze(1).broadcast_to([128, 8, CL]), op=ALU.mult),
                        reads=[babs] + b_bc, writes=(bCs,))
                if cfg.get('sstage', 99) <= 3:
                    continue
                px, bpx = next_ps()
                pxb = px.bitcast(BF16)
                S.op("pe", [lambda e, c=c: e.transpose(pxb[0:CL, c * 128:(c + 1) * 128], xcT[:, c, cols], identb[:, :])
                            for c in range(8)], reads=(bxc, b_ident), writes=(bpx,))
                S.op("dve", lambda e, pxb=pxb: e.tensor_tensor(
                    out=xd[0:CL, :, :], in0=pxb[0:CL, 0:1024].rearrange("p (h q) -> p h q", h=16),
                    in1=tsc[0:CL, 0:16].unsqueeze(2).broadcast_to([CL, 16, 64]), op=ALU.mult),
                    reads=(bpx, btsc), writes=(bxd,))
                S.op("dve", lambda e: e.tensor_tensor(
                    out=xdd[0:CL, :, :], in0=xd[0:CL, :, :],
                    in1=tsc[0:CL, 48:64].unsqueeze(2).broadcast_to([CL, 16, 64]), op=ALU.mult),
                    reads=(bxd, btsc), writes=(bxdd,))
                pb_, bpb = next_ps()
                pbb = pb_.bitcast(BF16)
                S.op("pe", [lambda e, g=g: e.transpose(pbb[0:CL, g * 128:(g + 1) * 128], bcT[:, g, cols], identb[:, :])
                            for g in range(2)], reads=b_bc + [b_ident], writes=(bpb,))
                S.op("act", lambda e, pbb=pbb: e.copy(Btok[0:CL, :, :], pbb[0:CL, 0:256].rearrange("p (g n) -> p g n", g=2)),
                     reads=(bpb,), writes=(bBt,))
                if cfg.get('sstage', 99) <= 4:
                    continue
                py, bpy = ps_group(2)
                py_banks = list(state["last_group"])
                fns = []
                for h in range(16):
                    dst = py[64 * (h % 2):64 * (h % 2) + 64, (h // 2) * CL:(h // 2 + 1) * CL]
                    first = (h < 2) or (not smp and ((h // 2) % 4 == 0))
                    if smp:
                        fns.append(lambda e, h=h, dst=dst, first=first: e.matmul(
                            dst, xd[0:CL, h, :], Eb[0:CL, h, 0:CL], start=first, stop=True, skip_group_check=True))
                    else:
                        fns.append(lambda e, h=h, dst=dst: e.matmul(
                            dst, xd[0:CL, h, :], Eb[0:CL, h, 0:CL], start=True, stop=False, skip_group_check=True))
                        fns.append(lambda e, h=h, dst=dst: e.matmul(
                            dst, STb[:, h * 64:(h + 1) * 64], Csb[:, h, 0:CL], start=False, stop=True,
                            skip_group_check=True))
                S.op("pe", fns, reads=(bxd, bEb, b_STb, bCs), writes=bpy)
                if smp:
                    reserved.update(py_banks)
                    sample_states(py, bpy, Snat, bSnat, STs, bSTs, Csb, bCs, xdd, bxdd, Btok, bBt, Bblk, bBblk,
                                  bmsk, cdT, bcdT)
                elif cfg.get('sstage', 99) > 5:
                    pst, bpst = ps_group(2)
                    S.op("pe", [lambda e, g=g: e.matmul(pst[:, g * 512:(g + 1) * 512], Btok[0:CL, g, :],
                                                         xdd[0:CL, g * 8:(g + 1) * 8, :].rearrange("p h q -> p (h q)"),
                                                         start=True, stop=True) for g in range(2)],
                         reads=(bBt, bxdd), writes=bpst)
                    S.op("dve", lambda e: e.tensor_tensor(
                        out=STf[:, :].rearrange("p (h q) -> p h q", h=16),
                        in0=STf[:, :].rearrange("p (h q) -> p h q", h=16),
                        in1=absx[:, :, CL - 1:CL].broadcast_to([128, 16, 64]), op=ALU.mult),
                        reads=(b_ST, babs), writes=(b_ST,))
                    S.op("dve", lambda e, pst=pst: e.tensor_tensor(out=STf[:, :], in0=STf[:, :], in1=pst[:, 0:1024], op=ALU.add),
                         reads=[b_ST] + bpst, writes=(b_ST,))
                    S.op("act", lambda e: e.copy(STb[:, :], STf[:, :]), reads=(b_ST,), writes=(b_STb,))
                if cfg.get('sstage', 99) <= 6:
                    continue
                o_ds = PV_OFF["dskip"]
                S.op("dve", lambda e: e.tensor_tensor(
                    out=ygb[:, :, 0:CL], in0=xcT[:, :, cols], in1=pv[:, o_ds:o_ds + 8].unsqueeze(2).broadcast_to([128, 8, CL]),
                    op=ALU.mult), reads=(bxc, b_pv), writes=(bygb,))
                S.op("dve", lambda e, py=py: e.tensor_tensor(
                    out=ygb[:, :, 0:CL], in0=ygb[:, :, 0:CL], in1=py[:, 0:8 * CL].rearrange("p (c i) -> p c i", c=8), op=ALU.add),
                    reads=[bygb] + bpy, writes=(bygb,))
                S.op("dve", lambda e: e.tensor_tensor(out=ygb[:, :, 0:CL], in0=ygb[:, :, 0:CL], in1=zT[:, :, cols], op=ALU.mult),
                     reads=(bygb, bz), writes=(bygb,))
                S.op("act", lambda e: e.activation(out=sqg[:, :, 0:CL], in_=ygb[:, :, 0:CL], func=AF.Square),
                     reads=(bygb,), writes=(bsqg,))
                pss, bpss = next_ps()
                for g in range(2):
                    S.op("pe", [lambda e, g=g, cc=cc, pss=pss: e.matmul(
                        pss[:, g * CL:(g + 1) * CL], onesb[:, :], sqg[:, g * 4 + cc, 0:CL], start=(cc == 0), stop=(cc == 3))
                        for cc in range(4)], reads=(bsqg, b_ident), writes=(bpss,))
                S.op("act", lambda e, pss=pss: e.activation(out=rsg[:, :, 0:CL], in_=pss[:, 0:2 * CL].rearrange("p (g i) -> p g i", g=2),
                                                             func=AF.Sqrt, bias=EPS, scale=1.0 / 512),
                     reads=(bpss,), writes=(brsg,))
                S.op("dve", lambda e: e.reciprocal(rsg[:, :, 0:CL], rsg[:, :, 0:CL]), reads=(brsg,), writes=(brsg,))
                reserved.difference_update(py_banks)
                o_ng = PV_OFF["norm_g"]
                S.op("dve", lambda e: e.tensor_tensor(
                    out=ygb[:, :, 0:CL], in0=ygb[:, :, 0:CL], in1=pv[:, o_ng:o_ng + 8].unsqueeze(2).broadcast_to([128, 8, CL]),
                    op=ALU.mult), reads=(bygb, b_pv), writes=(bygb,))
                for g in range(2):
                    S.op("dve", lambda e, g=g: e.tensor_tensor(
                        out=ybT[:, g * 4:(g + 1) * 4, cols], in0=ygb[:, g * 4:(g + 1) * 4, 0:CL],
                        in1=rsg[:, g:g + 1, 0:CL].broadcast_to([128, 4, CL]), op=ALU.mult),
                        reads=(bygb, brsg), writes=(byb,))

            if STOP <= 5:
                return
            if cfg.get("obar", 0):
                S.barrier(engines=("pe", "act", "dve", "sp", "pool"))
            for mi in range(4):
                if cfg.get("omode", 9) == 6:
                    w, bw = wring[mi], b_wr[mi]
                else:
                    w, bw = wload(wev_out[mi], 16 * 256)
                wv = w[:, :].rearrange("p (k n) -> p k n", k=16)
                for mm in range(2):
                    m = mi * 2 + mm
                    pt, bp = next_ps()
                    fns = []
                    for k in range(16):
                        rhs = yaT[:, k, 0:nt] if k < 8 else ybT[:, k - 8, 0:nt]
                        fns.append(lambda e, pt=pt, k=k, mm=mm, wv=wv, rhs=rhs: e.matmul(
                            pt[:, 0:nt], wv[:, k, mm * 128:(mm + 1) * 128], rhs, start=(k == 0), stop=(k == 15)))
                    OM = cfg.get("omode", 9)
                    if OM >= 1:
                        S.op("pe", fns[0:OM] if OM < 9 else fns, reads=[bw, byb] + b_ya, writes=(bp,))
                    if OM >= 9:
                        S.op("dve", lambda e, pt=pt, m=m: e.tensor_tensor(out=xT[:, m, 0:nt], in0=pt[:, 0:nt], in1=xT[:, m, 0:nt],
                                                                          op=ALU.add), reads=(bp, b_xT[m]), writes=(b_xT[m],))
            if STOP <= 6:
                return
            if is_last_ptile:
                so = absx[:, 0:8, :]
                bso = babs
                pg, bpg = ps_group(2)
                S.op("pe", [lambda e, c=c: e.transpose(pg[:, c * 128:(c + 1) * 128], STf[:, c * 128:(c + 1) * 128], ident[:, :])
                            for c in range(8)], reads=(b_ST, b_ident), writes=bpg)
                S.op("act", lambda e: e.copy(so.rearrange("p c n -> p (c n)"), pg[:, 0:1024]), reads=bpg, writes=(bso,))
                S.op("sp", lambda e: e.dma_start(out=o_ssm_p.rearrange("(c p) n -> p c n", p=128), in_=so),
                     reads=(bso,), dma=next_out(), final=True, arena=True)

        def sample_states(py, bpy, Snat, bSnat, STs, bSTs, Csb, bCs, xdd, bxdd, Btok, bBt, Bblk, bBblk, bmsk, cdT, bcdT):
            for g in range(2):
                S.op("dve", lambda e, g=g: e.tensor_tensor(
                    out=Bblk[:, g, :, :], in0=Btok[0:64, g, :].unsqueeze(1).broadcast_to([64, 16, 128]),
                    in1=bmsk.unsqueeze(2).broadcast_to([64, 16, 128]), op=ALU.mult),
                    reads=(bBt, b_cs), writes=(bBblk,))
            NB = 2
            for bg in range(16 // NB):
                S.op("sp", lambda e, bg=bg: e.dma_start(
                    out=Snat[:, :, :, :], in_=st_ssm[bg * NB:(bg + 1) * NB].rearrange("b (c p) n -> p b c n", p=128)),
                    writes=(bSnat,), dma=next_in(), arena=True)
                for bb in range(NB):
                    pg, bpg = ps_group(2)
                    S.op("pe", [lambda e, c=c, bb=bb, pg=pg: e.transpose(pg[:, c * 128:(c + 1) * 128], Snat[:, bb, c, :], ident[:, :])
                                for c in range(8)], reads=(bSnat, b_ident), writes=bpg)
                    if bb % 2:
                        S.op("act", lambda e, bb=bb, pg=pg: e.copy(STs[:, bb, :], pg[:, 0:1024]), reads=bpg, writes=(bSTs,))
                    else:
                        S.op("dve", lambda e, bb=bb, pg=pg: e.tensor_copy(STs[:, bb, :], pg[:, 0:1024]), reads=bpg, writes=(bSTs,))
                fns = []
                for bb in range(NB):
                    b = bg * NB + bb
                    for h in range(16):
                        dst = py[64 * (h % 2):64 * (h % 2) + 64, (h // 2) * 64 + 4 * b:(h // 2) * 64 + 4 * b + 4]
                        fns.append(lambda e, h=h, bb=bb, b=b, dst=dst: e.matmul(
                            dst, STs[:, bb, h * 64:(h + 1) * 64], Csb[:, h, 4 * b:4 * b + 4], start=False, stop=True,
                            skip_group_check=True))
                S.op("pe", fns, reads=(bSTs, bCs), writes=bpy)
                for c in range(8):
                    pt, bp = next_ps()
                    S.op("pe", lambda e, c=c, pt=pt, bg=bg: e.matmul(
                        pt[:, 0:NB * 128], xdd[0:64, 2 * c:2 * c + 2, :].rearrange("p h q -> p (h q)"),
                        Bblk[:, c // 4, bg * NB:(bg + 1) * NB, :].rearrange("p b n -> p (b n)"), start=True, stop=True),
                        reads=(bxdd, bBblk), writes=(bp,))
                    for bb in range(NB):
                        b = bg * NB + bb
                        S.op("dve", lambda e, c=c, bb=bb, b=b, pt=pt: e.scalar_tensor_tensor(
                            out=Snat[:, bb, c, :], in0=Snat[:, bb, c, :], scalar=cdT[:, c, b:b + 1],
                            in1=pt[:, bb * 128:(bb + 1) * 128], op0=ALU.mult, op1=ALU.add),
                            reads=(bSnat, bcdT, bp, bSTs), writes=(bSnat,))
                S.op("sp", lambda e, bg=bg: e.dma_start(
                    out=o_ssm_s[bg * NB:(bg + 1) * NB].rearrange("b (c p) n -> p b c n", p=128), in_=Snat[:, :, :, :]),
                    reads=(bSnat,), dma=next_out(), final=True, arena=True)

        WINS = (2, 4, 8, 16)

        def head_norm(pt, bp, nparts_dummy, nt, gname, out_ap, wbufs, scale, f32_out=None, f32_bufs=(), defer=False):
            si = state["sil"]
            state["sil"] = 1 - si
            raw = sil[si][:, 0:nt]
            hq_, bhq_ = (hsq, b_hsq) if si == 0 else (hsq2, b_hsq2)
            S.op("act", lambda e: e.copy(raw, pt), reads=(bp,), writes=(b_sil[si],))
            S.op("act", lambda e: e.activation(out=hq_[:, 0:nt], in_=pt, func=AF.Square), reads=(bp,), writes=(bhq_,))

            def part2():
                ps2, bps2 = next_ps()
                S.op("pe", lambda e: e.matmul(ps2[:, 0:nt], bd64[:, :], hq_[:, 0:nt], start=True, stop=True),
                     reads=(bhq_, b_ident), writes=(bps2,))
                if scale is None:
                    S.op("act", lambda e: e.activation(out=hrs[:, 0:nt], in_=ps2[:, 0:nt], func=AF.Sqrt, bias=EPS, scale=1.0 / 64),
                         reads=(bps2,), writes=(b_hrs,))
                else:
                    S.op("act", lambda e: e.activation(out=hrs[:, 0:nt], in_=ps2[:, 0:nt], func=AF.Sqrt, bias=EPS * scale * scale,
                                                       scale=scale * scale / 64), reads=(bps2,), writes=(b_hrs,))
                S.op("dve", lambda e: e.reciprocal(hrs[:, 0:nt], hrs[:, 0:nt]), reads=(b_hrs,), writes=(b_hrs,))
                S.op("dve", lambda e: e.scalar_tensor_tensor(out=out_ap, in0=raw, scalar=pvc(gname, 0), in1=hrs[:, 0:nt],
                                                             op0=ALU.mult, op1=ALU.mult),
                     reads=(b_sil[si], b_hrs, b_pv), writes=wbufs)
                if f32_out is not None:
                    S.op("dve", lambda e: e.scalar_tensor_tensor(out=f32_out, in0=raw, scalar=pvc(gname, 0), in1=hrs[:, 0:nt],
                                                                 op0=ALU.mult, op1=ALU.mult),
                         reads=(b_sil[si], b_hrs, b_pv), writes=f32_bufs)
            if defer:
                return part2
            part2()
            return None

        def odd_mixer(kind, nt, is_last_ptile, tile_idx):
            S.barrier()
            A.reset()
            smp = (kind == "s")
            qT = hT[:, 0:8, :]
            ycT = hT[:, 8:16, :]
            b_q, b_yc = b_h[0:8], b_h[8:16]
            ydT = A.take(128, [8, nt], BF16)
            byd = Buf("ydT")
            L = 19 if smp else (nt + 15)
            nb = 16 if smp else 1
            cext = A.take(128, [8, nb * L])
            bce = Buf("cext")
            pA = A.take(128, [nb * L])
            pB = A.take(128, [nb * L])
            bpA, bpB = Buf("pA"), Buf("pB")
            pooledT = A.take(128, [8, nt], BF16)
            bpool = Buf("pooled")
            KW = 64 if smp else (128 + nt)
            kdT = A.take(128, [4, KW], BF16)
            bkd = Buf("kdT")
            NVB = 1 if smp else 5
            vtok = A.take(128, [NVB, 256], BF16)
            bvt = Buf("vtok")
            kf32 = A.take(128, [4, 128 if not smp else 64])
            vf32 = A.take(128, [256])
            bkf, bvf = Buf("kf32"), Buf("vf32")
            ost = A.take(128, [1024])
            bost = Buf("ost")
            tS_ = [A.take(128, [1024]) for _ in range(2)]
            pT_ = [A.take(128, [1024], BF16) for _ in range(2)]
            btS_, bpT_ = [Buf("tS0"), Buf("tS1")], [Buf("pT0"), Buf("pT1")]
            tS, pT, btS, bpT = tS_[0], pT_[0], btS_[0], bpT_[0]
            rec = [A.take(128, [128]) for _ in range(2)]
            brec = [Buf("rec0"), Buf("rec1")]
            recb = A.take(128, [8, 128])
            brecb = Buf("recb")

            def cview(c, lo, n):
                if smp:
                    return cext[:, c, :].rearrange("p (b l) -> p b l", l=L)[:, :, lo:lo + n]
                return cext[:, c, lo:lo + n]

            def tview(t, lo, n):
                if smp:
                    return t[:, :].rearrange("p (b l) -> p b l", l=L)[:, :, lo:lo + n]
                return t[:, lo:lo + n]

            def ntv(ap2d):
                return ap2d.rearrange("p (b i) -> p b i", i=4) if smp else ap2d

            if smp:
                cin32 = A.take(128, [8, 64])
                bcin = Buf("cin32")
                stp = A.take(120, [2, 1024])
                bstp = Buf("stp")
                S.op("sp", lambda e: e.dma_start(out=stp, in_=st_pool.rearrange("(g r) d -> r g d", g=2)),
                     writes=(bstp,), dma=next_in(), arena=True)
                S.op("sp", lambda e: e.dma_start(out=o_pool_s[:, 0:11, :], in_=st_pool.rearrange("(b r) d -> b r d", r=15)[:, 4:15, :]),
                     dma=next_out(), final=True)
                S.op("sp", lambda e: e.dma_start(out=o_k_s[:, 0:124, :], in_=st_k[:, 4:128, :]), dma=next_out(), final=True)
                S.op("sp", lambda e: e.dma_start(out=o_v_s[:, 0:124, :], in_=st_v[:, 4:128, :]), dma=next_out(), final=True)
                for g in range(2):
                    for cq in range(2):
                        pg, bpg = ps_group(2)
                        S.op("pe", [lambda e, c4=c4, g=g, cq=cq, pg=pg: e.transpose(
                            pg[:, c4 * 128:c4 * 128 + 120], stp[0:120, g, (cq * 4 + c4) * 128:(cq * 4 + c4 + 1) * 128],
                            ident[0:120, 0:120]) for c4 in range(4)], reads=(bstp, b_ident), writes=bpg)
                        for c4 in range(4):
                            c = cq * 4 + c4
                            S.op("dve" if c4 % 2 else "act",
                                 (lambda e, c=c, c4=c4, g=g, pg=pg: e.tensor_copy(
                                     cext[:, c, :].rearrange("p (b l) -> p b l", l=L)[:, g * 8:(g + 1) * 8, 0:15],
                                     pg[:, c4 * 128:c4 * 128 + 120].rearrange("p (b r) -> p b r", r=15))) if c4 % 2 else
                                 (lambda e, c=c, c4=c4, g=g, pg=pg: e.copy(
                                     cext[:, c, :].rearrange("p (b l) -> p b l", l=L)[:, g * 8:(g + 1) * 8, 0:15],
                                     pg[:, c4 * 128:c4 * 128 + 120].rearrange("p (b r) -> p b r", r=15))),
                                 reads=bpg, writes=(bce,))
            else:
                S.op("dve", lambda e: e.tensor_copy(cext[:, :, 0:15], ptail[:, :, :]), reads=(b_ptail,), writes=(bce,))
                bmP = A.take(128, [2, 16, 128])
                bbm = Buf("bm")
                for kb, off in ((0, 256), (1, 128)):
                    S.op("sp", lambda e, kb=kb, off=off: e.dma_start(
                        out=bmP[:, kb, :, :], in_=bass.AP(tensor=dscr.tensor, offset=off, ap=[[383, 128], [128 * 384, 16], [1, 128]])),
                        reads=(b_dscr,), writes=(bbm,), dma=next_in(), arena=True)

            rms_norm("mix_norm1", nt)

            for gi in range(2):
                def ev_c(cc, pt, bp, gi=gi):
                    c = gi * 4 + cc
                    S.op("act", lambda e: e.copy(cview(c, 15, nt if not smp else 4), ntv(pt[:, 0:nt])), reads=(bp,), writes=(bce,))
                    if smp:
                        S.op("dve", lambda e: e.tensor_copy(cin32[:, c, :], pt[:, 0:nt]), reads=(bp,), writes=(bcin,))
                proj_fm(wod_fm[gi], 4, nt, ev_c)
            for gi in range(2):
                def ev_q(cc, pt, bp, gi=gi):
                    c = gi * 4 + cc
                    return head_norm(pt[:, 0:nt], bp, 128, nt, "q_norm", qT[:, c, 0:nt], (b_q[c],), 8.0, defer=True)
                proj_fm(wod_fm[2 + gi], 4, nt, ev_q)
            wk, bwk = wload(wod_k, 8 * 256)
            wkv = wk[:, 0:2048].rearrange("p (k n) -> p k n", k=8)
            k0 = 0 if smp else 128
            if not smp:
                S.op("dve", lambda e: e.tensor_copy(kdT[:, :, 0:128], kprev[:, :, :]), reads=(b_kprev,), writes=(bkd,))
                S.op("dve", lambda e: e.tensor_copy(vtok[:, 0, :], vprev[:, :]), reads=(b_kprev,), writes=(bvt,))
            for hk in range(4):
                pt, bp = next_ps()
                fns = []
                for half in range(2):
                    for k in range(8):
                        fns.append(lambda e, pt=pt, half=half, k=k, hk=hk: e.matmul(
                            pt[64 * half:64 * half + 64, 0:nt], wkv[:, k, hk * 64:(hk + 1) * 64], xn[:, k, 0:nt],
                            start=(k == 0), stop=(k == 7), skip_group_check=True))
                S.op("pe", fns, reads=(bwk, b_xn), writes=(bp,))
                _kn(pt, bp, hk, nt, k0, kdT, bkd, kf32, bkf, smp, is_last_ptile)
            wv_, bwv_ = wload(wod_v, 8 * 256)
            wvv_ = wv_[:, 0:2048].rearrange("p (k n) -> p k n", k=8)
            CLv = 64 if smp else 128
            for blk in range(nt // CLv):
                cols = slice(blk * CLv, (blk + 1) * CLv)
                pt, bp = next_ps()
                S.op("pe", [lambda e, pt=pt, k=k, cols=cols: e.matmul(pt[0:CLv, 0:256], xn[:, k, cols], wvv_[:, k, :],
                                                                       start=(k == 0), stop=(k == 7)) for k in range(8)],
                     reads=(bwv_, b_xn), writes=(bp,))
                vb = 0 if smp else blk + 1
                S.op("act", lambda e, pt=pt, vb=vb: e.copy(vtok[0:CLv, vb, :], pt[0:CLv, 0:256]), reads=(bp,), writes=(bvt,))
                if smp or (is_last_ptile and blk == 3):
                    S.op("dve", lambda e, pt=pt: e.tensor_copy(vf32[0:CLv, :], pt[0:CLv, 0:256]), reads=(bp,), writes=(bvf,))
            if smp or is_last_ptile:
                if smp:
                    for i in range(4):
                        S.op("sp", lambda e, i=i: e.dma_start(out=o_v_s[:, 124 + i, :], in_=vf32[i:64:4, :]),
                             reads=(bvf,), dma=next_out(), final=True, arena=True)
                else:
                    S.op("sp", lambda e: e.dma_start(out=o_v_p, in_=vf32[:, :]), reads=(bvf,), dma=next_out(), final=True, arena=True)
                pk, bpk = next_ps()
                nk = 64 if smp else 128
                S.op("pe", [lambda e, hk=hk: e.transpose(pk[0:nk, hk * 64:(hk + 1) * 64], kf32[0:64, hk, 0:nk], ident[0:64, 0:64])
                            for hk in range(4)], reads=(bkf, b_ident), writes=(bpk,))
                S.op("act", lambda e: e.copy(ost[0:nk, 0:256], pk[0:nk, 0:256]), reads=(bpk,), writes=(bost,))
                if smp:
                    for i in range(4):
                        S.op("sp", lambda e, i=i: e.dma_start(out=o_k_s[:, 124 + i, :], in_=ost[i:64:4, 0:256]),
                             reads=(bost,), dma=next_out(), final=True, arena=True)
                else:
                    S.op("sp", lambda e: e.dma_start(out=o_k_p, in_=ost[:, 0:256]), reads=(bost,), dma=next_out(), final=True, arena=True)

            first_tile = (not smp) and tile_idx == 0
            for c in range(8):
                gi = c // 2
                w = WINS[gi]
                Lx = L
                cur_t, cur_b, cur_lo = None, None, 0
                src_ap = lambda lo, n, c=c: cview(c, lo, n)
                width = 1
                bufs = [(pA, bpA), (pB, bpB)]
                bi = 0
                srcf, srcb, lo0 = src_ap, bce, 0
                while width < w:
                    dst, bdst = bufs[bi]
                    bi = 1 - bi
                    lo1 = lo0 + width
                    n = Lx - lo1
                    S.op("dve", lambda e, srcf=srcf, dst=dst, lo1=lo1, n=n, width=width: e.tensor_tensor(
                        out=tview(dst, lo1, n), in0=srcf(lo1, n), in1=srcf(lo1 - width, n), op=ALU.add),
                        reads=(srcb,), writes=(bdst,))
                    srcf = (lambda lo, n, dst=dst: tview(dst, lo, n))
                    srcb, lo0 = bdst, lo1
                    width *= 2
                nn = 4 if smp else nt
                S.op("dve", lambda e, srcf=srcf, c=c, w=w, nn=nn: e.scalar_tensor_tensor(
                    out=ntv(pooledT[:, c, 0:nt]), in0=srcf(15, nn), scalar=1.0 / w, in1=cview(c, 15, nn),
                    op0=ALU.mult, op1=ALU.subtract), reads=(srcb, bce), writes=(bpool,))
                if first_tile:
                    S.op("dve", lambda e, srcf=srcf, gi=gi: e.tensor_tensor(
                        out=rec[0][:, 0:16], in0=srcf(15, 16), in1=cst_ap("invc")[:, gi * 16:(gi + 1) * 16], op=ALU.mult),
                        reads=(srcb, b_cs), writes=(brec[0],))
                    S.op("dve", lambda e, c=c: e.tensor_tensor(
                        out=pooledT[:, c, 0:16], in0=rec[0][:, 0:16], in1=cview(c, 15, 16), op=ALU.subtract),
                        reads=(brec[0], bce), writes=(bpool,))
            if not smp:
                S.op("dve", lambda e: e.tensor_copy(ptail[:, :, :], cext[:, :, nt:nt + 15]), reads=(bce,), writes=(b_ptail,))
            if smp or is_last_ptile:
                nr = 64 if smp else 15
                pg, bpg = ps_group(2)
                if smp:
                    S.op("pe", [lambda e, c=c: e.transpose(pg[0:64, c * 128:(c + 1) * 128], cin32[:, c, :], ident[:, :])
                                for c in range(8)], reads=(bcin, b_ident), writes=bpg)
                else:
                    S.op("pe", [lambda e, c=c: e.transpose(pg[0:15, c * 128:(c + 1) * 128], cext[:, c, nt:nt + 15], ident[:, :])
                                for c in range(8)], reads=(bce, b_ident), writes=bpg)
                S.op("dve", lambda e: e.tensor_copy(ost[0:nr, :], pg[0:nr, 0:1024]), reads=bpg + [bost], writes=(bost,))
                if smp:
                    for i in range(4):
                        S.op("sp", lambda e, i=i: e.dma_start(out=o_pool_s[:, 11 + i, :], in_=ost[i:64:4, :]),
                             reads=(bost,), dma=next_out(), final=True, arena=True)
                else:
                    S.op("sp", lambda e: e.dma_start(out=o_pool_p, in_=ost[0:15, :]), reads=(bost,), dma=next_out(), final=True, arena=True)
            wl, bwl = wload(wod_lin, 4 * 2 * 256)
            wlv = wl[:, 0:2048].rearrange("p (g c d) -> p g c d", g=4, c=2)
            for c in range(8):
                gi, dd = c // 2, c % 2
                pt, bp = next_ps()
                S.op("pe", [lambda e, pt=pt, cc=cc, gi=gi, dd=dd: e.matmul(
                    pt[:, 0:nt], wlv[:, gi, cc, dd * 128:(dd + 1) * 128], pooledT[:, gi * 2 + cc, 0:nt],
                    start=(cc == 0), stop=(cc == 1)) for cc in range(2)], reads=(bwl, bpool), writes=(bp,))
                S.op("act", lambda e, pt=pt, c=c: e.activation(out=ycT[:, c, 0:nt], in_=pt[:, 0:nt], func=AF.Copy,
                                                               scale=pvc("c_scale", c)), reads=(bp, b_pv), writes=(b_yc[c],))

            if smp:
                sample_attention(nt, qT, b_q, kdT, bkd, vtok, bvt, ydT, byd, tS, btS, pT, bpT, rec, brec)
            else:
                po, bpo = ps_group(2)
                po_banks = list(state["last_group"])
                pd, bpd = ps_group(2)
                pd_banks = list(state["last_group"])
                reserved.update(po_banks + pd_banks)
                def kbs_of(qb):
                    return [1] if tile_idx * 4 + qb == 0 else [0, 1]

                def stage_a(qb, hq, par):
                    qcols = slice(qb * 128, (qb + 1) * 128)
                    kbs = kbs_of(qb)
                    nkb = len(kbs)
                    tS, pT, btS, bpT = tS_[par], pT_[par], btS_[par], bpT_[par]
                    psA, bpsA = next_ps()
                    psB, bpsB = next_ps()
                    pss_ = (psA, psB)
                    fns = []
                    for hl in range(4):
                        h = hq * 4 + hl
                        hh, s, hp = hl % 2, hl // 2, h // 2
                        for ki, kb in enumerate(kbs):
                            kc0 = qb * 128 + kb * 128
                            fns.append(lambda e, hh=hh, s=s, ki=ki, kc0=kc0, hp=hp, hq=hq, pss_=pss_, qcols=qcols: e.matmul(
                                pss_[hh][:, (s * 2 + ki) * 128:(s * 2 + ki + 1) * 128],
                                kdT[64 * hh:64 * hh + 64, hq, kc0:kc0 + 128], qT[64 * hh:64 * hh + 64, hp, qcols],
                                start=True, stop=True))
                    S.op("pe", fns, reads=[bkd] + b_q[hq * 2:hq * 2 + 2], writes=(bpsA, bpsB))
                    if nkb == 2:
                        for hh in range(2):
                            h0 = hq * 4 + hh
                            S.op("dve", lambda e, hh=hh, h0=h0, pss_=pss_, tS=tS: e.tensor_tensor(
                                out=tS[:, hh * 512:(hh + 1) * 512].rearrange("p (s k q) -> p s k q", s=2, k=2),
                                in0=pss_[hh][:, 0:512].rearrange("p (s k q) -> p s k q", s=2, k=2),
                                in1=bmP[:, :, h0:h0 + 3:2, :].rearrange("p k s q -> p s k q"), op=ALU.add),
                                reads=((bpsA, bpsB)[hh], bbm), writes=(btS,))
                        S.op("act", lambda e, tS=tS, pT=pT: e.activation(out=pT[:, 0:1024], in_=tS[:, 0:1024], func=AF.Exp),
                             reads=(btS,), writes=(bpT,))
                    else:
                        for hl in range(4):
                            h = hq * 4 + hl
                            hh, s = hl % 2, hl // 2
                            for ki, kb in enumerate(kbs):
                                j = s * 2 + ki
                                S.op("dve", lambda e, hh=hh, j=j, kb=kb, h=h, pss_=pss_, tS=tS: e.tensor_tensor(
                                    out=tS[:, hh * 512 + j * 128:hh * 512 + (j + 1) * 128], in0=pss_[hh][:, j * 128:(j + 1) * 128],
                                    in1=bmP[:, kb, h, :], op=ALU.add), reads=((bpsA, bpsB)[hh], bbm), writes=(btS,))
                        S.op("act", lambda e, tS=tS, pT=pT: e.activation(
                            out=pT[:, 0:1024].rearrange("p (a b) -> p a b", a=4)[:, :, 0:128],
                            in_=tS[:, 0:1024].rearrange("p (a b) -> p a b", a=4)[:, :, 0:128], func=AF.Exp),
                            reads=(btS,), writes=(bpT,))

                def stage_b(qb, hq, par):
                    kbs = kbs_of(qb)
                    nkb = len(kbs)
                    pT, bpT = pT_[par], bpT_[par]
                    fns = []
                    for hl in range(4):
                        h = hq * 4 + hl
                        hh, s, hp = hl % 2, hl // 2, h // 2
                        for ki, kb in enumerate(kbs):
                            j = hh * 4 + s * 2 + ki
                            fns.append(lambda e, hh=hh, ki=ki, kb=kb, hq=hq, hp=hp, j=j, qb=qb, pT=pT, nkb=nkb: e.matmul(
                                po[64 * hh:64 * hh + 64, hp * 128:(hp + 1) * 128], vtok[:, qb + kb, hq * 64:(hq + 1) * 64],
                                pT[:, j * 128:(j + 1) * 128], start=(ki == 0), stop=(ki == nkb - 1), skip_group_check=True))
                        for ki, kb in enumerate(kbs):
                            j = hh * 4 + s * 2 + ki
                            fns.append(lambda e, hh=hh, ki=ki, hp=hp, j=j, pT=pT, nkb=nkb: e.matmul(
                                pd[64 * hh:64 * hh + 64, hp * 128:(hp + 1) * 128], onesb[:, 0:64],
                                pT[:, j * 128:(j + 1) * 128], start=(ki == 0), stop=(ki == nkb - 1), skip_group_check=True))
                    S.op("pe", fns, reads=(bvt, bpT, b_ident), writes=bpo + bpd)

                def epilogue(qb):
                    qcols = slice(qb * 128, (qb + 1) * 128)
                    S.op("dve", lambda e: e.tensor_tensor(
                        out=recb, in0=pd[:, 0:1024].rearrange("p (c q) -> p c q", c=8),
                        in1=esink2[:, 0:8].unsqueeze(2).broadcast_to([128, 8, 128]), op=ALU.add),
                        reads=bpd + [b_gm2], writes=(brecb,))
                    S.op("dve", lambda e: e.reciprocal(recb, recb), reads=(brecb,), writes=(brecb,))
                    S.op("dve", lambda e, qcols=qcols: e.tensor_tensor(
                        out=ydT[:, :, qcols], in0=po[:, 0:1024].rearrange("p (c q) -> p c q", c=8), in1=recb, op=ALU.mult),
                        reads=bpo + [brecb], writes=(byd,))

                groups = [(qb, hq) for qb in range(4) for hq in range(4)]
                stage_a(groups[0][0], groups[0][1], 0)
                for gi_, (qb, hq) in enumerate(groups):
                    if gi_ + 1 < len(groups):
                        stage_a(groups[gi_ + 1][0], groups[gi_ + 1][1], (gi_ + 1) % 2)
                    stage_b(qb, hq, gi_ % 2)
                    if hq == 3:
                        epilogue(qb)
                reserved.difference_update(po_banks + pd_banks)
                S.op("dve", lambda e: e.tensor_copy(kprev[:, :, :], kdT[:, :, nt:nt + 128]), reads=(bkd,), writes=(b_kprev,))
                S.op("dve", lambda e: e.tensor_copy(vprev[:, :], vtok[:, 4, :]), reads=(bvt,), writes=(b_kprev,))

            for mi in range(4):
                w, bw = wload(wod_out[mi], 16 * 256)
                wv = w[:, :].rearrange("p (k n) -> p k n", k=16)
                for mm in range(2):
                    m = mi * 2 + mm
                    pt, bp = next_ps()
                    fns = []
                    ks = {"c": range(8), "d": range(8, 16)}.get(cfg.get("odd_part"), range(16))
                    for k in ks:
                        rhs = ycT[:, k, 0:nt] if k < 8 else ydT[:, k - 8, 0:nt]
                        fns.append(lambda e, pt=pt, k=k, mm=mm, wv=wv, rhs=rhs, ks=ks: e.matmul(
                            pt[:, 0:nt], wv[:, k, mm * 128:(mm + 1) * 128], rhs, start=(k == ks[0]), stop=(k == ks[-1])))
                    S.op("pe", fns, reads=[bw, byd] + b_yc, writes=(bp,))
                    S.op("dve", lambda e, pt=pt, m=m: e.tensor_tensor(out=xT[:, m, 0:nt], in0=pt[:, 0:nt], in1=xT[:, m, 0:nt],
                                                                      op=ALU.add), reads=(bp, b_xT[m]), writes=(b_xT[m],))

        def _kn(pt, bp, hk, nt, k0, kdT, bkd, kf32, bkf, smp, is_last_ptile):
            if smp:
                head_norm(pt[:, 0:nt], bp, 128, nt, "k_norm", kdT[:, hk, k0:k0 + nt], (bkd,), None,
                          f32_out=kf32[:, hk, 0:64], f32_bufs=(bkf,))
            elif is_last_ptile:
                head_norm(pt[:, 0:nt], bp, 128, nt, "k_norm", kdT[:, hk, k0:k0 + nt], (bkd,), None)
                si = 1 - state["sil"]
                S.op("dve", lambda e: e.scalar_tensor_tensor(out=kf32[:, hk, :], in0=sil[si][:, nt - 128:nt], scalar=pvc("k_norm", 0),
                                                             in1=hrs[:, nt - 128:nt], op0=ALU.mult, op1=ALU.mult),
                     reads=(b_sil[si], b_hrs, b_pv), writes=(bkf,))
            else:
                head_norm(pt[:, 0:nt], bp, 128, nt, "k_norm", kdT[:, hk, k0:k0 + nt], (bkd,), None)

        def sample_attention(nt, qT, b_q, kdT, bkd, vtok, bvt, ydT, byd, tS, btS, pT, bpT, rec, brec):
            kc32 = A.take(128, [256])
            kcb = A.take(128, [4, 2, 64], BF16)
            bkc, bkcb = Buf("kc32"), Buf("kcb")
            KdT = A.take(128, [16, 4, 128], BF16)
            bKd = Buf("KdT")
            vc32 = A.take(128, [2, 256])
            vcb = A.take(128, [16, 256], BF16)
            bvc, bvcb = Buf("vc32"), Buf("vcb")
            bm1 = A.take(128, [16, 4])
            bm2 = A.take(64, [16, 64])
            bbm1, bbm2 = Buf("bm1"), Buf("bm2")
            t1 = A.take(128, [16, 64])
            p1 = A.take(128, [16, 64], BF16)
            t2 = A.take(64, [16, 64])
            p2 = A.take(64, [16, 64], BF16)
            bt1, bp1, bt2, bp2 = Buf("t1"), Buf("p1"), Buf("t2"), Buf("p2")
            S.op("sp", lambda e: e.dma_start(out=bm1, in_=bass.AP(tensor=dscr.tensor, offset=256, ap=[[383, 128], [128 * 384, 16], [1, 4]])),
                 reads=(b_dscr,), writes=(bbm1,), dma=next_in(), arena=True)
            S.op("dve", lambda e: e.memset(bm2, NEG), writes=(bbm2,))
            for b in range(16):
                S.op("sp", lambda e, b=b: e.dma_start(
                    out=bm2[4 * b:4 * b + 4, :, 4 * b:4 * b + 4],
                    in_=bass.AP(tensor=dscr.tensor, offset=128, ap=[[383, 4], [128 * 384, 16], [1, 4]])),
                    reads=(b_dscr,), writes=(bbm2,), dma=next_in(), arena=True)
            for b in range(16):
                S.op("sp", lambda e, b=b: e.dma_start(out=kc32, in_=st_k[b]), writes=(bkc,), dma=next_in(), arena=True)
                S.op("dve", lambda e: e.tensor_copy(kcb[:, :, :, :], kc32[:, :].rearrange("p (h d) -> p h d", h=4).unsqueeze(2)
                                                    .broadcast_to([128, 4, 2, 64])), reads=(bkc,), writes=(bkcb,))
                pk, bpk = next_ps()
                pkb = pk.bitcast(BF16)
                S.op("pe", [lambda e, hk=hk, pkb=pkb: e.transpose(pkb[:, hk * 128:(hk + 1) * 128],
                                                                  kcb[:, hk, :, :].rearrange("p a d -> p (a d)"), identb[:, :])
                            for hk in range(4)], reads=(bkcb, b_ident), writes=(bpk,))
                S.op("act", lambda e, b=b, pkb=pkb: e.copy(KdT[:, b, :, :].rearrange("p h k -> p (h k)"), pkb[:, 0:512]),
                     reads=(bpk,), writes=(bKd,))
                if b % 2 == 0:
                    S.op("sp", lambda e, b=b: e.dma_start(out=vc32, in_=st_v[b:b + 2].rearrange("b k d -> k b d")),
                         writes=(bvc,), dma=next_in(), arena=True)
                    S.op("dve", lambda e, b=b: e.tensor_copy(vcb[:, b:b + 2, :], vc32[:, :, :]), reads=(bvc,), writes=(bvcb,))
            ps1, bps1 = ps_group(2)
            g1 = list(state["last_group"])
            reserved.update(g1)
            fns = []
            for b in range(16):
                for h in range(16):
                    hh, hp, hk = h % 2, h // 2, h // 4
                    col = hh * 512 + b * 32 + hp * 4
                    fns.append(lambda e, b=b, hh=hh, hp=hp, hk=hk, col=col: e.matmul(
                        ps1[:, col:col + 4], KdT[64 * hh:64 * hh + 64, b, hk, :],
                        qT[64 * hh:64 * hh + 64, hp, 4 * b:4 * b + 4], start=True, stop=True))
            S.op("pe", fns, reads=[bKd] + b_q, writes=bps1)
            bm1v = bm1.rearrange("p (hp t) i -> p hp t i", t=2)
            for hh in range(2):
                S.op("dve", lambda e, hh=hh: e.tensor_tensor(
                    out=t1[:, hh * 8:(hh + 1) * 8, :].rearrange("p a (c d) -> p (a c) d", d=32) if False else
                    t1.rearrange("p a x -> p (a x)")[:, hh * 512:(hh + 1) * 512].rearrange("p (b hp i) -> p b hp i", b=16, hp=8),
                    in0=ps1[:, hh * 512:(hh + 1) * 512].rearrange("p (b hp i) -> p b hp i", b=16, hp=8),
                    in1=bm1v[:, :, hh, :].unsqueeze(1).broadcast_to([128, 16, 8, 4]), op=ALU.add),
                    reads=bps1 + [bbm1], writes=(bt1,))
            reserved.difference_update(g1)
            S.op("act", lambda e: e.activation(out=p1[:, :, :], in_=t1[:, :, :], func=AF.Exp), reads=(bt1,), writes=(bp1,))
            p1f = p1.rearrange("p a x -> p (a x)")
            ps2, bps2 = ps_group(2)
            S.op("pe", [lambda e, h=h: e.matmul(ps2[0:64, (h % 2) * 512 + (h // 2) * 64:(h % 2) * 512 + (h // 2 + 1) * 64],
                                                 kdT[64 * (h % 2):64 * (h % 2) + 64, h // 4, 0:64],
                                                 qT[64 * (h % 2):64 * (h % 2) + 64, h // 2, 0:64], start=True, stop=True)
                        for h in range(16)], reads=[bkd] + b_q, writes=bps2)
            bm2v = bm2.rearrange("p (hp t) x -> p hp t x", t=2)
            t2f = t2.rearrange("p a x -> p (a x)")
            for hh in range(2):
                S.op("dve", lambda e, hh=hh: e.tensor_tensor(
                    out=t2f[:, hh * 512:(hh + 1) * 512].rearrange("p (hp x) -> p hp x", hp=8),
                    in0=ps2[0:64, hh * 512:(hh + 1) * 512].rearrange("p (hp x) -> p hp x", hp=8),
                    in1=bm2v[:, :, hh, :], op=ALU.add), reads=bps2 + [bbm2], writes=(bt2,))
            S.op("act", lambda e: e.activation(out=p2[:, :, :], in_=t2[:, :, :], func=AF.Exp), reads=(bt2,), writes=(bp2,))
            p2f = p2.rearrange("p a x -> p (a x)")
            po, bpo = next_ps()
            pd, bpd = next_ps()
            fns = []
            for h in range(16):
                hh, hp, hk = h % 2, h // 2, h // 4
                first = h < 2
                fns.append(lambda e, h=h, hh=hh, hp=hp, hk=hk, first=first: e.matmul(
                    po[64 * hh:64 * hh + 64, hp * 64:(hp + 1) * 64], vtok[0:64, 0, hk * 64:(hk + 1) * 64],
                    p2f[:, hh * 512 + hp * 64:hh * 512 + (hp + 1) * 64],
                    start=first, stop=False, skip_group_check=True))
                fns.append(lambda e, h=h, hh=hh, hp=hp, first=first: e.matmul(
                    pd[64 * hh:64 * hh + 64, hp * 64:(hp + 1) * 64], onesb[0:64, 0:64],
                    p2f[:, hh * 512 + hp * 64:hh * 512 + (hp + 1) * 64],
                    start=first, stop=False, skip_group_check=True))
            for b in range(16):
                for h in range(16):
                    hh, hp, hk = h % 2, h // 2, h // 4
                    fns.append(lambda e, b=b, h=h, hh=hh, hp=hp, hk=hk: e.matmul(
                        po[64 * hh:64 * hh + 64, hp * 64 + 4 * b:hp * 64 + 4 * b + 4], vcb[:, b, hk * 64:(hk + 1) * 64],
                        p1f[:, hh * 512 + b * 32 + hp * 4:hh * 512 + b * 32 + hp * 4 + 4], start=False, stop=True,
                        skip_group_check=True))
                    fns.append(lambda e, b=b, h=h, hh=hh, hp=hp: e.matmul(
                        pd[64 * hh:64 * hh + 64, hp * 64 + 4 * b:hp * 64 + 4 * b + 4], onesb[:, 0:64],
                        p1f[:, hh * 512 + b * 32 + hp * 4:hh * 512 + b * 32 + hp * 4 + 4], start=False, stop=True,
                        skip_group_check=True))
            S.op("pe", fns, reads=(bvt, bp2, bp1, bvcb, b_ident), writes=(bpo, bpd))
            for c in range(8):
                ri = c % 2
                S.op("dve", lambda e, c=c, ri=ri: e.tensor_scalar(
                    out=rec[ri][:, 0:64], in0=pd[:, c * 64:(c + 1) * 64], scalar1=esink2[:, c:c + 1], scalar2=None,
                    op0=ALU.add), reads=(bpd, b_gm2), writes=(brec[ri],))
                S.op("dve", lambda e, ri=ri: e.reciprocal(rec[ri][:, 0:64], rec[ri][:, 0:64]), reads=(brec[ri],), writes=(brec[ri],))
                S.op("dve", lambda e, c=c, ri=ri: e.tensor_tensor(
                    out=ydT[:, c, 0:64], in0=po[:, c * 64:(c + 1) * 64], in1=rec[ri][:, 0:64], op=ALU.mult),
                    reads=(bpo, brec[ri]), writes=(byd,))

        tiles = [("p", t) for t in range(n_ptiles)] + ([("s", 0)] if do_sample else [])

        def issue_load(kind, t):
            if kind == "p":
                load_x_tile(xp[t * NTP:(t + 1) * NTP, :], 4, 128)
            else:
                load_x_tile(xs[:, :], 1, NS)

        issue_load(*tiles[0])
        for ti, (kind, t) in enumerate(tiles):
            nt = NTP if kind == "p" else NS
            last_p = (kind == "p" and t == n_ptiles - 1)
            if kind == "p":
                transpose_in(4, 128)
            else:
                transpose_in(1, NS)
            for layer in range(nlayers):
                if on("ffn1"):
                    ffn(layer, 0, nt)
                if layer == 0 and on("even"):
                    even_mixer(kind, nt, last_p)
                if layer == 1 and on("odd"):
                    odd_mixer(kind, nt, last_p, t)
                if layer == nlayers - 1:
                    S.barrier()
                    if ti + 1 < len(tiles):
                        issue_load(*tiles[ti + 1])
                if on("ffn2"):
                    ffn(layer, 1, nt)
            if kind == "p":
                transpose_out(yp[t * NTP:(t + 1) * NTP, :], 4, 128)
            else:
                transpose_out(ys[:, :], 1, NS)

        with nc.Block() as block:
            S.emit(block)
    P.ninst = S.ninst
    return P


PV_OFF = {}
_col = 0


def _pv(name, n):
    global _col
    PV_OFF[name] = _col
    _col += n


for _l in range(2):
    for _w in range(2):
        _pv(f"ffn_norm{_l}{_w}", 8)
for _l in range(2):
    _pv(f"mix_norm{_l}", 8)
_pv("ln_g", 8)
_pv("ln_b", 8)
_pv("conv_w", 48)
_pv("conv_b", 12)
_pv("norm_g", 8)
_pv("dskip", 8)
_pv("c_scale", 8)
_pv("q_norm", 1)
_pv("k_norm", 1)
_pv("dt_bias", 1)
_pv("a_log", 1)
PV_COLS = _col

CST_OFF = {}
_ccol = 0


def _cs(name, n):
    global _ccol
    CST_OFF[name] = (_ccol, n)
    _ccol += n


_cs("maskT_causal", 128)
_cs("maskT_blk", 64)
_cs("bmask", 16)
_cs("rmask_p", 512)
_cs("rmask_s", 64)
_cs("onehot", 384)
_cs("negmask", 384)
_cs("invc", 64)
CST_COLS = _ccol


def make_consts():
    c = np.zeros((128, CST_COLS), np.float32)
    j = np.arange(128)[:, None]
    i = np.arange(128)[None, :]
    o, n = CST_OFF["maskT_causal"]
    c[:, o:o + n] = (i >= j)
    o, n = CST_OFF["maskT_blk"]
    jj = np.arange(64)[:, None]
    ii = np.arange(64)[None, :]
    c[:64, o:o + n] = (ii >= jj) & (ii // 4 == jj // 4)
    o, n = CST_OFF["bmask"]
    c[:64, o:o + n] = (np.arange(64)[:, None] // 4 == np.arange(16)[None, :])
    o, n = CST_OFF["rmask_p"]
    c[:, o:o + n] = (np.arange(512)[None, :] % 128 != 0)
    o, n = CST_OFF["rmask_s"]
    c[:, o:o + n] = (np.arange(64)[None, :] % 4 != 0)
    dist = np.arange(384) - 128
    valid = (dist >= 0) & (dist < 128)
    nn = np.maximum(dist, 0)
    n_safe = np.maximum(nn, 1).astype(np.float32)
    scale = np.float32((32 - 16) / math.log(128 / 16))
    large = np.minimum(16 + (np.log(n_safe / 16) * scale).astype(np.int32), 31)
    bucket = np.where(nn < 16, nn, large).astype(np.int32)
    o, n = CST_OFF["onehot"]
    oh = np.zeros((32, 384), np.float32)
    oh[bucket[valid], np.arange(384)[valid]] = 1.0
    c[:32, o:o + n] = oh
    o, n = CST_OFF["negmask"]
    c[:, o:o + n] = np.where(valid, 0.0, NEG)[None, :]
    o, n = CST_OFF["invc"]
    for gi, w in enumerate((2, 4, 8, 16)):
        c[:, o + gi * 16:o + (gi + 1) * 16] = (1.0 / np.minimum(np.arange(16) + 1, w))[None, :]
    return c


def fm(v):
    v = np.asarray(v, np.float32)
    return np.ascontiguousarray(v.reshape(-1, 128).T)


def pack_host(inp):
    f = lambda k: np.asarray(inp[k], np.float32)
    pvec = np.zeros((128, PV_COLS), np.float32)

    def put(name, arr):
        arr = np.asarray(arr, np.float32)
        pvec[:arr.shape[0], PV_OFF[name]:PV_OFF[name] + arr.shape[1]] = arr

    for l in range(2):
        put(f"ffn_norm{l}0", fm(f("ffn1_norm")[l]))
        put(f"ffn_norm{l}1", fm(f("ffn2_norm")[l]))
        put(f"mix_norm{l}", fm(f("mix_norm")[l]))
    put("ln_g", fm(f("a_ln_g")[0]))
    put("ln_b", fm(f("a_ln_b")[0]))
    cw = f("b_conv_w")[0]
    put("conv_w", cw.reshape(4, 12, 128).transpose(2, 1, 0).reshape(128, 48))
    put("conv_b", fm(f("b_conv_b")[0]))
    put("norm_g", fm(f("b_norm_g")[0]))
    put("dskip", fm(np.repeat(f("b_d_skip")[0], 64)))
    put("c_scale", fm(f("c_scale")[0]))
    put("q_norm", np.tile(f("d_q_norm")[0], 2).reshape(128, 1))
    put("k_norm", np.tile(f("d_k_norm")[0], 2).reshape(128, 1))
    put("dt_bias", f("b_dt_bias")[0].reshape(16, 1))
    put("a_log", f("b_a_log")[0].reshape(16, 1))

    wgu = np.empty((2, 2, 11, 128, 8, 2, 256), np.float32)
    wdn = np.empty((2, 2, 8, 128, FC, 128), np.float32)
    for l in range(2):
        for w, (kgu, kdn) in enumerate((("ffn1_w_gu", "ffn1_w_down"), ("ffn2_w_gu", "ffn2_w_down"))):
            g = f(kgu)[l].reshape(8, 128, 2, 11, 256)
            wgu[l, w] = g.transpose(3, 1, 0, 2, 4)
            d = f(kdn)[l].reshape(FC, 128, 8, 128)
            wdn[l, w] = d.transpose(2, 1, 0, 3)

    def fm_w(Wc):
        n = Wc.shape[1]
        return np.ascontiguousarray(Wc.reshape(8, 128, n).transpose(1, 0, 2).reshape(128, 8 * n))

    Wi = f("ev_w_in")[0]
    col_groups = [(0, 512), (512, 1024), (2048, 2560), (2560, 3072), (3072, 3584), (3584, 4096), (4096, 4608)]
    wev_fm = np.stack([fm_w(Wi[:, a:b]) for a, b in col_groups], 0)
    wev_v = np.stack([fm_w(Wi[:, 1024:1536]), fm_w(Wi[:, 1536:2048])], 0)
    wev_dt = fm_w(Wi[:, 4608:4624])
    Wo = f("ev_w_out")[0]
    wev_out = np.ascontiguousarray(Wo.reshape(16, 128, 4, 256).transpose(2, 1, 0, 3).reshape(4, 128, 16 * 256))
    Wd = f("od_w_in")[0]
    ws = f("a_w_s")[0]
    wsT = np.ascontiguousarray(ws.transpose(2, 0, 1).reshape(128, 8 * 128))
    wsT4 = np.ascontiguousarray(ws[:, :4, :4].transpose(2, 0, 1).reshape(4, 32))
    shared = {
        "pvec": pvec,
        "cst": make_consts(),
        "wgu": wgu.reshape(2, 2, 11, 128, 8 * 2 * 256),
        "wdn": wdn.reshape(2, 2, 8, 128, FC * 128),
        "wev_fm": wev_fm, "wev_v": wev_v, "wev_dt": wev_dt, "wev_out": wev_out,
        "wsT": wsT, "wsT4": wsT4,
        "bs_row": f("a_b_s")[0].reshape(1, 1024),
        "lnrow": np.concatenate([f("a_ln_g")[0], f("a_ln_b")[0]]).reshape(1, 2048),
        "wod_fm": np.stack([fm_w(Wd[:, a:a + 512]) for a in (0, 512, 1024, 1536)], 0),
        "wod_k": fm_w(Wd[:, 2048:2304]), "wod_v": fm_w(Wd[:, 2304:2560]),
        "wod_lin": np.ascontiguousarray(f("c_lin_w")[0].reshape(4, 2, 128, 256).transpose(2, 0, 1, 3).reshape(128, 2048)),
        "wod_out": np.ascontiguousarray(f("od_w_out")[0].reshape(16, 128, 4, 256).transpose(2, 1, 0, 3).reshape(4, 128, 16 * 256)),
        "rel_tab": f("rel_bias_table"), "sink_row": f("d_sinks")[0].reshape(1, 16),
    }
    return shared


def per_core_inputs(inp, c):
    f = lambda k: np.asarray(inp[k], np.float32)
    sl = slice(16 * c, 16 * c + 16)
    return {
        "xs": np.ascontiguousarray(f("x_sample")[sl].reshape(NS, D)),
        "st_ssm": np.ascontiguousarray(f("state_ssm")[0, sl].reshape(16, 1024, 128)),
        "st_conv": np.ascontiguousarray(f("state_conv")[0, sl].reshape(48, 1536)),
        "st_pool": np.ascontiguousarray(f("state_pool")[0, sl].reshape(240, 1024)),
        "st_k": np.ascontiguousarray(f("cache_k_win")[0, sl].reshape(16, 128, 256)),
        "st_v": np.ascontiguousarray(f("cache_v_win")[0, sl].reshape(16, 128, 256)),
    }


_CACHE = {}


def kernel(**inputs):
    cfg = {}
    key = "full"
    if key not in _CACHE:
        _CACHE[key] = build_program(cfg)
    P = _CACHE[key]
    shared = pack_host(inputs)
    xp = np.asarray(inputs["x_prompt"], np.float32)
    in_maps = []
    for c in range(NCORES):
        m = dict(shared)
        m["xp"] = np.ascontiguousarray(xp[c])
        m.update(per_core_inputs(inputs, c))
        in_maps.append(m)
    res = run_bass_kernel_spmd(P.nc, in_maps, core_ids=list(range(NCORES)))
    r = res.results
    y_p = np.stack([r[c]["yp"] for c in range(NCORES)], 0)
    y_s = np.stack([r[c]["ys"] for c in range(NCORES)], 0).reshape(128, 4, D)
    cat = lambda k: np.stack([r[c][k] for c in range(NCORES)], 0)
    av = cat("o_av").reshape(1, 128, 4, D)
    ssm_p = cat("o_ssm_p").reshape(1, 8, 16, 64, 128)
    ssm_s = cat("o_ssm_s").reshape(1, 128, 16, 64, 128)
    conv_p = cat("o_conv_p").reshape(1, 8, 3, 1536)
    conv_s = cat("o_conv_s").reshape(1, 128, 3, 1536)
    pool_p = cat("o_pool_p").reshape(1, 8, 15, 1024)
    pool_s = cat("o_pool_s").reshape(1, 128, 15, 1024)
    k_p = cat("o_k_p").reshape(1, 8, 128, 4, 64)
    k_s = cat("o_k_s").reshape(1, 128, 128, 4, 64)
    v_p = cat("o_v_p").reshape(1, 8, 128, 4, 64)
    v_s = cat("o_v_s").reshape(1, 128, 128, 4, 64)
    return (y_p, y_s, av, ssm_p, ssm_s, conv_p, conv_s, pool_p, pool_s, k_p, k_s, v_p, v_s)
```

```python
import contextlib
import math
import types
import numpy as np
import concourse.bass as bass
import concourse.mybir as mybir
from concourse.bass_utils import run_bass_kernel_spmd

F32 = mybir.dt.float32
BF16 = mybir.dt.bfloat16
I32 = mybir.dt.int32
AF = mybir.ActivationFunctionType
ALU = mybir.AluOpType
AX = mybir.AxisListType

NCORES = 8
D = 1024
DC = 8
DFF = 2816
FC = 22
SEQ = 4096
NTP = 512
NS = 64
EPS = 1e-6
NEG = -1e30


NATIVE_GELU = True
RELAX_SAME_ENGINE = True


def freeze(fn):
    if fn.__closure__ is None:
        return fn
    cells = []
    for c in fn.__closure__:
        try:
            cells.append(types.CellType(c.cell_contents))
        except ValueError:
            cells.append(c)
    return types.FunctionType(fn.__code__, fn.__globals__, fn.__name__, fn.__defaults__, tuple(cells))


class Buf:
    __slots__ = ("name", "w", "r", "excl")

    def __init__(self, name, excl=False):
        self.name = name
        self.w = None
        self.r = {}
        self.excl = excl


class DmaSlot:
    def __init__(self, S, name):
        self.S = S
        self.name = name
        self.sem = S.new_sem("d" + name)
        self.count = 0
        self.last = None

    def next_token(self):
        if self.count >= 16 * 1200:
            self.sem = self.S.new_sem("d" + self.name)
            self.count = 0
        self.count += 16
        self.last = (id(self.sem), self.sem, self.count, "dma")
        return self.last


class Sched:
    ENG = ("pe", "act", "dve", "pool", "sp")
    EPOCH = 6000

    def __init__(self, nc, stack):
        self.nc = nc
        self.stack = stack
        self.nsem = 0
        self.streams = {e: [] for e in self.ENG}
        self.sem = {e: self.new_sem(e) for e in self.ENG}
        self.cnt = {e: 0 for e in self.ENG}
        self.seen = {e: {} for e in self.ENG}
        self.final_tokens = []
        self.pending_dma = []
        self.ninst = 0
        self.oplog = []

    def new_sem(self, name):
        self.nsem += 1
        return self.stack.enter_context(self.nc.semaphore(f"s{self.nsem}_{name}"))

    def _wait(self, eng, tok):
        sid, sem, val, _ = tok
        if self.seen[eng].get(sid, 0) >= val:
            return
        self.seen[eng][sid] = val
        self.streams[eng].append(("wait", sem, val))

    def barrier(self, engines=("pe", "act", "dve", "sp")):
        toks = [(id(self.sem[e]), self.sem[e], self.cnt[e], e) for e in ("pe", "act", "dve", "pool") if self.cnt[e] > 0]
        toks += self.pending_dma
        for e in engines:
            for tok in toks:
                if tok[3] != e:
                    self._wait(e, tok)
        self.pending_dma = []

    def op(self, eng, fns, reads=(), writes=(), dma=None, final=False, arena=False):
        if not isinstance(fns, (list, tuple)):
            fns = [fns]
        fns = [freeze(f) for f in fns]
        writes = list(writes) + [b for b in reads if b.excl]
        reads = [b for b in reads if not b.excl]
        for b in reads:
            if b.w is not None and not (b.w[3] == eng == "pe"):
                self._wait(eng, b.w)
        for b in writes:
            if b.w is not None and not (b.w[3] == eng == "pe") and not (RELAX_SAME_ENGINE and b.w[3] == eng):
                self._wait(eng, b.w)
            for tok in b.r.values():
                if not (tok[3] == eng == "pe") and not (RELAX_SAME_ENGINE and tok[3] == eng):
                    self._wait(eng, tok)
        if dma is not None and dma.last is not None:
            self._wait(eng, dma.last)
        if dma is None:
            if self.cnt[eng] >= self.EPOCH:
                self.sem[eng] = self.new_sem(eng)
                self.cnt[eng] = 0
            self.cnt[eng] += 1
            tok = (id(self.sem[eng]), self.sem[eng], self.cnt[eng], eng)
            inc = 1
        else:
            tok = dma.next_token()
            inc = 16
        self.oplog.append((eng, tok, [x.name for x in reads], [x.name for x in writes], len(self.streams[eng])))
        st = self.streams[eng]
        for fn in fns[:-1]:
            st.append(("inst", fn, None, 0))
        st.append(("inst", fns[-1], tok[1], inc))
        self.ninst += len(fns)
        for b in reads:
            b.r[tok[0]] = tok
        for b in writes:
            b.w = tok
            b.r = {}
        if final:
            self.final_tokens.append(tok)
        if arena and dma is not None:
            self.pending_dma.append(tok)
        return tok

    def emit(self, block):
        nc = self.nc
        for tok in self.final_tokens:
            self._wait("sp", tok)
        streams = self.streams

        def run(engine, items):
            for it in items:
                if it[0] == "wait":
                    engine.wait_ge(it[1], it[2])
                else:
                    r = it[1](engine)
                    if it[2] is not None:
                        r.then_inc(it[2], it[3])

        @block.tensor
        def _(e):
            run(e, streams["pe"])

        @block.scalar
        def _(e):
            run(e, streams["act"])

        @block.vector
        def _(e):
            run(e, streams["dve"])

        @block.gpsimd
        def _(e):
            run(e, streams["pool"])

        @block.sync
        def _(e):
            run(e, streams["sp"])


class Prog:
    def __init__(self, cfg):
        self.cfg = cfg
        self.nc = bass.Bass("TRN2", target_bir_lowering=False)
        self.stack = contextlib.ExitStack()
        self.S = None
        self.dram = {}

    def din(self, name, shape, dtype=F32):
        t = self.nc.dram_tensor(name, list(shape), dtype, kind="ExternalInput")
        self.dram[name] = t
        return t.ap()

    def dout(self, name, shape, dtype=F32):
        t = self.nc.dram_tensor(name, list(shape), dtype, kind="ExternalOutput")
        self.dram[name] = t
        return t.ap()

    def dscratch(self, name, shape, dtype=F32):
        t = self.nc.dram_tensor(name, list(shape), dtype, kind="Internal")
        return t.ap()

    def sb(self, name, shape, dtype=F32):
        return self.stack.enter_context(self.nc.sbuf_tensor(name, list(shape), dtype))

    def ps(self, name, shape, dtype=F32):
        return self.stack.enter_context(self.nc.psum_tensor(name, list(shape), dtype))


WSLOT_ELEMS = 8 * 2 * 256
NWSLOT = 4
ARENA_WORDS = 23 * 1024 + 128


def _prod(xs):
    r = 1
    for x in xs:
        r *= x
    return r


def rs(ap, dims):
    if len(dims) == 1:
        return ap
    if len(dims) == 2:
        return ap.rearrange("p (a b) -> p a b", a=dims[0])
    if len(dims) == 3:
        return ap.rearrange("p (a b c) -> p a b c", a=dims[0], b=dims[1])
    raise ValueError(dims)


class Arena:
    def __init__(self, t, words):
        self.t = t
        self.words = words
        self.off = 0

    def reset(self):
        self.off = 0

    def take(self, nparts, dims, dtype=F32):
        n = _prod(dims)
        w = n if dtype == F32 else (n + 1) // 2
        w = (w + 1) // 2 * 2
        ap = self.t[0:nparts, self.off:self.off + w]
        if dtype != F32:
            ap = ap.bitcast(dtype)
        ap = ap[:, 0:n]
        self.off += w
        assert self.off <= self.words, ("arena overflow", self.off, self.words)
        return rs(ap, dims)


def build_program(cfg):
    P = Prog(cfg)
    nc = P.nc
    n_ptiles = cfg.get("n_ptiles", SEQ // NTP)
    do_sample = cfg.get("sample", True)
    stages = cfg.get("stages", "all")
    nlayers = cfg.get("layers", 2)
    npt_tokens = n_ptiles * NTP

    def on(name):
        return stages == "all" or name in stages

    xp = P.din("xp", [npt_tokens, D])
    xs = P.din("xs", [NS, D])
    pvec = P.din("pvec", [128, PV_COLS])
    cst = P.din("cst", [128, CST_COLS])
    wgu = P.din("wgu", [2, 2, 11, 128, 8 * 2 * 256])
    wdn = P.din("wdn", [2, 2, 8, 128, FC * 128])
    wev_fm = P.din("wev_fm", [7, 128, 8 * 512])
    wev_v = P.din("wev_v", [2, 128, 8 * 512])
    wev_dt = P.din("wev_dt", [128, 8 * 16])
    wev_out = P.din("wev_out", [4, 128, 16 * 256])
    wsT = P.din("wsT", [128, 8 * 128])
    wsT4 = P.din("wsT4", [4, 8 * 4])
    bs_row = P.din("bs_row", [1, 8 * 128])
    lnrow = P.din("lnrow", [1, 2048])
    wod_fm = P.din("wod_fm", [4, 128, 8 * 512])
    wod_k = P.din("wod_k", [128, 8 * 256])
    wod_v = P.din("wod_v", [128, 8 * 256])
    wod_lin = P.din("wod_lin", [128, 4 * 2 * 256])
    wod_out = P.din("wod_out", [4, 128, 16 * 256])
    rel_tab = P.din("rel_tab", [32, 16])
    sink_row = P.din("sink_row", [1, 16])
    st_pool = P.din("st_pool", [240, 1024])
    st_k = P.din("st_k", [16, 128, 256])
    st_v = P.din("st_v", [16, 128, 256])
    dscr = P.dscratch("dscr", [16, 128, 384])
    o_pool_p = P.dout("o_pool_p", [15, 1024])
    o_pool_s = P.dout("o_pool_s", [16, 15, 1024])
    o_k_p = P.dout("o_k_p", [128, 256])
    o_k_s = P.dout("o_k_s", [16, 128, 256])
    o_v_p = P.dout("o_v_p", [128, 256])
    o_v_s = P.dout("o_v_s", [16, 128, 256])
    st_ssm = P.din("st_ssm", [16, 1024, 128])
    st_conv = P.din("st_conv", [48, 1536])
    yp = P.dout("yp", [npt_tokens, D])
    ys = P.dout("ys", [NS, D])
    o_av = P.dout("o_av", [NS, D])
    o_ssm_p = P.dout("o_ssm_p", [1024, 128])
    o_ssm_s = P.dout("o_ssm_s", [16, 1024, 128])
    o_conv_p = P.dout("o_conv_p", [3, 1536])
    o_conv_s = P.dout("o_conv_s", [16, 3, 1536])

    with P.stack:
        S = Sched(nc, P.stack)
        P.S = S
        ident = P.sb("ident", [128, 128], F32)
        identb = P.sb("identb", [128, 128], BF16)
        onesb = P.sb("onesb", [128, 128], BF16)
        pv = P.sb("pv", [128, PV_COLS], F32)
        cs = P.sb("cs", [128, CST_COLS], F32)
        xT = P.sb("xT", [128, DC, NTP], F32)
        xn = P.sb("xn", [128, DC, NTP], BF16)
        hTt = P.sb("hT", [128, FC * NTP], BF16)
        hT = hTt[:, :].rearrange("p (j n) -> p j n", j=FC)
        rstd = P.sb("rstd", [128, NTP], F32)
        sil = [P.sb(f"sil{i}", [128, NTP], F32) for i in range(2)]
        wring = [P.sb(f"wr{i}", [128, WSLOT_ELEMS], BF16) for i in range(NWSLOT)]
        wTm = P.sb("wTm", [128, 8, 128], BF16)
        T2 = P.sb("T2", [128, 8, 128], F32)
        wTms = P.sb("wTms", [64, 8, 64], BF16)
        T2s = P.sb("T2s", [128, 8, 64], F32)
        aneg = P.sb("aneg", [16, 1], F32)
        STf = P.sb("STf", [128, 1024], F32)
        STb = P.sb("STb", [128, 1024], BF16)
        ctail = P.sb("ctail", [128, 12, 3], BF16)
        ctail32 = P.sb("ctail32", [128, 12, 3], F32)
        bd64 = P.sb("bd64", [128, 128], BF16)
        hsq = P.sb("hsq", [128, NTP], BF16)
        hsq2 = P.sb("hsq2", [128, NTP], BF16)
        hrs = P.sb("hrs", [128, NTP], F32)
        esink2 = P.sb("esink2", [128, 8], F32)
        ptail = P.sb("ptail", [128, 8, 15], F32)
        kprev = P.sb("kprev", [128, 4, 128], BF16)
        vprev = P.sb("vprev", [128, 256], BF16)
        arena_t = P.sb("arena", [128, ARENA_WORDS], F32)
        psall = P.ps("psall", [128, 8 * 512], F32)
        A = Arena(arena_t, ARENA_WORDS)
        sq = hT[:, 0:8, :]
        yst = hTt[:, 0:2 * 4 * D].bitcast(F32).rearrange("p (b d) -> p b d", b=4)
        xin = arena_t[:, 0:4 * D].rearrange("p (b d) -> p b d", b=4)

        b_ident = Buf("ident")
        b_pv = Buf("pv")
        b_cs = Buf("cs")
        b_xin = Buf("xin")
        b_xT = [Buf(f"xT{c}") for c in range(DC)]
        b_xn = Buf("xn")
        b_h = [Buf(f"h{j}") for j in range(FC)]
        b_rstd = Buf("rstd")
        b_sil = [Buf("sil0"), Buf("sil1")]
        b_wr = [Buf(f"wr{i}") for i in range(NWSLOT)]
        b_ps = [Buf(f"ps{i}", excl=True) for i in range(8)]
        b_gm = Buf("gmlp_consts")
        b_ST = Buf("ST")
        b_STb = Buf("STb")
        b_ctail = Buf("ctail")
        b_hsq2 = Buf("hsq2")
        b_hsq, b_hrs, b_gm2, b_ptail, b_kprev, b_dscr = Buf("hsq"), Buf("hrs"), Buf("gm2"), Buf("ptail"), Buf("kprev"), Buf("dscr")
        d_wr = [DmaSlot(S, f"wr{i}") for i in range(NWSLOT)]
        d_xin = DmaSlot(S, "xin")
        d_yst = DmaSlot(S, "yst")
        d_misc = DmaSlot(S, "misc")
        d_out = [DmaSlot(S, f"out{i}") for i in range(8)]
        d_in = [DmaSlot(S, f"in{i}") for i in range(8)]

        state = {"w": 0, "ps": 0, "sil": 0, "out": 0, "in": 0}

        def bank(i):
            return psall[:, i * 512:(i + 1) * 512]

        reserved = set()

        def next_ps():
            i = state["ps"]
            while i in reserved:
                i = (i + 1) % 8
            state["ps"] = (i + 1) % 8
            return bank(i), b_ps[i]

        def ps_group(n):
            i = (state["ps"] + n - 1) // n * n % 8
            while any((i + k) in reserved for k in range(n)):
                i = (i + n) % 8
            state["ps"] = (i + n) % 8
            state["last_group"] = list(range(i, i + n))
            return psall[:, i * 512:(i + n) * 512], [b_ps[i + k] for k in range(n)]

        def next_out():
            i = state["out"]
            state["out"] = (i + 1) % 8
            return d_out[i]

        def next_in():
            i = state["in"]
            state["in"] = (i + 1) % 8
            return d_in[i]

        wcache = {}
        d_ws = [DmaSlot(S, f"ws{i}") for i in range(NWSLOT)]
        use_wcache = cfg.get("wcache", True) and (len([1 for _ in range(n_ptiles)]) + (1 if do_sample else 0)) > 1

        def wload(src_ap, nelem):
            i = state["w"]
            state["w"] = (i + 1) % NWSLOT
            dst = wring[i][:, 0:nelem]
            key = (src_ap.tensor.name, src_ap.offset)
            ent = wcache.get(key) if use_wcache else None
            if ent is None:
                S.op("pool", lambda e, dst=dst, src=src_ap: e.dma_start(out=dst, in_=src),
                     reads=(), writes=(b_wr[i],), dma=d_wr[i])
                if use_wcache:
                    scr = P.dscratch(f"wb{len(wcache)}", [128, nelem], BF16)
                    bscr = Buf(f"wb{len(wcache)}")
                    wcache[key] = (scr, bscr)
                    S.op("sp", lambda e, dst=dst, scr=scr: e.dma_start(out=scr, in_=dst),
                         reads=(b_wr[i],), writes=(bscr,), dma=d_ws[i])
            else:
                scr, bscr = ent
                S.op("pool", lambda e, dst=dst, scr=scr: e.dma_start(out=dst, in_=scr),
                     reads=(bscr,), writes=(b_wr[i],), dma=d_wr[i])
            return wring[i], b_wr[i]

        def cst_ap(name, nparts=128):
            o, n = CST_OFF[name]
            return cs[0:nparts, o:o + n]

        def pvc(name, c=0, nparts=128):
            o = PV_OFF[name]
            return pv[0:nparts, o + c:o + c + 1]

        S.op("pool", lambda e: e.memset(ident[:], 0.0), writes=(b_ident,))
        S.op("pool", lambda e: e.affine_select(out=ident[:], in_=ident[:], pattern=[[-1, 128]],
                                               compare_op=ALU.not_equal, fill=1.0, base=0,
                                               channel_multiplier=1),
             reads=(b_ident,), writes=(b_ident,))
        S.op("pool", lambda e: e.tensor_copy(identb[:], ident[:]), reads=(b_ident,), writes=(b_ident,))
        S.op("pool", lambda e: e.memset(onesb[:], 1.0), writes=(b_ident,))
        S.op("pool", lambda e: e.memset(STf[:], 0.0), writes=(b_ST,))
        S.op("pool", lambda e: e.memset(STb[:], 0.0), writes=(b_STb,))
        S.op("pool", lambda e: e.memset(ctail[:], 0.0), writes=(b_ctail,))
        S.op("sp", lambda e: e.dma_start(out=pv[:], in_=pvec), writes=(b_pv,), dma=d_misc)
        S.op("sp", lambda e: e.dma_start(out=cs[:], in_=cst), writes=(b_cs,), dma=next_in())

        if on("even"):
            A.reset()
            b_tmp = Buf("setup_tmp")
            w32 = A.take(128, [8, 128])
            bsbc = A.take(128, [8, 128])
            w32s = A.take(64, [8, 64])
            S.op("sp", lambda e: e.dma_start(out=w32, in_=wsT.rearrange("p (h i) -> p h i", h=8)),
                 writes=(b_tmp,), dma=next_in(), arena=True)
            S.op("sp", lambda e: e.dma_start(out=bsbc.rearrange("p h i -> p (h i)"),
                                             in_=bs_row.partition_broadcast(128)),
                 writes=(b_tmp,), dma=next_in(), arena=True)
            S.op("pool", lambda e: e.affine_select(out=w32, in_=w32, pattern=[[0, 8], [1, 128]],
                                                   compare_op=ALU.is_ge, fill=0.0, base=0,
                                                   channel_multiplier=-1),
                 reads=(b_tmp,), writes=(b_tmp,))
            S.op("pool", lambda e: e.tensor_copy(wTm[:], w32), reads=(b_tmp,), writes=(b_gm,))
            pg, bpg = ps_group(2)
            S.op("pe", [lambda e, k=k: e.matmul(pg[:, k * 512:(k + 1) * 512], onesb[:, :],
                                                 wTm[:, k * 4:(k + 1) * 4, :].rearrange("p h i -> p (h i)"),
                                                 start=True, stop=True) for k in range(2)],
                 reads=(b_gm, b_ident), writes=bpg)
            for h in range(8):
                S.op("dve", lambda e, h=h: e.scalar_tensor_tensor(
                    out=T2[:, h, :], in0=pg[:, h * 128:(h + 1) * 128], scalar=pvc("ln_b", h),
                    in1=bsbc[:, h, :], op0=ALU.mult, op1=ALU.add),
                    reads=bpg + [b_tmp, b_pv], writes=(b_gm,))
            SST = cfg.get("setup_stop", 99)
            S.op("pool", lambda e: e.memset(w32s, 0.0), writes=(b_tmp,))
            for b in range(16 if SST > 1 else 0):
                S.op("sp", lambda e, b=b: e.dma_start(out=w32s[4 * b:4 * b + 4, :, 4 * b:4 * b + 4],
                                                      in_=wsT4.rearrange("p (h i) -> p h i", h=8)),
                     writes=(b_tmp,), dma=next_in(), arena=True)
            S.op("pool", lambda e: e.affine_select(out=w32s, in_=w32s, pattern=[[0, 8], [1, 64]],
                                                   compare_op=ALU.is_ge, fill=0.0, base=0,
                                                   channel_multiplier=-1),
                 reads=(b_tmp,), writes=(b_tmp,))
            S.op("pool", lambda e: e.tensor_copy(wTms[:], w32s), reads=(b_tmp,), writes=(b_gm,))
            pg2, bpg2 = next_ps()
            S.op("pe", lambda e: e.matmul(pg2[:, 0:512], onesb[0:64, :],
                                          wTms[:, :, :].rearrange("p h i -> p (h i)"), start=True, stop=True),
                 reads=(b_gm, b_ident), writes=(bpg2,))
            for h in range(8 if SST > 2 else 0):
                S.op("dve", lambda e, h=h: e.scalar_tensor_tensor(
                    out=T2s[:, h, :].rearrange("p (b i) -> p b i", i=4),
                    in0=pg2[:, h * 64:(h + 1) * 64].rearrange("p (b i) -> p b i", i=4),
                    scalar=pvc("ln_b", h),
                    in1=bsbc[:, h, 0:4].unsqueeze(1).broadcast_to([128, 16, 4]),
                    op0=ALU.mult, op1=ALU.add),
                    reads=(bpg2, b_tmp, b_pv), writes=(b_gm,))
            S.op("act", lambda e: e.activation(out=aneg[:, :], in_=pvc("a_log", 0, 16), func=AF.Exp),
                 reads=(b_pv,), writes=(b_gm,))
            S.op("dve", lambda e: e.tensor_scalar(out=aneg[:, :], in0=aneg[:, :], scalar1=-1.0, scalar2=None,
                                                  op0=ALU.mult), reads=(b_gm,), writes=(b_gm,))
            S.barrier()

        if on("odd"):
            A.reset()
            b_tmp2 = Buf("setup_tmp2")
            S.op("pool", lambda e: e.memset(bd64[:], 0.0), writes=(b_ident,))
            S.op("pool", lambda e: e.memset(bd64[0:64, 0:64], 1.0), writes=(b_ident,))
            S.op("pool", lambda e: e.memset(bd64[64:128, 64:128], 1.0), writes=(b_ident,))
            S.op("pool", lambda e: e.memset(ptail[:], 0.0), writes=(b_ptail,))
            S.op("pool", lambda e: e.memset(kprev[:], 0.0), writes=(b_kprev,))
            S.op("pool", lambda e: e.memset(vprev[:], 0.0), writes=(b_kprev,))
            es = A.take(128, [16])
            rt = A.take(32, [16])
            dv = A.take(16, [384])
            S.op("sp", lambda e: e.dma_start(out=es, in_=sink_row.partition_broadcast(128)), writes=(b_tmp2,), dma=next_in(), arena=True)
            S.op("sp", lambda e: e.dma_start(out=rt, in_=rel_tab), writes=(b_tmp2,), dma=next_in(), arena=True)
            S.op("act", lambda e: e.activation(out=es, in_=es, func=AF.Exp), reads=(b_tmp2,), writes=(b_tmp2,))
            esv = es.rearrange("p (c t) -> p c t", t=2)
            S.op("dve", lambda e: e.tensor_copy(esink2[0:64, :], esv[0:64, :, 0]), reads=(b_tmp2,), writes=(b_gm2,))
            S.op("dve", lambda e: e.tensor_copy(esink2[64:128, :], esv[64:128, :, 1]), reads=(b_tmp2,), writes=(b_gm2,))
            pgd, bpgd = next_ps()
            S.op("pe", lambda e: e.matmul(pgd[0:16, 0:384], rt[:, :], cst_ap("onehot", 32), start=True, stop=True),
                 reads=(b_tmp2, b_cs), writes=(bpgd,))
            S.op("dve", lambda e: e.tensor_tensor(out=dv, in0=pgd[0:16, 0:384], in1=cst_ap("negmask", 16), op=ALU.add),
                 reads=(bpgd, b_cs), writes=(b_tmp2,))
            S.op("sp", lambda e: e.dma_start(out=dscr, in_=dv.unsqueeze(1).broadcast_to([16, 128, 384])),
                 reads=(b_tmp2,), writes=(b_dscr,), dma=next_in(), arena=True)
            S.barrier()

        def load_x_tile(src_rows, nblk, rows):
            S.op("sp", lambda e: e.dma_start(out=xin[0:rows, 0:nblk, :],
                                             in_=src_rows.rearrange("(b p) d -> p b d", p=rows)),
                 writes=(b_xin,), dma=d_xin, arena=True)

        def transpose_in(nblk, rows):
            nt = nblk * rows
            for c in range(DC):
                pt, bp = next_ps()
                fns = []
                for blk in range(nblk):
                    fns.append(lambda e, pt=pt, blk=blk, c=c: e.transpose(
                        pt[:, blk * rows:(blk + 1) * rows], xin[0:rows, blk, c * 128:(c + 1) * 128],
                        ident[0:rows, 0:rows]))
                S.op("pe", fns, reads=(b_xin, b_ident), writes=(bp,))
                if c % 2:
                    S.op("act", lambda e, pt=pt, c=c: e.copy(xT[:, c, 0:nt], pt[:, 0:nt]),
                         reads=(bp,), writes=(b_xT[c],))
                else:
                    S.op("dve", lambda e, pt=pt, c=c: e.tensor_copy(xT[:, c, 0:nt], pt[:, 0:nt]),
                         reads=(bp,), writes=(b_xT[c],))

        def transpose_out(dst_rows, nblk, rows):
            for blk in range(nblk):
                for half in range(2):
                    pt, bp = next_ps()
                    fns = []
                    for cc in range(4):
                        c = half * 4 + cc
                        fns.append(lambda e, pt=pt, blk=blk, c=c, cc=cc: e.transpose(
                            pt[0:rows, cc * 128:(cc + 1) * 128], xT[:, c, blk * rows:(blk + 1) * rows],
                            ident[:, :]))
                    S.op("pe", fns, reads=[b_xT[half * 4 + cc] for cc in range(4)] + [b_ident],
                         writes=(bp,))
                    if half:
                        S.op("act", lambda e, pt=pt, blk=blk: e.copy(yst[0:rows, blk, 512:1024], pt[0:rows, :]),
                             reads=(bp,), writes=b_h[0:16])
                    else:
                        S.op("dve", lambda e, pt=pt, blk=blk: e.tensor_copy(yst[0:rows, blk, 0:512], pt[0:rows, :]),
                             reads=(bp,), writes=b_h[0:16])
            S.op("sp", lambda e: e.dma_start(out=dst_rows.rearrange("(b p) d -> p b d", p=rows),
                                             in_=yst[0:rows, 0:nblk, :]),
                 reads=b_h[0:16], dma=d_yst, final=True)

        def rms_norm(gname, nt):
            S.op("act", lambda e: e.activation(out=sq[:, :, 0:nt], in_=xT[:, :, 0:nt], func=AF.Square),
                 reads=b_xT, writes=b_h[0:8])
            pt, bp = next_ps()
            S.op("pe", [lambda e, pt=pt, c=c: e.matmul(pt[:, 0:nt], onesb[:, :], sq[:, c, 0:nt],
                                                        start=(c == 0), stop=(c == DC - 1))
                        for c in range(DC)], reads=b_h[0:8] + [b_ident], writes=(bp,))
            S.op("act", lambda e, pt=pt: e.activation(out=rstd[:, 0:nt], in_=pt[:, 0:nt], func=AF.Sqrt,
                                                       bias=EPS, scale=1.0 / D),
                 reads=(bp,), writes=(b_rstd,))
            S.op("dve", lambda e: e.reciprocal(rstd[:, 0:nt], rstd[:, 0:nt]), reads=(b_rstd,), writes=(b_rstd,))
            for c in range(DC):
                S.op("dve", lambda e, c=c: e.scalar_tensor_tensor(
                    out=xn[:, c, 0:nt], in0=xT[:, c, 0:nt], scalar=pvc(gname, c),
                    in1=rstd[:, 0:nt], op0=ALU.mult, op1=ALU.mult),
                    reads=(b_xT[c], b_rstd, b_pv), writes=(b_xn,))

        def ffn(layer, which, nt):
            rms_norm(f"ffn_norm{layer}{which}", nt)
            for grp in range(11):
                w, bw = wload(wgu[layer, which, grp], 8 * 2 * 256)
                wv = w[:, :].rearrange("p (k g n) -> p k g n", k=8, g=2)
                for jj in range(2):
                    j = grp * 2 + jj
                    pg, bpg = next_ps()
                    S.op("pe", [lambda e, pg=pg, k=k, jj=jj, wv=wv: e.matmul(
                        pg[:, 0:nt], wv[:, k, 0, jj * 128:(jj + 1) * 128], xn[:, k, 0:nt],
                        start=(k == 0), stop=(k == 7)) for k in range(8)],
                        reads=(bw, b_xn), writes=(bpg,))
                    pu, bpu = next_ps()
                    S.op("pe", [lambda e, pu=pu, k=k, jj=jj, wv=wv: e.matmul(
                        pu[:, 0:nt], wv[:, k, 1, jj * 128:(jj + 1) * 128], xn[:, k, 0:nt],
                        start=(k == 0), stop=(k == 7)) for k in range(8)],
                        reads=(bw, b_xn), writes=(bpu,))
                    si = state["sil"]
                    state["sil"] = 1 - si
                    S.op("act", lambda e, pg=pg, si=si: e.activation(out=sil[si][:, 0:nt], in_=pg[:, 0:nt],
                                                                      func=AF.Silu),
                         reads=(bpg,), writes=(b_sil[si],))
                    S.op("dve", lambda e, pu=pu, si=si, j=j: e.tensor_tensor(
                        out=hT[:, j, 0:nt], in0=pu[:, 0:nt], in1=sil[si][:, 0:nt], op=ALU.mult),
                        reads=(bpu, b_sil[si]), writes=(b_h[j],))
            for m in range(DC):
                w, bw = wload(wdn[layer, which, m], FC * 128)
                wv = w[:, 0:FC * 128].rearrange("p (j n) -> p j n", j=FC)
                py, bpy = next_ps()
                S.op("pe", [lambda e, py=py, j=j, wv=wv: e.matmul(
                    py[:, 0:nt], wv[:, j, :], hT[:, j, 0:nt], start=(j == 0), stop=(j == FC - 1))
                    for j in range(FC)], reads=[bw] + b_h, writes=(bpy,))
                S.op("dve", lambda e, py=py, m=m: e.scalar_tensor_tensor(
                    out=xT[:, m, 0:nt], in0=py[:, 0:nt], scalar=0.5, in1=xT[:, m, 0:nt],
                    op0=ALU.mult, op1=ALU.add), reads=(bpy, b_xT[m]), writes=(b_xT[m],))

        def gelu_evac(pt, np_, nf, out_ap, bp, wbufs):
            if NATIVE_GELU:
                S.op("act", lambda e: e.activation(out=out_ap, in_=pt, func=AF.Gelu_apprx_tanh), reads=(bp,), writes=wbufs)
                return
            si = state["sil"]
            state["sil"] = 1 - si
            t1 = sil[si][0:np_, 0:nf]
            S.op("act", lambda e: e.activation(out=t1, in_=pt, func=AF.Square), reads=(bp,), writes=(b_sil[si],))
            S.op("dve", lambda e: e.tensor_scalar(out=t1, in0=t1, scalar1=0.044715, scalar2=1.0, op0=ALU.mult, op1=ALU.add),
                 reads=(b_sil[si],), writes=(b_sil[si],))
            S.op("dve", lambda e: e.tensor_tensor(out=t1, in0=t1, in1=pt, op=ALU.mult), reads=(b_sil[si], bp), writes=(b_sil[si],))
            S.op("act", lambda e: e.activation(out=t1, in_=t1, func=AF.Sigmoid, scale=1.5957691216057308),
                 reads=(b_sil[si],), writes=(b_sil[si],))
            S.op("dve", lambda e: e.tensor_tensor(out=out_ap, in0=t1, in1=pt, op=ALU.mult), reads=(b_sil[si], bp), writes=wbufs)

        def proj_fm(wsrc, ncc, nt, evac):
            w, bw = wload(wsrc, 8 * 512)
            wv = w[:, :].rearrange("p (k n) -> p k n", k=8)
            pending = None
            for cc in range(ncc):
                pt, bp = next_ps()
                S.op("pe", [lambda e, pt=pt, k=k, cc=cc, wv=wv: e.matmul(
                    pt[:, 0:nt], wv[:, k, cc * 128:(cc + 1) * 128], xn[:, k, 0:nt],
                    start=(k == 0), stop=(k == 7)) for k in range(8)], reads=(bw, b_xn), writes=(bp,))
                if pending is not None:
                    pending()
                pending = evac(cc, pt, bp)
            if pending is not None:
                pending()

        def even_mixer(kind, nt, is_last_ptile):
            STOP = cfg.get("even_stop", 99)
            if STOP <= 0:
                return
            S.barrier()
            A.reset()
            smp = (kind == "s")
            CL = 64 if smp else 128
            nblk = nt // CL
            Lc = 4 if smp else 128
            nch = nt // Lc
            uT = hT[:, 0:8, :]
            yaT = hT[:, 8:16, :]
            bcT = hT[:, 16:20, :]
            b_u, b_ya, b_bc = b_h[0:8], b_h[8:16], b_h[16:20]
            zT = A.take(128, [8, nt], BF16)
            ybT = A.take(128, [8, nt], BF16)
            xcT = A.take(128, [8, nt], BF16)
            ext = A.take(128, [12, (16 * 7) if smp else (NTP + 3)], BF16)
            cvA = A.take(128, [nt])
            cvB = A.take(128, [nt])
            vg = [A.take(128, [1024]) for _ in range(1 if smp else 2)] * (2 if smp else 1)
            vhb = [A.take(128, [1024], BF16) for _ in range(1 if smp else 2)] * (2 if smp else 1)
            mvst = A.take(128, [2, 6])
            mv = A.take(128, [2])
            gt = None
            dsc = A.take(16, [4, nt])
            tsc = A.take(128, [64])
            absx = A.take(128, [16, CL])
            Eb = A.take(128, [16, CL], BF16)
            Csb = A.take(128, [16, CL], BF16)
            cbm = A.take(128, [2, CL])
            xd = A.take(128, [16, 64], BF16)
            xdd = A.take(128, [16, 64], BF16)
            Btok = A.take(128, [2, 128], BF16)
            ygb = A.take(128, [8, 128])
            sqg = A.take(128, [8, 128], BF16)
            rsg = A.take(128, [2, 128])
            bz, byb, bxc, bext, bcv = Buf("zT"), Buf("ybT"), Buf("xcT"), Buf("ext"), [Buf("cvA"), Buf("cvB")]
            bvg, bvh, bmv, bgt = [Buf("vg0"), Buf("vg1")], [Buf("vh0"), Buf("vh1")], Buf("mv"), [Buf("gt0"), Buf("gt1")]
            bdsc, btsc, babs, bEb, bCs, bcbm = Buf("dsc"), Buf("tsc"), Buf("absx"), Buf("Eb"), Buf("Csb"), Buf("cbm")
            bxd, bxdd, bBt, bygb, bsqg, brsg = Buf("xd"), Buf("xdd"), Buf("Btok"), Buf("ygb"), Buf("sqg"), Buf("rsg")
            if smp:
                xpre = A.take(128, [12, 64])
                cso = A.take(64, [1536])
                stc = cso[0:48, :]
                lnbc = A.take(128, [2048])
                vln = A.take(64, [1024])
                bmsk = cst_ap("bmask", 64)
                Bblk = A.take(64, [2, 16, 128], BF16)
                cdT = A.take(128, [8, 16])
                eal = A.take(16, [16])
                Snat = A.take(128, [2, 8, 128])
                STs = A.take(128, [2, 1024], BF16)
                bxpre, bcso, blnbc, bvln = Buf("xpre"), Buf("cso"), Buf("lnbc"), Buf("vln")
                bstc = bcso
                bBblk, bcdT, beal, bSnat, bSTs = Buf("Bblk"), Buf("cdT"), Buf("eal"), Buf("Snat"), Buf("STs")
                S.op("sp", lambda e: e.dma_start(out=lnbc, in_=lnrow.partition_broadcast(128)),
                     writes=(blnbc,), dma=next_in(), arena=True)
                S.op("sp", lambda e: e.dma_start(out=stc, in_=st_conv), writes=(bstc,), dma=next_in(), arena=True)

            rms_norm(f"mix_norm{0}", nt)
            if STOP <= 0.5:
                return

            for gi in range(2):
                def ev_u(cc, pt, bp, gi=gi):
                    c = gi * 4 + cc
                    gelu_evac(pt[:, 0:nt], 128, nt, uT[:, c, 0:nt], bp, (b_u[c],))
                proj_fm(wev_fm[gi], 4, nt, ev_u)

            if STOP <= 1:
                return
            wv0, bwv0 = wload(wev_v[0], 8 * 512)
            wv1, bwv1 = wload(wev_v[1], 8 * 512)
            wvv = [wv0[:, :].rearrange("p (k n) -> p k n", k=8), wv1[:, :].rearrange("p (k n) -> p k n", k=8)]
            bwv = [bwv0, bwv1]
            wg = wTms if smp else wTm
            T2x = T2s if smp else T2
            def stage_v(blk):
                cols = slice(blk * CL, (blk + 1) * CL)
                bi = blk % 2
                for half in range(2):
                    pt, bp = next_ps()
                    S.op("pe", [lambda e, pt=pt, k=k, half=half: e.matmul(
                        pt[0:CL, :], xn[:, k, cols], wvv[half][:, k, :], start=(k == 0), stop=(k == 7))
                        for k in range(8)], reads=(bwv[half], b_xn), writes=(bp,))
                    gelu_evac(pt[0:CL, :], CL, 512, vg[bi][0:CL, half * 512:(half + 1) * 512], bp, (bvg[bi],))
                S.op("dve", [lambda e, bi=bi: e.bn_stats(mvst[0:CL, 0, :], vg[bi][0:CL, 0:512]),
                             lambda e, bi=bi: e.bn_stats(mvst[0:CL, 1, :], vg[bi][0:CL, 512:1024])],
                     reads=(bvg[bi],), writes=(bmv,))
                S.op("dve", lambda e: e.bn_aggr(mv[0:CL, :], mvst[0:CL, :, :].rearrange("p a b -> p (a b)")),
                     reads=(bmv,), writes=(bmv,))
                S.op("act", lambda e: e.activation(out=mv[0:CL, 1:2], in_=mv[0:CL, 1:2], func=AF.Sqrt, bias=EPS, scale=1.0),
                     reads=(bmv,), writes=(bmv,))
                S.op("dve", lambda e: e.reciprocal(mv[0:CL, 1:2], mv[0:CL, 1:2]), reads=(bmv,), writes=(bmv,))
                S.op("dve", lambda e, bi=bi: e.tensor_scalar(
                    out=vhb[bi][0:CL, :], in0=vg[bi][0:CL, :], scalar1=mv[0:CL, 0:1], scalar2=mv[0:CL, 1:2],
                    op0=ALU.subtract, op1=ALU.mult), reads=(bmv, bvg[bi]), writes=(bvh[bi],))
                if smp:
                    S.op("dve", lambda e, bi=bi: e.tensor_scalar(
                        out=vln[:, :], in0=vg[bi][0:CL, :], scalar1=mv[0:CL, 0:1], scalar2=mv[0:CL, 1:2],
                        op0=ALU.subtract, op1=ALU.mult), reads=(bmv, bvg[bi]), writes=(bvln,))
                    S.op("dve", lambda e: e.tensor_tensor(out=vln[:, :], in0=vln[:, :], in1=lnbc[0:64, 0:1024], op=ALU.mult),
                         reads=(bvln, blnbc), writes=(bvln,))
                    S.op("dve", lambda e: e.tensor_tensor(out=vln[:, :], in0=vln[:, :], in1=lnbc[0:64, 1024:2048], op=ALU.add),
                         reads=(bvln, blnbc), writes=(bvln,))
                    S.op("sp", lambda e: e.dma_start(out=o_av, in_=vln[:, :]), reads=(bvln,), dma=next_out(),
                         final=True, arena=True)

            def stage_g(blk):
                cols = slice(blk * CL, (blk + 1) * CL)
                bi = blk % 2
                for hg in range(2):
                    pt, bp = next_ps()
                    S.op("pe", [lambda e, pt=pt, hh=hh, hg=hg, bi=bi: e.matmul(
                        pt[:, hh * CL:(hh + 1) * CL], vhb[bi][0:CL, (hg * 4 + hh) * 128:(hg * 4 + hh + 1) * 128],
                        wg[0:CL, hg * 4 + hh, 0:CL], start=True, stop=True) for hh in range(4)],
                        reads=(bvh[bi], b_gm), writes=(bp,))
                    hs = slice(hg * 4, hg * 4 + 4)
                    gtb = ygb[:, 0:4, 0:CL]
                    o_lng = PV_OFF["ln_g"]
                    S.op("dve", lambda e, pt=pt, hg=hg: e.tensor_tensor(
                        out=gtb, in0=pt[:, 0:4 * CL].rearrange("p (h i) -> p h i", h=4),
                        in1=pv[:, o_lng + hg * 4:o_lng + hg * 4 + 4].unsqueeze(2).broadcast_to([128, 4, CL]), op=ALU.mult),
                        reads=(bp, b_pv), writes=(bygb,))
                    S.op("dve", lambda e, hs=hs: e.tensor_tensor(out=gtb, in0=gtb, in1=T2x[:, hs, 0:CL], op=ALU.add),
                         reads=(bygb, b_gm), writes=(bygb,))
                    S.op("dve", lambda e, hs=hs: e.tensor_tensor(out=yaT[:, hs, cols], in0=gtb, in1=uT[:, hs, cols], op=ALU.mult),
                         reads=[bygb] + b_u[hg * 4:hg * 4 + 4], writes=b_ya[hg * 4:hg * 4 + 4])


            stage_v(0)
            for blk in range(nblk):
                if blk + 1 < nblk:
                    stage_v(blk + 1)
                stage_g(blk)

            if STOP <= 2:
                return
            for gi in range(2):
                def ev_z(cc, pt, bp, gi=gi):
                    c = gi * 4 + cc
                    if cfg.get("zmode", 0) == 0:
                        S.op("act", lambda e: e.activation(out=zT[:, c, 0:nt], in_=pt[:, 0:nt], func=AF.Silu),
                             reads=(bp,), writes=(bz,))
                    elif cfg.get("zmode", 0) == 1:
                        S.op("act", lambda e: e.activation(out=ybT[:, c, 0:nt], in_=pt[:, 0:nt], func=AF.Silu),
                             reads=(bp,), writes=(bz,))
                proj_fm(wev_fm[2 + gi], 4, nt, ev_z)
            if STOP <= 2.2:
                return
            if smp:
                extv = ext.rearrange("p c (b r) -> p c b r", r=7)
                pg, bpg = ps_group(2)
                S.op("pe", [lambda e, c=c: e.transpose(pg[:, c * 64:c * 64 + 48], stc[0:48, c * 128:(c + 1) * 128],
                                                        ident[0:48, 0:48]) for c in range(12)],
                     reads=(bstc, b_ident), writes=bpg)
                S.op("dve", lambda e: e.tensor_copy(
                    extv[:, :, :, 0:3],
                    pg[:, 0:768].rearrange("p (c x) -> p c x", c=12)[:, :, 0:48].rearrange("p c (b r) -> p c b r", r=3)),
                     reads=bpg, writes=(bext,))
            else:
                S.op("dve", lambda e: e.tensor_copy(ext[:, :, 0:3], ctail[:, :, :]), reads=(b_ctail,), writes=(bext,))
            XM = cfg.get("xmode", 9)
            for gi in range(3 if XM >= 1 else 0):
                def ev_x(cc, pt, bp, gi=gi):
                    c = gi * 4 + cc
                    if XM < 2:
                        return
                    if smp:
                        S.op("act", lambda e: e.copy(extv[:, c, :, 3:7], pt[:, 0:64].rearrange("p (b i) -> p b i", i=4)),
                             reads=(bp,), writes=(bext,))
                        S.op("dve", lambda e: e.tensor_copy(xpre[:, c, :], pt[:, 0:64]), reads=(bp,), writes=(bxpre,))
                    else:
                        S.op("act", lambda e: e.copy(ext[:, c, 3:3 + nt], pt[:, 0:nt]), reads=(bp,), writes=(bext,))
                        if is_last_ptile and XM >= 3:
                            S.op("dve", lambda e: e.tensor_copy(ctail32[:, c, :], pt[:, nt - 3:nt]),
                                 reads=(bp,), writes=(b_ctail,))
                proj_fm(wev_fm[4 + gi], 4, nt, ev_x)
            if STOP <= 2.4:
                return
            wd, bwd = wload(wev_dt, 8 * 16)
            wdv = wd[:, 0:128].rearrange("p (k n) -> p k n", k=8)
            pt, bp = next_ps()
            S.op("pe", [lambda e, k=k: e.matmul(pt[0:16, 0:nt], wdv[:, k, :], xn[:, k, 0:nt],
                                                 start=(k == 0), stop=(k == 7)) for k in range(8)],
                 reads=(bwd, b_xn), writes=(bp,))
            S.op("act", lambda e: e.activation(out=dsc[:, 0, 0:nt], in_=pt[0:16, 0:nt], func=AF.Exp,
                                               bias=pvc("dt_bias", 0, 16), scale=1.0),
                 reads=(bp, b_pv), writes=(bdsc,))
            if STOP <= 2.6:
                return
            S.op("act", lambda e: e.activation(out=dsc[:, 0, 0:nt], in_=dsc[:, 0, 0:nt], func=AF.Ln, bias=1.0, scale=1.0),
                 reads=(bdsc,), writes=(bdsc,))
            S.op("dve", lambda e: e.tensor_scalar(out=dsc[:, 1, 0:nt], in0=dsc[:, 0, 0:nt], scalar1=aneg[:, 0:1],
                                                  scalar2=None, op0=ALU.mult), reads=(bdsc, b_gm), writes=(bdsc,))
            if STOP <= 2.8:
                return
            rmask = cst_ap("rmask_s", 16)[:, 0:nt] if smp else cst_ap("rmask_p", 16)[:, 0:nt]
            S.op("dve", lambda e: e.tensor_tensor_scan(out=dsc[:, 2, 0:nt], data0=rmask, data1=dsc[:, 1, 0:nt],
                                                       initial=0.0, op0=ALU.mult, op1=ALU.add),
                 reads=(bdsc, b_cs), writes=(bdsc,))
            S.op("dve", lambda e: e.tensor_copy(
                dsc[:, 3, 0:nt].rearrange("p (c l) -> p c l", l=Lc),
                dsc[:, 2, 0:nt].rearrange("p (c l) -> p c l", l=Lc)[:, :, Lc - 1:Lc].broadcast_to([16, nch, Lc])),
                reads=(bdsc,), writes=(bdsc,))

            if STOP <= 3:
                return
            for c in range(12):
                if smp:
                    src = lambda tap, c=c: extv[:, c, :, tap:tap + 4]
                    v3 = lambda ap: ap[:, 0:64].rearrange("p (b i) -> p b i", i=4)
                else:
                    src = lambda tap, c=c: ext[:, c, tap:tap + nt]
                    v3 = lambda ap: ap[:, 0:nt]
                S.op("dve", lambda e, c=c, src=src, v3=v3: e.tensor_scalar(
                    out=v3(cvA), in0=src(0), scalar1=pvc("conv_w", c * 4 + 0), scalar2=pvc("conv_b", c),
                    op0=ALU.mult, op1=ALU.add), reads=(bext, b_pv), writes=(bcv[0],))
                S.op("dve", lambda e, c=c, src=src, v3=v3: e.scalar_tensor_tensor(
                    out=v3(cvB), in0=src(1), scalar=pvc("conv_w", c * 4 + 1), in1=v3(cvA),
                    op0=ALU.mult, op1=ALU.add), reads=(bext, b_pv, bcv[0]), writes=(bcv[1],))
                S.op("dve", lambda e, c=c, src=src, v3=v3: e.scalar_tensor_tensor(
                    out=v3(cvA), in0=src(2), scalar=pvc("conv_w", c * 4 + 2), in1=v3(cvB),
                    op0=ALU.mult, op1=ALU.add), reads=(bext, b_pv, bcv[1]), writes=(bcv[0],))
                S.op("dve", lambda e, c=c, src=src, v3=v3: e.scalar_tensor_tensor(
                    out=v3(cvB), in0=src(3), scalar=pvc("conv_w", c * 4 + 3), in1=v3(cvA),
                    op0=ALU.mult, op1=ALU.add), reads=(bext, b_pv, bcv[0]), writes=(bcv[1],))
                if c < 8:
                    S.op("act", lambda e, c=c: e.activation(out=xcT[:, c, 0:nt], in_=cvB[:, 0:nt], func=AF.Silu),
                         reads=(bcv[1],), writes=(bxc,))
                else:
                    S.op("act", lambda e, c=c: e.activation(out=bcT[:, c - 8, 0:nt], in_=cvB[:, 0:nt], func=AF.Silu),
                         reads=(bcv[1],), writes=b_bc)
            if not smp:
                S.op("dve", lambda e: e.tensor_copy(ctail[:, :, :], ext[:, :, nt:nt + 3]), reads=(bext,), writes=(b_ctail,))
                if is_last_ptile:
                    for r in range(3):
                        S.op("sp", lambda e, r=r: e.dma_start(
                            out=o_conv_p[r:r + 1, :].rearrange("r (c p) -> p (r c)", p=128),
                            in_=ctail32[:, :, r], allow_slow_non_contiguous=True),
                            reads=(b_ctail,), dma=next_out(), final=True)
            else:
                pg, bpg = ps_group(4)
                S.op("pe", [lambda e, c=c: e.transpose(pg[0:64, c * 128:(c + 1) * 128], xpre[:, c, :], ident[:, :])
                            for c in range(12)], reads=(bxpre, b_ident), writes=bpg)
                S.op("act", lambda e: e.copy(cso[:, :], pg[0:64, 0:1536]), reads=bpg, writes=(bcso,))
                for r in range(3):
                    S.op("sp", lambda e, r=r: e.dma_start(out=o_conv_s[:, r, :], in_=cso[1 + r:64:4, :]),
                         reads=(bcso,), dma=next_out(), final=True, arena=True)

            if STOP <= 4:
                return
            if smp:
                S.op("act", lambda e: e.activation(out=eal[:, :].unsqueeze(2), in_=dsc[:, 2, 0:64].rearrange("p (b i) -> p b i", i=4)[:, :, 3:4],
                                                   func=AF.Exp), reads=(bdsc,), writes=(beal,))
                pt, bp = next_ps()
                S.op("pe", [lambda e, h=h: e.matmul(
                    pt[:, h * 16:(h + 1) * 16], ident[0:16, h:h + 1].broadcast_to([16, 128]),
                    eal[:, :], start=True, stop=True) for h in range(16)], reads=(beal, b_ident), writes=(bp,))
                ptv = pt[:, 0:256].rearrange("p (c t b) -> p c t b", c=8, t=2)
                S.op("dve", lambda e: e.tensor_copy(cdT[0:64, :, :], ptv[0:64, :, 0, :]), reads=(bp,), writes=(bcdT,))
                S.op("dve", lambda e: e.tensor_copy(cdT[64:128, :, :], ptv[64:128, :, 1, :]), reads=(bp,), writes=(bcdT,))
            maskT = cst_ap("maskT_blk", 64) if smp else cst_ap("maskT_causal", 128)
            for blk in range(cfg.get("nssd", nblk) if not smp else nblk):
                cols = slice(blk * CL, (blk + 1) * CL)
                pt, bp = next_ps()
                S.op("pe", [lambda e, q=q: e.transpose(pt[0:CL, q * 16:(q + 1) * 16], dsc[:, (0, 2, 3)[q], cols],
                                                        ident[0:16, 0:16]) for q in range(3)],
                     reads=(bdsc, b_ident), writes=(bp,))
                S.op("dve", lambda e, pt=pt: e.tensor_copy(tsc[0:CL, 0:48], pt[0:CL, 0:48]), reads=(bp,), writes=(btsc,))
                S.op("dve", lambda e: e.tensor_tensor(out=tsc[0:CL, 48:64], in0=tsc[0:CL, 32:48], in1=tsc[0:CL, 16:32],
                                                      op=ALU.subtract), reads=(btsc,), writes=(btsc,))
                S.op("act", lambda e: e.activation(out=tsc[0:CL, 48:64], in_=tsc[0:CL, 48:64], func=AF.Exp),
                     reads=(btsc,), writes=(btsc,))
                if cfg.get('sstage', 99) <= 1:
                    continue
                pa, bpa = ps_group(4)
                S.op("pe", [lambda e, h=h: e.matmul(pa[:, h * CL:(h + 1) * CL],
                                                     ident[0:16, h:h + 1].broadcast_to([16, 128]),
                                                     dsc[:, 2, cols], start=True, stop=True) for h in range(16)],
                     reads=(bdsc, b_ident), writes=bpa)
                pav = pa[:, 0:16 * CL].rearrange("p (h i) -> p h i", h=16)
                S.op("dve", lambda e: e.tensor_tensor(
                    out=absx[0:CL, :, 0:CL], in0=pav[0:CL, :, :],
                    in1=tsc[0:CL, 16:32].unsqueeze(2).broadcast_to([CL, 16, CL]), op=ALU.subtract),
                    reads=bpa + [btsc], writes=(babs,))
                S.op("dve", lambda e: e.tensor_scalar(out=absx[0:CL, :, 0:CL], in0=absx[0:CL, :, 0:CL], scalar1=0.0, scalar2=None,
                                                      op0=ALU.min), reads=(babs,), writes=(babs,))
                S.op("act", lambda e: e.activation(out=Eb[0:CL, :, 0:CL], in_=absx[0:CL, :, 0:CL], func=AF.Exp),
                     reads=(babs,), writes=(bEb,))
                S.op("act", lambda e: e.activation(out=absx[:, :, 0:CL], in_=pav, func=AF.Exp),
                     reads=bpa, writes=(babs,))
                if cfg.get('sstage', 99) <= 2:
                    continue
                pc, bpc = next_ps()
                S.op("pe", [lambda e, g=g: e.matmul(pc[0:CL, g * CL:(g + 1) * CL], bcT[:, g, cols], bcT[:, 2 + g, cols],
                                                     start=True, stop=True) for g in range(2)],
                     reads=b_bc, writes=(bpc,))
                if cfg.get("cbx", 9) >= 1:
                  S.op("dve", lambda e, pc=pc: e.tensor_tensor(
                    out=cbm[0:CL, :, 0:CL], in0=pc[0:CL, 0:2 * CL].rearrange("p (g i) -> p g i", g=2),
                    in1=maskT[:, 0:CL].unsqueeze(1).broadcast_to([CL, 2, CL]), op=ALU.mult),
                    reads=(bpc, b_cs), writes=(bcbm,))
                for g in range(2 if cfg.get("cbx", 9) >= 2 else 0):
                    S.op("dve", lambda e, g=g: e.tensor_tensor(
                        out=Eb[0:CL, g * 8:(g + 1) * 8, 0:CL], in0=Eb[0:CL, g * 8:(g + 1) * 8, 0:CL],
                        in1=cbm[0:CL, g:g + 1, 0:CL].broadcast_to([CL, 8, CL]), op=ALU.mult),
                        reads=(bEb, bcbm), writes=(bEb,))
                    S.op("dve", lambda e, g=g: e.tensor_tensor(
                        out=Csb[:, g * 8:(g + 1) * 8, 0:CL], in0=absx[:, g * 8:(g + 1) * 8, 0:CL],
                        in1=bcT[:, 2 + g, cols].unsqueeze(1).broadcast_to([128, 8, CL]), op=ALU.mult),
                        reads=[babs] + b_bc, writes=(bCs,))
                if cfg.get('sstage', 99) <= 3:
                    continue
                px, bpx = next_ps()
                pxb = px.bitcast(BF16)
                S.op("pe", [lambda e, c=c: e.transpose(pxb[0:CL, c * 128:(c + 1) * 128], xcT[:, c, cols], identb[:, :])
                            for c in range(8)], reads=(bxc, b_ident), writes=(bpx,))
                S.op("dve", lambda e, pxb=pxb: e.tensor_tensor(
                    out=xd[0:CL, :, :], in0=pxb[0:CL, 0:1024].rearrange("p (h q) -> p h q", h=16),
                    in1=tsc[0:CL, 0:16].unsqueeze(2).broadcast_to([CL, 16, 64]), op=ALU.mult),
                    reads=(bpx, btsc), writes=(bxd,))
                S.op("dve", lambda e: e.tensor_tensor(
                    out=xdd[0:CL, :, :], in0=xd[0:CL, :, :],
                    in1=tsc[0:CL, 48:64].unsqueeze(2).broadcast_to([CL, 16, 64]), op=ALU.mult),
                    reads=(bxd, btsc), writes=(bxdd,))
                pb_, bpb = next_ps()
                pbb = pb_.bitcast(BF16)
                S.op("pe", [lambda e, g=g: e.transpose(pbb[0:CL, g * 128:(g + 1) * 128], bcT[:, g, cols], identb[:, :])
                            for g in range(2)], reads=b_bc + [b_ident], writes=(bpb,))
                S.op("act", lambda e, pbb=pbb: e.copy(Btok[0:CL, :, :], pbb[0:CL, 0:256].rearrange("p (g n) -> p g n", g=2)),
                     reads=(bpb,), writes=(bBt,))
                if cfg.get('sstage', 99) <= 4:
                    continue
                py, bpy = ps_group(2)
                py_banks = list(state["last_group"])
                fns = []
                for h in range(16):
                    dst = py[64 * (h % 2):64 * (h % 2) + 64, (h // 2) * CL:(h // 2 + 1) * CL]
                    first = (h < 2) or (not smp and ((h // 2) % 4 == 0))
                    if smp:
                        fns.append(lambda e, h=h, dst=dst, first=first: e.matmul(
                            dst, xd[0:CL, h, :], Eb[0:CL, h, 0:CL], start=first, stop=True, skip_group_check=True))
                    else:
                        fns.append(lambda e, h=h, dst=dst: e.matmul(
                            dst, xd[0:CL, h, :], Eb[0:CL, h, 0:CL], start=True, stop=False, skip_group_check=True))
                        fns.append(lambda e, h=h, dst=dst: e.matmul(
                            dst, STb[:, h * 64:(h + 1) * 64], Csb[:, h, 0:CL], start=False, stop=True,
                            skip_group_check=True))
                S.op("pe", fns, reads=(bxd, bEb, b_STb, bCs), writes=bpy)
                if smp:
                    reserved.update(py_banks)
                    sample_states(py, bpy, Snat, bSnat, STs, bSTs, Csb, bCs, xdd, bxdd, Btok, bBt, Bblk, bBblk,
                                  bmsk, cdT, bcdT)
                elif cfg.get('sstage', 99) > 5:
                    pst, bpst = ps_group(2)
                    S.op("pe", [lambda e, g=g: e.matmul(pst[:, g * 512:(g + 1) * 512], Btok[0:CL, g, :],
                                                         xdd[0:CL, g * 8:(g + 1) * 8, :].rearrange("p h q -> p (h q)"),
                                                         start=True, stop=True) for g in range(2)],
                         reads=(bBt, bxdd), writes=bpst)
                    S.op("dve", lambda e: e.tensor_tensor(
                        out=STf[:, :].rearrange("p (h q) -> p h q", h=16),
                        in0=STf[:, :].rearrange("p (h q) -> p h q", h=16),
                        in1=absx[:, :, CL - 1:CL].broadcast_to([128, 16, 64]), op=ALU.mult),
                        reads=(b_ST, babs), writes=(b_ST,))
                    S.op("dve", lambda e, pst=pst: e.tensor_tensor(out=STf[:, :], in0=STf[:, :], in1=pst[:, 0:1024], op=ALU.add),
                         reads=[b_ST] + bpst, writes=(b_ST,))
                    S.op("act", lambda e: e.copy(STb[:, :], STf[:, :]), reads=(b_ST,), writes=(b_STb,))
                if cfg.get('sstage', 99) <= 6:
                    continue
                o_ds = PV_OFF["dskip"]
                S.op("dve", lambda e: e.tensor_tensor(
                    out=ygb[:, :, 0:CL], in0=xcT[:, :, cols], in1=pv[:, o_ds:o_ds + 8].unsqueeze(2).broadcast_to([128, 8, CL]),
                    op=ALU.mult), reads=(bxc, b_pv), writes=(bygb,))
                S.op("dve", lambda e, py=py: e.tensor_tensor(
                    out=ygb[:, :, 0:CL], in0=ygb[:, :, 0:CL], in1=py[:, 0:8 * CL].rearrange("p (c i) -> p c i", c=8), op=ALU.add),
                    reads=[bygb] + bpy, writes=(bygb,))
                S.op("dve", lambda e: e.tensor_tensor(out=ygb[:, :, 0:CL], in0=ygb[:, :, 0:CL], in1=zT[:, :, cols], op=ALU.mult),
                     reads=(bygb, bz), writes=(bygb,))
                S.op("act", lambda e: e.activation(out=sqg[:, :, 0:CL], in_=ygb[:, :, 0:CL], func=AF.Square),
                     reads=(bygb,), writes=(bsqg,))
                pss, bpss = next_ps()
                for g in range(2):
                    S.op("pe", [lambda e, g=g, cc=cc, pss=pss: e.matmul(
                        pss[:, g * CL:(g + 1) * CL], onesb[:, :], sqg[:, g * 4 + cc, 0:CL], start=(cc == 0), stop=(cc == 3))
                        for cc in range(4)], reads=(bsqg, b_ident), writes=(bpss,))
                S.op("act", lambda e, pss=pss: e.activation(out=rsg[:, :, 0:CL], in_=pss[:, 0:2 * CL].rearrange("p (g i) -> p g i", g=2),
                                                             func=AF.Sqrt, bias=EPS, scale=1.0 / 512),
                     reads=(bpss,), writes=(brsg,))
                S.op("dve", lambda e: e.reciprocal(rsg[:, :, 0:CL], rsg[:, :, 0:CL]), reads=(brsg,), writes=(brsg,))
                reserved.difference_update(py_banks)
                o_ng = PV_OFF["norm_g"]
                S.op("dve", lambda e: e.tensor_tensor(
                    out=ygb[:, :, 0:CL], in0=ygb[:, :, 0:CL], in1=pv[:, o_ng:o_ng + 8].unsqueeze(2).broadcast_to([128, 8, CL]),
                    op=ALU.mult), reads=(bygb, b_pv), writes=(bygb,))
                for g in range(2):
                    S.op("dve", lambda e, g=g: e.tensor_tensor(
                        out=ybT[:, g * 4:(g + 1) * 4, cols], in0=ygb[:, g * 4:(g + 1) * 4, 0:CL],
                        in1=rsg[:, g:g + 1, 0:CL].broadcast_to([128, 4, CL]), op=ALU.mult),
                        reads=(bygb, brsg), writes=(byb,))

            if STOP <= 5:
                return
            if cfg.get("obar", 0):
                S.barrier(engines=("pe", "act", "dve", "sp", "pool"))
            for mi in range(4):
                if cfg.get("omode", 9) == 6:
                    w, bw = wring[mi], b_wr[mi]
                else:
                    w, bw = wload(wev_out[mi], 16 * 256)
                wv = w[:, :].rearrange("p (k n) -> p k n", k=16)
                for mm in range(2):
                    m = mi * 2 + mm
                    pt, bp = next_ps()
                    fns = []
                    for k in range(16):
                        rhs = yaT[:, k, 0:nt] if k < 8 else ybT[:, k - 8, 0:nt]
                        fns.append(lambda e, pt=pt, k=k, mm=mm, wv=wv, rhs=rhs: e.matmul(
                            pt[:, 0:nt], wv[:, k, mm * 128:(mm + 1) * 128], rhs, start=(k == 0), stop=(k == 15)))
                    OM = cfg.get("omode", 9)
                    if OM >= 1:
                        S.op("pe", fns[0:OM] if OM < 9 else fns, reads=[bw, byb] + b_ya, writes=(bp,))
                    if OM >= 9:
                        S.op("dve", lambda e, pt=pt, m=m: e.tensor_tensor(out=xT[:, m, 0:nt], in0=pt[:, 0:nt], in1=xT[:, m, 0:nt],
                                                                          op=ALU.add), reads=(bp, b_xT[m]), writes=(b_xT[m],))
            if STOP <= 6:
                return
            if is_last_ptile:
                so = absx[:, 0:8, :]
                bso = babs
                pg, bpg = ps_group(2)
                S.op("pe", [lambda e, c=c: e.transpose(pg[:, c * 128:(c + 1) * 128], STf[:, c * 128:(c + 1) * 128], ident[:, :])
                            for c in range(8)], reads=(b_ST, b_ident), writes=bpg)
                S.op("act", lambda e: e.copy(so.rearrange("p c n -> p (c n)"), pg[:, 0:1024]), reads=bpg, writes=(bso,))
                S.op("sp", lambda e: e.dma_start(out=o_ssm_p.rearrange("(c p) n -> p c n", p=128), in_=so),
                     reads=(bso,), dma=next_out(), final=True, arena=True)

        def sample_states(py, bpy, Snat, bSnat, STs, bSTs, Csb, bCs, xdd, bxdd, Btok, bBt, Bblk, bBblk, bmsk, cdT, bcdT):
            for g in range(2):
                S.op("dve", lambda e, g=g: e.tensor_tensor(
                    out=Bblk[:, g, :, :], in0=Btok[0:64, g, :].unsqueeze(1).broadcast_to([64, 16, 128]),
                    in1=bmsk.unsqueeze(2).broadcast_to([64, 16, 128]), op=ALU.mult),
                    reads=(bBt, b_cs), writes=(bBblk,))
            NB = 2
            for bg in range(16 // NB):
                S.op("sp", lambda e, bg=bg: e.dma_start(
                    out=Snat[:, :, :, :], in_=st_ssm[bg * NB:(bg + 1) * NB].rearrange("b (c p) n -> p b c n", p=128)),
                    writes=(bSnat,), dma=next_in(), arena=True)
                for bb in range(NB):
                    pg, bpg = ps_group(2)
                    S.op("pe", [lambda e, c=c, bb=bb, pg=pg: e.transpose(pg[:, c * 128:(c + 1) * 128], Snat[:, bb, c, :], ident[:, :])
                                for c in range(8)], reads=(bSnat, b_ident), writes=bpg)
                    if bb % 2:
                        S.op("act", lambda e, bb=bb, pg=pg: e.copy(STs[:, bb, :], pg[:, 0:1024]), reads=bpg, writes=(bSTs,))
                    else:
                        S.op("dve", lambda e, bb=bb, pg=pg: e.tensor_copy(STs[:, bb, :], pg[:, 0:1024]), reads=bpg, writes=(bSTs,))
                fns = []
                for bb in range(NB):
                    b = bg * NB + bb
                    for h in range(16):
                        dst = py[64 * (h % 2):64 * (h % 2) + 64, (h // 2) * 64 + 4 * b:(h // 2) * 64 + 4 * b + 4]
                        fns.append(lambda e, h=h, bb=bb, b=b, dst=dst: e.matmul(
                            dst, STs[:, bb, h * 64:(h + 1) * 64], Csb[:, h, 4 * b:4 * b + 4], start=False, stop=True,
                            skip_group_check=True))
                S.op("pe", fns, reads=(bSTs, bCs), writes=bpy)
                for c in range(8):
                    pt, bp = next_ps()
                    S.op("pe", lambda e, c=c, pt=pt, bg=bg: e.matmul(
                        pt[:, 0:NB * 128], xdd[0:64, 2 * c:2 * c + 2, :].rearrange("p h q -> p (h q)"),
                        Bblk[:, c // 4, bg * NB:(bg + 1) * NB, :].rearrange("p b n -> p (b n)"), start=True, stop=True),
                        reads=(bxdd, bBblk), writes=(bp,))
                    for bb in range(NB):
                        b = bg * NB + bb
                        S.op("dve", lambda e, c=c, bb=bb, b=b, pt=pt: e.scalar_tensor_tensor(
                            out=Snat[:, bb, c, :], in0=Snat[:, bb, c, :], scalar=cdT[:, c, b:b + 1],
                            in1=pt[:, bb * 128:(bb + 1) * 128], op0=ALU.mult, op1=ALU.add),
                            reads=(bSnat, bcdT, bp, bSTs), writes=(bSnat,))
                S.op("sp", lambda e, bg=bg: e.dma_start(
                    out=o_ssm_s[bg * NB:(bg + 1) * NB].rearrange("b (c p) n -> p b c n", p=128), in_=Snat[:, :, :, :]),
                    reads=(bSnat,), dma=next_out(), final=True, arena=True)

        WINS = (2, 4, 8, 16)

        def head_norm(pt, bp, nparts_dummy, nt, gname, out_ap, wbufs, scale, f32_out=None, f32_bufs=(), defer=False):
            si = state["sil"]
            state["sil"] = 1 - si
            raw = sil[si][:, 0:nt]
            hq_, bhq_ = (hsq, b_hsq) if si == 0 else (hsq2, b_hsq2)
            S.op("act", lambda e: e.copy(raw, pt), reads=(bp,), writes=(b_sil[si],))
            S.op("act", lambda e: e.activation(out=hq_[:, 0:nt], in_=pt, func=AF.Square), reads=(bp,), writes=(bhq_,))

            def part2():
                ps2, bps2 = next_ps()
                S.op("pe", lambda e: e.matmul(ps2[:, 0:nt], bd64[:, :], hq_[:, 0:nt], start=True, stop=True),
                     reads=(bhq_, b_ident), writes=(bps2,))
                if scale is None:
                    S.op("act", lambda e: e.activation(out=hrs[:, 0:nt], in_=ps2[:, 0:nt], func=AF.Sqrt, bias=EPS, scale=1.0 / 64),
                         reads=(bps2,), writes=(b_hrs,))
                else:
                    S.op("act", lambda e: e.activation(out=hrs[:, 0:nt], in_=ps2[:, 0:nt], func=AF.Sqrt, bias=EPS * scale * scale,
                                                       scale=scale * scale / 64), reads=(bps2,), writes=(b_hrs,))
                S.op("dve", lambda e: e.reciprocal(hrs[:, 0:nt], hrs[:, 0:nt]), reads=(b_hrs,), writes=(b_hrs,))
                S.op("dve", lambda e: e.scalar_tensor_tensor(out=out_ap, in0=raw, scalar=pvc(gname, 0), in1=hrs[:, 0:nt],
                                                             op0=ALU.mult, op1=ALU.mult),
                     reads=(b_sil[si], b_hrs, b_pv), writes=wbufs)
                if f32_out is not None:
                    S.op("dve", lambda e: e.scalar_tensor_tensor(out=f32_out, in0=raw, scalar=pvc(gname, 0), in1=hrs[:, 0:nt],
                                                                 op0=ALU.mult, op1=ALU.mult),
                         reads=(b_sil[si], b_hrs, b_pv), writes=f32_bufs)
            if defer:
                return part2
            part2()
            return None

        def odd_mixer(kind, nt, is_last_ptile, tile_idx):
            S.barrier()
            A.reset()
            smp = (kind == "s")
            qT = hT[:, 0:8, :]
            ycT = hT[:, 8:16, :]
            b_q, b_yc = b_h[0:8], b_h[8:16]
            ydT = A.take(128, [8, nt], BF16)
            byd = Buf("ydT")
            L = 19 if smp else (nt + 15)
            nb = 16 if smp else 1
            cext = A.take(128, [8, nb * L])
            bce = Buf("cext")
            pA = A.take(128, [nb * L])
            pB = A.take(128, [nb * L])
            bpA, bpB = Buf("pA"), Buf("pB")
            pooledT = A.take(128, [8, nt], BF16)
            bpool = Buf("pooled")
            KW = 64 if smp else (128 + nt)
            kdT = A.take(128, [4, KW], BF16)
            bkd = Buf("kdT")
            NVB = 1 if smp else 5
            vtok = A.take(128, [NVB, 256], BF16)
            bvt = Buf("vtok")
            kf32 = A.take(128, [4, 128 if not smp else 64])
            vf32 = A.take(128, [256])
            bkf, bvf = Buf("kf32"), Buf("vf32")
            ost = A.take(128, [1024])
            bost = Buf("ost")
            tS_ = [A.take(128, [1024]) for _ in range(2)]
            pT_ = [A.take(128, [1024], BF16) for _ in range(2)]
            btS_, bpT_ = [Buf("tS0"), Buf("tS1")], [Buf("pT0"), Buf("pT1")]
            tS, pT, btS, bpT = tS_[0], pT_[0], btS_[0], bpT_[0]
            rec = [A.take(128, [128]) for _ in range(2)]
            brec = [Buf("rec0"), Buf("rec1")]
            recb = A.take(128, [8, 128])
            brecb = Buf("recb")

            def cview(c, lo, n):
                if smp:
                    return cext[:, c, :].rearrange("p (b l) -> p b l", l=L)[:, :, lo:lo + n]
                return cext[:, c, lo:lo + n]

            def tview(t, lo, n):
                if smp:
                    return t[:, :].rearrange("p (b l) -> p b l", l=L)[:, :, lo:lo + n]
                return t[:, lo:lo + n]

            def ntv(ap2d):
                return ap2d.rearrange("p (b i) -> p b i", i=4) if smp else ap2d

            if smp:
                cin32 = A.take(128, [8, 64])
                bcin = Buf("cin32")
                stp = A.take(120, [2, 1024])
                bstp = Buf("stp")
                S.op("sp", lambda e: e.dma_start(out=stp, in_=st_pool.rearrange("(g r) d -> r g d", g=2)),
                     writes=(bstp,), dma=next_in(), arena=True)
                S.op("sp", lambda e: e.dma_start(out=o_pool_s[:, 0:11, :], in_=st_pool.rearrange("(b r) d -> b r d", r=15)[:, 4:15, :]),
                     dma=next_out(), final=True)
                S.op("sp", lambda e: e.dma_start(out=o_k_s[:, 0:124, :], in_=st_k[:, 4:128, :]), dma=next_out(), final=True)
                S.op("sp", lambda e: e.dma_start(out=o_v_s[:, 0:124, :], in_=st_v[:, 4:128, :]), dma=next_out(), final=True)
                for g in range(2):
                    for cq in range(2):
                        pg, bpg = ps_group(2)
                        S.op("pe", [lambda e, c4=c4, g=g, cq=cq, pg=pg: e.transpose(
                            pg[:, c4 * 128:c4 * 128 + 120], stp[0:120, g, (cq * 4 + c4) * 128:(cq * 4 + c4 + 1) * 128],
                            ident[0:120, 0:120]) for c4 in range(4)], reads=(bstp, b_ident), writes=bpg)
                        for c4 in range(4):
                            c = cq * 4 + c4
                            S.op("dve" if c4 % 2 else "act",
                                 (lambda e, c=c, c4=c4, g=g, pg=pg: e.tensor_copy(
                                     cext[:, c, :].rearrange("p (b l) -> p b l", l=L)[:, g * 8:(g + 1) * 8, 0:15],
                                     pg[:, c4 * 128:c4 * 128 + 120].rearrange("p (b r) -> p b r", r=15))) if c4 % 2 else
                                 (lambda e, c=c, c4=c4, g=g, pg=pg: e.copy(
                                     cext[:, c, :].rearrange("p (b l) -> p b l", l=L)[:, g * 8:(g + 1) * 8, 0:15],
                                     pg[:, c4 * 128:c4 * 128 + 120].rearrange("p (b r) -> p b r", r=15))),
                                 reads=bpg, writes=(bce,))
            else:
                S.op("dve", lambda e: e.tensor_copy(cext[:, :, 0:15], ptail[:, :, :]), reads=(b_ptail,), writes=(bce,))
                bmP = A.take(128, [2, 16, 128])
                bbm = Buf("bm")
                for kb, off in ((0, 256), (1, 128)):
                    S.op("sp", lambda e, kb=kb, off=off: e.dma_start(
                        out=bmP[:, kb, :, :], in_=bass.AP(tensor=dscr.tensor, offset=off, ap=[[383, 128], [128 * 384, 16], [1, 128]])),
                        reads=(b_dscr,), writes=(bbm,), dma=next_in(), arena=True)

            rms_norm("mix_norm1", nt)

            for gi in range(2):
                def ev_c(cc, pt, bp, gi=gi):
                    c = gi * 4 + cc
                    S.op("act", lambda e: e.copy(cview(c, 15, nt if not smp else 4), ntv(pt[:, 0:nt])), reads=(bp,), writes=(bce,))
                    if smp:
                        S.op("dve", lambda e: e.tensor_copy(cin32[:, c, :], pt[:, 0:nt]), reads=(bp,), writes=(bcin,))
                proj_fm(wod_fm[gi], 4, nt, ev_c)
            for gi in range(2):
                def ev_q(cc, pt, bp, gi=gi):
                    c = gi * 4 + cc
                    return head_norm(pt[:, 0:nt], bp, 128, nt, "q_norm", qT[:, c, 0:nt], (b_q[c],), 8.0, defer=True)
                proj_fm(wod_fm[2 + gi], 4, nt, ev_q)
            wk, bwk = wload(wod_k, 8 * 256)
            wkv = wk[:, 0:2048].rearrange("p (k n) -> p k n", k=8)
            k0 = 0 if smp else 128
            if not smp:
                S.op("dve", lambda e: e.tensor_copy(kdT[:, :, 0:128], kprev[:, :, :]), reads=(b_kprev,), writes=(bkd,))
                S.op("dve", lambda e: e.tensor_copy(vtok[:, 0, :], vprev[:, :]), reads=(b_kprev,), writes=(bvt,))
            for hk in range(4):
                pt, bp = next_ps()
                fns = []
                for half in range(2):
                    for k in range(8):
                        fns.append(lambda e, pt=pt, half=half, k=k, hk=hk: e.matmul(
                            pt[64 * half:64 * half + 64, 0:nt], wkv[:, k, hk * 64:(hk + 1) * 64], xn[:, k, 0:nt],
                            start=(k == 0), stop=(k == 7), skip_group_check=True))
                S.op("pe", fns, reads=(bwk, b_xn), writes=(bp,))
                _kn(pt, bp, hk, nt, k0, kdT, bkd, kf32, bkf, smp, is_last_ptile)
            wv_, bwv_ = wload(wod_v, 8 * 256)
            wvv_ = wv_[:, 0:2048].rearrange("p (k n) -> p k n", k=8)
            CLv = 64 if smp else 128
            for blk in range(nt // CLv):
                cols = slice(blk * CLv, (blk + 1) * CLv)
                pt, bp = next_ps()
                S.op("pe", [lambda e, pt=pt, k=k, cols=cols: e.matmul(pt[0:CLv, 0:256], xn[:, k, cols], wvv_[:, k, :],
                                                                       start=(k == 0), stop=(k == 7)) for k in range(8)],
                     reads=(bwv_, b_xn), writes=(bp,))
                vb = 0 if smp else blk + 1
                S.op("act", lambda e, pt=pt, vb=vb: e.copy(vtok[0:CLv, vb, :], pt[0:CLv, 0:256]), reads=(bp,), writes=(bvt,))
                if smp or (is_last_ptile and blk == 3):
                    S.op("dve", lambda e, pt=pt: e.tensor_copy(vf32[0:CLv, :], pt[0:CLv, 0:256]), reads=(bp,), writes=(bvf,))
            if smp or is_last_ptile:
                if smp:
                    for i in range(4):
                        S.op("sp", lambda e, i=i: e.dma_start(out=o_v_s[:, 124 + i, :], in_=vf32[i:64:4, :]),
                             reads=(bvf,), dma=next_out(), final=True, arena=True)
                else:
                    S.op("sp", lambda e: e.dma_start(out=o_v_p, in_=vf32[:, :]), reads=(bvf,), dma=next_out(), final=True, arena=True)
                pk, bpk = next_ps()
                nk = 64 if smp else 128
                S.op("pe", [lambda e, hk=hk: e.transpose(pk[0:nk, hk * 64:(hk + 1) * 64], kf32[0:64, hk, 0:nk], ident[0:64, 0:64])
                            for hk in range(4)], reads=(bkf, b_ident), writes=(bpk,))
                S.op("act", lambda e: e.copy(ost[0:nk, 0:256], pk[0:nk, 0:256]), reads=(bpk,), writes=(bost,))
                if smp:
                    for i in range(4):
                        S.op("sp", lambda e, i=i: e.dma_start(out=o_k_s[:, 124 + i, :], in_=ost[i:64:4, 0:256]),
                             reads=(bost,), dma=next_out(), final=True, arena=True)
                else:
                    S.op("sp", lambda e: e.dma_start(out=o_k_p, in_=ost[:, 0:256]), reads=(bost,), dma=next_out(), final=True, arena=True)

            first_tile = (not smp) and tile_idx == 0
            for c in range(8):
                gi = c // 2
                w = WINS[gi]
                Lx = L
                cur_t, cur_b, cur_lo = None, None, 0
                src_ap = lambda lo, n, c=c: cview(c, lo, n)
                width = 1
                bufs = [(pA, bpA), (pB, bpB)]
                bi = 0
                srcf, srcb, lo0 = src_ap, bce, 0
                while width < w:
                    dst, bdst = bufs[bi]
                    bi = 1 - bi
                    lo1 = lo0 + width
                    n = Lx - lo1
                    S.op("dve", lambda e, srcf=srcf, dst=dst, lo1=lo1, n=n, width=width: e.tensor_tensor(
                        out=tview(dst, lo1, n), in0=srcf(lo1, n), in1=srcf(lo1 - width, n), op=ALU.add),
                        reads=(srcb,), writes=(bdst,))
                    srcf = (lambda lo, n, dst=dst: tview(dst, lo, n))
                    srcb, lo0 = bdst, lo1
                    width *= 2
                nn = 4 if smp else nt
                S.op("dve", lambda e, srcf=srcf, c=c, w=w, nn=nn: e.scalar_tensor_tensor(
                    out=ntv(pooledT[:, c, 0:nt]), in0=srcf(15, nn), scalar=1.0 / w, in1=cview(c, 15, nn),
                    op0=ALU.mult, op1=ALU.subtract), reads=(srcb, bce), writes=(bpool,))
                if first_tile:
                    S.op("dve", lambda e, srcf=srcf, gi=gi: e.tensor_tensor(
                        out=rec[0][:, 0:16], in0=srcf(15, 16), in1=cst_ap("invc")[:, gi * 16:(gi + 1) * 16], op=ALU.mult),
                        reads=(srcb, b_cs), writes=(brec[0],))
                    S.op("dve", lambda e, c=c: e.tensor_tensor(
                        out=pooledT[:, c, 0:16], in0=rec[0][:, 0:16], in1=cview(c, 15, 16), op=ALU.subtract),
                        reads=(brec[0], bce), writes=(bpool,))
            if not smp:
                S.op("dve", lambda e: e.tensor_copy(ptail[:, :, :], cext[:, :, nt:nt + 15]), reads=(bce,), writes=(b_ptail,))
            if smp or is_last_ptile:
                nr = 64 if smp else 15
                pg, bpg = ps_group(2)
                if smp:
                    S.op("pe", [lambda e, c=c: e.transpose(pg[0:64, c * 128:(c + 1) * 128], cin32[:, c, :], ident[:, :])
                                for c in range(8)], reads=(bcin, b_ident), writes=bpg)
                else:
                    S.op("pe", [lambda e, c=c: e.transpose(pg[0:15, c * 128:(c + 1) * 128], cext[:, c, nt:nt + 15], ident[:, :])
                                for c in range(8)], reads=(bce, b_ident), writes=bpg)
                S.op("dve", lambda e: e.tensor_copy(ost[0:nr, :], pg[0:nr, 0:1024]), reads=bpg + [bost], writes=(bost,))
                if smp:
                    for i in range(4):
                        S.op("sp", lambda e, i=i: e.dma_start(out=o_pool_s[:, 11 + i, :], in_=ost[i:64:4, :]),
                             reads=(bost,), dma=next_out(), final=True, arena=True)
                else:
                    S.op("sp", lambda e: e.dma_start(out=o_pool_p, in_=ost[0:15, :]), reads=(bost,), dma=next_out(), final=True, arena=True)
            wl, bwl = wload(wod_lin, 4 * 2 * 256)
            wlv = wl[:, 0:2048].rearrange("p (g c d) -> p g c d", g=4, c=2)
            for c in range(8):
                gi, dd = c // 2, c % 2
                pt, bp = next_ps()
                S.op("pe", [lambda e, pt=pt, cc=cc, gi=gi, dd=dd: e.matmul(
                    pt[:, 0:nt], wlv[:, gi, cc, dd * 128:(dd + 1) * 128], pooledT[:, gi * 2 + cc, 0:nt],
                    start=(cc == 0), stop=(cc == 1)) for cc in range(2)], reads=(bwl, bpool), writes=(bp,))
                S.op("act", lambda e, pt=pt, c=c: e.activation(out=ycT[:, c, 0:nt], in_=pt[:, 0:nt], func=AF.Copy,
                                                               scale=pvc("c_scale", c)), reads=(bp, b_pv), writes=(b_yc[c],))

            if smp:
                sample_attention(nt, qT, b_q, kdT, bkd, vtok, bvt, ydT, byd, tS, btS, pT, bpT, rec, brec)
            else:
                po, bpo = ps_group(2)
                po_banks = list(state["last_group"])
                pd, bpd = ps_group(2)
                pd_banks = list(state["last_group"])
                reserved.update(po_banks + pd_banks)
                def kbs_of(qb):
                    return [1] if tile_idx * 4 + qb == 0 else [0, 1]

                def stage_a(qb, hq, par):
                    qcols = slice(qb * 128, (qb + 1) * 128)
                    kbs = kbs_of(qb)
                    nkb = len(kbs)
                    tS, pT, btS, bpT = tS_[par], pT_[par], btS_[par], bpT_[par]
                    psA, bpsA = next_ps()
                    psB, bpsB = next_ps()
                    pss_ = (psA, psB)
                    fns = []
                    for hl in range(4):
                        h = hq * 4 + hl
                        hh, s, hp = hl % 2, hl // 2, h // 2
                        for ki, kb in enumerate(kbs):
                            kc0 = qb * 128 + kb * 128
                            fns.append(lambda e, hh=hh, s=s, ki=ki, kc0=kc0, hp=hp, hq=hq, pss_=pss_, qcols=qcols: e.matmul(
                                pss_[hh][:, (s * 2 + ki) * 128:(s * 2 + ki + 1) * 128],
                                kdT[64 * hh:64 * hh + 64, hq, kc0:kc0 + 128], qT[64 * hh:64 * hh + 64, hp, qcols],
                                start=True, stop=True))
                    S.op("pe", fns, reads=[bkd] + b_q[hq * 2:hq * 2 + 2], writes=(bpsA, bpsB))
                    if nkb == 2:
                        for hh in range(2):
                            h0 = hq * 4 + hh
                            S.op("dve", lambda e, hh=hh, h0=h0, pss_=pss_, tS=tS: e.tensor_tensor(
                                out=tS[:, hh * 512:(hh + 1) * 512].rearrange("p (s k q) -> p s k q", s=2, k=2),
                                in0=pss_[hh][:, 0:512].rearrange("p (s k q) -> p s k q", s=2, k=2),
                                in1=bmP[:, :, h0:h0 + 3:2, :].rearrange("p k s q -> p s k q"), op=ALU.add),
                                reads=((bpsA, bpsB)[hh], bbm), writes=(btS,))
                        S.op("act", lambda e, tS=tS, pT=pT: e.activation(out=pT[:, 0:1024], in_=tS[:, 0:1024], func=AF.Exp),
                             reads=(btS,), writes=(bpT,))
                    else:
                        for hl in range(4):
                            h = hq * 4 + hl
                            hh, s = hl % 2, hl // 2
                            for ki, kb in enumerate(kbs):
                                j = s * 2 + ki
                                S.op("dve", lambda e, hh=hh, j=j, kb=kb, h=h, pss_=pss_, tS=tS: e.tensor_tensor(
                                    out=tS[:, hh * 512 + j * 128:hh * 512 + (j + 1) * 128], in0=pss_[hh][:, j * 128:(j + 1) * 128],
                                    in1=bmP[:, kb, h, :], op=ALU.add), reads=((bpsA, bpsB)[hh], bbm), writes=(btS,))
                        S.op("act", lambda e, tS=tS, pT=pT: e.activation(
                            out=pT[:, 0:1024].rearrange("p (a b) -> p a b", a=4)[:, :, 0:128],
                            in_=tS[:, 0:1024].rearrange("p (a b) -> p a b", a=4)[:, :, 0:128], func=AF.Exp),
                            reads=(btS,), writes=(bpT,))

                def stage_b(qb, hq, par):
                    kbs = kbs_of(qb)
                    nkb = len(kbs)
                    pT, bpT = pT_[par], bpT_[par]
                    fns = []
                    for hl in range(4):
                        h = hq * 4 + hl
                        hh, s, hp = hl % 2, hl // 2, h // 2
                        for ki, kb in enumerate(kbs):
                            j = hh * 4 + s * 2 + ki
                            fns.append(lambda e, hh=hh, ki=ki, kb=kb, hq=hq, hp=hp, j=j, qb=qb, pT=pT, nkb=nkb: e.matmul(
                                po[64 * hh:64 * hh + 64, hp * 128:(hp + 1) * 128], vtok[:, qb + kb, hq * 64:(hq + 1) * 64],
                                pT[:, j * 128:(j + 1) * 128], start=(ki == 0), stop=(ki == nkb - 1), skip_group_check=True))
                        for ki, kb in enumerate(kbs):
                            j = hh * 4 + s * 2 + ki
                            fns.append(lambda e, hh=hh, ki=ki, hp=hp, j=j, pT=pT, nkb=nkb: e.matmul(
                                pd[64 * hh:64 * hh + 64, hp * 128:(hp + 1) * 128], onesb[:, 0:64],
                                pT[:, j * 128:(j + 1) * 128], start=(ki == 0), stop=(ki == nkb - 1), skip_group_check=True))
                    S.op("pe", fns, reads=(bvt, bpT, b_ident), writes=bpo + bpd)

                def epilogue(qb):
                    qcols = slice(qb * 128, (qb + 1) * 128)
                    S.op("dve", lambda e: e.tensor_tensor(
                        out=recb, in0=pd[:, 0:1024].rearrange("p (c q) -> p c q", c=8),
                        in1=esink2[:, 0:8].unsqueeze(2).broadcast_to([128, 8, 128]), op=ALU.add),
                        reads=bpd + [b_gm2], writes=(brecb,))
                    S.op("dve", lambda e: e.reciprocal(recb, recb), reads=(brecb,), writes=(brecb,))
                    S.op("dve", lambda e, qcols=qcols: e.tensor_tensor(
                        out=ydT[:, :, qcols], in0=po[:, 0:1024].rearrange("p (c q) -> p c q", c=8), in1=recb, op=ALU.mult),
                        reads=bpo + [brecb], writes=(byd,))

                groups = [(qb, hq) for qb in range(4) for hq in range(4)]
                stage_a(groups[0][0], groups[0][1], 0)
                for gi_, (qb, hq) in enumerate(groups):
                    if gi_ + 1 < len(groups):
                        stage_a(groups[gi_ + 1][0], groups[gi_ + 1][1], (gi_ + 1) % 2)
                    stage_b(qb, hq, gi_ % 2)
                    if hq == 3:
                        epilogue(qb)
                reserved.difference_update(po_banks + pd_banks)
                S.op("dve", lambda e: e.tensor_copy(kprev[:, :, :], kdT[:, :, nt:nt + 128]), reads=(bkd,), writes=(b_kprev,))
                S.op("dve", lambda e: e.tensor_copy(vprev[:, :], vtok[:, 4, :]), reads=(bvt,), writes=(b_kprev,))

            for mi in range(4):
                w, bw = wload(wod_out[mi], 16 * 256)
                wv = w[:, :].rearrange("p (k n) -> p k n", k=16)
                for mm in range(2):
                    m = mi * 2 + mm
                    pt, bp = next_ps()
                    fns = []
                    ks = {"c": range(8), "d": range(8, 16)}.get(cfg.get("odd_part"), range(16))
                    for k in ks:
                        rhs = ycT[:, k, 0:nt] if k < 8 else ydT[:, k - 8, 0:nt]
                        fns.append(lambda e, pt=pt, k=k, mm=mm, wv=wv, rhs=rhs, ks=ks: e.matmul(
                            pt[:, 0:nt], wv[:, k, mm * 128:(mm + 1) * 128], rhs, start=(k == ks[0]), stop=(k == ks[-1])))
                    S.op("pe", fns, reads=[bw, byd] + b_yc, writes=(bp,))
                    S.op("dve", lambda e, pt=pt, m=m: e.tensor_tensor(out=xT[:, m, 0:nt], in0=pt[:, 0:nt], in1=xT[:, m, 0:nt],
                                                                      op=ALU.add), reads=(bp, b_xT[m]), writes=(b_xT[m],))

        def _kn(pt, bp, hk, nt, k0, kdT, bkd, kf32, bkf, smp, is_last_ptile):
            if smp:
                head_norm(pt[:, 0:nt], bp, 128, nt, "k_norm", kdT[:, hk, k0:k0 + nt], (bkd,), None,
                          f32_out=kf32[:, hk, 0:64], f32_bufs=(bkf,))
            elif is_last_ptile:
                head_norm(pt[:, 0:nt], bp, 128, nt, "k_norm", kdT[:, hk, k0:k0 + nt], (bkd,), None)
                si = 1 - state["sil"]
                S.op("dve", lambda e: e.scalar_tensor_tensor(out=kf32[:, hk, :], in0=sil[si][:, nt - 128:nt], scalar=pvc("k_norm", 0),
                                                             in1=hrs[:, nt - 128:nt], op0=ALU.mult, op1=ALU.mult),
                     reads=(b_sil[si], b_hrs, b_pv), writes=(bkf,))
            else:
                head_norm(pt[:, 0:nt], bp, 128, nt, "k_norm", kdT[:, hk, k0:k0 + nt], (bkd,), None)

        def sample_attention(nt, qT, b_q, kdT, bkd, vtok, bvt, ydT, byd, tS, btS, pT, bpT, rec, brec):
            kc32 = A.take(128, [256])
            kcb = A.take(128, [4, 2, 64], BF16)
            bkc, bkcb = Buf("kc32"), Buf("kcb")
            KdT = A.take(128, [16, 4, 128], BF16)
            bKd = Buf("KdT")
            vc32 = A.take(128, [2, 256])
            vcb = A.take(128, [16, 256], BF16)
            bvc, bvcb = Buf("vc32"), Buf("vcb")
            bm1 = A.take(128, [16, 4])
            bm2 = A.take(64, [16, 64])
            bbm1, bbm2 = Buf("bm1"), Buf("bm2")
            t1 = A.take(128, [16, 64])
            p1 = A.take(128, [16, 64], BF16)
            t2 = A.take(64, [16, 64])
            p2 = A.take(64, [16, 64], BF16)
            bt1, bp1, bt2, bp2 = Buf("t1"), Buf("p1"), Buf("t2"), Buf("p2")
            S.op("sp", lambda e: e.dma_start(out=bm1, in_=bass.AP(tensor=dscr.tensor, offset=256, ap=[[383, 128], [128 * 384, 16], [1, 4]])),
                 reads=(b_dscr,), writes=(bbm1,), dma=next_in(), arena=True)
            S.op("dve", lambda e: e.memset(bm2, NEG), writes=(bbm2,))
            for b in range(16):
                S.op("sp", lambda e, b=b: e.dma_start(
                    out=bm2[4 * b:4 * b + 4, :, 4 * b:4 * b + 4],
                    in_=bass.AP(tensor=dscr.tensor, offset=128, ap=[[383, 4], [128 * 384, 16], [1, 4]])),
                    reads=(b_dscr,), writes=(bbm2,), dma=next_in(), arena=True)
            for b in range(16):
                S.op("sp", lambda e, b=b: e.dma_start(out=kc32, in_=st_k[b]), writes=(bkc,), dma=next_in(), arena=True)
                S.op("dve", lambda e: e.tensor_copy(kcb[:, :, :, :], kc32[:, :].rearrange("p (h d) -> p h d", h=4).unsqueeze(2)
                                                    .broadcast_to([128, 4, 2, 64])), reads=(bkc,), writes=(bkcb,))
                pk, bpk = next_ps()
                pkb = pk.bitcast(BF16)
                S.op("pe", [lambda e, hk=hk, pkb=pkb: e.transpose(pkb[:, hk * 128:(hk + 1) * 128],
                                                                  kcb[:, hk, :, :].rearrange("p a d -> p (a d)"), identb[:, :])
                            for hk in range(4)], reads=(bkcb, b_ident), writes=(bpk,))
                S.op("act", lambda e, b=b, pkb=pkb: e.copy(KdT[:, b, :, :].rearrange("p h k -> p (h k)"), pkb[:, 0:512]),
                     reads=(bpk,), writes=(bKd,))
                if b % 2 == 0:
                    S.op("sp", lambda e, b=b: e.dma_start(out=vc32, in_=st_v[b:b + 2].rearrange("b k d -> k b d")),
                         writes=(bvc,), dma=next_in(), arena=True)
                    S.op("dve", lambda e, b=b: e.tensor_copy(vcb[:, b:b + 2, :], vc32[:, :, :]), reads=(bvc,), writes=(bvcb,))
            ps1, bps1 = ps_group(2)
            g1 = list(state["last_group"])
            reserved.update(g1)
            fns = []
            for b in range(16):
                for h in range(16):
                    hh, hp, hk = h % 2, h // 2, h // 4
                    col = hh * 512 + b * 32 + hp * 4
                    fns.append(lambda e, b=b, hh=hh, hp=hp, hk=hk, col=col: e.matmul(
                        ps1[:, col:col + 4], KdT[64 * hh:64 * hh + 64, b, hk, :],
                        qT[64 * hh:64 * hh + 64, hp, 4 * b:4 * b + 4], start=True, stop=True))
            S.op("pe", fns, reads=[bKd] + b_q, writes=bps1)
            bm1v = bm1.rearrange("p (hp t) i -> p hp t i", t=2)
            for hh in range(2):
                S.op("dve", lambda e, hh=hh: e.tensor_tensor(
                    out=t1[:, hh * 8:(hh + 1) * 8, :].rearrange("p a (c d) -> p (a c) d", d=32) if False else
                    t1.rearrange("p a x -> p (a x)")[:, hh * 512:(hh + 1) * 512].rearrange("p (b hp i) -> p b hp i", b=16, hp=8),
                    in0=ps1[:, hh * 512:(hh + 1) * 512].rearrange("p (b hp i) -> p b hp i", b=16, hp=8),
                    in1=bm1v[:, :, hh, :].unsqueeze(1).broadcast_to([128, 16, 8, 4]), op=ALU.add),
                    reads=bps1 + [bbm1], writes=(bt1,))
            reserved.difference_update(g1)
            S.op("act", lambda e: e.activation(out=p1[:, :, :], in_=t1[:, :, :], func=AF.Exp), reads=(bt1,), writes=(bp1,))
            p1f = p1.rearrange("p a x -> p (a x)")
            ps2, bps2 = ps_group(2)
            S.op("pe", [lambda e, h=h: e.matmul(ps2[0:64, (h % 2) * 512 + (h // 2) * 64:(h % 2) * 512 + (h // 2 + 1) * 64],
                                                 kdT[64 * (h % 2):64 * (h % 2) + 64, h // 4, 0:64],
                                                 qT[64 * (h % 2):64 * (h % 2) + 64, h // 2, 0:64], start=True, stop=True)
                        for h in range(16)], reads=[bkd] + b_q, writes=bps2)
            bm2v = bm2.rearrange("p (hp t) x -> p hp t x", t=2)
            t2f = t2.rearrange("p a x -> p (a x)")
            for hh in range(2):
                S.op("dve", lambda e, hh=hh: e.tensor_tensor(
                    out=t2f[:, hh * 512:(hh + 1) * 512].rearrange("p (hp x) -> p hp x", hp=8),
                    in0=ps2[0:64, hh * 512:(hh + 1) * 512].rearrange("p (hp x) -> p hp x", hp=8),
                    in1=bm2v[:, :, hh, :], op=ALU.add), reads=bps2 + [bbm2], writes=(bt2,))
            S.op("act", lambda e: e.activation(out=p2[:, :, :], in_=t2[:, :, :], func=AF.Exp), reads=(bt2,), writes=(bp2,))
            p2f = p2.rearrange("p a x -> p (a x)")
            po, bpo = next_ps()
            pd, bpd = next_ps()
            fns = []
            for h in range(16):
                hh, hp, hk = h % 2, h // 2, h // 4
                first = h < 2
                fns.append(lambda e, h=h, hh=hh, hp=hp, hk=hk, first=first: e.matmul(
                    po[64 * hh:64 * hh + 64, hp * 64:(hp + 1) * 64], vtok[0:64, 0, hk * 64:(hk + 1) * 64],
                    p2f[:, hh * 512 + hp * 64:hh * 512 + (hp + 1) * 64],
                    start=first, stop=False, skip_group_check=True))
                fns.append(lambda e, h=h, hh=hh, hp=hp, first=first: e.matmul(
                    pd[64 * hh:64 * hh + 64, hp * 64:(hp + 1) * 64], onesb[0:64, 0:64],
                    p2f[:, hh * 512 + hp * 64:hh * 512 + (hp + 1) * 64],
                    start=first, stop=False, skip_group_check=True))
            for b in range(16):
                for h in range(16):
                    hh, hp, hk = h % 2, h // 2, h // 4
                    fns.append(lambda e, b=b, h=h, hh=hh, hp=hp, hk=hk: e.matmul(
                        po[64 * hh:64 * hh + 64, hp * 64 + 4 * b:hp * 64 + 4 * b + 4], vcb[:, b, hk * 64:(hk + 1) * 64],
                        p1f[:, hh * 512 + b * 32 + hp * 4:hh * 512 + b * 32 + hp * 4 + 4], start=False, stop=True,
                        skip_group_check=True))
                    fns.append(lambda e, b=b, h=h, hh=hh, hp=hp: e.matmul(
                        pd[64 * hh:64 * hh + 64, hp * 64 + 4 * b:hp * 64 + 4 * b + 4], onesb[:, 0:64],
                        p1f[:, hh * 512 + b * 32 + hp * 4:hh * 512 + b * 32 + hp * 4 + 4], start=False, stop=True,
                        skip_group_check=True))
            S.op("pe", fns, reads=(bvt, bp2, bp1, bvcb, b_ident), writes=(bpo, bpd))
            for c in range(8):
                ri = c % 2
                S.op("dve", lambda e, c=c, ri=ri: e.tensor_scalar(
                    out=rec[ri][:, 0:64], in0=pd[:, c * 64:(c + 1) * 64], scalar1=esink2[:, c:c + 1], scalar2=None,
                    op0=ALU.add), reads=(bpd, b_gm2), writes=(brec[ri],))
                S.op("dve", lambda e, ri=ri: e.reciprocal(rec[ri][:, 0:64], rec[ri][:, 0:64]), reads=(brec[ri],), writes=(brec[ri],))
                S.op("dve", lambda e, c=c, ri=ri: e.tensor_tensor(
                    out=ydT[:, c, 0:64], in0=po[:, c * 64:(c + 1) * 64], in1=rec[ri][:, 0:64], op=ALU.mult),
                    reads=(bpo, brec[ri]), writes=(byd,))

        tiles = [("p", t) for t in range(n_ptiles)] + ([("s", 0)] if do_sample else [])

        def issue_load(kind, t):
            if kind == "p":
                load_x_tile(xp[t * NTP:(t + 1) * NTP, :], 4, 128)
            else:
                load_x_tile(xs[:, :], 1, NS)

        issue_load(*tiles[0])
        for ti, (kind, t) in enumerate(tiles):
            nt = NTP if kind == "p" else NS
            last_p = (kind == "p" and t == n_ptiles - 1)
            if kind == "p":
                transpose_in(4, 128)
            else:
                transpose_in(1, NS)
            for layer in range(nlayers):
                if on("ffn1"):
                    ffn(layer, 0, nt)
                if layer == 0 and on("even"):
                    even_mixer(kind, nt, last_p)
                if layer == 1 and on("odd"):
                    odd_mixer(kind, nt, last_p, t)
                if layer == nlayers - 1:
                    S.barrier()
                    if ti + 1 < len(tiles):
                        issue_load(*tiles[ti + 1])
                if on("ffn2"):
                    ffn(layer, 1, nt)
            if kind == "p":
                transpose_out(yp[t * NTP:(t + 1) * NTP, :], 4, 128)
            else:
                transpose_out(ys[:, :], 1, NS)

        with nc.Block() as block:
            S.emit(block)
    P.ninst = S.ninst
    return P


PV_OFF = {}
_col = 0


def _pv(name, n):
    global _col
    PV_OFF[name] = _col
    _col += n


for _l in range(2):
    for _w in range(2):
        _pv(f"ffn_norm{_l}{_w}", 8)
for _l in range(2):
    _pv(f"mix_norm{_l}", 8)
_pv("ln_g", 8)
_pv("ln_b", 8)
_pv("conv_w", 48)
_pv("conv_b", 12)
_pv("norm_g", 8)
_pv("dskip", 8)
_pv("c_scale", 8)
_pv("q_norm", 1)
_pv("k_norm", 1)
_pv("dt_bias", 1)
_pv("a_log", 1)
PV_COLS = _col

CST_OFF = {}
_ccol = 0


def _cs(name, n):
    global _ccol
    CST_OFF[name] = (_ccol, n)
    _ccol += n


_cs("maskT_causal", 128)
_cs("maskT_blk", 64)
_cs("bmask", 16)
_cs("rmask_p", 512)
_cs("rmask_s", 64)
_cs("onehot", 384)
_cs("negmask", 384)
_cs("invc", 64)
CST_COLS = _ccol


def make_consts():
    c = np.zeros((128, CST_COLS), np.float32)
    j = np.arange(128)[:, None]
    i = np.arange(128)[None, :]
    o, n = CST_OFF["maskT_causal"]
    c[:, o:o + n] = (i >= j)
    o, n = CST_OFF["maskT_blk"]
    jj = np.arange(64)[:, None]
    ii = np.arange(64)[None, :]
    c[:64, o:o + n] = (ii >= jj) & (ii // 4 == jj // 4)
    o, n = CST_OFF["bmask"]
    c[:64, o:o + n] = (np.arange(64)[:, None] // 4 == np.arange(16)[None, :])
    o, n = CST_OFF["rmask_p"]
    c[:, o:o + n] = (np.arange(512)[None, :] % 128 != 0)
    o, n = CST_OFF["rmask_s"]
    c[:, o:o + n] = (np.arange(64)[None, :] % 4 != 0)
    dist = np.arange(384) - 128
    valid = (dist >= 0) & (dist < 128)
    nn = np.maximum(dist, 0)
    n_safe = np.maximum(nn, 1).astype(np.float32)
    scale = np.float32((32 - 16) / math.log(128 / 16))
    large = np.minimum(16 + (np.log(n_safe / 16) * scale).astype(np.int32), 31)
    bucket = np.where(nn < 16, nn, large).astype(np.int32)
    o, n = CST_OFF["onehot"]
    oh = np.zeros((32, 384), np.float32)
    oh[bucket[valid], np.arange(384)[valid]] = 1.0
    c[:32, o:o + n] = oh
    o, n = CST_OFF["negmask"]
    c[:, o:o + n] = np.where(valid, 0.0, NEG)[None, :]
    o, n = CST_OFF["invc"]
    for gi, w in enumerate((2, 4, 8, 16)):
        c[:, o + gi * 16:o + (gi + 1) * 16] = (1.0 / np.minimum(np.arange(16) + 1, w))[None, :]
    return c


def fm(v):
    v = np.asarray(v, np.float32)
    return np.ascontiguousarray(v.reshape(-1, 128).T)


def pack_host(inp):
    f = lambda k: np.asarray(inp[k], np.float32)
    pvec = np.zeros((128, PV_COLS), np.float32)

    def put(name, arr):
        arr = np.asarray(arr, np.float32)
        pvec[:arr.shape[0], PV_OFF[name]:PV_OFF[name] + arr.shape[1]] = arr

    for l in range(2):
        put(f"ffn_norm{l}0", fm(f("ffn1_norm")[l]))
        put(f"ffn_norm{l}1", fm(f("ffn2_norm")[l]))
        put(f"mix_norm{l}", fm(f("mix_norm")[l]))
    put("ln_g", fm(f("a_ln_g")[0]))
    put("ln_b", fm(f("a_ln_b")[0]))
    cw = f("b_conv_w")[0]
    put("conv_w", cw.reshape(4, 12, 128).transpose(2, 1, 0).reshape(128, 48))
    put("conv_b", fm(f("b_conv_b")[0]))
    put("norm_g", fm(f("b_norm_g")[0]))
    put("dskip", fm(np.repeat(f("b_d_skip")[0], 64)))
    put("c_scale", fm(f("c_scale")[0]))
    put("q_norm", np.tile(f("d_q_norm")[0], 2).reshape(128, 1))
    put("k_norm", np.tile(f("d_k_norm")[0], 2).reshape(128, 1))
    put("dt_bias", f("b_dt_bias")[0].reshape(16, 1))
    put("a_log", f("b_a_log")[0].reshape(16, 1))

    wgu = np.empty((2, 2, 11, 128, 8, 2, 256), np.float32)
    wdn = np.empty((2, 2, 8, 128, FC, 128), np.float32)
    for l in range(2):
        for w, (kgu, kdn) in enumerate((("ffn1_w_gu", "ffn1_w_down"), ("ffn2_w_gu", "ffn2_w_down"))):
            g = f(kgu)[l].reshape(8, 128, 2, 11, 256)
            wgu[l, w] = g.transpose(3, 1, 0, 2, 4)
            d = f(kdn)[l].reshape(FC, 128, 8, 128)
            wdn[l, w] = d.transpose(2, 1, 0, 3)

    def fm_w(Wc):
        n = Wc.shape[1]
        return np.ascontiguousarray(Wc.reshape(8, 128, n).transpose(1, 0, 2).reshape(128, 8 * n))

    Wi = f("ev_w_in")[0]
    col_groups = [(0, 512), (512, 1024), (2048, 2560), (2560, 3072), (3072, 3584), (3584, 4096), (4096, 4608)]
    wev_fm = np.stack([fm_w(Wi[:, a:b]) for a, b in col_groups], 0)
    wev_v = np.stack([fm_w(Wi[:, 1024:1536]), fm_w(Wi[:, 1536:2048])], 0)
    wev_dt = fm_w(Wi[:, 4608:4624])
    Wo = f("ev_w_out")[0]
    wev_out = np.ascontiguousarray(Wo.reshape(16, 128, 4, 256).transpose(2, 1, 0, 3).reshape(4, 128, 16 * 256))
    Wd = f("od_w_in")[0]
    ws = f("a_w_s")[0]
    wsT = np.ascontiguousarray(ws.transpose(2, 0, 1).reshape(128, 8 * 128))
    wsT4 = np.ascontiguousarray(ws[:, :4, :4].transpose(2, 0, 1).reshape(4, 32))
    shared = {
        "pvec": pvec,
        "cst": make_consts(),
        "wgu": wgu.reshape(2, 2, 11, 128, 8 * 2 * 256),
        "wdn": wdn.reshape(2, 2, 8, 128, FC * 128),
        "wev_fm": wev_fm, "wev_v": wev_v, "wev_dt": wev_dt, "wev_out": wev_out,
        "wsT": wsT, "wsT4": wsT4,
        "bs_row": f("a_b_s")[0].reshape(1, 1024),
        "lnrow": np.concatenate([f("a_ln_g")[0], f("a_ln_b")[0]]).reshape(1, 2048),
        "wod_fm": np.stack([fm_w(Wd[:, a:a + 512]) for a in (0, 512, 1024, 1536)], 0),
        "wod_k": fm_w(Wd[:, 2048:2304]), "wod_v": fm_w(Wd[:, 2304:2560]),
        "wod_lin": np.ascontiguousarray(f("c_lin_w")[0].reshape(4, 2, 128, 256).transpose(2, 0, 1, 3).reshape(128, 2048)),
        "wod_out": np.ascontiguousarray(f("od_w_out")[0].reshape(16, 128, 4, 256).transpose(2, 1, 0, 3).reshape(4, 128, 16 * 256)),
        "rel_tab": f("rel_bias_table"), "sink_row": f("d_sinks")[0].reshape(1, 16),
    }
    return shared


def per_core_inputs(inp, c):
    f = lambda k: np.asarray(inp[k], np.float32)
    sl = slice(16 * c, 16 * c + 16)
    return {
        "xs": np.ascontiguousarray(f("x_sample")[sl].reshape(NS, D)),
        "st_ssm": np.ascontiguousarray(f("state_ssm")[0, sl].reshape(16, 1024, 128)),
        "st_conv": np.ascontiguousarray(f("state_conv")[0, sl].reshape(48, 1536)),
        "st_pool": np.ascontiguousarray(f("state_pool")[0, sl].reshape(240, 1024)),
        "st_k": np.ascontiguousarray(f("cache_k_win")[0, sl].reshape(16, 128, 256)),
        "st_v": np.ascontiguousarray(f("cache_v_win")[0, sl].reshape(16, 128, 256)),
    }


_CACHE = {}


def kernel(**inputs):
    cfg = {}
    key = "full"
    if key not in _CACHE:
        _CACHE[key] = build_program(cfg)
    P = _CACHE[key]
    shared = pack_host(inputs)
    xp = np.asarray(inputs["x_prompt"], np.float32)
    in_maps = []
    for c in range(NCORES):
        m = dict(shared)
        m["xp"] = np.ascontiguousarray(xp[c])
        m.update(per_core_inputs(inputs, c))
        in_maps.append(m)
    res = run_bass_kernel_spmd(P.nc, in_maps, core_ids=list(range(NCORES)))
    r = res.results
    y_p = np.stack([r[c]["yp"] for c in range(NCORES)], 0)
    y_s = np.stack([r[c]["ys"] for c in range(NCORES)], 0).reshape(128, 4, D)
    cat = lambda k: np.stack([r[c][k] for c in range(NCORES)], 0)
    av = cat("o_av").reshape(1, 128, 4, D)
    ssm_p = cat("o_ssm_p").reshape(1, 8, 16, 64, 128)
    ssm_s = cat("o_ssm_s").reshape(1, 128, 16, 64, 128)
    conv_p = cat("o_conv_p").reshape(1, 8, 3, 1536)
    conv_s = cat("o_conv_s").reshape(1, 128, 3, 1536)
    pool_p = cat("o_pool_p").reshape(1, 8, 15, 1024)
    pool_s = cat("o_pool_s").reshape(1, 128, 15, 1024)
    k_p = cat("o_k_p").reshape(1, 8, 128, 4, 64)
    k_s = cat("o_k_s").reshape(1, 128, 128, 4, 64)
    v_p = cat("o_v_p").reshape(1, 8, 128, 4, 64)
    v_s = cat("o_v_s").reshape(1, 128, 128, 4, 64)
    return (y_p, y_s, av, ssm_p, ssm_s, conv_p, conv_s, pool_p, pool_s, k_p, k_s, v_p, v_s)
```
